# Optimizing a Trainium2 kernel written in Bass

```python
import jax
import jax.numpy as jnp
from jax import lax
import numpy as np

D_MODEL = 1024
BATCH = 8
SEQ = 2048
DEPTH = 2
DEC_BATCH = 128
DEC_SEQ = 8
PAST_LEN = 2048
PAGE_SIZE = 128

H_A = 4
DK_A = D_MODEL // H_A
DV_A = 2 * DK_A
RET_CHUNK = 128
H_B = 16
DH_B = D_MODEL // H_B
G_B = 4
HPG = H_B // G_B
L_CMP = 32
S_CMP = 16
L_SEL = 64
N_SEL = 16
WINDOW = 512
CMP_HID = 2 * DH_B
Q_BLOCK = 128
FORCE_BONUS = 1e4
D_FF = 2816
CONV_W = 3
EPS = 1e-6

kernel_name = 'yoco_retention_nsa_decode_step'


def rmsnorm(x, g):
    xf = x.astype(jnp.float32)
    y = xf * lax.rsqrt(jnp.mean(xf * xf, axis=-1, keepdims=True) + EPS)
    return (y * g.astype(jnp.float32)).astype(x.dtype)


def modulate(h, shift, scale):
    return h * (1.0 + scale[:, None, :]) + shift[:, None, :]


def masked_softmax(s, mask):
    s = jnp.where(mask, s.astype(jnp.float32), -jnp.inf)
    m = jnp.max(s, axis=-1, keepdims=True)
    m = jnp.where(jnp.isfinite(m), m, 0.0)
    e = jnp.where(mask, jnp.exp(s - m), 0.0)
    return e / jnp.maximum(jnp.sum(e, axis=-1, keepdims=True), 1e-30)


def alibi_slopes():
    return jnp.exp2(-8.0 * jnp.arange(1, H_B + 1, dtype=jnp.float32) / H_B)


def retention_log_decay():
    return jnp.log1p(-jnp.exp2(-5.0 - jnp.arange(H_A, dtype=jnp.float32)))


def retention_chunk(S, q, k, v, log_g):
    c = q.shape[1]
    i = jnp.arange(c, dtype=jnp.float32)
    diff = i[:, None] - i[None, :]
    dmask = jnp.where(diff >= 0, jnp.exp(jnp.maximum(diff, 0.0)[None] * log_g[:, None, None]), 0.0)
    scores = jnp.einsum('bihd,bjhd->bhij', q, k) * dmask
    o = jnp.einsum('bhij,bjhv->bihv', scores, v)
    o = o + jnp.einsum('bihd,bhdv->bihv', q, S) * jnp.exp((i[:, None] + 1.0) * log_g[None, :])[None, :, :, None]
    w = jnp.exp((c - 1.0 - i)[:, None] * log_g[None, :])
    kw = k * w[None, :, :, None]
    S_new = jnp.exp(c * log_g)[None, :, None, None] * S + jnp.einsum('bjhd,bjhv->bhdv', kw, v)
    return S_new, o


def retention_mixer(h, S0, w_in, w_out):
    b, t, _ = h.shape
    nq = H_A * DK_A
    nv = H_A * DV_A
    pr = h @ w_in
    q = pr[..., :nq].reshape(b, t, H_A, DK_A)
    k = pr[..., nq:2 * nq].reshape(b, t, H_A, DK_A) * (DK_A ** -0.5)
    v = pr[..., 2 * nq:2 * nq + nv].reshape(b, t, H_A, DV_A)
    g = pr[..., 2 * nq + nv:]
    log_g = retention_log_decay()
    c = RET_CHUNK if t % RET_CHUNK == 0 else t
    nc = t // c
    if nc == 1:
        S, o = retention_chunk(S0, q, k, v, log_g)
    else:
        def to_chunks(a):
            return a.reshape(b, nc, c, *a.shape[2:]).swapaxes(0, 1)

        def step(S, inp):
            qc, kc, vc = inp
            return retention_chunk(S, qc, kc, vc, log_g)

        S, o = lax.scan(step, S0, (to_chunks(q), to_chunks(k), to_chunks(v)))
        o = o.swapaxes(0, 1).reshape(b, t, H_A, DV_A)
    of = o.astype(jnp.float32)
    of = of * lax.rsqrt(jnp.mean(of * of, axis=-1, keepdims=True) + EPS)
    y = (of.reshape(b, t, nv).astype(h.dtype) * jax.nn.silu(g)) @ w_out
    return y, S


def conv_ffn(h, buf, w_in, cw, cb, w_out):
    t = h.shape[1]
    u = h @ w_in
    ext = jnp.concatenate([buf.astype(u.dtype), u], axis=1)
    z = cb
    for j in range(CONV_W):
        z = z + cw[j] * ext[:, j:j + t]
    a, g = jnp.split(z, 2, axis=-1)
    return (jax.nn.silu(g) * a) @ w_out, ext[:, -(CONV_W - 1):]


def compress_branch(rows, pe, w1, w2):
    bx, L = rows.shape[:2]
    r_n = L_CMP // S_CMP
    n_pc = L // S_CMP
    nb = n_pc - r_n + 1
    P = rows[:, :n_pc * S_CMP].reshape(bx, n_pc, S_CMP, G_B, DH_B).transpose(0, 1, 3, 2, 4)
    pe_r = pe.reshape(r_n, S_CMP, DH_B)
    w1_r = w1.reshape(r_n, S_CMP * DH_B, CMP_HID)
    z = 0.0
    for r in range(r_n):
        part = (P[:, r:r + nb] + pe_r[r][None, None, None]).reshape(bx, nb, G_B, S_CMP * DH_B)
        z = z + part @ w1_r[r]
    return jax.nn.silu(z) @ w2


def sel_overlap(nb, nsel):
    i = jnp.arange(nb)[:, None] * S_CMP
    j = jnp.arange(nsel)[None, :] * L_SEL
    return ((i <= j + L_SEL - 1) & (i + L_CMP - 1 >= j)).astype(jnp.float32)


def pad_to_sel(rows):
    L = rows.shape[1]
    nsel = -(-L // L_SEL)
    return jnp.pad(rows, ((0, 0), (0, nsel * L_SEL - L), (0, 0), (0, 0)))


def nsa_block(q, qpos, gates, kc, vc, cend, ks, vs, kw, vw, wpos):
    tq = q.shape[0]
    nb = kc.shape[0]
    nsel = ks.shape[0] // L_SEL
    slopes = alibi_slopes().reshape(G_B, HPG)[None, :, :, None]
    qg = q.reshape(tq, G_B, HPG, DH_B) * (DH_B ** -0.5)
    tpos = qpos[:, None, None, None]
    s = jnp.einsum('tghd,ngd->tghn', qg, kc).astype(jnp.float32) - slopes * (tpos - cend[None, None, None, :])
    ok = (cend[None, :] <= qpos[:, None])[:, None, None, :]
    p_cmp = masked_softmax(s, ok)
    o_cmp = jnp.einsum('tghn,ngd->tghd', p_cmp.astype(vc.dtype), vc)
    imp = jnp.einsum('tghn,nj->tgj', p_cmp, sel_overlap(nb, nsel))
    blk = jnp.arange(nsel)[None, :]
    cur = (qpos // L_SEL)[:, None]
    forced = (blk == 0) | (blk == cur) | (blk == cur - 1)
    score = jnp.where((blk <= cur)[:, None, :], imp + FORCE_BONUS * forced[:, None, :].astype(jnp.float32), -jnp.inf)
    _, sel = lax.top_k(score, min(N_SEL, nsel))
    g_idx = jnp.arange(G_B)[None, :, None]
    k_blk = ks.reshape(nsel, L_SEL, G_B, DH_B).transpose(2, 0, 1, 3)
    v_blk = vs.reshape(nsel, L_SEL, G_B, DH_B).transpose(2, 0, 1, 3)
    n_k = sel.shape[-1] * L_SEL
    k_sel = k_blk[g_idx, sel].reshape(tq, G_B, n_k, DH_B)
    v_sel = v_blk[g_idx, sel].reshape(tq, G_B, n_k, DH_B)
    spos = (sel[..., None] * L_SEL + jnp.arange(L_SEL)).reshape(tq, G_B, n_k)[:, :, None, :]
    s = jnp.einsum('tghd,tgsd->tghs', qg, k_sel).astype(jnp.float32) - slopes * (tpos - spos)
    p_slc = masked_softmax(s, spos <= tpos)
    o_slc = jnp.einsum('tghs,tgsd->tghd', p_slc.astype(v_sel.dtype), v_sel)
    dist = qpos[:, None] - wpos[None, :]
    okw = ((dist >= 0) & (dist <= WINDOW) & (wpos[None, :] >= 0))[:, None, None, :]
    s = jnp.einsum('tghd,sgd->tghs', qg, kw).astype(jnp.float32) - slopes * dist[:, None, None, :]
    p_win = masked_softmax(s, okw)
    o_win = jnp.einsum('tghs,sgd->tghd', p_win.astype(vw.dtype), vw)
    g = gates.reshape(tq, G_B, HPG, 3).astype(jnp.float32)
    out = g[..., 0:1] * o_cmp + g[..., 1:2] * o_slc + g[..., 2:3] * o_win
    return out.reshape(tq, H_B * DH_B).astype(q.dtype)


def nsa_prompt(q, gates, kc, vc, cend, ks, vs, kw, vw):
    nb_, t_ = q.shape[:2]
    nqb = t_ // Q_BLOCK
    pad = ((0, 0), (WINDOW, 0), (0, 0), (0, 0))
    kw_p = jnp.pad(kw, pad)
    vw_p = jnp.pad(vw, pad)
    span = WINDOW + Q_BLOCK

    def item(n):
        b = n // nqb
        s0 = (n % nqb) * Q_BLOCK
        qpos = s0 + jnp.arange(Q_BLOCK)
        wpos = s0 - WINDOW + jnp.arange(span)
        return nsa_block(lax.dynamic_slice_in_dim(q[b], s0, Q_BLOCK, 0), qpos,
                         lax.dynamic_slice_in_dim(gates[b], s0, Q_BLOCK, 0),
                         kc[b], vc[b], cend, ks[b], vs[b],
                         lax.dynamic_slice_in_dim(kw_p[b], s0, span, 0),
                         lax.dynamic_slice_in_dim(vw_p[b], s0, span, 0), wpos)

    o = lax.map(item, jnp.arange(nb_ * nqb))
    return o.reshape(nb_, t_, H_B * DH_B)


def nsa_sample(q, gates, kc, vc, cend, ks, vs, kw, vw, qpos, wpos):
    def item(b):
        return nsa_block(q[b], qpos, gates[b], kc[b], vc[b], cend, ks[b], vs[b], kw[b], vw[b], wpos)
    return lax.map(item, jnp.arange(q.shape[0]))


def trunk(x, c, S_ret, conv_buf, past_kv, win_buf, p, sample):
    b, t, _ = x.shape
    n_a = DEPTH // 2
    new_S, new_conv = [], []
    for l in range(DEPTH):
        mod = jax.nn.silu(c) @ p['w_ada'][l] + p['b_ada'][l]
        sh1, sc1, ga1, sh2, sc2, ga2 = jnp.split(mod, 6, axis=-1)
        if l == n_a:
            mkv = jax.nn.silu(c) @ p['w_ada_kv'] + p['b_ada_kv']
            sh_kv, sc_kv = jnp.split(mkv, 2, axis=-1)
            hk = modulate(rmsnorm(x, p['g_kv']), sh_kv, sc_kv)
            kv = (hk @ p['w_kv']).reshape(b, t, 6, G_B, DH_B)
            kv_rows, win_rows = kv[:, :, :4], kv[:, :, 4:]
            if sample:
                ctx = jnp.concatenate([past_kv.astype(kv.dtype), kv_rows], axis=1)
                win_ctx = jnp.concatenate([win_buf.astype(kv.dtype), win_rows], axis=1)
                past = past_kv.shape[1]
                qpos = past + jnp.arange(t)
                wpos = past - win_buf.shape[1] + jnp.arange(win_ctx.shape[1])
            else:
                ctx = kv_rows
                win_ctx = win_rows
            kc = compress_branch(ctx[:, :, 0], p['pe_ck'], p['w_ck1'], p['w_ck2'])
            vc = compress_branch(ctx[:, :, 1], p['pe_cv'], p['w_cv1'], p['w_cv2'])
            cend = jnp.arange(kc.shape[1]) * S_CMP + (L_CMP - 1)
            ks = pad_to_sel(ctx[:, :, 2])
            vs = pad_to_sel(ctx[:, :, 3])
            kw, vw = win_ctx[:, :, 0], win_ctx[:, :, 1]
            new_win = win_ctx[:, -min(WINDOW, win_ctx.shape[1]):]
        h = modulate(rmsnorm(x, p['g_mix'][l]), sh1, sc1)
        if l < n_a:
            y, S = retention_mixer(h, S_ret[l], p['w_ret_in'][l], p['w_ret_out'][l])
            new_S.append(S)
        else:
            lb = l - n_a
            pr = h @ p['w_nsa_in'][lb]
            q = pr[..., :H_B * DH_B].reshape(b, t, H_B, DH_B)
            gates = jax.nn.sigmoid(pr[..., H_B * DH_B:].astype(jnp.float32)).reshape(b, t, H_B, 3)
            if sample:
                o = nsa_sample(q, gates, kc, vc, cend, ks, vs, kw, vw, qpos, wpos)
            else:
                o = nsa_prompt(q, gates, kc, vc, cend, ks, vs, kw, vw)
            y = o @ p['w_nsa_out'][lb]
        x = x + ga1[:, None, :] * y
        h = modulate(rmsnorm(x, p['g_ffn'][l]), sh2, sc2)
        y, buf = conv_ffn(h, conv_buf[l], p['w_ffn_in'][l], p['conv_w'][l], p['conv_b'][l], p['w_ffn_out'][l])
        new_conv.append(buf)
        x = x + ga2[:, None, :] * y
    return rmsnorm(x, p['g_final']), jnp.stack(new_S), jnp.stack(new_conv), kv_rows, new_win


def setup_inputs(seed: int = 0) -> dict:
    key = jax.random.key(seed)
    kit = iter(list(jax.random.split(key, 40)))
    n_a = DEPTH // 2
    n_b = DEPTH - n_a
    n_pages = PAST_LEN // PAGE_SIZE
    n_used = DEC_BATCH * n_pages
    n_phys = n_used + max(1, n_used // 4)
    w_buf = min(WINDOW, PAST_LEN)
    d = D_MODEL
    f2 = 2 * D_FF
    ret_proj = H_A * (2 * DK_A + 2 * DV_A)

    def nrm(shape, scale):
        return jax.random.normal(next(kit), shape, jnp.float32) * scale

    x_prompt = nrm((BATCH, SEQ, d), 1.0)
    x_sample = nrm((DEC_BATCH, DEC_SEQ, d), 1.0)
    c_prompt = nrm((BATCH, d), 1.0)
    c_sample = nrm((DEC_BATCH, d), 1.0)
    state_ret = nrm((n_a, DEC_BATCH, H_A, DK_A, DV_A), 1.0)
    state_conv = nrm((DEPTH, DEC_BATCH, CONV_W - 1, f2), 1.0)
    cache_kv = nrm((n_phys, PAGE_SIZE, 4, G_B, DH_B), 1.0)
    state_win = nrm((DEC_BATCH, w_buf, 2, G_B, DH_B), 1.0)
    page_table = jax.random.permutation(next(kit), n_phys)[:n_used].reshape(DEC_BATCH, n_pages).astype(jnp.int32)
    return {
        'x_prompt': x_prompt, 'x_sample': x_sample, 'c_prompt': c_prompt, 'c_sample': c_sample,
        'state_ret': state_ret, 'state_conv': state_conv, 'cache_kv': cache_kv, 'state_win': state_win,
        'page_table': page_table,
        'w_ada': nrm((DEPTH, d, 6 * d), 0.5 * d ** -0.5),
        'b_ada': nrm((DEPTH, 6 * d), 0.02),
        'g_mix': 1.0 + nrm((DEPTH, d), 0.05),
        'g_ffn': 1.0 + nrm((DEPTH, d), 0.05),
        'w_ffn_in': nrm((DEPTH, d, f2), d ** -0.5),
        'conv_w': nrm((DEPTH, CONV_W, f2), CONV_W ** -0.5),
        'conv_b': nrm((DEPTH, f2), 0.02),
        'w_ffn_out': nrm((DEPTH, D_FF, d), D_FF ** -0.5),
        'w_ret_in': nrm((n_a, d, ret_proj), d ** -0.5),
        'w_ret_out': nrm((n_a, H_A * DV_A, d), (H_A * DV_A) ** -0.5),
        'g_kv': 1.0 + nrm((d,), 0.05),
        'w_ada_kv': nrm((d, 2 * d), 0.5 * d ** -0.5),
        'b_ada_kv': nrm((2 * d,), 0.02),
        'w_kv': nrm((d, 6 * G_B * DH_B), d ** -0.5),
        'pe_ck': nrm((L_CMP, DH_B), 0.1),
        'pe_cv': nrm((L_CMP, DH_B), 0.1),
        'w_ck1': nrm((L_CMP * DH_B, CMP_HID), (L_CMP * DH_B) ** -0.5),
        'w_ck2': nrm((CMP_HID, DH_B), CMP_HID ** -0.5),
        'w_cv1': nrm((L_CMP * DH_B, CMP_HID), (L_CMP * DH_B) ** -0.5),
        'w_cv2': nrm((CMP_HID, DH_B), CMP_HID ** -0.5),
        'w_nsa_in': nrm((n_b, d, H_B * DH_B + 3 * H_B), d ** -0.5),
        'w_nsa_out': nrm((n_b, H_B * DH_B, d), (H_B * DH_B) ** -0.5),
        'g_final': 1.0 + nrm((d,), 0.05),
    }


def reference(x_prompt, x_sample, c_prompt, c_sample, state_ret, state_conv, cache_kv, state_win, page_table,
              w_ada, b_ada, g_mix, g_ffn, w_ffn_in, conv_w, conv_b, w_ffn_out, w_ret_in, w_ret_out,
              g_kv, w_ada_kv, b_ada_kv, w_kv, pe_ck, pe_cv, w_ck1, w_ck2, w_cv1, w_cv2,
              w_nsa_in, w_nsa_out, g_final):
    p = dict(w_ada=w_ada, b_ada=b_ada, g_mix=g_mix, g_ffn=g_ffn, w_ffn_in=w_ffn_in, conv_w=conv_w,
             conv_b=conv_b, w_ffn_out=w_ffn_out, w_ret_in=w_ret_in, w_ret_out=w_ret_out, g_kv=g_kv,
             w_ada_kv=w_ada_kv, b_ada_kv=b_ada_kv, w_kv=w_kv, pe_ck=pe_ck, pe_cv=pe_cv, w_ck1=w_ck1,
             w_ck2=w_ck2, w_cv1=w_cv1, w_cv2=w_cv2, w_nsa_in=w_nsa_in, w_nsa_out=w_nsa_out, g_final=g_final)
    n_a = DEPTH // 2
    bp = x_prompt.shape[0]
    db = x_sample.shape[0]
    past = page_table.shape[1] * PAGE_SIZE
    past_kv = cache_kv[page_table].reshape(db, past, 4, G_B, DH_B)
    S0_p = jnp.zeros((n_a, bp, H_A, DK_A, DV_A), jnp.float32)
    conv0_p = jnp.zeros((DEPTH, bp, CONV_W - 1, 2 * D_FF), x_prompt.dtype)
    y_prompt, ret_prompt, conv_prompt, kv_prompt, win_prompt = trunk(
        x_prompt, c_prompt, S0_p, conv0_p, None, None, p, False)
    y_sample, ret_sample, conv_sample, kv_sample, win_sample = trunk(
        x_sample, c_sample, state_ret.astype(jnp.float32), state_conv, past_kv, state_win, p, True)
    return (y_prompt, y_sample, ret_prompt, ret_sample, conv_prompt, conv_sample,
            kv_prompt, kv_sample, win_prompt, win_sample)
```

```python
from contextlib import ExitStack
import numpy as np
import ml_dtypes
import concourse.bass as bass
import concourse.mybir as mybir
from concourse.bass_utils import run_bass_kernel_spmd

F32 = mybir.dt.float32
BF16 = mybir.dt.bfloat16
I32 = mybir.dt.int32
AF = mybir.ActivationFunctionType
ALU = mybir.AluOpType

D = 1024
KC = 8
SEQ = 2048
ST = 256
NST = SEQ // ST
H_A, DK_A, DV_A = 4, 256, 512
D_FF = 2816
F2 = 2 * D_FF
NFC = F2 // 128
NPAIR = D_FF // 128
EPS = 1e-6
NSEQ_S = 16
DEC_SEQ = 8
N_CORES = 8
NEG = -30000.0
NPHYS = 2560
NPAGE = 16
G_B, DH = 4, 64


class Eng:
    def __init__(self, name, e, sem, step=1, is_pe=False):
        self.name, self.e, self.sem, self.step, self.is_pe = name, e, sem, step, is_pe
        self.cnt = 0
        self.waited = {}


class Buf:
    __slots__ = ("name", "w", "r")

    def __init__(self, name):
        self.name, self.w, self.r = name, None, {}


class Sched:
    def __init__(self, nc, es, n_sp=12, n_pool=12, n_act=4):
        self.nc = nc
        sem = lambda n: es.enter_context(nc.semaphore(n))
        self.pe = Eng("pe", nc.tensor, sem("s_pe"), is_pe=True)
        self.act = Eng("act", nc.scalar, sem("s_act"))
        self.dve = Eng("dve", nc.vector, sem("s_dve"))
        self.pool = Eng("pool", nc.gpsimd, sem("s_pool"))
        self.sp = Eng("sp", nc.sync, sem("s_sp"))
        self.engs = [self.pe, self.act, self.dve, self.pool, self.sp]
        self.chans = {
            "sp": [Eng(f"c_sp{i}", None, sem(f"c_sp{i}"), step=16) for i in range(n_sp)],
            "pool": [Eng(f"c_pl{i}", None, sem(f"c_pl{i}"), step=16) for i in range(n_pool)],
            "act": [Eng(f"c_ac{i}", None, sem(f"c_ac{i}"), step=16) for i in range(n_act)],
        }
        self.rr = {"sp": 0, "pool": 0, "act": 0}

    def _deps(self, reads, writes):
        deps = {}
        for b in reads:
            if b.w is not None:
                deps[b.w[0]] = max(deps.get(b.w[0], 0), b.w[1])
        for b in writes:
            if b.w is not None:
                deps[b.w[0]] = max(deps.get(b.w[0], 0), b.w[1])
            for e, n in b.r.items():
                deps[e] = max(deps.get(e, 0), n)
        return deps

    def _wait(self, eng, deps):
        for f, n in deps.items():
            if f is eng and eng.is_pe:
                continue
            if eng.waited.get(f, 0) < n:
                eng.e.wait_ge(f.sem, n * f.step)
                eng.waited[f] = n

    def op(self, eng, fn, reads=(), writes=(), inc=True):
        self._wait(eng, self._deps(reads, writes))
        ins = fn(eng.e)
        n = eng.cnt + 1
        if inc:
            ins.then_inc(eng.sem, eng.step)
            eng.cnt = n
        for b in reads:
            b.r[eng] = max(b.r.get(eng, 0), n)
        for b in writes:
            b.w = (eng, n)
            b.r = {}
        return ins

    def dma(self, issuer, out, in_, reads=(), writes=(), **kw):
        lst = self.chans[issuer.name]
        ch = lst[self.rr[issuer.name] % len(lst)]
        self.rr[issuer.name] += 1
        deps = self._deps(reads, writes)
        if ch.cnt > 0:
            deps[ch] = max(deps.get(ch, 0), ch.cnt)
        self._wait(issuer, deps)
        issuer.e.dma_start(out=out, in_=in_, **kw).then_inc(ch.sem, 16)
        ch.cnt += 1
        for b in reads:
            b.r[ch] = max(b.r.get(ch, 0), ch.cnt)
        for b in writes:
            b.w = (ch, ch.cnt)
            b.r = {}

    def dma_gather(self, issuer, out, in_, idx_ap, reads=(), writes=()):
        lst = self.chans[issuer.name]
        ch = lst[self.rr[issuer.name] % len(lst)]
        self.rr[issuer.name] += 1
        deps = self._deps(reads, writes)
        if ch.cnt > 0:
            deps[ch] = max(deps.get(ch, 0), ch.cnt)
        self._wait(issuer, deps)
        issuer.e.indirect_dma_start(out=out, out_offset=None, in_=in_,
                                    in_offset=bass.IndirectOffsetOnAxis(ap=idx_ap, axis=0)).then_inc(ch.sem, 16)
        ch.cnt += 1
        for b in reads:
            b.r[ch] = max(b.r.get(ch, 0), ch.cnt)
        for b in writes:
            b.w = (ch, ch.cnt)
            b.r = {}

    def all_srcs(self):
        out = list(self.engs)
        for l in self.chans.values():
            out += l
        return out

    def barrier(self, engs=None):
        for e in (engs or self.engs):
            deps = {f: f.cnt for f in self.all_srcs() if f.cnt > 0 and not (f is e and e.is_pe)}
            self._wait(e, deps)

    def finish(self):
        deps = {f: f.cnt for f in self.all_srcs() if f.cnt > 0 and f is not self.sp}
        self._wait(self.sp, deps)


def _consts():
    c = {}
    c["ident_f"] = np.eye(128, dtype=np.float32)
    c["ones_b"] = np.ones((128, 128), dtype=ml_dtypes.bfloat16)
    lg = np.log1p(-np.exp2(-5.0 - np.arange(H_A, dtype=np.float64)))
    i = np.arange(128, dtype=np.float64)
    qdec = np.exp(lg[:, None] * i[None, :])
    kdec = np.exp(-lg[:, None] * i[None, :]) * (DK_A ** -0.5)
    c["qdec"] = np.broadcast_to(qdec[None], (128, H_A, 128)).astype(np.float32).copy()
    c["kdec"] = np.broadcast_to(kdec[None], (128, H_A, 128)).astype(np.float32).copy()
    c["kwdec"] = (np.exp(lg[None, :] * (127.0 - i[:, None])) * (DK_A ** -0.5)).astype(np.float32)
    c["cmask"] = (i[:, None] <= i[None, :]).astype(np.float32)
    c["gam"] = np.exp(lg).astype(np.float64)
    c["gam128"] = np.exp(128.0 * lg).astype(np.float64)
    c["gam8"] = np.exp(8.0 * lg).astype(np.float64)
    t8 = (np.arange(128) % 8).astype(np.float64)
    b8 = np.arange(128) // 8
    c["qdecS"] = np.broadcast_to(np.exp(lg[:, None] * t8[None, :])[None], (128, H_A, 128)).astype(np.float32).copy()
    c["kdecS"] = np.broadcast_to((np.exp(-lg[:, None] * t8[None, :]) * (DK_A ** -0.5))[None], (128, H_A, 128)).astype(np.float32).copy()
    c["kwdecS"] = (np.exp(lg[None, :] * (7.0 - t8[:, None])) * (DK_A ** -0.5)).astype(np.float32)
    c["cmaskS"] = ((b8[:, None] == b8[None, :]) & (t8[:, None] <= t8[None, :])).astype(np.float32)
    c["rowmask"] = (b8[:, None] == np.arange(16)[None, :]).astype(np.float32)
    bf = ml_dtypes.bfloat16
    c["ident_b"] = np.eye(128, dtype=bf)
    J = ST // 128
    k128 = np.arange(128)
    cc = np.arange(ST + 128 * (J - 1))
    c["Tc"] = np.where(k128[:, None] <= cc[None, :] - 128 * (J - 1), 0.0, NEG).astype(bf)
    c["Tl"] = np.where(k128[:, None] < cc[None, :] - 128 * (J - 1), NEG, 0.0).astype(bf)
    pos = np.arange(SEQ)
    c["Tcmp"] = np.where((k128[:, None] >= 1) & (16 * k128[:, None] + 15 <= pos[None, :]), 0.0, NEG).astype(bf)
    c["Emat"] = (pos[None, :] // 64 == np.arange(32)[:, None]).astype(bf)
    ib = k128 - 1
    jb = np.arange(32)
    c["ovl"] = ((ib[:, None] >= 0) & (16 * ib[:, None] <= 64 * jb[None, :] + 63)
                & (16 * ib[:, None] + 31 >= 64 * jb[None, :])).astype(np.float32)
    cur = pos // 64
    fbt = np.where(jb[None, :] > cur[:, None], -1e30,
                   np.where((jb[None, :] == 0) | (jb[None, :] == cur[:, None]) | (jb[None, :] == cur[:, None] - 1), 1e4, 0.0))
    c["fb"] = np.ascontiguousarray(fbt.reshape(SEQ // 128, 128, 32).transpose(1, 0, 2)).astype(np.float32)

    def split3(v):
        v = np.asarray(v, dtype=np.float64)
        a = v.astype(bf).astype(np.float64)
        b = (v - a).astype(bf).astype(np.float64)
        d = (v - a - b).astype(bf).astype(np.float64)
        return a, b, d

    def kconst(p):
        a, b = (p // 64).astype(np.float64), (p % 64).astype(np.float64)
        one = np.ones_like(a)
        return np.stack([a, a, a, b, b, b, one, one, one]).astype(bf)

    c["kconst"] = kconst(pos)
    cend = np.maximum(16 * k128 + 15, 0)
    c["kconst_cmp"] = kconst(cend)
    slopes = np.exp2(-8.0 * np.arange(1, 17, dtype=np.float32) / 16).astype(np.float32).astype(np.float64)
    s1, s2, s3 = split3(slopes)
    NP = SEQ + 64
    pp = np.arange(NP, dtype=np.float64)
    v1, v2, v3 = split3(-slopes[:, None] * pp[None, :])
    qc = np.zeros((9, 16, NP), dtype=np.float64)
    for r, sv in enumerate((s1, s2, s3)):
        qc[r] = 64.0 * sv[:, None]
        qc[3 + r] = sv[:, None]
    qc[6], qc[7], qc[8] = v1, v2, v3
    c["qconst"] = qc.astype(bf)
    c["kconst_new"] = kconst(SEQ + (k128 % 8))
    colt = np.arange(32) % 8
    kb, kt_ = k128 // 8, k128 % 8
    tn = np.where((kb[:, None, None] == np.arange(16)[None, :, None]) & (kt_[:, None, None] <= colt[None, None, :]), 0.0, NEG)
    c["TnewS"] = tn.astype(bf)
    c["TlS"] = np.where(k128[:, None] < colt[None, :], NEG, 0.0).astype(bf)
    c["TcS"] = np.where(k128[:, None] >= 1, 0.0, NEG).astype(bf)
    fbs = np.zeros((8, 33), dtype=np.float32)
    fbs[:, [0, 31, 32]] = 1e4
    c["fbS"] = fbs
    c["ovl33"] = np.concatenate([c["ovl"], np.zeros((128, 1), np.float32)], axis=1)
    c["pidx"] = np.arange(128, dtype=np.float32).reshape(128, 1)
    return c


CONST = _consts()
CONST_IN = ["ident_f", "ones_b", "qdec", "kdec", "kwdec", "cmask", "qdecS", "kdecS", "kwdecS", "cmaskS", "rowmask",
            "ident_b", "Tc", "Tl", "ovl"]
CONST_DRAM = ["Tcmp", "Emat", "fb", "kconst", "kconst_cmp", "qconst", "kconst_new", "TnewS", "TlS", "TcS", "fbS", "ovl33", "pidx"]
DEV = {"cores": None, "prompt": True, "sample": True}


def build_nc():
    nc = bass.Bass("TRN2", target_bir_lowering=False)
    es = ExitStack()
    with es:
        _build(nc, es)
    return nc


def _build(nc, es):
    def din(name, shape, dt=F32):
        return nc.dram_tensor(name, list(shape), dt, kind="ExternalInput").ap()

    def dout(name, shape, dt=F32):
        return nc.dram_tensor(name, list(shape), dt, kind="ExternalOutput").ap()

    x_p = din("x_p", [SEQ, D])
    x_s = din("x_s", [NSEQ_S * DEC_SEQ, D])
    c_p = din("c_p", [1, D])
    c_s = din("c_s", [NSEQ_S, D])
    state_ret = din("state_ret", [NSEQ_S, H_A, DK_A, DV_A])
    state_conv = din("state_conv", [2, NSEQ_S * 2, F2])
    state_win = din("state_win", [NSEQ_S, 512, 512])
    cache_kv = din("cache_kv", [NPHYS * 128, 1024])
    page_table = din("page_table", [NSEQ_S, NPAGE], I32)
    w_ada = din("w_ada", [2, D, 6 * D])
    b_ada = din("b_ada", [2, 6 * D])
    g_mix = din("g_mix", [2, D])
    g_ffn = din("g_ffn", [2, D])
    w_ffn_in = din("w_ffn_in", [2, D, F2])
    conv_w = din("conv_w", [2, 3, F2])
    conv_b = din("conv_b", [2, F2])
    w_ffn_out = din("w_ffn_out", [2, D_FF, D])
    w_ret_in = din("w_ret_in", [1, D, 6144])
    w_ret_out = din("w_ret_out", [1, 2048, D])
    g_final = din("g_final", [D])
    g_kv = din("g_kv", [D])
    w_ada_kv = din("w_ada_kv", [D, 2 * D])
    b_ada_kv = din("b_ada_kv", [2 * D])
    w_kv = din("w_kv", [D, 1536])
    w_nsa_in = din("w_nsa_in", [1, D, 1072])
    w_nsa_out = din("w_nsa_out", [1, D, D])
    pe_ck = din("pe_ck", [32, 64])
    pe_cv = din("pe_cv", [32, 64])
    w_ck1 = din("w_ck1", [2048, 128])
    w_ck2 = din("w_ck2", [128, 64])
    w_cv1 = din("w_cv1", [2048, 128])
    w_cv2 = din("w_cv2", [128, 64])
    cin = {}
    for k in CONST_IN + CONST_DRAM:
        a = CONST[k]
        cin[k] = din("k_" + k, a.shape, BF16 if a.dtype == ml_dtypes.bfloat16 else F32)

    y_p = dout("y_p", [SEQ, D])
    y_s = dout("y_s", [NSEQ_S * DEC_SEQ, D])
    ret_p = dout("ret_p", [H_A, DK_A, DV_A])
    ret_s = dout("ret_s", [NSEQ_S, H_A, DK_A, DV_A])
    conv_p = dout("conv_p", [2, 2, F2])
    conv_s = dout("conv_s", [2, NSEQ_S, 2, F2])
    kv_p = dout("kv_p", [SEQ, 1024])
    kv_s = dout("kv_s", [NSEQ_S * DEC_SEQ, 1024])
    win_p = dout("win_p", [512, 512])
    win_s = dout("win_s", [NSEQ_S, 512, 512])

    S = Sched(nc, es)
    pe, act, dve, pool, sp = S.pe, S.act, S.dve, S.pool, S.sp
    uid = [0]

    def sb(name, shape, dt=F32, stack=es):
        uid[0] += 1
        t = stack.enter_context(nc.sbuf_tensor(f"{name}_{uid[0]}", list(shape), dt))
        return t, Buf(name)

    banks = []
    for i in range(8):
        t = es.enter_context(nc.psum_tensor(f"bank{i}", [128, 512], F32))
        banks.append((t, Buf(f"bank{i}")))
    bank_rr = [0]

    NROT = [7]

    def next_bank():
        b = banks[bank_rr[0] % NROT[0]]
        bank_rr[0] += 1
        return b

    RBANK = banks[7]
    RBANKS = banks[4:8]

    ctab = {}
    for k in CONST_IN:
        a = CONST[k]
        t, b = sb(k, a.shape, BF16 if a.dtype == ml_dtypes.bfloat16 else F32)
        S.dma(sp, t[:], cin[k], writes=[b])
        ctab[k] = (t, b)
    ident_f, b_ident = ctab["ident_f"]
    ones_b, b_ones = ctab["ones_b"]
    b_ctab = [ctab[k][1] for k in CONST_IN]

    xT, b_xT = sb("xT", [128, KC, ST])
    hT, b_hT = sb("hT", [128, KC, ST], BF16)
    WB = 8192
    NWB = 3
    wbufs = [sb(f"wbuf{i}", [128, WB], BF16) for i in range(NWB)]
    wb_rr = [0]

    def next_wbuf():
        w = wbufs[wb_rr[0] % NWB]
        wb_rr[0] += 1
        return w

    NM = 1 + NSEQ_S
    modall, b_mod = sb("modall", [128, 2, 6 * KC, NM])
    gmul, b_gmul = sb("gmul", [128, 2, 2, KC, NM])
    gvec, b_gvec = sb("gvec", [128, 6, KC])
    modkv, b_modkv = sb("modkv", [128, 2 * KC, NM])
    gmkv, b_gmkv = sb("gmkv", [128, KC, NM])
    gfin, b_gfin = sb("gfin", [128, KC, NM])
    zero17, b_zero17 = sb("zero17", [128, KC, NM])
    cwt, b_cwt = sb("cwt", [128, 2, 3, NFC])
    cbt, b_cbt = sb("cbt", [128, 2, NFC])
    uhalo, b_uhalo = sb("uhalo", [128, 2, NFC, 2])
    b_Sst, b_Sbf, b_cbufT = Buf("Sst"), Buf("Sbf"), Buf("cbufT")
    P = {}
    b_modall = [b_mod, b_gmul, b_gvec, b_modkv, b_gmkv, b_gfin, b_zero17]

    for l in range(2):
        S.dma(sp, cwt[:, l], conv_w[l].rearrange("t (c p) -> p t c", p=128), writes=[b_cwt],
              allow_slow_non_contiguous=True)
        S.dma(sp, cbt[:, l], conv_b[l].rearrange("(c p) -> p c", p=128), writes=[b_cbt],
              allow_slow_non_contiguous=True)
    for i, src in enumerate([g_mix[0], g_mix[1], g_ffn[0], g_ffn[1], g_final, g_kv]):
        S.dma(sp, gvec[:, i], src.rearrange("(c p) -> p c", p=128), writes=[b_gvec],
              allow_slow_non_contiguous=True)
    S.op(dve, lambda e: e.memset(uhalo[:], 0.0), writes=[b_uhalo])
    S.op(dve, lambda e: e.memset(zero17[:], 0.0), writes=[b_zero17])

    def bc(ap2, n):
        return ap2.unsqueeze(2).broadcast_to([128, ap2.shape[1], n])

    with ExitStack() as ph:
        cT, b_cT = sb("cT", [128, KC, NM], F32, ph)
        cTb, b_cTb = sb("cTb", [128, KC, NM], BF16, ph)
        badT, b_badT = sb("badT", [128, 2, 6 * KC], F32, ph)
        bkvT, b_bkvT = sb("bkvT", [128, 2 * KC], F32, ph)
        S.dma(sp, cT[:, :, 0], c_p[0].rearrange("(c p) -> p c", p=128), writes=[b_cT], allow_slow_non_contiguous=True)
        for sq_ in range(NSEQ_S):
            S.dma(sp, cT[:, :, 1 + sq_], c_s[sq_].rearrange("(c p) -> p c", p=128), writes=[b_cT], allow_slow_non_contiguous=True)
        for l in range(2):
            S.dma(sp, badT[:, l], b_ada[l].rearrange("(c p) -> p c", p=128), writes=[b_badT],
                  allow_slow_non_contiguous=True)
        S.dma(sp, bkvT[:], b_ada_kv.rearrange("(c p) -> p c", p=128), writes=[b_bkvT], allow_slow_non_contiguous=True)
        S.op(act, lambda e: e.activation(cTb[:], cT[:], AF.Silu), reads=[b_cT], writes=[b_cTb])

        def mod_block(wsrc, dst3, bias2):
            wt, b_w = next_wbuf()
            wv = wt[:, 0:KC * 1024].rearrange("p (k n) -> p k n", k=KC)
            S.dma(pool, wv, wsrc.rearrange("(k p) n -> p k n", p=128), writes=[b_w])
            pt, b_p = next_bank()
            for oc in range(8):
                for k in range(KC):
                    S.op(pe, lambda e, oc=oc, k=k: e.matmul(pt[:, oc * NM:(oc + 1) * NM], wv[:, k, oc * 128:(oc + 1) * 128],
                                                          cTb[:, k, :], start=(k == 0), stop=(k == KC - 1)),
                         reads=[b_w, b_cTb], writes=[b_p], inc=(oc == 7 and k == KC - 1))
            S.op(dve, lambda e: e.tensor_tensor(dst3, pt[:, 0:KC * NM].rearrange("p (k n) -> p k n", k=KC),
                                                bc(bias2, NM), op=ALU.add),
                 reads=[b_p, b_badT, b_bkvT], writes=b_modall)

        for l in range(2):
            for blk in range(6):
                mod_block(w_ada[l, :, blk * 1024:(blk + 1) * 1024], modall[:, l, blk * KC:(blk + 1) * KC, :],
                          badT[:, l, blk * KC:(blk + 1) * KC])
        for blk in range(2):
            mod_block(w_ada_kv[:, blk * 1024:(blk + 1) * 1024], modkv[:, blk * KC:(blk + 1) * KC, :],
                      bkvT[:, blk * KC:(blk + 1) * KC])
        for l in range(2):
            for sub in range(2):
                gi = l if sub == 0 else 2 + l
                sc = modall[:, l, (1 + 3 * sub) * KC:(2 + 3 * sub) * KC, :]
                S.op(dve, lambda e, l=l, sub=sub, gi=gi, sc=sc: e.scalar_tensor_tensor(
                    gmul[:, l, sub], sc, 1.0, bc(gvec[:, gi], NM), op0=ALU.add, op1=ALU.mult),
                    reads=b_modall, writes=b_modall)
        S.op(dve, lambda e: e.scalar_tensor_tensor(gmkv[:], modkv[:, KC:2 * KC, :], 1.0, bc(gvec[:, 5], NM),
                                                   op0=ALU.add, op1=ALU.mult), reads=b_modall, writes=b_modall)
        S.op(dve, lambda e: e.tensor_copy(gfin[:], bc(gvec[:, 4], NM)), reads=b_modall, writes=b_modall)
        S.barrier()

    def modv(l, which):
        return modall[:, l, which * KC:(which + 1) * KC, :]

    def v3(ap2):
        return ap2.rearrange("p (b t) -> p b t", t=DEC_SEQ)

    def load_xT(src_rows, ntok):
        with ExitStack() as ph:
            xin, b_xin = sb("xin", [128, ntok // 128, D], F32, ph)
            S.dma(sp, xin[:], src_rows.rearrange("(t p) d -> p t d", p=128), writes=[b_xin])
            for t in range(ntok // 128):
                for half in range(2):
                    pt, b_p = next_bank()
                    for j in range(4):
                        k = half * 4 + j
                        S.op(pe, lambda e, t=t, k=k, j=j: e.transpose(pt[:, j * 128:(j + 1) * 128],
                                                                    xin[:, t, k * 128:(k + 1) * 128], ident_f[:]),
                             reads=[b_xin, b_ident], writes=[b_p], inc=(j == 3))
                    S.op(act, lambda e, t=t, half=half: e.activation(
                        xT[:, half * 4:half * 4 + 4, t * 128:(t + 1) * 128],
                        pt[:].rearrange("p (k n) -> p k n", k=4), AF.Copy), reads=[b_p], writes=[b_xT])
            S.barrier()

    def store_yT(dst_rows, ntok, src, b_src):
        with ExitStack() as ph:
            yo, b_yo = sb("yo", [128, ntok // 128, D], F32, ph)
            for t in range(ntok // 128):
                for half in range(2):
                    pt, b_p = next_bank()
                    for j in range(4):
                        k = half * 4 + j
                        S.op(pe, lambda e, t=t, k=k, j=j: e.transpose(pt[:, j * 128:(j + 1) * 128],
                                                                    src[:, k, t * 128:(t + 1) * 128], ident_f[:]),
                             reads=[b_src, b_ident], writes=[b_p], inc=(j == 3))
                    S.op(act, lambda e, t=t, half=half: e.activation(yo[:, t, half * 512:(half + 1) * 512], pt[:], AF.Copy),
                         reads=[b_p], writes=[b_yo])
            S.dma(sp, dst_rows.rearrange("(t p) d -> p t d", p=128), yo[:], reads=[b_yo])
            S.barrier()

    def norm_mod(ntok, sample, gm3, sh3, dst, b_dst, ph):
        sq, b_sq = sb("nm_sq", [128, KC, ntok], BF16, ph)
        rstd, b_rstd = sb("nm_rstd", [128, ntok], F32, ph)
        tmp, b_tmp = sb("nm_tmp", [128, 2, ntok], F32, ph)
        S.op(act, lambda e: e.activation(sq[:], xT[:, :, 0:ntok], AF.Square), reads=[b_xT], writes=[b_sq])
        pt, b_p = next_bank()
        for k in range(KC):
            S.op(pe, lambda e, k=k: e.matmul(pt[:, 0:ntok], ones_b[:], sq[:, k, :], start=(k == 0), stop=(k == KC - 1)),
                 reads=[b_sq, b_ones], writes=[b_p], inc=(k == KC - 1))
        S.op(act, lambda e: e.activation(rstd[:], pt[:, 0:ntok], AF.Sqrt, bias=EPS, scale=1.0 / D),
             reads=[b_p], writes=[b_rstd])
        S.op(dve, lambda e: e.reciprocal(rstd[:], rstd[:]), reads=[b_rstd], writes=[b_rstd])
        tb = [Buf("nm_t0"), Buf("nm_t1")]
        for k in range(KC):
            S.op(dve, lambda e, k=k: e.tensor_tensor(tmp[:, k % 2], xT[:, k, 0:ntok], rstd[:], op=ALU.mult),
                 reads=[b_xT, b_rstd], writes=[tb[k % 2]])
            if not sample:
                S.op(act, lambda e, k=k: e.activation(dst[:, k, 0:ntok], tmp[:, k % 2], AF.Identity,
                                                    bias=sh3[:, k, 0:1], scale=gm3[:, k, 0:1]),
                     reads=[tb[k % 2]] + b_modall, writes=[b_dst])
            else:
                S.op(dve, lambda e, k=k: e.tensor_tensor(v3(tmp[:, k % 2]), v3(tmp[:, k % 2]), bc(gm3[:, k, 1:NM], DEC_SEQ), op=ALU.mult),
                     reads=[tb[k % 2]] + b_modall, writes=[tb[k % 2]])
                S.op(dve, lambda e, k=k: e.tensor_tensor(v3(dst[:, k, 0:ntok]), v3(tmp[:, k % 2]), bc(sh3[:, k, 1:NM], DEC_SEQ), op=ALU.add),
                     reads=[tb[k % 2]] + b_modall, writes=[b_dst])

    rtmp, b_rtmp = sb("rtmp", [128, 128])

    def resid_add(oc, pt, ntok, sample, ga3, b_p):
        if not sample:
            S.op(dve, lambda e: e.scalar_tensor_tensor(xT[:, oc, 0:ntok], pt[:, 0:ntok], ga3[:, oc, 0:1], xT[:, oc, 0:ntok],
                                                       op0=ALU.mult, op1=ALU.add),
                 reads=[b_p, b_xT] + b_modall, writes=[b_xT])
        else:
            S.op(dve, lambda e: e.tensor_tensor(v3(rtmp[:]), v3(pt[:, 0:ntok]), bc(ga3[:, oc, 1:NM], DEC_SEQ), op=ALU.mult),
                 reads=[b_p] + b_modall, writes=[b_rtmp])
            S.op(dve, lambda e: e.tensor_tensor(xT[:, oc, 0:ntok], xT[:, oc, 0:ntok], rtmp[:], op=ALU.add),
                 reads=[b_rtmp, b_xT], writes=[b_xT])

    WSRC = {"w_ret_in": w_ret_in[0], "w_ret_out": w_ret_out[0], "w_ffn_in0": w_ffn_in[0], "w_ffn_in1": w_ffn_in[1],
            "w_ffn_out0": w_ffn_out[0], "w_ffn_out1": w_ffn_out[1], "w_kv": w_kv, "w_nsa_in": w_nsa_in[0],
            "w_nsa_out": w_nsa_out[0]}
    WBF, b_WBF = {}, {}
    for nm, src in WSRC.items():
        WBF[nm] = nc.dram_tensor("bf_" + nm, list(src.shape), BF16, kind="Internal").ap()
        b_WBF[nm] = Buf("bf_" + nm)
    converted = set()

    def convert_w(nm):
        src = WSRC[nm]
        K_, N_ = src.shape
        for r0 in range(0, K_, 128):
            t_, b_t = next_wbuf()
            S.dma(pool, t_[:, 0:N_], src[r0:r0 + 128, :], writes=[b_t])
            S.dma(sp, WBF[nm][r0:r0 + 128, :], t_[:, 0:N_], reads=[b_t], writes=[b_WBF[nm]])
        converted.add(nm)

    def load_w(wname, col_ranges, nk, rows_per_k=128):
        if wname not in converted:
            convert_w(wname)
        wt, b_w = next_wbuf()
        tot = sum(n for _, n in col_ranges)
        assert nk * tot <= WB
        wv = wt[0:rows_per_k, 0:nk * tot].rearrange("p (k n) -> p k n", k=nk)
        o = 0
        for c0, n in col_ranges:
            S.dma(sp, wv[:, :, o:o + n], WBF[wname][:, c0:c0 + n].rearrange("(k p) n -> p k n", p=rows_per_k),
                  reads=[b_WBF[wname]], writes=[b_w])
            o += n
        return wv, b_w

    def proj_fm(wv, b_w, col, src, b_src, ntok, nk):
        pt, b_p = next_bank()
        for k in range(nk):
            S.op(pe, lambda e, k=k: e.matmul(pt[:, 0:ntok], wv[:, k, col:col + 128], src[:, k, 0:ntok],
                                            start=(k == 0), stop=(k == nk - 1)),
                 reads=[b_w, b_src], writes=[b_p], inc=(k == nk - 1))
        return pt, b_p

    def proj_tm(wv, b_w, col, ncol, src, b_src, t, nk):
        pt, b_p = next_bank()
        for k in range(nk):
            S.op(pe, lambda e, k=k: e.matmul(pt[:, 0:ncol], src[:, k, t * 128:(t + 1) * 128], wv[:, k, col:col + ncol],
                                            start=(k == 0), stop=(k == nk - 1)),
                 reads=[b_w, b_src], writes=[b_p], inc=(k == nk - 1))
        return pt, b_p

    def retention(st, sample):
        ntok = 128 if sample else ST
        nt = ntok // 128
        sfx = "S" if sample else ""
        qdec, kdec, kwdec, cmask = (ctab[k + sfx][0] for k in ("qdec", "kdec", "kwdec", "cmask"))
        rowmask = ctab["rowmask"][0]
        NROT[0] = 4 if sample else 7
        with ExitStack() as ph:
            norm_mod(ntok, sample, gmul[:, 0, 0], modv(0, 0), hT, b_hT, ph)
            yin, b_yin = sb("yin", [128, 16, ntok], BF16, ph)
            qT, b_qT = sb("qT", [128, 2, ntok], BF16, ph)
            kT, b_kT = sb("kT", [128, 2, ntok], BF16, ph)
            gT, b_gT = sb("gT", [128, 4, ntok], BF16, ph)
            vtk, b_vtk = sb("vtk", [128, nt, DV_A], BF16, ph)
            kwt, b_kwt = sb("kwt", [128, nt, DK_A], BF16, ph)
            scT, b_scT = sb("scT", [128, 128], BF16, ph)
            oT, b_oT = sb("oT", [128, 4, ntok], F32, ph)
            osq, b_osq = sb("osq", [128, 4, ntok], BF16, ph)
            orstd, b_orstd = sb("orstd", [128, ntok], F32, ph)
            otmp, b_otmp = sb("otmp", [128, ntok], F32, ph)
            if sample:
                s0 = [sb(f"s0_{i}", [128, 2, DV_A], F32, ph) for i in range(2)]
                s0b = [sb(f"s0b_{i}", [128, 2, DV_A], BF16, ph) for i in range(2)]
                sn = [sb(f"sn_{i}", [128, 2, DV_A], F32, ph) for i in range(2)]
                kwm = [sb(f"kwm_{i}", [128, DK_A], BF16, ph) for i in range(2)]
            W = "w_ret_in"
            for h in range(H_A):
                wv, b_w = load_w(W, [(h * 256, 256), (1024 + h * 256, 256)], KC)
                for which, dstT, b_d, dec in ((0, qT, b_qT, qdec), (1, kT, b_kT, kdec)):
                    for c in range(2):
                        pt, b_p = proj_fm(wv, b_w, which * 256 + c * 128, hT, b_hT, ntok, KC)
                        S.op(dve, lambda e, c=c, dstT=dstT, dec=dec, pt=pt: e.tensor_tensor(
                            dstT[:, c, :].rearrange("p (t n) -> p t n", n=128), pt[:, 0:ntok].rearrange("p (t n) -> p t n", n=128),
                            dec[:, h, :].unsqueeze(1).broadcast_to([128, nt, 128]), op=ALU.mult),
                             reads=[b_p] + b_ctab, writes=[b_d])
                for t in range(nt):
                    pt2, b_p2 = proj_tm(wv, b_w, 256, 256, hT, b_hT, t, KC)
                    S.op(dve, lambda e, t=t, pt2=pt2: e.tensor_scalar(kwt[:, t, :], pt2[:, 0:256], kwdec[:, h:h + 1], None, op0=ALU.mult),
                         reads=[b_p2] + b_ctab, writes=[b_kwt])
                wv, b_w = load_w(W, [(4096 + h * 512, 512)], KC)
                for c in range(4):
                    pt, b_p = proj_fm(wv, b_w, c * 128, hT, b_hT, ntok, KC)
                    S.op(act, lambda e, c=c, pt=pt: e.activation(gT[:, c, :], pt[:, 0:ntok], AF.Silu), reads=[b_p], writes=[b_gT])
                wv, b_w = load_w(W, [(2048 + h * 512, 512)], KC)
                for t in range(nt):
                    pt, b_p = proj_tm(wv, b_w, 0, 512, hT, b_hT, t, KC)
                    S.op(act, lambda e, t=t, pt=pt: e.activation(vtk[:, t, :], pt[:], AF.Copy), reads=[b_p], writes=[b_vtk])
                for t in range(nt):
                    first = (not sample) and (st == 0 and t == 0)
                    ts = slice(t * 128, (t + 1) * 128)
                    pt, b_p = next_bank()
                    for c in range(2):
                        S.op(pe, lambda e, c=c: e.matmul(pt[:, 0:128], kT[:, c, ts], qT[:, c, ts], start=(c == 0), stop=(c == 1)),
                             reads=[b_kT, b_qT], writes=[b_p], inc=(c == 1))
                    S.op(dve, lambda e: e.tensor_tensor(scT[:], pt[:, 0:128], cmask[:], op=ALU.mult),
                         reads=[b_p] + b_ctab, writes=[b_scT])
                    po, b_po = RBANK
                    if not sample:
                        for vc in range(4):
                            vs = slice(vc * 128, (vc + 1) * 128)
                            S.op(pe, lambda e, vc=vc, vs=vs: e.matmul(po[:, vs], vtk[:, t, vs], scT[:], start=True, stop=first),
                                 reads=[b_vtk, b_scT], writes=[b_po], inc=(first and vc == 3))
                            if not first:
                                for c in range(2):
                                    S.op(pe, lambda e, vc=vc, vs=vs, c=c: e.matmul(po[:, vs], P['Sbf'][:, h, c, vs], qT[:, c, ts],
                                                                               start=False, stop=(c == 1)),
                                         reads=[b_Sbf, b_qT], writes=[b_po], inc=(vc == 3 and c == 1))
                    else:
                        for vc in range(4):
                            vs = slice(vc * 128, (vc + 1) * 128)
                            S.op(pe, lambda e, vc=vc, vs=vs: e.matmul(RBANKS[vc][0][:, 0:128], vtk[:, t, vs], scT[:], start=True, stop=False),
                                 reads=[b_vtk, b_scT], writes=[RBANKS[vc][1]], inc=False)
                        for b in range(NSEQ_S):
                            i3 = (h * NSEQ_S + b) % 2
                            (s0t, b_s0), (s0bt, b_s0b), (snt, b_sn), (kwmt, b_kwm) = s0[i3], s0b[i3], sn[i3], kwm[i3]
                            S.dma(sp, s0t[:], state_ret[b, h].rearrange("(c p) v -> p c v", p=128), writes=[b_s0])
                            S.op(act, lambda e, s0t=s0t, s0bt=s0bt: e.activation(s0bt[:], s0t[:], AF.Copy, scale=float(CONST["gam"][h])),
                                 reads=[b_s0], writes=[b_s0b])
                            for vc in range(4):
                                for c in range(2):
                                    lastmm = (b == NSEQ_S - 1 and vc == 3 and c == 1)
                                    S.op(pe, lambda e, vc=vc, c=c, b=b, s0bt=s0bt: e.matmul(
                                        RBANKS[vc][0][:, b * 8: b * 8 + 8], s0bt[:, c, vc * 128:(vc + 1) * 128],
                                        qT[:, c, b * 8:(b + 1) * 8], start=False, stop=(b == NSEQ_S - 1 and c == 1)),
                                        reads=[b_s0b, b_qT], writes=[RBANKS[vc][1]], inc=(lastmm or (vc == 3 and c == 1)))
                            S.op(dve, lambda e, b=b, kwmt=kwmt: e.tensor_scalar(kwmt[:], kwt[:, 0, :], rowmask[:, b:b + 1], None, op0=ALU.mult),
                                 reads=[b_kwt] + b_ctab, writes=[b_kwm])
                            for c in range(2):
                                ps_, b_ps = next_bank()
                                S.op(pe, lambda e, c=c, kwmt=kwmt, ps_=ps_: e.matmul(ps_[:], kwmt[:, c * 128:(c + 1) * 128], vtk[:, 0, :], start=True, stop=True),
                                     reads=[b_kwm, b_vtk], writes=[b_ps])
                                S.op(dve, lambda e, c=c, s0t=s0t, snt=snt, ps_=ps_: e.scalar_tensor_tensor(
                                    snt[:, c, :], s0t[:, c, :], float(CONST["gam8"][h]), ps_[:], op0=ALU.mult, op1=ALU.add),
                                    reads=[b_ps, b_s0], writes=[b_sn])
                            S.dma(sp, ret_s[b, h].rearrange("(c p) v -> p c v", p=128), snt[:], reads=[b_sn])
                    if sample:
                        for vc in range(4):
                            S.op(act, lambda e, vc=vc: e.activation(oT[:, vc, ts], RBANKS[vc][0][:, 0:128], AF.Copy),
                                 reads=[RBANKS[vc][1]], writes=[b_oT])
                    else:
                        S.op(act, lambda e: e.activation(oT[:, :, ts], po[:].rearrange("p (v n) -> p v n", v=4), AF.Copy),
                             reads=[b_po], writes=[b_oT])
                    if not sample:
                        for c in range(2):
                            ps_, b_ps = next_bank()
                            S.op(pe, lambda e, c=c, ps_=ps_: e.matmul(ps_[:], kwt[:, t, c * 128:(c + 1) * 128], vtk[:, t, :], start=True, stop=True),
                                 reads=[b_kwt, b_vtk], writes=[b_ps])
                            S.op(dve, lambda e, c=c, ps_=ps_: e.scalar_tensor_tensor(P['Sst'][:, h, c, :], P['Sst'][:, h, c, :], float(CONST["gam128"][h]),
                                                                                  ps_[:], op0=ALU.mult, op1=ALU.add),
                                 reads=[b_ps, b_Sst], writes=[b_Sst])
                        S.op(act, lambda e: e.activation(P['Sbf'][:, h], P['Sst'][:, h], AF.Copy, scale=float(CONST["gam"][h])),
                             reads=[b_Sst], writes=[b_Sbf])
                S.op(act, lambda e: e.activation(osq[:], oT[:], AF.Square), reads=[b_oT], writes=[b_osq])
                pt, b_p = next_bank()
                for vc in range(4):
                    S.op(pe, lambda e, vc=vc: e.matmul(pt[:, 0:ntok], ones_b[:], osq[:, vc, :], start=(vc == 0), stop=(vc == 3)),
                         reads=[b_osq, b_ones], writes=[b_p], inc=(vc == 3))
                S.op(act, lambda e: e.activation(orstd[:], pt[:, 0:ntok], AF.Sqrt, bias=EPS, scale=1.0 / DV_A), reads=[b_p], writes=[b_orstd])
                S.op(dve, lambda e: e.reciprocal(orstd[:], orstd[:]), reads=[b_orstd], writes=[b_orstd])
                for vc in range(4):
                    S.op(dve, lambda e, vc=vc: e.tensor_tensor(otmp[:], oT[:, vc, :], orstd[:], op=ALU.mult),
                         reads=[b_oT, b_orstd], writes=[b_otmp])
                    S.op(dve, lambda e, vc=vc: e.tensor_tensor(yin[:, h * 4 + vc, :], otmp[:], gT[:, vc, :], op=ALU.mult),
                         reads=[b_otmp, b_gT], writes=[b_yin])
            for half in range(2):
                wv, b_w = load_w("w_ret_out", [(half * 512, 512)], 16)
                for oc4 in range(4):
                    oc = half * 4 + oc4
                    pt, b_p = proj_fm(wv, b_w, oc4 * 128, yin, b_yin, ntok, 16)
                    resid_add(oc, pt, ntok, sample, modv(0, 2), b_p)
            S.barrier()


    def load_conv_bufs():
        with ExitStack() as ph:
            rows, b_rows = sb("scrows", [2 * NSEQ_S, F2], F32, ph)
            for l in range(2):
                S.dma(sp, rows[:], state_conv[l], writes=[b_rows])
                for g0 in range(0, NFC, 16):
                    n = min(16, NFC - g0)
                    pt, b_p = next_bank()
                    for j in range(n):
                        ch = g0 + j
                        S.op(pe, lambda e, j=j, ch=ch: e.transpose(pt[:, j * 32:(j + 1) * 32], rows[:, ch * 128:(ch + 1) * 128],
                                                                 ident_f[0:32, 0:32]),
                             reads=[b_rows, b_ident], writes=[b_p], inc=(j == n - 1))
                    S.op(act, lambda e, l=l, g0=g0, n=n: e.activation(P['cbufT'][:, l, g0:g0 + n, :],
                                                                    pt[:, 0:n * 32].rearrange("p (c m) -> p c m", c=n), AF.Copy),
                         reads=[b_p], writes=[b_cbufT])
            S.barrier()

    def ffn(l, st, last, sample):
        ntok = 128 if sample else ST
        NROT[0] = 7
        with ExitStack() as ph:
            norm_mod(ntok, sample, gmul[:, l, 1], modv(l, 3), hT, b_hT, ph)
            actT, b_actT = sb("actT", [128, NPAIR, ntok], BF16, ph)
            ncol = ntok + 2 if not sample else NSEQ_S * (DEC_SEQ + 2)
            ue = [sb(f"ue{i}", [128, ncol], F32, ph) for i in range(4)]
            zz = [sb(f"zz{i}", [128, ntok], F32, ph) for i in range(4)]
            sg, b_sg = sb("sg", [128, ntok], F32, ph)
            if sample:
                utok, b_utok = sb("utok", [128, F2], F32, ph)
            W = f"w_ffn_in{l}"
            for blk in range(NPAIR // 2):
                wv, b_w = load_w(W, [(blk * 256, 256), (D_FF + blk * 256, 256)], KC)
                if sample:
                    for ag in range(2):
                        pt, b_p = proj_tm(wv, b_w, ag * 256, 256, hT, b_hT, 0, KC)
                        c0 = ag * D_FF + blk * 256
                        S.op(act, lambda e, pt=pt, c0=c0: e.activation(utok[:, c0:c0 + 256], pt[:, 0:256], AF.Copy),
                             reads=[b_p], writes=[b_utok])
                for pi in range(2):
                    pair = blk * 2 + pi
                    zs = []
                    for ag in range(2):
                        chunk = pair + ag * NPAIR
                        pt, b_p = proj_fm(wv, b_w, ag * 256 + pi * 128, hT, b_hT, ntok, KC)
                        (u, b_u) = ue[(pair * 2 + ag) % 4]
                        (z, b_z) = zz[(pair * 2 + ag) % 4]
                        if not sample:
                            S.op(act, lambda e, u=u, chunk=chunk: e.activation(u[:, 0:2], uhalo[:, l, chunk, :], AF.Copy),
                                 reads=[b_uhalo], writes=[b_u])
                            S.op(act, lambda e, u=u, pt=pt: e.activation(u[:, 2:ntok + 2], pt[:, 0:ntok], AF.Copy), reads=[b_p], writes=[b_u])
                            S.op(act, lambda e, u=u, chunk=chunk: e.activation(uhalo[:, l, chunk, :], u[:, ntok:ntok + 2], AF.Copy),
                                 reads=[b_u], writes=[b_uhalo])
                            u2, u1, u0, zv = u[:, 2:ntok + 2], u[:, 1:ntok + 1], u[:, 0:ntok], z[:]
                        else:
                            u3 = u[:].rearrange("p (b t) -> p b t", t=DEC_SEQ + 2)
                            S.op(pool, lambda e, u3=u3, chunk=chunk: e.tensor_copy(
                                u3[:, :, 0:2], P['cbufT'][:, l, chunk, :].rearrange("p (b j) -> p b j", j=2)),
                                reads=[b_cbufT], writes=[b_u])
                            S.op(act, lambda e, u3=u3, pt=pt: e.activation(u3[:, :, 2:DEC_SEQ + 2], v3(pt[:, 0:ntok]), AF.Copy),
                                 reads=[b_p], writes=[b_u])
                            u2, u1, u0, zv = u3[:, :, 2:DEC_SEQ + 2], u3[:, :, 1:DEC_SEQ + 1], u3[:, :, 0:DEC_SEQ], v3(z[:])
                        if not sample:
                            S.op(act, lambda e, zv=zv, pt=pt, chunk=chunk: e.activation(zv, pt[:, 0:ntok], AF.Identity,
                                                                                     bias=cbt[:, l, chunk:chunk + 1], scale=cwt[:, l, 2, chunk:chunk + 1]),
                                 reads=[b_p, b_cwt, b_cbt], writes=[b_z])
                        else:
                            S.op(dve, lambda e, u2=u2, zv=zv, chunk=chunk: e.tensor_scalar(
                                zv, u2, cwt[:, l, 2, chunk:chunk + 1], cbt[:, l, chunk:chunk + 1],
                                op0=ALU.mult, op1=ALU.add), reads=[b_u, b_cwt, b_cbt], writes=[b_z])
                        S.op(dve, lambda e, u1=u1, zv=zv, chunk=chunk: e.scalar_tensor_tensor(
                            zv, u1, cwt[:, l, 1, chunk:chunk + 1], zv, op0=ALU.mult, op1=ALU.add),
                            reads=[b_u, b_cwt, b_z], writes=[b_z])
                        S.op(dve, lambda e, u0=u0, zv=zv, chunk=chunk: e.scalar_tensor_tensor(
                            zv, u0, cwt[:, l, 0, chunk:chunk + 1], zv, op0=ALU.mult, op1=ALU.add),
                            reads=[b_u, b_cwt, b_z], writes=[b_z])
                        zs.append((z, b_z))
                    S.op(act, lambda e, z=zs[1][0]: e.activation(sg[:], z[:], AF.Silu), reads=[zs[1][1]], writes=[b_sg])
                    S.op(dve, lambda e, z=zs[0][0], pair=pair: e.tensor_tensor(actT[:, pair, :], z[:], sg[:], op=ALU.mult),
                         reads=[zs[0][1], b_sg], writes=[b_actT])
            if sample:
                for tt in range(2):
                    S.dma(sp, conv_s[l, :, tt, :], utok[6 + tt:128:8, :], reads=[b_utok])
            elif last:
                for tt in range(2):
                    S.dma(sp, conv_p[l, tt].rearrange("(c p) -> p c", p=128), uhalo[:, l, :, tt], reads=[b_uhalo],
                          allow_slow_non_contiguous=True)
            for qt in range(4):
                wv, b_w = load_w(f"w_ffn_out{l}", [(qt * 256, 256)], NPAIR)
                for oc2 in range(2):
                    oc = qt * 2 + oc2
                    pt, b_p = proj_fm(wv, b_w, oc2 * 128, actT, b_actT, ntok, NPAIR)
                    resid_add(oc, pt, ntok, sample, modv(l, 5), b_p)
            S.barrier()

    NS = {}
    JT = ST // 128
    b_K = {k: Buf(k) for k in ("Kslc", "Kwin", "Kcmp", "Vslc", "Vwin", "Vcmp", "kraw", "vraw", "szv", "cmpw", "ntab",
                               "KslcN", "KwinN", "VslcN", "VwinN")}

    def kv_proj(st, sample):
        ntok = 128 if sample else ST
        NROT[0] = 7
        q0 = st * ST
        with ExitStack() as ph:
            norm_mod(ntok, sample, gmkv, modkv[:, 0:KC, :], hT, b_hT, ph)
            kvo, b_kvo = sb("kvo", [128, ntok // 128, 1536], F32, ph)
            for cb in range(3):
                wv, b_w = load_w("w_kv", [(cb * 512, 512)], KC)
                for t in range(ntok // 128):
                    pt, b_p = proj_tm(wv, b_w, 0, 512, hT, b_hT, t, KC)
                    S.op(act, lambda e, t=t, cb=cb, pt=pt: e.activation(kvo[:, t, cb * 512:(cb + 1) * 512], pt[:], AF.Copy),
                         reads=[b_p], writes=[b_kvo])
                if not DEV.get("nsa", True):
                    continue
                if sample:
                    fm = {0: [], 1: [(0, "KslcN", 0)], 2: [(0, "KwinN", 0)]}[cb]
                else:
                    fm = {0: [(0, "kraw", 16), (256, "vraw", 16)], 1: [(0, "Kslc", q0)], 2: [(0, "Kwin", q0)]}[cb]
                for lc, name, c0 in fm:
                    for g in range(G_B):
                        pt, b_p = next_bank()
                        for k in range(KC):
                            S.op(pe, lambda e, k=k, g=g, lc=lc, pt=pt: e.matmul(pt[0:64, 0:ntok], wv[:, k, lc + g * 64: lc + (g + 1) * 64],
                                                                              hT[:, k, 0:ntok], start=(k == 0), stop=(k == KC - 1)),
                                 reads=[b_w, b_hT], writes=[b_p], inc=(k == KC - 1))
                        S.op(dve, lambda e, g=g, name=name, c0=c0, pt=pt: e.tensor_copy(NS[name][0:64, g, c0:c0 + ntok], pt[0:64, 0:ntok]),
                             reads=[b_p], writes=[b_K[name]])
            if not sample:
                S.dma(sp, kv_p[q0:q0 + ST, :].rearrange("(t p) n -> p t n", p=128), kvo[:, :, 0:1024], reads=[b_kvo])
                if q0 >= SEQ - 512:
                    w0 = q0 - (SEQ - 512)
                    S.dma(sp, win_p[w0:w0 + ST, :].rearrange("(t p) n -> p t n", p=128), kvo[:, :, 1024:1536], reads=[b_kvo])
                if DEV.get("nsa", True):
                    for t in range(ntok // 128):
                        kt = q0 // 128 + t
                        S.op(dve, lambda e, t=t, kt=kt: e.tensor_copy(NS["Vslc"][:, kt], kvo[:, t, 768:1024].rearrange("p (g d) -> p g d", g=G_B)),
                             reads=[b_kvo], writes=[b_K["Vslc"]])
                        S.op(dve, lambda e, t=t, kt=kt: e.tensor_copy(NS["Vwin"][:, kt], kvo[:, t, 1280:1536].rearrange("p (g d) -> p g d", g=G_B)),
                             reads=[b_kvo], writes=[b_K["Vwin"]])
            else:
                if DEV.get("nsa", True):
                    S.op(dve, lambda e: e.tensor_copy(NS["VslcN"][:], kvo[:, 0, 768:1024].rearrange("p (g d) -> p g d", g=G_B)),
                         reads=[b_kvo], writes=[b_K["VslcN"]])
                    S.op(dve, lambda e: e.tensor_copy(NS["VwinN"][:], kvo[:, 0, 1280:1536].rearrange("p (g d) -> p g d", g=G_B)),
                         reads=[b_kvo], writes=[b_K["VwinN"]])
                S.dma(sp, kv_s, kvo[:, 0, 0:1024], reads=[b_kvo])
                for b in range(NSEQ_S):
                    S.dma(sp, win_s[b, 504:512, :], kvo[b * 8:(b + 1) * 8, 0, 1024:1536], reads=[b_kvo])
                    S.dma(sp, win_s[b, 0:504, :], state_win[b, 8:512, :])
            S.barrier()

    def compress_setup():
        with ExitStack() as ph:
            peT, b_peT = sb("peT", [64, 2, 32], F32, ph)
            peTb, b_peTb = sb("peTb", [64, 2, 32], BF16, ph)
            for i, src in enumerate((pe_ck, pe_cv)):
                S.dma(sp, peT[:, i, :], src.rearrange("s d -> d s"), writes=[b_peT], allow_slow_non_contiguous=True)
            S.op(dve, lambda e: e.tensor_copy(peTb[:], peT[:]), reads=[b_peT], writes=[b_peTb])
            for i, (w1, w2) in enumerate(((w_ck1, w_ck2), (w_cv1, w_cv2))):
                S.dma(pool, NS["w2"][:, i, :], w2, writes=[b_K["cmpw"]])
                wt, b_w = next_wbuf()
                wv = wt[0:64, 0:32 * 128].rearrange("p (s n) -> p s n", s=32)
                S.dma(pool, wv, w1.rearrange("(s p) n -> p s n", p=64), writes=[b_w])
                pt, b_p = next_bank()
                for sp_ in range(32):
                    S.op(pe, lambda e, sp_=sp_, i=i, wv=wv, pt=pt: e.matmul(pt[:, 0:1], wv[:, sp_, :], peTb[:, i, sp_:sp_ + 1],
                                                                         start=(sp_ == 0), stop=(sp_ == 31)),
                         reads=[b_w, b_peTb], writes=[b_p], inc=(sp_ == 31))
                S.op(dve, lambda e, i=i, pt=pt: e.tensor_copy(NS["bz"][:, i:i + 1], pt[:, 0:1]), reads=[b_p], writes=[b_K["cmpw"]])
            S.barrier()

    def compress(st, ntok=ST, s0_=None, barrier=True):
        if s0_ is None:
            s0_ = (st * ST) // 16
        nsl = ntok // 16
        with ExitStack() as ph:
            sz, b_sz = sb("sz", [128, 2, G_B * nsl], BF16, ph)
            for i, (w1, raw) in enumerate(((w_ck1, "kraw"), (w_cv1, "vraw"))):
                wt, b_w = next_wbuf()
                wv = wt[0:64, 0:32 * 128].rearrange("p (s n) -> p s n", s=32)
                S.dma(pool, wv, w1.rearrange("(s p) n -> p s n", p=64), writes=[b_w])
                pz, b_pz = next_bank()
                for sp_ in range(32):
                    rhs = NS[raw][0:64, :, sp_:sp_ + 16 * (nsl - 1) + 1:16]
                    S.op(pe, lambda e, sp_=sp_, wv=wv, rhs=rhs, pz=pz: e.matmul(pz[:, 0:G_B * nsl].rearrange("p (g m) -> p g m", g=G_B),
                                                                             wv[:, sp_, :], rhs, start=(sp_ == 0), stop=(sp_ == 31)),
                         reads=[b_w, b_K[raw]], writes=[b_pz], inc=(sp_ == 31))
                S.op(act, lambda e, i=i, pz=pz: e.activation(sz[:, i, :], pz[:, 0:G_B * nsl], AF.Silu, bias=NS["bz"][:, i:i + 1]),
                     reads=[b_pz, b_K["cmpw"]], writes=[b_sz])
                S.op(dve, lambda e, raw=raw: e.tensor_copy(NS[raw][0:64, :, 0:16], NS[raw][0:64, :, ntok:ntok + 16]),
                     reads=[b_K[raw]], writes=[b_K[raw]])
            pk, b_pk = next_bank()
            S.op(pe, lambda e: e.matmul(pk[0:64, 0:G_B * nsl], NS["w2"][:, 0, :], sz[:, 0, :], start=True, stop=True),
                 reads=[b_sz, b_K["cmpw"]], writes=[b_pk])
            S.op(act, lambda e: e.activation(NS["Kcmp"][0:64, :, s0_:s0_ + nsl], pk[0:64, 0:G_B * nsl].rearrange("p (g m) -> p g m", g=G_B), AF.Copy),
                 reads=[b_pk], writes=[b_K["Kcmp"]])
            S.op(dve, lambda e: e.tensor_copy(NS["szv"][:, :, s0_:s0_ + nsl], sz[:, 1, :].rearrange("p (g m) -> p g m", g=G_B)),
                 reads=[b_sz], writes=[b_K["szv"]])
            for g in range(G_B):
                pv_, b_pv = next_bank()
                S.op(pe, lambda e, g=g, pv_=pv_: e.matmul(pv_[:, 0:64], NS["szv"][:, g, :], NS["w2"][:, 1, :], start=True, stop=True),
                     reads=[b_K["szv"], b_K["cmpw"]], writes=[b_pv])
                S.op(act, lambda e, g=g, pv_=pv_: e.activation(NS["Vcmp"][:, g, :], pv_[:, 0:64], AF.Copy), reads=[b_pv], writes=[b_K["Vcmp"]])
            S.barrier()

    def nsa_prompt(st):
        q0 = st * ST
        NROT[0] = 4
        Tc, Tl, ovl, ident_b = (ctab[k][0] for k in ("Tc", "Tl", "ovl", "ident_b"))
        with ExitStack() as ph:
            norm_mod(ST, False, gmul[:, 1, 0], modv(1, 0), hT, b_hT, ph)
            S.barrier()
        with ExitStack() as ph:
            Qaug, b_Q = sb("Qaug", [128, 16, ST], BF16, ph)
            oTn, b_oTn = sb("oTn", [64, 16, ST], BF16, ph)
            negT, b_negT = sb("negT", [32, ST], BF16, ph)
            PTs = [sb(f"PT{i}", [128, ST], BF16, ph) for i in range(3)]
            pt_rr = [0]
            Pn = [sb(f"Pn{i}", [128, ST], F32, ph) for i in range(4)]
            ocmp, b_ocmp = sb("ocmp", [64, 4, ST], F32, ph)
            rec, b_rec = sb("rec", [128, ST], F32, ph)
            r2, b_r2 = sb("r2", [64, ST], F32, ph)
            oacc, b_oacc = sb("oacc", [64, ST], F32, ph)
            otm, b_otm = sb("otm", [64, ST], F32, ph)
            scq, b_scq = sb("scq", [128, 32], F32, ph)
            scq2, b_scq2 = sb("scq2", [128, 32], F32, ph)
            m8, b_m8 = sb("m8", [128, 8], F32, ph)
            S.dma(sp, Qaug[64:73, :, :], cin["qconst"][:, :, q0:q0 + ST], writes=[b_Q])
            for half in range(2):
                wv, b_w = load_w("w_nsa_in", [(half * 512, 512)], KC)
                for hh in range(8):
                    h = half * 8 + hh
                    pt, b_p = next_bank()
                    for k in range(KC):
                        S.op(pe, lambda e, k=k, hh=hh, pt=pt, wv=wv: e.matmul(pt[0:64, 0:ST], wv[:, k, hh * 64:(hh + 1) * 64], hT[:, k, 0:ST],
                                                                           start=(k == 0), stop=(k == KC - 1)),
                             reads=[b_w, b_hT], writes=[b_p], inc=(k == KC - 1))
                    S.op(act, lambda e, h=h, pt=pt: e.activation(Qaug[0:64, h, :], pt[0:64, 0:ST], AF.Copy, scale=DH ** -0.5),
                         reads=[b_p], writes=[b_Q])
            wg, b_wg = load_w("w_nsa_in", [(1024, 48)], KC)
            Gs, b_Gs = sb("Gs", [48, ST], F32, ph)
            dsb, b_dsb = sb("dsb", [64, ST], F32, ph)
            pgl, b_pgl = next_bank()
            for k in range(KC):
                S.op(pe, lambda e, k=k: e.matmul(pgl[0:48, 0:ST], wg[:, k, 0:48], hT[:, k, 0:ST], start=(k == 0), stop=(k == KC - 1)),
                     reads=[b_wg, b_hT], writes=[b_pgl], inc=(k == KC - 1))
            S.op(act, lambda e: e.activation(Gs[:], pgl[0:48, 0:ST], AF.Sigmoid), reads=[b_pgl], writes=[b_Gs])

            def gate(h, br):
                col = 3 * h + br
                pg, b_pg = next_bank()
                S.op(pe, lambda e, pg=pg: e.matmul(pg[0:64, 0:ST], ident_f[0:48, col:col + 1].broadcast_to([48, 64]), Gs[:],
                                                  start=True, stop=True), reads=[b_Gs, b_ident], writes=[b_pg])
                return pg, b_pg

            def score_tile(mms):
                ps_, b_ps = next_bank()
                for i, (l_, r_, rb) in enumerate(mms):
                    S.op(pe, lambda e, l_=l_, r_=r_, i=i, ps_=ps_: e.matmul(ps_[:, 0:ST], l_, r_, start=(i == 0), stop=(i == len(mms) - 1)),
                         reads=rb, writes=[b_ps], inc=(i == len(mms) - 1))
                PT, b_PT = PTs[pt_rr[0] % 3]
                pt_rr[0] += 1
                S.op(act, lambda e, PT=PT, ps_=ps_: e.activation(PT[:], ps_[:, 0:ST], AF.Exp), reads=[b_ps], writes=[b_PT])
                return PT, b_PT

            for g in range(G_B):
                for hi in range(4):
                    h = 4 * g + hi
                    PT, b_PT = score_tile([(NS["Kcmp"][0:73, g, :], Qaug[0:73, h, :], [b_K["Kcmp"], b_Q]),
                                           (ident_b[:], NS["Tcmp"][:, q0:q0 + ST], b_ctab + [b_K["ntab"]])])
                    px, b_px = next_bank()
                    S.op(pe, lambda e, PT=PT, px=px: e.matmul(px[:, 0:ST], ones_b[:], PT[:], start=True, stop=True),
                         reads=[b_PT, b_ones], writes=[b_px])
                    S.op(dve, lambda e, px=px: e.tensor_scalar(rec[:], px[:, 0:ST], 1e-30, None, op0=ALU.max), reads=[b_px], writes=[b_rec])
                    S.op(dve, lambda e: e.reciprocal(rec[:], rec[:]), reads=[b_rec], writes=[b_rec])
                    S.op(dve, lambda e, PT=PT, hi=hi: e.tensor_tensor(Pn[hi][0][:], PT[:], rec[:], op=ALU.mult),
                         reads=[b_PT, b_rec], writes=[Pn[hi][1]])
                    po_, b_po_ = next_bank()
                    S.op(pe, lambda e, PT=PT, po_=po_: e.matmul(po_[0:64, 0:ST], NS["Vcmp"][:, g, :], PT[:], start=True, stop=True),
                         reads=[b_PT, b_K["Vcmp"]], writes=[b_po_])
                    pg, b_pg = gate(h, 0)
                    S.op(dve, lambda e, pg=pg: e.tensor_tensor(r2[:], pg[0:64, 0:ST], rec[0:64, :], op=ALU.mult), reads=[b_rec, b_pg], writes=[b_r2])
                    S.op(dve, lambda e, hi=hi, po_=po_: e.tensor_tensor(ocmp[:, hi, :], po_[0:64, 0:ST], r2[:], op=ALU.mult),
                         reads=[b_po_, b_r2], writes=[b_ocmp])
                for tq in range(JT):
                    pi_, b_pi = next_bank()
                    for hi in range(4):
                        S.op(pe, lambda e, hi=hi, tq=tq, pi_=pi_: e.matmul(pi_[:, 0:32], Pn[hi][0][:, tq * 128:(tq + 1) * 128], ovl[:],
                                                                        start=(hi == 0), stop=(hi == 3)),
                             reads=[Pn[hi][1]] + b_ctab, writes=[b_pi], inc=(hi == 3))
                    S.op(dve, lambda e, tq=tq, pi_=pi_: e.tensor_tensor(scq[:], pi_[:, 0:32], NS["fb"][:, st * JT + tq, :], op=ALU.add),
                         reads=[b_pi, b_K["ntab"]], writes=[b_scq])
                    S.op(dve, lambda e: e.max(out=m8[:], in_=scq[:]), reads=[b_scq], writes=[b_m8])
                    S.op(dve, lambda e: e.match_replace(out=scq2[:], in_to_replace=m8[:], in_values=scq[:], imm_value=-1e30),
                         reads=[b_scq, b_m8], writes=[b_scq2])
                    S.op(dve, lambda e: e.max(out=m8[:], in_=scq2[:]), reads=[b_scq2], writes=[b_m8])
                    S.op(dve, lambda e: e.tensor_scalar(scq2[:], scq[:], m8[:, 7:8], None, op0=ALU.is_ge), reads=[b_scq, b_m8], writes=[b_scq2])
                    S.op(dve, lambda e: e.tensor_scalar(scq2[:], scq2[:], -NEG, NEG, op0=ALU.mult, op1=ALU.add), reads=[b_scq2], writes=[b_scq2])
                    pT_, b_pT = next_bank()
                    S.op(pe, lambda e, pT_=pT_: e.transpose(pT_[0:32, 0:128], scq2[:], ident_f[:]), reads=[b_scq2, b_ident], writes=[b_pT])
                    S.op(act, lambda e, tq=tq, pT_=pT_: e.activation(negT[:, tq * 128:(tq + 1) * 128], pT_[0:32, 0:128], AF.Copy),
                         reads=[b_pT], writes=[b_negT])
                work = []
                for hi in range(4):
                    h = 4 * g + hi
                    for bi, br in enumerate((1, 2)):
                        if br == 1:
                            tiles = list(range(0, (q0 + ST) // 128))
                            Kt, Vt, bK, bV = NS["Kslc"], NS["Vslc"], b_K["Kslc"], b_K["Vslc"]
                        else:
                            tiles = list(range(max(0, (q0 - 512) // 128), (q0 + ST) // 128))
                            Kt, Vt, bK, bV = NS["Kwin"], NS["Vwin"], b_K["Kwin"], b_K["Vwin"]
                        for idx, kt in enumerate(tiles):
                            k0 = kt * 128
                            mms = [(Kt[0:73, g, k0:k0 + 128], Qaug[0:73, h, :], [bK, b_Q])]
                            if br == 1:
                                mms.append((NS["Emat"][:, k0:k0 + 128], negT[:], [b_K["ntab"], b_negT]))
                            if k0 >= q0:
                                j = (k0 - q0) // 128
                                mms.append((ident_b[:], Tc[:, 128 * (JT - 1 - j): 128 * (JT - 1 - j) + ST], b_ctab))
                            elif br == 2 and k0 < q0 - 512 + 128 * JT:
                                j = (k0 - (q0 - 512)) // 128
                                mms.append((ident_b[:], Tl[:, 128 * (JT - 1 - j): 128 * (JT - 1 - j) + ST], b_ctab))
                            work.append(dict(mms=mms, V=Vt[:, kt, g, :], bV=bV, first=(idx == 0), last=(idx == len(tiles) - 1),
                                             h=h, hi=hi, br=br, bi=bi))

                def emit_pv(w, PT, b_PT):
                    (pso, b_pso), (psd, b_psd) = RBANKS[2 * (w["bi"] % 2)], RBANKS[2 * (w["bi"] % 2) + 1]
                    S.op(pe, lambda e: e.matmul(pso[0:64, 0:ST], w["V"], PT[:], start=w["first"], stop=w["last"]),
                         reads=[b_PT, w["bV"]], writes=[b_pso], inc=w["last"])
                    S.op(pe, lambda e: e.matmul(psd[0:64, 0:ST], ones_b[:, 0:64], PT[:], start=w["first"], stop=w["last"]),
                         reads=[b_PT, b_ones], writes=[b_psd], inc=w["last"])
                    if not w["last"]:
                        return
                    h, hi, br = w["h"], w["hi"], w["br"]
                    S.op(dve, lambda e: e.reciprocal(dsb[:], psd[0:64, 0:ST]), reads=[b_psd], writes=[b_dsb])
                    pg, b_pg = gate(h, br)
                    S.op(dve, lambda e: e.tensor_tensor(r2[:], pg[0:64, 0:ST], dsb[:], op=ALU.mult), reads=[b_dsb, b_pg], writes=[b_r2])
                    S.op(dve, lambda e: e.tensor_tensor(otm[:], pso[0:64, 0:ST], r2[:], op=ALU.mult),
                         reads=[b_pso, b_r2], writes=[b_otm])
                    if br == 1:
                        S.op(dve, lambda e: e.tensor_tensor(oacc[:], otm[:], ocmp[:, hi, :], op=ALU.add),
                             reads=[b_otm, b_ocmp], writes=[b_oacc])
                    else:
                        S.op(dve, lambda e: e.tensor_tensor(oTn[:, h, :], otm[:], oacc[:], op=ALU.add),
                             reads=[b_otm, b_oacc], writes=[b_oTn])

                DEPTH = 2
                pend = []
                for w in work:
                    PT, b_PT = score_tile(w["mms"])
                    pend.append((w, PT, b_PT))
                    if len(pend) > DEPTH:
                        emit_pv(*pend.pop(0))
                while pend:
                    emit_pv(*pend.pop(0))
            for half in range(2):
                wv, b_w = load_w("w_nsa_out", [(half * 512, 512)], 16, rows_per_k=64)
                for oc4 in range(4):
                    oc = half * 4 + oc4
                    pt, b_p = next_bank()
                    for h in range(16):
                        S.op(pe, lambda e, h=h, oc4=oc4, pt=pt, wv=wv: e.matmul(pt[:, 0:ST], wv[:, h, oc4 * 128:(oc4 + 1) * 128], oTn[:, h, :],
                                                                             start=(h == 0), stop=(h == 15)),
                             reads=[b_w, b_oTn], writes=[b_p], inc=(h == 15))
                    resid_add(oc, pt, ST, False, modv(1, 2), b_p)
            S.barrier()

    def nsa_sample():
        ovl, ident_b = ctab["ovl"][0], ctab["ident_b"][0]
        NROT[0] = 4
        NT = 128
        with ExitStack() as ph:
            norm_mod(NT, True, gmul[:, 1, 0], modv(1, 0), hT, b_hT, ph)
            S.barrier()
        with ExitStack() as ph:
            def tab(name, shape, dt):
                t, b = sb(name, shape, dt, ph)
                S.dma(sp, t[:], cin[name], writes=[b])
                return t, b
            NS["Emat"], _ = sb("EmatS", [32, SEQ], BF16, ph)
            S.dma(sp, NS["Emat"][:], cin["Emat"], writes=[b_K["ntab"]])
            TnewS, b_Tnew = tab("TnewS", [128, 16, 32], BF16)
            TlS, b_TlS = tab("TlS", [128, 32], BF16)
            TcS, b_TcS = tab("TcS", [128, 1], BF16)
            fbS, b_fbS = tab("fbS", [8, 33], F32)
            ovl33, b_ovl33 = tab("ovl33", [128, 33], F32)
            pidx, b_pidx = tab("pidx", [128, 1], F32)
            ptab, b_ptab = sb("ptab", [128, NSEQ_S * NPAGE], I32, ph)
            ptf, b_ptf = sb("ptf", [128, NSEQ_S * NPAGE], F32, ph)
            pgidx, b_pgidx = sb("pgidx", [128, NSEQ_S * NPAGE], I32, ph)
            S.dma(sp, ptab[:], page_table.rearrange("b j -> (b j)").partition_broadcast(128), writes=[b_ptab])
            S.op(dve, lambda e: e.tensor_copy(ptf[:], ptab[:]), reads=[b_ptab], writes=[b_ptf])
            S.op(dve, lambda e: e.tensor_scalar(ptf[:], ptf[:], 128.0, pidx[:, 0:1], op0=ALU.mult, op1=ALU.add),
                 reads=[b_ptf, b_pidx], writes=[b_ptf])
            S.op(dve, lambda e: e.tensor_copy(pgidx[:], ptf[:]), reads=[b_ptf], writes=[b_pgidx])
            b_tabs = [b_Tnew, b_TlS, b_TcS, b_fbS, b_ovl33, b_K["ntab"]] + b_ctab
            for nm, ncol in (("Kslc", SEQ), ("Kwin", 512), ("Kcmp", 128)):
                NS[nm], _ = sb(nm + "S", [128, G_B, ncol], BF16, ph)
            NS["Vslc"], _ = sb("VslcS", [128, NPAGE, G_B, DH], BF16, ph)
            NS["Vwin"], _ = sb("VwinS", [128, 4, G_B, DH], BF16, ph)
            NS["Vcmp"], _ = sb("VcmpS", [128, G_B, DH], BF16, ph)
            NS["szv"], _ = sb("szvS", [128, G_B, 128], BF16, ph)
            NS["kraw"], _ = sb("krawS", [64, G_B, 512 + 16], BF16, ph)
            NS["vraw"], _ = sb("vrawS", [64, G_B, 512 + 16], BF16, ph)
            for g in range(G_B):
                S.dma(sp, NS["Kslc"][64:73, g, :], cin["kconst"], writes=[b_K["Kslc"]])
                S.dma(sp, NS["Kwin"][64:73, g, :], cin["kconst"][:, SEQ - 512:SEQ], writes=[b_K["Kwin"]])
                S.dma(sp, NS["Kcmp"][64:73, g, :], cin["kconst_cmp"], writes=[b_K["Kcmp"]])
            S.op(dve, lambda e: e.memset(NS["Kcmp"][0:64], 0.0), writes=[b_K["Kcmp"]])
            pages = [sb(f"page{i}", [128, 1024], F32, ph) for i in range(3)]
            wins = [sb(f"wint{i}", [128, 512], F32, ph) for i in range(2)]
            QaugS, b_Q = sb("QaugS", [128, NSEQ_S, 128], BF16, ph)
            obr = [sb(f"obr{i}", [64, 16, NT], F32, ph) for i in range(3)]
            negTS, b_negT = sb("negTS", [32, 128], BF16, ph)
            PTs = [sb(f"PTS{i}", [128, 256], BF16, ph) for i in range(3)]
            pt_rr = [0]
            Pn, b_Pn = sb("PnS", [128, 32], F32, ph)
            rec, b_rec = sb("recS", [128, 32], F32, ph)
            scq, b_scq = sb("scqS", [8, 33], F32, ph)
            scq2, b_scq2 = sb("scq2S", [8, 33], F32, ph)
            m8, b_m8 = sb("m8S", [8, 8], F32, ph)
            oacc, b_oacc = sb("oaccS", [64, NT], F32, ph)
            otm, b_otm = sb("otmS", [64, NT], F32, ph)
            oTn, b_oTn = sb("oTnS", [64, 16, NT], BF16, ph)
            for b in range(NSEQ_S):
                S.dma(sp, QaugS[64:73, b, :].rearrange("p (h t) -> p h t", t=DEC_SEQ), cin["qconst"][:, :, SEQ:SEQ + DEC_SEQ], writes=[b_Q])
            for half in range(2):
                wv, b_w = load_w("w_nsa_in", [(half * 512, 512)], KC)
                for hh in range(8):
                    h = half * 8 + hh
                    pt, b_p = next_bank()
                    for k in range(KC):
                        S.op(pe, lambda e, k=k, hh=hh, pt=pt, wv=wv: e.matmul(pt[0:64, 0:NT], wv[:, k, hh * 64:(hh + 1) * 64], hT[:, k, 0:NT],
                                                                           start=(k == 0), stop=(k == KC - 1)),
                             reads=[b_w, b_hT], writes=[b_p], inc=(k == KC - 1))
                    S.op(act, lambda e, h=h, pt=pt: e.activation(QaugS[0:64, :, h * 8:(h + 1) * 8], v3(pt[0:64, 0:NT]), AF.Copy, scale=DH ** -0.5),
                         reads=[b_p], writes=[b_Q])
            def score_chunk(tiles):
                ps_, b_ps = next_bank()
                n = len(tiles)
                for i, mms in enumerate(tiles):
                    for m, (l_, r_, rb) in enumerate(mms):
                        S.op(pe, lambda e, l_=l_, r_=r_, i=i, m=m, ps_=ps_, mms=mms: e.matmul(
                            ps_[:, i * 32:(i + 1) * 32], l_, r_, start=(m == 0), stop=(m == len(mms) - 1)),
                            reads=rb, writes=[b_ps], inc=(i == n - 1 and m == len(mms) - 1))
                PT, b_PT = PTs[pt_rr[0] % 3]
                pt_rr[0] += 1
                S.op(act, lambda e, PT=PT, ps_=ps_: e.activation(PT[:, 0:32 * n], ps_[:, 0:32 * n], AF.Exp), reads=[b_ps], writes=[b_PT])
                return PT, b_PT

            SPE = mybir.EngineType.SP
            for b in range(NSEQ_S):
                S.op(dve, lambda e: e.memset(NS["kraw"][:, :, 0:16], 0.0), writes=[b_K["kraw"]])
                S.op(dve, lambda e: e.memset(NS["vraw"][:, :, 0:16], 0.0), writes=[b_K["vraw"]])
                for j in range(NPAGE):
                    pg, b_pg = pages[(b * NPAGE + j) % 3]
                    cidx = b * NPAGE + j
                    S.dma_gather(pool, pg[:], cache_kv, pgidx[:, cidx:cidx + 1], reads=[b_pgidx], writes=[b_pg])
                    S.op(dve, lambda e, j=j, pg=pg: e.tensor_copy(NS["Vslc"][:, j], pg[:, 768:1024].rearrange("p (g d) -> p g d", g=G_B)),
                         reads=[b_pg], writes=[b_K["Vslc"]])
                    for wi, (base, name, c0) in enumerate(((0, "kraw", 16 + (j % 4) * 128), (256, "vraw", 16 + (j % 4) * 128), (512, "Kslc", j * 128))):
                        pT_, b_pT = next_bank()
                        for g in range(G_B):
                            S.op(pe, lambda e, g=g, base=base, pg=pg, pT_=pT_: e.transpose(pT_[0:64, g * 128:(g + 1) * 128],
                                                                                        pg[:, base + g * 64: base + (g + 1) * 64], ident_f[:]),
                                 reads=[b_pg, b_ident], writes=[b_pT], inc=(g == G_B - 1))
                        dst = NS[name][0:64, :, c0:c0 + 128]
                        src = pT_[0:64, :].rearrange("p (g n) -> p g n", g=G_B)
                        if wi == 1:
                            S.op(dve, lambda e, dst=dst, src=src: e.tensor_copy(dst, src), reads=[b_pT], writes=[b_K[name]])
                        else:
                            S.op(act, lambda e, dst=dst, src=src: e.activation(dst, src, AF.Copy), reads=[b_pT], writes=[b_K[name]])
                    if j % 4 == 3:
                        compress(0, ntok=512, s0_=32 * (j // 4))
                for tl in range(4):
                    wt_, b_wt = wins[(b * 4 + tl) % 2]
                    S.dma(sp, wt_[:], state_win[b, tl * 128:(tl + 1) * 128, :], writes=[b_wt])
                    S.op(dve, lambda e, tl=tl, wt_=wt_: e.tensor_copy(NS["Vwin"][:, tl], wt_[:, 256:512].rearrange("p (g d) -> p g d", g=G_B)),
                         reads=[b_wt], writes=[b_K["Vwin"]])
                    pT_, b_pT = next_bank()
                    for g in range(G_B):
                        S.op(pe, lambda e, g=g, wt_=wt_, pT_=pT_: e.transpose(pT_[0:64, g * 128:(g + 1) * 128], wt_[:, g * 64:(g + 1) * 64], ident_f[:]),
                             reads=[b_wt, b_ident], writes=[b_pT], inc=(g == G_B - 1))
                    S.op(act, lambda e, tl=tl, pT_=pT_: e.activation(NS["Kwin"][0:64, :, tl * 128:(tl + 1) * 128],
                                                                   pT_[0:64, :].rearrange("p (g n) -> p g n", g=G_B), AF.Copy),
                         reads=[b_pT], writes=[b_K["Kwin"]])
                bs = slice(b * DEC_SEQ, (b + 1) * DEC_SEQ)
                for g in range(G_B):
                    Qg = QaugS[0:73, b, g * 32:(g + 1) * 32]
                    gs = slice(4 * g, 4 * g + 4)
                    PT, b_PT = score_chunk([[(NS["Kcmp"][0:73, g, :], Qg, [b_K["Kcmp"], b_Q]),
                                             (ident_b[:], TcS[:, 0:1].broadcast_to([128, 32]), b_tabs)]])
                    px, b_px = next_bank()
                    S.op(pe, lambda e, PT=PT, px=px: e.matmul(px[:, 0:32], ones_b[:], PT[:, 0:32], start=True, stop=True),
                         reads=[b_PT, b_ones], writes=[b_px])
                    S.op(dve, lambda e, px=px: e.tensor_scalar(rec[:], px[:, 0:32], 1e-30, None, op0=ALU.max), reads=[b_px], writes=[b_rec])
                    S.op(dve, lambda e: e.reciprocal(rec[:], rec[:]), reads=[b_rec], writes=[b_rec])
                    S.op(dve, lambda e, PT=PT: e.tensor_tensor(Pn[:], PT[:, 0:32], rec[:], op=ALU.mult), reads=[b_PT, b_rec], writes=[b_Pn])
                    po_, b_po_ = next_bank()
                    S.op(pe, lambda e, PT=PT, po_=po_: e.matmul(po_[0:64, 0:32], NS["Vcmp"][:, g, :], PT[:, 0:32], start=True, stop=True),
                         reads=[b_PT, b_K["Vcmp"]], writes=[b_po_])
                    S.op(dve, lambda e, po_=po_: e.tensor_tensor(obr[0][0][:, gs, bs], po_[0:64, 0:32].rearrange("p (h t) -> p h t", t=DEC_SEQ),
                                                                rec[0:64, :].rearrange("p (h t) -> p h t", t=DEC_SEQ), op=ALU.mult),
                         reads=[b_po_, b_rec], writes=[obr[0][1]])
                    pi_, b_pi = next_bank()
                    for hi in range(4):
                        S.op(pe, lambda e, hi=hi, pi_=pi_: e.matmul(pi_[0:8, 0:33], Pn[:, hi * 8:(hi + 1) * 8], ovl33[:], start=(hi == 0), stop=(hi == 3)),
                             reads=[b_Pn] + b_tabs, writes=[b_pi], inc=(hi == 3))
                    S.op(dve, lambda e, pi_=pi_: e.tensor_tensor(scq[:], pi_[0:8, 0:33], fbS[:], op=ALU.add), reads=[b_pi] + b_tabs, writes=[b_scq])
                    S.op(dve, lambda e: e.max(out=m8[:], in_=scq[:]), reads=[b_scq], writes=[b_m8])
                    S.op(dve, lambda e: e.match_replace(out=scq2[:], in_to_replace=m8[:], in_values=scq[:], imm_value=-1e30),
                         reads=[b_scq, b_m8], writes=[b_scq2])
                    S.op(dve, lambda e: e.max(out=m8[:], in_=scq2[:]), reads=[b_scq2], writes=[b_m8])
                    S.op(dve, lambda e: e.tensor_scalar(scq2[:], scq[:], m8[:, 7:8], None, op0=ALU.is_ge), reads=[b_scq, b_m8], writes=[b_scq2])
                    S.op(dve, lambda e: e.tensor_scalar(scq2[:], scq2[:], -NEG, NEG, op0=ALU.mult, op1=ALU.add), reads=[b_scq2], writes=[b_scq2])
                    pT_, b_pT = next_bank()
                    S.op(pe, lambda e, pT_=pT_: e.transpose(pT_[0:32, 0:8], scq2[:, 0:32], ident_f[0:8, 0:8]), reads=[b_scq2, b_ident], writes=[b_pT])
                    S.op(act, lambda e, pT_=pT_, g=g: e.activation(negTS[:, g * 32:(g + 1) * 32].rearrange("p (h t) -> p h t", t=DEC_SEQ),
                                                                 pT_[0:32, 0:8].unsqueeze(1).broadcast_to([32, 4, DEC_SEQ]), AF.Copy),
                         reads=[b_pT], writes=[b_negT])
                    for bi, br in enumerate((1, 2)):
                        (pso, b_pso), (psd, b_psd) = RBANKS[2 * bi], RBANKS[2 * bi + 1]
                        tl_list = []
                        if br == 1:
                            for kt in range(NPAGE):
                                tl_list.append(([(NS["Kslc"][0:73, g, kt * 128:(kt + 1) * 128], Qg, [b_K["Kslc"], b_Q]),
                                                 (NS["Emat"][:, kt * 128:(kt + 1) * 128], negTS[:, g * 32:(g + 1) * 32], [b_K["ntab"], b_negT])],
                                                NS["Vslc"][:, kt, g, :], b_K["Vslc"]))
                            tl_list.append(([(NS["KslcN"][0:73, g, :], Qg, [b_K["KslcN"], b_Q]), (ident_b[:], TnewS[:, b, :], b_tabs)],
                                            NS["VslcN"][:, g, :], b_K["VslcN"]))
                        else:
                            for kt in range(4):
                                mm_ = [(NS["Kwin"][0:73, g, kt * 128:(kt + 1) * 128], Qg, [b_K["Kwin"], b_Q])]
                                if kt == 0:
                                    mm_.append((ident_b[:], TlS[:], b_tabs))
                                tl_list.append((mm_, NS["Vwin"][:, kt, g, :], b_K["Vwin"]))
                            tl_list.append(([(NS["KwinN"][0:73, g, :], Qg, [b_K["KwinN"], b_Q]), (ident_b[:], TnewS[:, b, :], b_tabs)],
                                            NS["VwinN"][:, g, :], b_K["VwinN"]))
                        ntl = len(tl_list)
                        done = 0
                        for c0 in range(0, ntl, 8):
                            chunk = tl_list[c0:c0 + 8]
                            PT, b_PT = score_chunk([c[0] for c in chunk])
                            for i, (_, Vap, bV) in enumerate(chunk):
                                first, last = (done == 0), (done == ntl - 1)
                                S.op(pe, lambda e, PT=PT, i=i, Vap=Vap, first=first, last=last, pso=pso: e.matmul(
                                    pso[0:64, 0:32], Vap, PT[:, i * 32:(i + 1) * 32], start=first, stop=last),
                                    reads=[b_PT, bV], writes=[b_pso], inc=last)
                                S.op(pe, lambda e, PT=PT, i=i, first=first, last=last, psd=psd: e.matmul(
                                    psd[0:64, 0:32], ones_b[:, 0:64], PT[:, i * 32:(i + 1) * 32], start=first, stop=last),
                                    reads=[b_PT, b_ones], writes=[b_psd], inc=last)
                                done += 1
                        S.op(dve, lambda e, psd=psd: e.reciprocal(rec[0:64, :], psd[0:64, 0:32]), reads=[b_psd], writes=[b_rec])
                        S.op(dve, lambda e, pso=pso, br=br: e.tensor_tensor(obr[br][0][:, gs, bs], pso[0:64, 0:32].rearrange("p (h t) -> p h t", t=DEC_SEQ),
                                                                          rec[0:64, :].rearrange("p (h t) -> p h t", t=DEC_SEQ), op=ALU.mult),
                             reads=[b_pso, b_rec], writes=[obr[br][1]])
            wg, b_wg = load_w("w_nsa_in", [(1024, 48)], KC)
            GsS, b_GsS = sb("GsS", [48, NT], F32, ph)
            pgl, b_pgl = next_bank()
            for k in range(KC):
                S.op(pe, lambda e, k=k: e.matmul(pgl[0:48, 0:NT], wg[:, k, 0:48], hT[:, k, 0:NT], start=(k == 0), stop=(k == KC - 1)),
                     reads=[b_wg, b_hT], writes=[b_pgl], inc=(k == KC - 1))
            S.op(act, lambda e: e.activation(GsS[:], pgl[0:48, 0:NT], AF.Sigmoid), reads=[b_pgl], writes=[b_GsS])
            for h in range(16):
                for br in range(3):
                    col = 3 * h + br
                    pg_, b_pg_ = next_bank()
                    S.op(pe, lambda e, pg_=pg_, col=col: e.matmul(pg_[0:64, 0:NT], ident_f[0:48, col:col + 1].broadcast_to([48, 64]), GsS[:],
                                                               start=True, stop=True), reads=[b_GsS, b_ident], writes=[b_pg_])
                    if br == 0:
                        S.op(dve, lambda e, h=h, pg_=pg_: e.tensor_tensor(oacc[:], obr[0][0][:, h, :], pg_[0:64, 0:NT], op=ALU.mult),
                             reads=[obr[0][1], b_pg_], writes=[b_oacc])
                    else:
                        S.op(dve, lambda e, h=h, br=br, pg_=pg_: e.tensor_tensor(otm[:], obr[br][0][:, h, :], pg_[0:64, 0:NT], op=ALU.mult),
                             reads=[obr[br][1], b_pg_], writes=[b_otm])
                        if br == 1:
                            S.op(dve, lambda e: e.tensor_tensor(oacc[:], oacc[:], otm[:], op=ALU.add), reads=[b_oacc, b_otm], writes=[b_oacc])
                        else:
                            S.op(dve, lambda e, h=h: e.tensor_tensor(oTn[:, h, :], oacc[:], otm[:], op=ALU.add), reads=[b_oacc, b_otm], writes=[b_oTn])
            for half in range(2):
                wv, b_w = load_w("w_nsa_out", [(half * 512, 512)], 16, rows_per_k=64)
                for oc4 in range(4):
                    oc = half * 4 + oc4
                    pt, b_p = next_bank()
                    for h in range(16):
                        S.op(pe, lambda e, h=h, oc4=oc4, pt=pt, wv=wv: e.matmul(pt[:, 0:NT], wv[:, h, oc4 * 128:(oc4 + 1) * 128], oTn[:, h, :],
                                                                             start=(h == 0), stop=(h == 15)),
                             reads=[b_w, b_oTn], writes=[b_p], inc=(h == 15))
                    resid_add(oc, pt, NT, True, modv(1, 2), b_p)
            S.barrier()

    def final_norm_store(dst_rows, ntok, sample):
        with ExitStack() as ph:
            yT, b_yT = sb("yT", [128, KC, ntok], F32, ph)
            norm_mod(ntok, sample, gfin, zero17, yT, b_yT, ph)
            store_yT(dst_rows, ntok, yT, b_yT)

    if DEV["prompt"]:
        with ExitStack() as pst:
            P["Sst"], _ = sb("Sst", [128, H_A, 2, DV_A], F32, pst)
            P["Sbf"], _ = sb("Sbf", [128, H_A, 2, DV_A], BF16, pst)
            S.op(dve, lambda e: e.memset(P["Sst"][:], 0.0), writes=[b_Sst])
            nsa_on = DEV.get("nsa", True)
            if nsa_on:
                for nm in ("Kslc", "Kwin"):
                    NS[nm], _ = sb(nm, [128, G_B, SEQ], BF16, pst)
                NS["Kcmp"], _ = sb("Kcmp", [128, G_B, 128], BF16, pst)
                NS["Vslc"], _ = sb("Vslc", [128, SEQ // 128, G_B, DH], BF16, pst)
                NS["Vwin"], _ = sb("Vwin", [128, SEQ // 128, G_B, DH], BF16, pst)
                NS["Vcmp"], _ = sb("Vcmp", [128, G_B, DH], BF16, pst)
                NS["kraw"], _ = sb("kraw", [64, G_B, ST + 16], BF16, pst)
                NS["vraw"], _ = sb("vraw", [64, G_B, ST + 16], BF16, pst)
                NS["szv"], _ = sb("szv", [128, G_B, 128], BF16, pst)
                NS["w2"], _ = sb("w2", [128, 2, DH], BF16, pst)
                NS["bz"], _ = sb("bz", [128, 2], F32, pst)
                NS["Tcmp"], _ = sb("Tcmp", [128, SEQ], BF16, pst)
                NS["Emat"], _ = sb("Emat", [32, SEQ], BF16, pst)
                NS["fb"], _ = sb("fb", [128, SEQ // 128, 32], F32, pst)
                for nm in ("Tcmp", "Emat", "fb"):
                    S.dma(sp, NS[nm][:], cin[nm], writes=[b_K["ntab"]])
                for nm in ("Kslc", "Kwin"):
                    for g in range(G_B):
                        S.dma(sp, NS[nm][64:73, g, :], cin["kconst"], writes=[b_K[nm]])
                for g in range(G_B):
                    S.dma(sp, NS["Kcmp"][64:73, g, :], cin["kconst_cmp"], writes=[b_K["Kcmp"]])
                S.op(dve, lambda e: e.memset(NS["Kcmp"][0:64], 0.0), writes=[b_K["Kcmp"]])
                S.op(dve, lambda e: e.memset(NS["szv"][:], 0.0), writes=[b_K["szv"]])
                S.op(dve, lambda e: e.memset(NS["kraw"][:], 0.0), writes=[b_K["kraw"]])
                S.op(dve, lambda e: e.memset(NS["vraw"][:], 0.0), writes=[b_K["vraw"]])
                compress_setup()
            for st in range(NST):
                load_xT(x_p[st * ST:(st + 1) * ST, :], ST)
                retention(st, False)
                ffn(0, st, st == NST - 1, False)
                kv_proj(st, False)
                if nsa_on:
                    compress(st)
                    nsa_prompt(st)
                ffn(1, st, st == NST - 1, False)
                final_norm_store(y_p[st * ST:(st + 1) * ST, :], ST, False)
            S.dma(sp, ret_p.rearrange("h (c p) v -> p h c v", p=128), P["Sst"][:], reads=[b_Sst])
            S.barrier()
    if DEV["sample"]:
        with ExitStack() as sst_:
            P["cbufT"], _ = sb("cbufT", [128, 2, NFC, 2 * NSEQ_S], F32, sst_)
            nsa_on = DEV.get("nsa", True)
            if nsa_on:
                NS.clear()
                NS["w2"], _ = sb("w2S", [128, 2, DH], BF16, sst_)
                NS["bz"], _ = sb("bzS", [128, 2], F32, sst_)
                for nm in ("KslcN", "KwinN"):
                    NS[nm], _ = sb(nm, [128, G_B, 128], BF16, sst_)
                    for g in range(G_B):
                        S.dma(sp, NS[nm][64:73, g, :], cin["kconst_new"], writes=[b_K[nm]])
                NS["VslcN"], _ = sb("VslcN", [128, G_B, DH], BF16, sst_)
                NS["VwinN"], _ = sb("VwinN", [128, G_B, DH], BF16, sst_)
                compress_setup()
            load_conv_bufs()
            load_xT(x_s, 128)
            retention(0, True)
            ffn(0, 0, False, True)
            kv_proj(0, True)
            if nsa_on:
                nsa_sample()
            ffn(1, 0, False, True)
            final_norm_store(y_s, 128, True)
            S.barrier()
    S.finish()


_NC_CACHE = {}


def kernel(**inputs):
    f32 = lambda a: np.ascontiguousarray(np.asarray(a, dtype=np.float32))
    if "nc" not in _NC_CACHE:
        _NC_CACHE["nc"] = build_nc()
    nc = _NC_CACHE["nc"]
    shared = {k: f32(inputs[k]) for k in ["w_ada", "b_ada", "g_mix", "g_ffn", "w_ffn_in", "conv_w", "conv_b",
                                           "w_ffn_out", "w_ret_in", "w_ret_out", "g_final",
                                           "g_kv", "w_ada_kv", "b_ada_kv", "w_kv", "w_nsa_in", "w_nsa_out",
                                           "pe_ck", "pe_cv", "w_ck1", "w_ck2", "w_cv1", "w_cv2"]}
    for k in CONST_IN + CONST_DRAM:
        shared["k_" + k] = np.ascontiguousarray(CONST[k])
    xp, xs = f32(inputs["x_prompt"]), f32(inputs["x_sample"])
    cp, cs = f32(inputs["c_prompt"]), f32(inputs["c_sample"])
    sret, sconv, swin = f32(inputs["state_ret"]), f32(inputs["state_conv"]), f32(inputs["state_win"])
    cache = f32(inputs["cache_kv"]).reshape(NPHYS * 128, 1024)
    ptab = np.ascontiguousarray(np.asarray(inputs["page_table"], dtype=np.int32))
    cores = DEV["cores"] or list(range(N_CORES))
    in_maps = []
    for c in cores:
        m = dict(shared)
        sl = slice(c * NSEQ_S, (c + 1) * NSEQ_S)
        m["x_p"] = xp[c]
        m["x_s"] = xs[sl].reshape(NSEQ_S * DEC_SEQ, D)
        m["c_p"] = cp[c:c + 1]
        m["c_s"] = cs[sl]
        m["state_ret"] = sret[0, sl]
        m["state_conv"] = np.ascontiguousarray(sconv[:, sl]).reshape(2, NSEQ_S * 2, F2)
        m["state_win"] = swin[sl].reshape(NSEQ_S, 512, 512)
        m["cache_kv"] = cache
        m["page_table"] = ptab[sl]
        in_maps.append(m)
    if DEV.get("trace"):
        res = run_bass_kernel_spmd(nc, in_maps, core_ids=list(range(len(cores))), trace=True)
        print("DEV exec_time_ns:", res.exec_time_ns)
    else:
        res = run_bass_kernel_spmd(nc, in_maps, core_ids=list(range(len(cores))))
    R = list(res.results)
    if len(R) < N_CORES:
        full = [None] * N_CORES
        for c, r in zip(cores, R):
            full[c] = r
        z = {k: np.zeros_like(np.asarray(v)) for k, v in R[0].items()}
        R = [r if r is not None else z for r in full]
    cat = lambda k: np.stack([np.asarray(r[k], dtype=np.float32) for r in R])
    y_prompt = cat("y_p")
    y_sample = cat("y_s").reshape(128, DEC_SEQ, D)
    ret_prompt = cat("ret_p")[None]
    ret_sample = cat("ret_s").reshape(1, 128, H_A, DK_A, DV_A)
    conv_prompt = np.ascontiguousarray(cat("conv_p").transpose(1, 0, 2, 3))
    conv_sample = np.ascontiguousarray(cat("conv_s").transpose(1, 0, 2, 3, 4)).reshape(2, 128, 2, F2)
    kv_prompt = cat("kv_p").reshape(8, SEQ, 4, 4, 64)
    kv_sample = cat("kv_s").reshape(128, DEC_SEQ, 4, 4, 64)
    win_prompt = cat("win_p").reshape(8, 512, 2, 4, 64)
    win_sample = cat("win_s").reshape(128, 512, 2, 4, 64)
    return (y_prompt, y_sample, ret_prompt, ret_sample, conv_prompt, conv_sample,
            kv_prompt, kv_sample, win_prompt, win_sample)
```

```python
from contextlib import ExitStack
import numpy as np
import ml_dtypes
import concourse.bass as bass
import concourse.mybir as mybir
from concourse.bass_utils import run_bass_kernel_spmd

F32 = mybir.dt.float32
BF16 = mybir.dt.bfloat16
I32 = mybir.dt.int32
AF = mybir.ActivationFunctionType
ALU = mybir.AluOpType

D = 1024
KC = 8
SEQ = 2048
ST = 256
NST = SEQ // ST
H_A, DK_A, DV_A = 4, 256, 512
D_FF = 2816
F2 = 2 * D_FF
NFC = F2 // 128
NPAIR = D_FF // 128
EPS = 1e-6
NSEQ_S = 16
DEC_SEQ = 8
N_CORES = 8
NEG = -30000.0
NPHYS = 2560
NPAGE = 16
G_B, DH = 4, 64


class Eng:
    def __init__(self, name, e, sem, step=1, is_pe=False):
        self.name, self.e, self.sem, self.step, self.is_pe = name, e, sem, step, is_pe
        self.cnt = 0
        self.waited = {}


class Buf:
    __slots__ = ("name", "w", "r")

    def __init__(self, name):
        self.name, self.w, self.r = name, None, {}


class Sched:
    def __init__(self, nc, es, n_sp=12, n_pool=12, n_act=4):
        self.nc = nc
        sem = lambda n: es.enter_context(nc.semaphore(n))
        self.pe = Eng("pe", nc.tensor, sem("s_pe"), is_pe=True)
        self.act = Eng("act", nc.scalar, sem("s_act"))
        self.dve = Eng("dve", nc.vector, sem("s_dve"))
        self.pool = Eng("pool", nc.gpsimd, sem("s_pool"))
        self.sp = Eng("sp", nc.sync, sem("s_sp"))
        self.engs = [self.pe, self.act, self.dve, self.pool, self.sp]
        self.chans = {
            "sp": [Eng(f"c_sp{i}", None, sem(f"c_sp{i}"), step=16) for i in range(n_sp)],
            "pool": [Eng(f"c_pl{i}", None, sem(f"c_pl{i}"), step=16) for i in range(n_pool)],
            "act": [Eng(f"c_ac{i}", None, sem(f"c_ac{i}"), step=16) for i in range(n_act)],
        }
        self.rr = {"sp": 0, "pool": 0, "act": 0}
        self.bar_deps = {}

    def _deps(self, reads, writes):
        deps = {}
        for b in reads:
            if b.w is not None:
                deps[b.w[0]] = max(deps.get(b.w[0], 0), b.w[1])
        for b in writes:
            if b.w is not None:
                deps[b.w[0]] = max(deps.get(b.w[0], 0), b.w[1])
            for e, n in b.r.items():
                deps[e] = max(deps.get(e, 0), n)
        return deps

    def _wait(self, eng, deps):
        for f, n in deps.items():
            if f is eng and eng.is_pe:
                continue
            if eng.waited.get(f, 0) < n:
                eng.e.wait_ge(f.sem, n * f.step)
                eng.waited[f] = n

    def op(self, eng, fn, reads=(), writes=(), inc=True):
        self._wait(eng, self._deps(reads, writes))
        ins = fn(eng.e)
        n = eng.cnt + 1
        if inc:
            ins.then_inc(eng.sem, eng.step)
            eng.cnt = n
        for b in reads:
            b.r[eng] = max(b.r.get(eng, 0), n)
        for b in writes:
            b.w = (eng, n)
            b.r = {}
        return ins

    def dma(self, issuer, out, in_, reads=(), writes=(), persistent=False, **kw):
        lst = self.chans[issuer.name]
        ch = lst[self.rr[issuer.name] % len(lst)]
        self.rr[issuer.name] += 1
        deps = self._deps(reads, writes)
        if issuer is self.sp and not persistent:
            for f, n in self.bar_deps.items():
                deps[f] = max(deps.get(f, 0), n)
        if ch.cnt > 0:
            deps[ch] = max(deps.get(ch, 0), ch.cnt)
        self._wait(issuer, deps)
        issuer.e.dma_start(out=out, in_=in_, **kw).then_inc(ch.sem, 16)
        ch.cnt += 1
        for b in reads:
            b.r[ch] = max(b.r.get(ch, 0), ch.cnt)
        for b in writes:
            b.w = (ch, ch.cnt)
            b.r = {}

    def dma_gather(self, issuer, out, in_, idx_ap, reads=(), writes=()):
        lst = self.chans[issuer.name]
        ch = lst[self.rr[issuer.name] % len(lst)]
        self.rr[issuer.name] += 1
        deps = self._deps(reads, writes)
        if ch.cnt > 0:
            deps[ch] = max(deps.get(ch, 0), ch.cnt)
        self._wait(issuer, deps)
        issuer.e.indirect_dma_start(out=out, out_offset=None, in_=in_,
                                    in_offset=bass.IndirectOffsetOnAxis(ap=idx_ap, axis=0)).then_inc(ch.sem, 16)
        ch.cnt += 1
        for b in reads:
            b.r[ch] = max(b.r.get(ch, 0), ch.cnt)
        for b in writes:
            b.w = (ch, ch.cnt)
            b.r = {}

    def all_srcs(self):
        out = list(self.engs)
        for l in self.chans.values():
            out += l
        return out

    def barrier(self, engs=None):
        self.bar_deps = {f: f.cnt for f in self.all_srcs() if f.cnt > 0 and f is not self.sp}
        for e in (engs or self.engs):
            if e is self.sp:
                continue
            deps = {f: f.cnt for f in self.all_srcs() if f.cnt > 0 and not (f is e and e.is_pe)}
            self._wait(e, deps)

    def finish(self):
        deps = {f: f.cnt for f in self.all_srcs() if f.cnt > 0 and f is not self.sp}
        self._wait(self.sp, deps)


def _consts():
    c = {}
    c["ident_f"] = np.eye(128, dtype=np.float32)
    c["ones_b"] = np.ones((128, 128), dtype=ml_dtypes.bfloat16)
    lg = np.log1p(-np.exp2(-5.0 - np.arange(H_A, dtype=np.float64)))
    i = np.arange(128, dtype=np.float64)
    qdec = np.exp(lg[:, None] * i[None, :])
    kdec = np.exp(-lg[:, None] * i[None, :]) * (DK_A ** -0.5)
    c["qdec"] = np.broadcast_to(qdec[None], (128, H_A, 128)).astype(np.float32).copy()
    c["kdec"] = np.broadcast_to(kdec[None], (128, H_A, 128)).astype(np.float32).copy()
    c["kwdec"] = (np.exp(lg[None, :] * (127.0 - i[:, None])) * (DK_A ** -0.5)).astype(np.float32)
    c["cmask"] = (i[:, None] <= i[None, :]).astype(np.float32)
    c["gam"] = np.exp(lg).astype(np.float64)
    c["gam128"] = np.exp(128.0 * lg).astype(np.float64)
    c["gam8"] = np.exp(8.0 * lg).astype(np.float64)
    t8 = (np.arange(128) % 8).astype(np.float64)
    b8 = np.arange(128) // 8
    c["qdecS"] = np.broadcast_to(np.exp(lg[:, None] * t8[None, :])[None], (128, H_A, 128)).astype(np.float32).copy()
    c["kdecS"] = np.broadcast_to((np.exp(-lg[:, None] * t8[None, :]) * (DK_A ** -0.5))[None], (128, H_A, 128)).astype(np.float32).copy()
    c["kwdecS"] = (np.exp(lg[None, :] * (7.0 - t8[:, None])) * (DK_A ** -0.5)).astype(np.float32)
    c["cmaskS"] = ((b8[:, None] == b8[None, :]) & (t8[:, None] <= t8[None, :])).astype(np.float32)
    c["rowmask"] = (b8[:, None] == np.arange(16)[None, :]).astype(np.float32)
    bf = ml_dtypes.bfloat16
    c["ident_b"] = np.eye(128, dtype=bf)
    J = ST // 128
    k128 = np.arange(128)
    cc = np.arange(ST + 128 * (J - 1))
    c["Tc"] = np.where(k128[:, None] <= cc[None, :] - 128 * (J - 1), 0.0, NEG).astype(bf)
    c["Tl"] = np.where(k128[:, None] < cc[None, :] - 128 * (J - 1), NEG, 0.0).astype(bf)
    pos = np.arange(SEQ)
    c["Tcmp"] = np.where((k128[:, None] >= 1) & (16 * k128[:, None] + 15 <= pos[None, :]), 0.0, NEG).astype(bf)
    c["Emat"] = (pos[None, :] // 64 == np.arange(32)[:, None]).astype(bf)
    ib = k128 - 1
    jb = np.arange(32)
    c["ovl"] = ((ib[:, None] >= 0) & (16 * ib[:, None] <= 64 * jb[None, :] + 63)
                & (16 * ib[:, None] + 31 >= 64 * jb[None, :])).astype(np.float32)
    cur = pos // 64
    fbt = np.where(jb[None, :] > cur[:, None], -1e30,
                   np.where((jb[None, :] == 0) | (jb[None, :] == cur[:, None]) | (jb[None, :] == cur[:, None] - 1), 1e4, 0.0))
    c["fb"] = np.ascontiguousarray(fbt.reshape(SEQ // 128, 128, 32).transpose(1, 0, 2)).astype(np.float32)

    def split3(v):
        v = np.asarray(v, dtype=np.float64)
        a = v.astype(bf).astype(np.float64)
        b = (v - a).astype(bf).astype(np.float64)
        d = (v - a - b).astype(bf).astype(np.float64)
        return a, b, d

    def kconst(p):
        a, b = (p // 64).astype(np.float64), (p % 64).astype(np.float64)
        one = np.ones_like(a)
        return np.stack([a, a, a, b, b, b, one, one, one]).astype(bf)

    c["kconst"] = kconst(pos)
    cend = np.maximum(16 * k128 + 15, 0)
    c["kconst_cmp"] = kconst(cend)
    slopes = np.exp2(-8.0 * np.arange(1, 17, dtype=np.float32) / 16).astype(np.float32).astype(np.float64)
    s1, s2, s3 = split3(slopes)
    NP = SEQ + 64
    pp = np.arange(NP, dtype=np.float64)
    v1, v2, v3 = split3(-slopes[:, None] * pp[None, :])
    qc = np.zeros((9, 16, NP), dtype=np.float64)
    for r, sv in enumerate((s1, s2, s3)):
        qc[r] = 64.0 * sv[:, None]
        qc[3 + r] = sv[:, None]
    qc[6], qc[7], qc[8] = v1, v2, v3
    c["qconst"] = qc.astype(bf)
    c["kconst_new"] = kconst(SEQ + (k128 % 8))
    colt = np.arange(32) % 8
    kb, kt_ = k128 // 8, k128 % 8
    tn = np.where((kb[:, None, None] == np.arange(16)[None, :, None]) & (kt_[:, None, None] <= colt[None, None, :]), 0.0, NEG)
    c["TnewS"] = tn.astype(bf)
    c["TlS"] = np.where(k128[:, None] < colt[None, :], NEG, 0.0).astype(bf)
    c["TcS"] = np.where(k128[:, None] >= 1, 0.0, NEG).astype(bf)
    fbs = np.zeros((8, 33), dtype=np.float32)
    fbs[:, [0, 31, 32]] = 1e4
    c["fbS"] = fbs
    c["ovl33"] = np.concatenate([c["ovl"], np.zeros((128, 1), np.float32)], axis=1)
    c["pidx"] = np.arange(128, dtype=np.float32).reshape(128, 1)
    return c


CONST = _consts()
CONST_IN = ["ident_f", "ones_b", "qdec", "kdec", "kwdec", "cmask", "qdecS", "kdecS", "kwdecS", "cmaskS", "rowmask",
            "ident_b", "Tc", "Tl", "ovl"]
CONST_DRAM = ["Tcmp", "Emat", "fb", "kconst", "kconst_cmp", "qconst", "kconst_new", "TnewS", "TlS", "TcS", "fbS", "ovl33", "pidx"]
DEV = {"cores": None, "prompt": True, "sample": True}


def build_nc():
    nc = bass.Bass("TRN2", target_bir_lowering=False)
    es = ExitStack()
    with es:
        _build(nc, es)
    return nc


def _build(nc, es):
    def din(name, shape, dt=F32):
        return nc.dram_tensor(name, list(shape), dt, kind="ExternalInput").ap()

    def dout(name, shape, dt=F32):
        return nc.dram_tensor(name, list(shape), dt, kind="ExternalOutput").ap()

    x_p = din("x_p", [SEQ, D])
    x_s = din("x_s", [NSEQ_S * DEC_SEQ, D])
    c_p = din("c_p", [1, D])
    c_s = din("c_s", [NSEQ_S, D])
    state_ret = din("state_ret", [NSEQ_S, H_A, DK_A, DV_A])
    state_conv = din("state_conv", [2, NSEQ_S * 2, F2])
    state_win = din("state_win", [NSEQ_S, 512, 512])
    cache_kv = din("cache_kv", [NPHYS * 128, 1024])
    page_table = din("page_table", [NSEQ_S, NPAGE], I32)
    w_ada = din("w_ada", [2, D, 6 * D])
    b_ada = din("b_ada", [2, 6 * D])
    g_mix = din("g_mix", [2, D])
    g_ffn = din("g_ffn", [2, D])
    w_ffn_in = din("w_ffn_in", [2, D, F2])
    conv_w = din("conv_w", [2, 3, F2])
    conv_b = din("conv_b", [2, F2])
    w_ffn_out = din("w_ffn_out", [2, D_FF, D])
    w_ret_in = din("w_ret_in", [1, D, 6144])
    w_ret_out = din("w_ret_out", [1, 2048, D])
    g_final = din("g_final", [D])
    g_kv = din("g_kv", [D])
    w_ada_kv = din("w_ada_kv", [D, 2 * D])
    b_ada_kv = din("b_ada_kv", [2 * D])
    w_kv = din("w_kv", [D, 1536])
    w_nsa_in = din("w_nsa_in", [1, D, 1072])
    w_nsa_out = din("w_nsa_out", [1, D, D])
    pe_ck = din("pe_ck", [32, 64])
    pe_cv = din("pe_cv", [32, 64])
    w_ck1 = din("w_ck1", [2048, 128])
    w_ck2 = din("w_ck2", [128, 64])
    w_cv1 = din("w_cv1", [2048, 128])
    w_cv2 = din("w_cv2", [128, 64])
    cin = {}
    for k in CONST_IN + CONST_DRAM:
        a = CONST[k]
        cin[k] = din("k_" + k, a.shape, BF16 if a.dtype == ml_dtypes.bfloat16 else F32)

    y_p = dout("y_p", [SEQ, D])
    y_s = dout("y_s", [NSEQ_S * DEC_SEQ, D])
    ret_p = dout("ret_p", [H_A, DK_A, DV_A])
    ret_s = dout("ret_s", [NSEQ_S, H_A, DK_A, DV_A])
    conv_p = dout("conv_p", [2, 2, F2])
    conv_s = dout("conv_s", [2, NSEQ_S, 2, F2])
    kv_p = dout("kv_p", [SEQ, 1024])
    kv_s = dout("kv_s", [NSEQ_S * DEC_SEQ, 1024])
    win_p = dout("win_p", [512, 512])
    win_s = dout("win_s", [NSEQ_S, 512, 512])

    S = Sched(nc, es)
    pe, act, dve, pool, sp = S.pe, S.act, S.dve, S.pool, S.sp
    uid = [0]

    def sb(name, shape, dt=F32, stack=es):
        uid[0] += 1
        t = stack.enter_context(nc.sbuf_tensor(f"{name}_{uid[0]}", list(shape), dt))
        return t, Buf(name)

    banks = []
    for i in range(8):
        t = es.enter_context(nc.psum_tensor(f"bank{i}", [128, 512], F32))
        banks.append((t, Buf(f"bank{i}")))
    bank_rr = [0]

    NROT = [7]

    def next_bank():
        b = banks[bank_rr[0] % NROT[0]]
        bank_rr[0] += 1
        return b

    RBANK = banks[7]
    RBANKS = banks[4:8]

    ctab = {}
    for k in CONST_IN:
        a = CONST[k]
        t, b = sb(k, a.shape, BF16 if a.dtype == ml_dtypes.bfloat16 else F32)
        S.dma(sp, t[:], cin[k], writes=[b])
        ctab[k] = (t, b)
    ident_f, b_ident = ctab["ident_f"]
    ones_b, b_ones = ctab["ones_b"]
    b_ctab = [ctab[k][1] for k in CONST_IN]

    xT, b_xT = sb("xT", [128, KC, ST])
    hT, b_hT = sb("hT", [128, KC, ST], BF16)
    WB = 8192
    NWB = 3
    wbufs = [sb(f"wbuf{i}", [128, WB], BF16) for i in range(NWB)]
    wb_rr = [0]

    def next_wbuf():
        w = wbufs[wb_rr[0] % NWB]
        wb_rr[0] += 1
        return w

    NM = 1 + NSEQ_S
    modall, b_mod = sb("modall", [128, 2, 6 * KC, NM])
    gmul, b_gmul = sb("gmul", [128, 2, 2, KC, NM])
    gvec, b_gvec = sb("gvec", [128, 6, KC])
    modkv, b_modkv = sb("modkv", [128, 2 * KC, NM])
    gmkv, b_gmkv = sb("gmkv", [128, KC, NM])
    gfin, b_gfin = sb("gfin", [128, KC, NM])
    zero17, b_zero17 = sb("zero17", [128, KC, NM])
    cwt, b_cwt = sb("cwt", [128, 2, 3, NFC])
    cbt, b_cbt = sb("cbt", [128, 2, NFC])
    uhalo, b_uhalo = sb("uhalo", [128, 2, NFC, 2])
    b_Sst, b_Sbf, b_cbufT = Buf("Sst"), Buf("Sbf"), Buf("cbufT")
    P = {}
    b_modall = [b_mod, b_gmul, b_gvec, b_modkv, b_gmkv, b_gfin, b_zero17]

    for l in range(2):
        S.dma(sp, cwt[:, l], conv_w[l].rearrange("t (c p) -> p t c", p=128), writes=[b_cwt],
              allow_slow_non_contiguous=True)
        S.dma(sp, cbt[:, l], conv_b[l].rearrange("(c p) -> p c", p=128), writes=[b_cbt],
              allow_slow_non_contiguous=True)
    for i, src in enumerate([g_mix[0], g_mix[1], g_ffn[0], g_ffn[1], g_final, g_kv]):
        S.dma(sp, gvec[:, i], src.rearrange("(c p) -> p c", p=128), writes=[b_gvec],
              allow_slow_non_contiguous=True)
    S.op(dve, lambda e: e.memset(uhalo[:], 0.0), writes=[b_uhalo])
    S.op(dve, lambda e: e.memset(zero17[:], 0.0), writes=[b_zero17])

    def bc(ap2, n):
        return ap2.unsqueeze(2).broadcast_to([128, ap2.shape[1], n])

    with ExitStack() as ph:
        cT, b_cT = sb("cT", [128, KC, NM], F32, ph)
        cTb, b_cTb = sb("cTb", [128, KC, NM], BF16, ph)
        badT, b_badT = sb("badT", [128, 2, 6 * KC], F32, ph)
        bkvT, b_bkvT = sb("bkvT", [128, 2 * KC], F32, ph)
        S.dma(sp, cT[:, :, 0], c_p[0].rearrange("(c p) -> p c", p=128), writes=[b_cT], allow_slow_non_contiguous=True)
        for sq_ in range(NSEQ_S):
            S.dma(sp, cT[:, :, 1 + sq_], c_s[sq_].rearrange("(c p) -> p c", p=128), writes=[b_cT], allow_slow_non_contiguous=True)
        for l in range(2):
            S.dma(sp, badT[:, l], b_ada[l].rearrange("(c p) -> p c", p=128), writes=[b_badT],
                  allow_slow_non_contiguous=True)
        S.dma(sp, bkvT[:], b_ada_kv.rearrange("(c p) -> p c", p=128), writes=[b_bkvT], allow_slow_non_contiguous=True)
        S.op(act, lambda e: e.activation(cTb[:], cT[:], AF.Silu), reads=[b_cT], writes=[b_cTb])

        def mod_block(wsrc, dst3, bias2):
            wt, b_w = next_wbuf()
            wv = wt[:, 0:KC * 1024].rearrange("p (k n) -> p k n", k=KC)
            S.dma(pool, wv, wsrc.rearrange("(k p) n -> p k n", p=128), writes=[b_w])
            pt, b_p = next_bank()
            for oc in range(8):
                for k in range(KC):
                    S.op(pe, lambda e, oc=oc, k=k: e.matmul(pt[:, oc * NM:(oc + 1) * NM], wv[:, k, oc * 128:(oc + 1) * 128],
                                                          cTb[:, k, :], start=(k == 0), stop=(k == KC - 1)),
                         reads=[b_w, b_cTb], writes=[b_p], inc=(oc == 7 and k == KC - 1))
            S.op(dve, lambda e: e.tensor_tensor(dst3, pt[:, 0:KC * NM].rearrange("p (k n) -> p k n", k=KC),
                                                bc(bias2, NM), op=ALU.add),
                 reads=[b_p, b_badT, b_bkvT], writes=b_modall)

        for l in range(2):
            for blk in range(6):
                mod_block(w_ada[l, :, blk * 1024:(blk + 1) * 1024], modall[:, l, blk * KC:(blk + 1) * KC, :],
                          badT[:, l, blk * KC:(blk + 1) * KC])
        for blk in range(2):
            mod_block(w_ada_kv[:, blk * 1024:(blk + 1) * 1024], modkv[:, blk * KC:(blk + 1) * KC, :],
                      bkvT[:, blk * KC:(blk + 1) * KC])
        for l in range(2):
            for sub in range(2):
                gi = l if sub == 0 else 2 + l
                sc = modall[:, l, (1 + 3 * sub) * KC:(2 + 3 * sub) * KC, :]
                S.op(dve, lambda e, l=l, sub=sub, gi=gi, sc=sc: e.scalar_tensor_tensor(
                    gmul[:, l, sub], sc, 1.0, bc(gvec[:, gi], NM), op0=ALU.add, op1=ALU.mult),
                    reads=b_modall, writes=b_modall)
        S.op(dve, lambda e: e.scalar_tensor_tensor(gmkv[:], modkv[:, KC:2 * KC, :], 1.0, bc(gvec[:, 5], NM),
                                                   op0=ALU.add, op1=ALU.mult), reads=b_modall, writes=b_modall)
        S.op(dve, lambda e: e.tensor_copy(gfin[:], bc(gvec[:, 4], NM)), reads=b_modall, writes=b_modall)
        S.barrier()

    def modv(l, which):
        return modall[:, l, which * KC:(which + 1) * KC, :]

    def v3(ap2):
        return ap2.rearrange("p (b t) -> p b t", t=DEC_SEQ)

    def load_xT(src_rows, ntok):
        with ExitStack() as ph:
            xin, b_xin = sb("xin", [128, ntok // 128, D], F32, ph)
            S.dma(sp, xin[:], src_rows.rearrange("(t p) d -> p t d", p=128), writes=[b_xin])
            for t in range(ntok // 128):
                for half in range(2):
                    pt, b_p = next_bank()
                    for j in range(4):
                        k = half * 4 + j
                        S.op(pe, lambda e, t=t, k=k, j=j: e.transpose(pt[:, j * 128:(j + 1) * 128],
                                                                    xin[:, t, k * 128:(k + 1) * 128], ident_f[:]),
                             reads=[b_xin, b_ident], writes=[b_p], inc=(j == 3))
                    S.op(act, lambda e, t=t, half=half: e.activation(
                        xT[:, half * 4:half * 4 + 4, t * 128:(t + 1) * 128],
                        pt[:].rearrange("p (k n) -> p k n", k=4), AF.Copy), reads=[b_p], writes=[b_xT])
            S.barrier()

    def store_yT(dst_rows, ntok, src, b_src):
        with ExitStack() as ph:
            yo, b_yo = sb("yo", [128, ntok // 128, D], F32, ph)
            for t in range(ntok // 128):
                for half in range(2):
                    pt, b_p = next_bank()
                    for j in range(4):
                        k = half * 4 + j
                        S.op(pe, lambda e, t=t, k=k, j=j: e.transpose(pt[:, j * 128:(j + 1) * 128],
                                                                    src[:, k, t * 128:(t + 1) * 128], ident_f[:]),
                             reads=[b_src, b_ident], writes=[b_p], inc=(j == 3))
                    S.op(act, lambda e, t=t, half=half: e.activation(yo[:, t, half * 512:(half + 1) * 512], pt[:], AF.Copy),
                         reads=[b_p], writes=[b_yo])
            S.dma(sp, dst_rows.rearrange("(t p) d -> p t d", p=128), yo[:], reads=[b_yo])
            S.barrier()

    def norm_mod(ntok, sample, gm3, sh3, dst, b_dst, ph):
        sq, b_sq = sb("nm_sq", [128, KC, ntok], BF16, ph)
        rstd, b_rstd = sb("nm_rstd", [128, ntok], F32, ph)
        tmp, b_tmp = sb("nm_tmp", [128, 2, ntok], F32, ph)
        S.op(act, lambda e: e.activation(sq[:], xT[:, :, 0:ntok], AF.Square), reads=[b_xT], writes=[b_sq])
        pt, b_p = next_bank()
        for k in range(KC):
            S.op(pe, lambda e, k=k: e.matmul(pt[:, 0:ntok], ones_b[:], sq[:, k, :], start=(k == 0), stop=(k == KC - 1)),
                 reads=[b_sq, b_ones], writes=[b_p], inc=(k == KC - 1))
        S.op(act, lambda e: e.activation(rstd[:], pt[:, 0:ntok], AF.Sqrt, bias=EPS, scale=1.0 / D),
             reads=[b_p], writes=[b_rstd])
        S.op(dve, lambda e: e.reciprocal(rstd[:], rstd[:]), reads=[b_rstd], writes=[b_rstd])
        tb = [Buf("nm_t0"), Buf("nm_t1")]
        for k in range(KC):
            S.op(dve, lambda e, k=k: e.tensor_tensor(tmp[:, k % 2], xT[:, k, 0:ntok], rstd[:], op=ALU.mult),
                 reads=[b_xT, b_rstd], writes=[tb[k % 2]])
            if not sample:
                S.op(act, lambda e, k=k: e.activation(dst[:, k, 0:ntok], tmp[:, k % 2], AF.Identity,
                                                    bias=sh3[:, k, 0:1], scale=gm3[:, k, 0:1]),
                     reads=[tb[k % 2]] + b_modall, writes=[b_dst])
            else:
                S.op(dve, lambda e, k=k: e.tensor_tensor(v3(tmp[:, k % 2]), v3(tmp[:, k % 2]), bc(gm3[:, k, 1:NM], DEC_SEQ), op=ALU.mult),
                     reads=[tb[k % 2]] + b_modall, writes=[tb[k % 2]])
                S.op(dve, lambda e, k=k: e.tensor_tensor(v3(dst[:, k, 0:ntok]), v3(tmp[:, k % 2]), bc(sh3[:, k, 1:NM], DEC_SEQ), op=ALU.add),
                     reads=[tb[k % 2]] + b_modall, writes=[b_dst])

    rtmp, b_rtmp = sb("rtmp", [128, 128])

    def resid_add(oc, pt, ntok, sample, ga3, b_p):
        if not sample:
            S.op(dve, lambda e: e.scalar_tensor_tensor(xT[:, oc, 0:ntok], pt[:, 0:ntok], ga3[:, oc, 0:1], xT[:, oc, 0:ntok],
                                                       op0=ALU.mult, op1=ALU.add),
                 reads=[b_p, b_xT] + b_modall, writes=[b_xT])
        else:
            S.op(dve, lambda e: e.tensor_tensor(v3(rtmp[:]), v3(pt[:, 0:ntok]), bc(ga3[:, oc, 1:NM], DEC_SEQ), op=ALU.mult),
                 reads=[b_p] + b_modall, writes=[b_rtmp])
            S.op(dve, lambda e: e.tensor_tensor(xT[:, oc, 0:ntok], xT[:, oc, 0:ntok], rtmp[:], op=ALU.add),
                 reads=[b_rtmp, b_xT], writes=[b_xT])

    WSRC = {"w_ret_in": w_ret_in[0], "w_ret_out": w_ret_out[0], "w_ffn_in0": w_ffn_in[0], "w_ffn_in1": w_ffn_in[1],
            "w_ffn_out0": w_ffn_out[0], "w_ffn_out1": w_ffn_out[1], "w_kv": w_kv, "w_nsa_in": w_nsa_in[0],
            "w_nsa_out": w_nsa_out[0]}
    WBF, b_WBF = {}, {}
    for nm, src in WSRC.items():
        WBF[nm] = nc.dram_tensor("bf_" + nm, list(src.shape), BF16, kind="Internal").ap()
        b_WBF[nm] = Buf("bf_" + nm)
    converted = set()

    def convert_w(nm):
        src = WSRC[nm]
        K_, N_ = src.shape
        for r0 in range(0, K_, 128):
            t_, b_t = next_wbuf()
            S.dma(pool, t_[:, 0:N_], src[r0:r0 + 128, :], writes=[b_t])
            S.dma(sp, WBF[nm][r0:r0 + 128, :], t_[:, 0:N_], reads=[b_t], writes=[b_WBF[nm]], persistent=True)
        converted.add(nm)

    def load_w(wname, col_ranges, nk, rows_per_k=128):
        if wname not in converted:
            convert_w(wname)
        wt, b_w = next_wbuf()
        tot = sum(n for _, n in col_ranges)
        assert nk * tot <= WB
        wv = wt[0:rows_per_k, 0:nk * tot].rearrange("p (k n) -> p k n", k=nk)
        o = 0
        for c0, n in col_ranges:
            S.dma(sp, wv[:, :, o:o + n], WBF[wname][:, c0:c0 + n].rearrange("(k p) n -> p k n", p=rows_per_k),
                  reads=[b_WBF[wname]], writes=[b_w], persistent=True)
            o += n
        return wv, b_w

    def proj_fm(wv, b_w, col, src, b_src, ntok, nk):
        pt, b_p = next_bank()
        for k in range(nk):
            S.op(pe, lambda e, k=k: e.matmul(pt[:, 0:ntok], wv[:, k, col:col + 128], src[:, k, 0:ntok],
                                            start=(k == 0), stop=(k == nk - 1)),
                 reads=[b_w, b_src], writes=[b_p], inc=(k == nk - 1))
        return pt, b_p

    def proj_tm(wv, b_w, col, ncol, src, b_src, t, nk):
        pt, b_p = next_bank()
        for k in range(nk):
            S.op(pe, lambda e, k=k: e.matmul(pt[:, 0:ncol], src[:, k, t * 128:(t + 1) * 128], wv[:, k, col:col + ncol],
                                            start=(k == 0), stop=(k == nk - 1)),
                 reads=[b_w, b_src], writes=[b_p], inc=(k == nk - 1))
        return pt, b_p

    def retention(st, sample):
        ntok = 128 if sample else ST
        nt = ntok // 128
        sfx = "S" if sample else ""
        qdec, kdec, kwdec, cmask = (ctab[k + sfx][0] for k in ("qdec", "kdec", "kwdec", "cmask"))
        rowmask = ctab["rowmask"][0]
        NROT[0] = 4 if sample else 7
        with ExitStack() as ph:
            norm_mod(ntok, sample, gmul[:, 0, 0], modv(0, 0), hT, b_hT, ph)
            yin, b_yin = sb("yin", [128, 16, ntok], BF16, ph)
            qT, b_qT = sb("qT", [128, 2, ntok], BF16, ph)
            kT, b_kT = sb("kT", [128, 2, ntok], BF16, ph)
            gT, b_gT = sb("gT", [128, 4, ntok], BF16, ph)
            vtk, b_vtk = sb("vtk", [128, nt, DV_A], BF16, ph)
            kwt, b_kwt = sb("kwt", [128, nt, DK_A], BF16, ph)
            scT, b_scT = sb("scT", [128, 128], BF16, ph)
            oT, b_oT = sb("oT", [128, 4, ntok], F32, ph)
            osq, b_osq = sb("osq", [128, 4, ntok], BF16, ph)
            orstd, b_orstd = sb("orstd", [128, ntok], F32, ph)
            otmp, b_otmp = sb("otmp", [128, ntok], F32, ph)
            if sample:
                s0 = [sb(f"s0_{i}", [128, 2, DV_A], F32, ph) for i in range(3)]
                s0b = [sb(f"s0b_{i}", [128, 2, DV_A], BF16, ph) for i in range(3)]
                sn = [sb(f"sn_{i}", [128, 2, DV_A], F32, ph) for i in range(3)]
                kwm = [sb(f"kwm_{i}", [128, DK_A], BF16, ph) for i in range(3)]
            W = "w_ret_in"
            for h in range(H_A):
                wv, b_w = load_w(W, [(h * 256, 256), (1024 + h * 256, 256)], KC)
                for which, dstT, b_d, dec in ((0, qT, b_qT, qdec), (1, kT, b_kT, kdec)):
                    for c in range(2):
                        pt, b_p = proj_fm(wv, b_w, which * 256 + c * 128, hT, b_hT, ntok, KC)
                        S.op(dve, lambda e, c=c, dstT=dstT, dec=dec, pt=pt: e.tensor_tensor(
                            dstT[:, c, :].rearrange("p (t n) -> p t n", n=128), pt[:, 0:ntok].rearrange("p (t n) -> p t n", n=128),
                            dec[:, h, :].unsqueeze(1).broadcast_to([128, nt, 128]), op=ALU.mult),
                             reads=[b_p] + b_ctab, writes=[b_d])
                for t in range(nt):
                    pt2, b_p2 = proj_tm(wv, b_w, 256, 256, hT, b_hT, t, KC)
                    S.op(dve, lambda e, t=t, pt2=pt2: e.tensor_scalar(kwt[:, t, :], pt2[:, 0:256], kwdec[:, h:h + 1], None, op0=ALU.mult),
                         reads=[b_p2] + b_ctab, writes=[b_kwt])
                wv, b_w = load_w(W, [(4096 + h * 512, 512)], KC)
                for c in range(4):
                    pt, b_p = proj_fm(wv, b_w, c * 128, hT, b_hT, ntok, KC)
                    S.op(act, lambda e, c=c, pt=pt: e.activation(gT[:, c, :], pt[:, 0:ntok], AF.Silu), reads=[b_p], writes=[b_gT])
                wv, b_w = load_w(W, [(2048 + h * 512, 512)], KC)
                for t in range(nt):
                    pt, b_p = proj_tm(wv, b_w, 0, 512, hT, b_hT, t, KC)
                    S.op(act, lambda e, t=t, pt=pt: e.activation(vtk[:, t, :], pt[:], AF.Copy), reads=[b_p], writes=[b_vtk])
                for t in range(nt):
                    first = (not sample) and (st == 0 and t == 0)
                    ts = slice(t * 128, (t + 1) * 128)
                    pt, b_p = next_bank()
                    for c in range(2):
                        S.op(pe, lambda e, c=c: e.matmul(pt[:, 0:128], kT[:, c, ts], qT[:, c, ts], start=(c == 0), stop=(c == 1)),
                             reads=[b_kT, b_qT], writes=[b_p], inc=(c == 1))
                    S.op(dve, lambda e: e.tensor_tensor(scT[:], pt[:, 0:128], cmask[:], op=ALU.mult),
                         reads=[b_p] + b_ctab, writes=[b_scT])
                    po, b_po = RBANK
                    if not sample:
                        for vc in range(4):
                            vs = slice(vc * 128, (vc + 1) * 128)
                            S.op(pe, lambda e, vc=vc, vs=vs: e.matmul(po[:, vs], vtk[:, t, vs], scT[:], start=True, stop=first),
                                 reads=[b_vtk, b_scT], writes=[b_po], inc=(first and vc == 3))
                            if not first:
                                for c in range(2):
                                    S.op(pe, lambda e, vc=vc, vs=vs, c=c: e.matmul(po[:, vs], P['Sbf'][:, h, c, vs], qT[:, c, ts],
                                                                               start=False, stop=(c == 1)),
                                         reads=[b_Sbf, b_qT], writes=[b_po], inc=(vc == 3 and c == 1))
                    else:
                        for vc in range(4):
                            vs = slice(vc * 128, (vc + 1) * 128)
                            S.op(pe, lambda e, vc=vc, vs=vs: e.matmul(RBANKS[vc][0][:, 0:128], vtk[:, t, vs], scT[:], start=True, stop=False),
                                 reads=[b_vtk, b_scT], writes=[RBANKS[vc][1]], inc=False)
                        for b in range(NSEQ_S):
                            i3 = (h * NSEQ_S + b) % 3
                            (s0t, b_s0), (s0bt, b_s0b), (snt, b_sn), (kwmt, b_kwm) = s0[i3], s0b[i3], sn[i3], kwm[i3]
                            S.dma(sp, s0t[:], state_ret[b, h].rearrange("(c p) v -> p c v", p=128), writes=[b_s0])
                            S.op(act, lambda e, s0t=s0t, s0bt=s0bt: e.activation(s0bt[:], s0t[:], AF.Copy, scale=float(CONST["gam"][h])),
                                 reads=[b_s0], writes=[b_s0b])
                            for vc in range(4):
                                for c in range(2):
                                    lastmm = (b == NSEQ_S - 1 and vc == 3 and c == 1)
                                    S.op(pe, lambda e, vc=vc, c=c, b=b, s0bt=s0bt: e.matmul(
                                        RBANKS[vc][0][:, b * 8: b * 8 + 8], s0bt[:, c, vc * 128:(vc + 1) * 128],
                                        qT[:, c, b * 8:(b + 1) * 8], start=False, stop=(b == NSEQ_S - 1 and c == 1)),
                                        reads=[b_s0b, b_qT], writes=[RBANKS[vc][1]], inc=(lastmm or (vc == 3 and c == 1)))
                            S.op(dve, lambda e, b=b, kwmt=kwmt: e.tensor_scalar(kwmt[:], kwt[:, 0, :], rowmask[:, b:b + 1], None, op0=ALU.mult),
                                 reads=[b_kwt] + b_ctab, writes=[b_kwm])
                            for c in range(2):
                                ps_, b_ps = next_bank()
                                S.op(pe, lambda e, c=c, kwmt=kwmt, ps_=ps_: e.matmul(ps_[:], kwmt[:, c * 128:(c + 1) * 128], vtk[:, 0, :], start=True, stop=True),
                                     reads=[b_kwm, b_vtk], writes=[b_ps])
                                S.op(dve, lambda e, c=c, s0t=s0t, snt=snt, ps_=ps_: e.scalar_tensor_tensor(
                                    snt[:, c, :], s0t[:, c, :], float(CONST["gam8"][h]), ps_[:], op0=ALU.mult, op1=ALU.add),
                                    reads=[b_ps, b_s0], writes=[b_sn])
                            S.dma(sp, ret_s[b, h].rearrange("(c p) v -> p c v", p=128), snt[:], reads=[b_sn])
                    if sample:
                        for vc in range(4):
                            S.op(act, lambda e, vc=vc: e.activation(oT[:, vc, ts], RBANKS[vc][0][:, 0:128], AF.Copy),
                                 reads=[RBANKS[vc][1]], writes=[b_oT])
                    else:
                        S.op(act, lambda e: e.activation(oT[:, :, ts], po[:].rearrange("p (v n) -> p v n", v=4), AF.Copy),
                             reads=[b_po], writes=[b_oT])
                    if not sample:
                        for c in range(2):
                            ps_, b_ps = next_bank()
                            S.op(pe, lambda e, c=c, ps_=ps_: e.matmul(ps_[:], kwt[:, t, c * 128:(c + 1) * 128], vtk[:, t, :], start=True, stop=True),
                                 reads=[b_kwt, b_vtk], writes=[b_ps])
                            S.op(dve, lambda e, c=c, ps_=ps_: e.scalar_tensor_tensor(P['Sst'][:, h, c, :], P['Sst'][:, h, c, :], float(CONST["gam128"][h]),
                                                                                  ps_[:], op0=ALU.mult, op1=ALU.add),
                                 reads=[b_ps, b_Sst], writes=[b_Sst])
                        S.op(act, lambda e: e.activation(P['Sbf'][:, h], P['Sst'][:, h], AF.Copy, scale=float(CONST["gam"][h])),
                             reads=[b_Sst], writes=[b_Sbf])
                S.op(act, lambda e: e.activation(osq[:], oT[:], AF.Square), reads=[b_oT], writes=[b_osq])
                pt, b_p = next_bank()
                for vc in range(4):
                    S.op(pe, lambda e, vc=vc: e.matmul(pt[:, 0:ntok], ones_b[:], osq[:, vc, :], start=(vc == 0), stop=(vc == 3)),
                         reads=[b_osq, b_ones], writes=[b_p], inc=(vc == 3))
                S.op(act, lambda e: e.activation(orstd[:], pt[:, 0:ntok], AF.Sqrt, bias=EPS, scale=1.0 / DV_A), reads=[b_p], writes=[b_orstd])
                S.op(dve, lambda e: e.reciprocal(orstd[:], orstd[:]), reads=[b_orstd], writes=[b_orstd])
                for vc in range(4):
                    S.op(dve, lambda e, vc=vc: e.tensor_tensor(otmp[:], oT[:, vc, :], orstd[:], op=ALU.mult),
                         reads=[b_oT, b_orstd], writes=[b_otmp])
                    S.op(dve, lambda e, vc=vc: e.tensor_tensor(yin[:, h * 4 + vc, :], otmp[:], gT[:, vc, :], op=ALU.mult),
                         reads=[b_otmp, b_gT], writes=[b_yin])
            for half in range(2):
                wv, b_w = load_w("w_ret_out", [(half * 512, 512)], 16)
                for oc4 in range(4):
                    oc = half * 4 + oc4
                    pt, b_p = proj_fm(wv, b_w, oc4 * 128, yin, b_yin, ntok, 16)
                    resid_add(oc, pt, ntok, sample, modv(0, 2), b_p)
            S.barrier()


    def load_conv_bufs():
        with ExitStack() as ph:
            rows, b_rows = sb("scrows", [2 * NSEQ_S, F2], F32, ph)
            for l in range(2):
                S.dma(sp, rows[:], state_conv[l], writes=[b_rows])
                for g0 in range(0, NFC, 16):
                    n = min(16, NFC - g0)
                    pt, b_p = next_bank()
                    for j in range(n):
                        ch = g0 + j
                        S.op(pe, lambda e, j=j, ch=ch: e.transpose(pt[:, j * 32:(j + 1) * 32], rows[:, ch * 128:(ch + 1) * 128],
                                                                 ident_f[0:32, 0:32]),
                             reads=[b_rows, b_ident], writes=[b_p], inc=(j == n - 1))
                    S.op(act, lambda e, l=l, g0=g0, n=n: e.activation(P['cbufT'][:, l, g0:g0 + n, :],
                                                                    pt[:, 0:n * 32].rearrange("p (c m) -> p c m", c=n), AF.Copy),
                         reads=[b_p], writes=[b_cbufT])
            S.barrier()

    def ffn(l, st, last, sample):
        ntok = 128 if sample else ST
        NROT[0] = 7
        with ExitStack() as ph:
            norm_mod(ntok, sample, gmul[:, l, 1], modv(l, 3), hT, b_hT, ph)
            actT, b_actT = sb("actT", [128, NPAIR, ntok], BF16, ph)
            ncol = ntok + 2 if not sample else NSEQ_S * (DEC_SEQ + 2)
            ue = [sb(f"ue{i}", [128, ncol], F32, ph) for i in range(4)]
            zz = [sb(f"zz{i}", [128, ntok], F32, ph) for i in range(4)]
            sg, b_sg = sb("sg", [128, ntok], F32, ph)
            if sample:
                utok, b_utok = sb("utok", [128, F2], F32, ph)
            W = f"w_ffn_in{l}"
            for blk in range(NPAIR // 2):
                wv, b_w = load_w(W, [(blk * 256, 256), (D_FF + blk * 256, 256)], KC)
                if sample:
                    for ag in range(2):
                        pt, b_p = proj_tm(wv, b_w, ag * 256, 256, hT, b_hT, 0, KC)
                        c0 = ag * D_FF + blk * 256
                        S.op(act, lambda e, pt=pt, c0=c0: e.activation(utok[:, c0:c0 + 256], pt[:, 0:256], AF.Copy),
                             reads=[b_p], writes=[b_utok])
                for pi in range(2):
                    pair = blk * 2 + pi
                    zs = []
                    for ag in range(2):
                        chunk = pair + ag * NPAIR
                        pt, b_p = proj_fm(wv, b_w, ag * 256 + pi * 128, hT, b_hT, ntok, KC)
                        (u, b_u) = ue[(pair * 2 + ag) % 4]
                        (z, b_z) = zz[(pair * 2 + ag) % 4]
                        if not sample:
                            S.op(act, lambda e, u=u, chunk=chunk: e.activation(u[:, 0:2], uhalo[:, l, chunk, :], AF.Copy),
                                 reads=[b_uhalo], writes=[b_u])
                            S.op(act, lambda e, u=u, pt=pt: e.activation(u[:, 2:ntok + 2], pt[:, 0:ntok], AF.Copy), reads=[b_p], writes=[b_u])
                            S.op(act, lambda e, u=u, chunk=chunk: e.activation(uhalo[:, l, chunk, :], u[:, ntok:ntok + 2], AF.Copy),
                                 reads=[b_u], writes=[b_uhalo])
                            u2, u1, u0, zv = u[:, 2:ntok + 2], u[:, 1:ntok + 1], u[:, 0:ntok], z[:]
                        else:
                            u3 = u[:].rearrange("p (b t) -> p b t", t=DEC_SEQ + 2)
                            S.op(pool, lambda e, u3=u3, chunk=chunk: e.tensor_copy(
                                u3[:, :, 0:2], P['cbufT'][:, l, chunk, :].rearrange("p (b j) -> p b j", j=2)),
                                reads=[b_cbufT], writes=[b_u])
                            S.op(act, lambda e, u3=u3, pt=pt: e.activation(u3[:, :, 2:DEC_SEQ + 2], v3(pt[:, 0:ntok]), AF.Copy),
                                 reads=[b_p], writes=[b_u])
                            u2, u1, u0, zv = u3[:, :, 2:DEC_SEQ + 2], u3[:, :, 1:DEC_SEQ + 1], u3[:, :, 0:DEC_SEQ], v3(z[:])
                        if not sample:
                            S.op(act, lambda e, zv=zv, pt=pt, chunk=chunk: e.activation(zv, pt[:, 0:ntok], AF.Identity,
                                                                                     bias=cbt[:, l, chunk:chunk + 1], scale=cwt[:, l, 2, chunk:chunk + 1]),
                                 reads=[b_p, b_cwt, b_cbt], writes=[b_z])
                        else:
                            S.op(dve, lambda e, u2=u2, zv=zv, chunk=chunk: e.tensor_scalar(
                                zv, u2, cwt[:, l, 2, chunk:chunk + 1], cbt[:, l, chunk:chunk + 1],
                                op0=ALU.mult, op1=ALU.add), reads=[b_u, b_cwt, b_cbt], writes=[b_z])
                        S.op(dve, lambda e, u1=u1, zv=zv, chunk=chunk: e.scalar_tensor_tensor(
                            zv, u1, cwt[:, l, 1, chunk:chunk + 1], zv, op0=ALU.mult, op1=ALU.add),
                            reads=[b_u, b_cwt, b_z], writes=[b_z])
                        S.op(dve, lambda e, u0=u0, zv=zv, chunk=chunk: e.scalar_tensor_tensor(
                            zv, u0, cwt[:, l, 0, chunk:chunk + 1], zv, op0=ALU.mult, op1=ALU.add),
                            reads=[b_u, b_cwt, b_z], writes=[b_z])
                        zs.append((z, b_z))
                    S.op(act, lambda e, z=zs[1][0]: e.activation(sg[:], z[:], AF.Silu), reads=[zs[1][1]], writes=[b_sg])
                    S.op(dve, lambda e, z=zs[0][0], pair=pair: e.tensor_tensor(actT[:, pair, :], z[:], sg[:], op=ALU.mult),
                         reads=[zs[0][1], b_sg], writes=[b_actT])
            if sample:
                for tt in range(2):
                    S.dma(sp, conv_s[l, :, tt, :], utok[6 + tt:128:8, :], reads=[b_utok])
            elif last:
                for tt in range(2):
                    S.dma(sp, conv_p[l, tt].rearrange("(c p) -> p c", p=128), uhalo[:, l, :, tt], reads=[b_uhalo],
                          allow_slow_non_contiguous=True)
            for qt in range(4):
                wv, b_w = load_w(f"w_ffn_out{l}", [(qt * 256, 256)], NPAIR)
                for oc2 in range(2):
                    oc = qt * 2 + oc2
                    pt, b_p = proj_fm(wv, b_w, oc2 * 128, actT, b_actT, ntok, NPAIR)
                    resid_add(oc, pt, ntok, sample, modv(l, 5), b_p)
            S.barrier()

    NS = {}
    JT = ST // 128
    b_K = {k: Buf(k) for k in ("Kslc", "Kwin", "Kcmp", "Vslc", "Vwin", "Vcmp", "kraw", "vraw", "szv", "cmpw", "ntab",
                               "KslcN", "KwinN", "VslcN", "VwinN")}

    def kv_proj(st, sample):
        ntok = 128 if sample else ST
        NROT[0] = 7
        q0 = st * ST
        with ExitStack() as ph:
            norm_mod(ntok, sample, gmkv, modkv[:, 0:KC, :], hT, b_hT, ph)
            kvo, b_kvo = sb("kvo", [128, ntok // 128, 1536], F32, ph)
            for cb in range(3):
                wv, b_w = load_w("w_kv", [(cb * 512, 512)], KC)
                for t in range(ntok // 128):
                    pt, b_p = proj_tm(wv, b_w, 0, 512, hT, b_hT, t, KC)
                    S.op(act, lambda e, t=t, cb=cb, pt=pt: e.activation(kvo[:, t, cb * 512:(cb + 1) * 512], pt[:], AF.Copy),
                         reads=[b_p], writes=[b_kvo])
                if not DEV.get("nsa", True):
                    continue
                if sample:
                    fm = {0: [], 1: [(0, "KslcN", 0)], 2: [(0, "KwinN", 0)]}[cb]
                else:
                    fm = {0: [(0, "kraw", 16), (256, "vraw", 16)], 1: [(0, "Kslc", q0)], 2: [(0, "Kwin", q0)]}[cb]
                for lc, name, c0 in fm:
                    for g in range(G_B):
                        pt, b_p = next_bank()
                        for k in range(KC):
                            S.op(pe, lambda e, k=k, g=g, lc=lc, pt=pt: e.matmul(pt[0:64, 0:ntok], wv[:, k, lc + g * 64: lc + (g + 1) * 64],
                                                                              hT[:, k, 0:ntok], start=(k == 0), stop=(k == KC - 1)),
                                 reads=[b_w, b_hT], writes=[b_p], inc=(k == KC - 1))
                        S.op(dve, lambda e, g=g, name=name, c0=c0, pt=pt: e.tensor_copy(NS[name][0:64, g, c0:c0 + ntok], pt[0:64, 0:ntok]),
                             reads=[b_p], writes=[b_K[name]])
                        if name in ("kraw", "vraw"):
                            S.op(act, lambda e, g=g, name=name, c0=c0, pt=pt: e.activation(NS[name][64:128, g, c0 - 1:c0 - 1 + ntok], pt[0:64, 0:ntok], AF.Copy),
                                 reads=[b_p], writes=[b_K[name]])
            if not sample:
                S.dma(sp, kv_p[q0:q0 + ST, :].rearrange("(t p) n -> p t n", p=128), kvo[:, :, 0:1024], reads=[b_kvo])
                if q0 >= SEQ - 512:
                    w0 = q0 - (SEQ - 512)
                    S.dma(sp, win_p[w0:w0 + ST, :].rearrange("(t p) n -> p t n", p=128), kvo[:, :, 1024:1536], reads=[b_kvo])
                if DEV.get("nsa", True):
                    for t in range(ntok // 128):
                        kt = q0 // 128 + t
                        S.op(dve, lambda e, t=t, kt=kt: e.tensor_copy(NS["Vslc"][:, kt], kvo[:, t, 768:1024].rearrange("p (g d) -> p g d", g=G_B)),
                             reads=[b_kvo], writes=[b_K["Vslc"]])
                        S.op(dve, lambda e, t=t, kt=kt: e.tensor_copy(NS["Vwin"][:, kt], kvo[:, t, 1280:1536].rearrange("p (g d) -> p g d", g=G_B)),
                             reads=[b_kvo], writes=[b_K["Vwin"]])
            else:
                if DEV.get("nsa", True):
                    S.op(dve, lambda e: e.tensor_copy(NS["VslcN"][:], kvo[:, 0, 768:1024].rearrange("p (g d) -> p g d", g=G_B)),
                         reads=[b_kvo], writes=[b_K["VslcN"]])
                    S.op(dve, lambda e: e.tensor_copy(NS["VwinN"][:], kvo[:, 0, 1280:1536].rearrange("p (g d) -> p g d", g=G_B)),
                         reads=[b_kvo], writes=[b_K["VwinN"]])
                S.dma(sp, kv_s, kvo[:, 0, 0:1024], reads=[b_kvo])
                for b in range(NSEQ_S):
                    S.dma(sp, win_s[b, 504:512, :], kvo[b * 8:(b + 1) * 8, 0, 1024:1536], reads=[b_kvo])
                    S.dma(sp, win_s[b, 0:504, :], state_win[b, 8:512, :])
            S.barrier()

    def compress_setup():
        with ExitStack() as ph:
            peT, b_peT = sb("peT", [128, 2, 16], F32, ph)
            peTb, b_peTb = sb("peTb", [128, 2, 16], BF16, ph)
            for i, src in enumerate((pe_ck, pe_cv)):
                S.dma(sp, peT[:, i, :], src.rearrange("(i t) d -> (t d) i", t=2), writes=[b_peT], allow_slow_non_contiguous=True)
            S.op(dve, lambda e: e.tensor_copy(peTb[:], peT[:]), reads=[b_peT], writes=[b_peTb])
            for i, (w1, w2) in enumerate(((w_ck1, w_ck2), (w_cv1, w_cv2))):
                S.dma(pool, NS["w2"][:, i, :], w2, writes=[b_K["cmpw"]])
                wt, b_w = next_wbuf()
                wv = wt[:, 0:16 * 128].rearrange("p (s n) -> p s n", s=16)
                S.dma(pool, wv, w1.rearrange("(i q) n -> q i n", q=128), writes=[b_w])
                pt, b_p = next_bank()
                for sp_ in range(16):
                    S.op(pe, lambda e, sp_=sp_, i=i, wv=wv, pt=pt: e.matmul(pt[:, 0:1], wv[:, sp_, :], peTb[:, i, sp_:sp_ + 1],
                                                                         start=(sp_ == 0), stop=(sp_ == 15)),
                         reads=[b_w, b_peTb], writes=[b_p], inc=(sp_ == 15))
                S.op(dve, lambda e, i=i, pt=pt: e.tensor_copy(NS["bz"][:, i:i + 1], pt[:, 0:1]), reads=[b_p], writes=[b_K["cmpw"]])
            S.barrier()

    def compress(st, ntok=ST, s0_=None, sz_pre=None):
        if s0_ is None:
            s0_ = (st * ST) // 16
        nsl = ntok // 16
        with ExitStack() as ph:
            if sz_pre is None:
                sz, b_sz = sb("sz", [128, 2, G_B * nsl], BF16, ph)
            else:
                sz, b_sz = sz_pre
            for i, (w1, raw) in enumerate(((w_ck1, "kraw"), (w_cv1, "vraw"))):
                if "w1s" in NS:
                    wv, b_w = NS["w1s"][:, i], b_K["cmpw"]
                else:
                    wt, b_w = next_wbuf()
                    wv = wt[:, 0:16 * 128].rearrange("p (s n) -> p s n", s=16)
                    S.dma(pool, wv, w1.rearrange("(i q) n -> q i n", q=128), writes=[b_w])
                pz, b_pz = next_bank()
                for sp_ in range(16):
                    rhs = NS[raw][:, :, 2 * sp_:2 * sp_ + 16 * (nsl - 1) + 1:16]
                    S.op(pe, lambda e, sp_=sp_, wv=wv, rhs=rhs, pz=pz: e.matmul(pz[:, 0:G_B * nsl].rearrange("p (g m) -> p g m", g=G_B),
                                                                             wv[:, sp_, :], rhs, start=(sp_ == 0), stop=(sp_ == 15)),
                         reads=[b_w, b_K[raw]], writes=[b_pz], inc=(sp_ == 15))
                S.op(act, lambda e, i=i, pz=pz: e.activation(sz[:, i, :], pz[:, 0:G_B * nsl], AF.Silu, bias=NS["bz"][:, i:i + 1]),
                     reads=[b_pz, b_K["cmpw"]], writes=[b_sz])
                S.op(dve, lambda e, raw=raw: e.tensor_copy(NS[raw][0:64, :, 0:16], NS[raw][0:64, :, ntok:ntok + 16]),
                     reads=[b_K[raw]], writes=[b_K[raw]])
                S.op(dve, lambda e, raw=raw: e.tensor_copy(NS[raw][64:128, :, 0:15], NS[raw][64:128, :, ntok:ntok + 15]),
                     reads=[b_K[raw]], writes=[b_K[raw]])
            pk, b_pk = next_bank()
            S.op(pe, lambda e: e.matmul(pk[0:64, 0:G_B * nsl], NS["w2"][:, 0, :], sz[:, 0, :], start=True, stop=True),
                 reads=[b_sz, b_K["cmpw"]], writes=[b_pk])
            S.op(act, lambda e: e.activation(NS["Kcmp"][0:64, :, s0_:s0_ + nsl], pk[0:64, 0:G_B * nsl].rearrange("p (g m) -> p g m", g=G_B), AF.Copy),
                 reads=[b_pk], writes=[b_K["Kcmp"]])
            S.op(dve, lambda e: e.tensor_copy(NS["szv"][:, :, s0_:s0_ + nsl], sz[:, 1, :].rearrange("p (g m) -> p g m", g=G_B)),
                 reads=[b_sz], writes=[b_K["szv"]])
            for g in range(G_B):
                pv_, b_pv = next_bank()
                S.op(pe, lambda e, g=g, pv_=pv_: e.matmul(pv_[:, 0:64], NS["szv"][:, g, :], NS["w2"][:, 1, :], start=True, stop=True),
                     reads=[b_K["szv"], b_K["cmpw"]], writes=[b_pv])
                S.op(act, lambda e, g=g, pv_=pv_: e.activation(NS["Vcmp"][:, g, :], pv_[:, 0:64], AF.Copy), reads=[b_pv], writes=[b_K["Vcmp"]])
            if sz_pre is None:
                S.barrier()

    def nsa_prompt(st):
        q0 = st * ST
        NROT[0] = 4
        Tc, Tl, ovl, ident_b = (ctab[k][0] for k in ("Tc", "Tl", "ovl", "ident_b"))
        with ExitStack() as ph:
            norm_mod(ST, False, gmul[:, 1, 0], modv(1, 0), hT, b_hT, ph)
            S.barrier()
        with ExitStack() as ph:
            Qaug, b_Q = sb("Qaug", [128, 16, ST], BF16, ph)
            oTn, b_oTn = sb("oTn", [64, 16, ST], BF16, ph)
            negT, b_negT = sb("negT", [32, ST], BF16, ph)
            PTs = [sb(f"PT{i}", [128, ST], BF16, ph) for i in range(4)]
            pt_rr = [0]
            Pn = [sb(f"Pn{i}", [128, ST], F32, ph) for i in range(4)]
            ocmp, b_ocmp = sb("ocmp", [64, 4, ST], F32, ph)
            rec, b_rec = sb("rec", [128, ST], F32, ph)
            r2, b_r2 = sb("r2", [64, ST], F32, ph)
            oacc, b_oacc = sb("oacc", [64, ST], F32, ph)
            otm, b_otm = sb("otm", [64, ST], F32, ph)
            scq, b_scq = sb("scq", [128, 32], F32, ph)
            scq2, b_scq2 = sb("scq2", [128, 32], F32, ph)
            m8, b_m8 = sb("m8", [128, 8], F32, ph)
            S.dma(sp, Qaug[64:73, :, :], cin["qconst"][:, :, q0:q0 + ST], writes=[b_Q])
            for half in range(2):
                wv, b_w = load_w("w_nsa_in", [(half * 512, 512)], KC)
                for hh in range(8):
                    h = half * 8 + hh
                    pt, b_p = next_bank()
                    for k in range(KC):
                        S.op(pe, lambda e, k=k, hh=hh, pt=pt, wv=wv: e.matmul(pt[0:64, 0:ST], wv[:, k, hh * 64:(hh + 1) * 64], hT[:, k, 0:ST],
                                                                           start=(k == 0), stop=(k == KC - 1)),
                             reads=[b_w, b_hT], writes=[b_p], inc=(k == KC - 1))
                    S.op(act, lambda e, h=h, pt=pt: e.activation(Qaug[0:64, h, :], pt[0:64, 0:ST], AF.Copy, scale=DH ** -0.5),
                         reads=[b_p], writes=[b_Q])
            wg, b_wg = load_w("w_nsa_in", [(1024, 48)], KC)
            Gs, b_Gs = sb("Gs", [48, ST], F32, ph)
            dsb, b_dsb = sb("dsb", [64, ST], F32, ph)
            pgl, b_pgl = next_bank()
            for k in range(KC):
                S.op(pe, lambda e, k=k: e.matmul(pgl[0:48, 0:ST], wg[:, k, 0:48], hT[:, k, 0:ST], start=(k == 0), stop=(k == KC - 1)),
                     reads=[b_wg, b_hT], writes=[b_pgl], inc=(k == KC - 1))
            S.op(act, lambda e: e.activation(Gs[:], pgl[0:48, 0:ST], AF.Sigmoid), reads=[b_pgl], writes=[b_Gs])

            def gate(h, br):
                col = 3 * h + br
                pg, b_pg = next_bank()
                S.op(pe, lambda e, pg=pg: e.matmul(pg[0:64, 0:ST], ident_f[0:48, col:col + 1].broadcast_to([48, 64]), Gs[:],
                                                  start=True, stop=True), reads=[b_Gs, b_ident], writes=[b_pg])
                return pg, b_pg

            def score_tile(mms):
                ps_, b_ps = next_bank()
                for i, (l_, r_, rb) in enumerate(mms):
                    S.op(pe, lambda e, l_=l_, r_=r_, i=i, ps_=ps_: e.matmul(ps_[:, 0:ST], l_, r_, start=(i == 0), stop=(i == len(mms) - 1)),
                         reads=rb, writes=[b_ps], inc=(i == len(mms) - 1))
                PT, b_PT = PTs[pt_rr[0] % 4]
                pt_rr[0] += 1
                S.op(act, lambda e, PT=PT, ps_=ps_: e.activation(PT[:], ps_[:, 0:ST], AF.Exp), reads=[b_ps], writes=[b_PT])
                return PT, b_PT

            for g in range(G_B):
                for hi in range(4):
                    h = 4 * g + hi
                    PT, b_PT = score_tile([(NS["Kcmp"][0:73, g, :], Qaug[0:73, h, :], [b_K["Kcmp"], b_Q]),
                                           (ident_b[:], NS["Tcmp"][:, q0:q0 + ST], b_ctab + [b_K["ntab"]])])
                    px, b_px = next_bank()
                    S.op(pe, lambda e, PT=PT, px=px: e.matmul(px[:, 0:ST], ones_b[:], PT[:], start=True, stop=True),
                         reads=[b_PT, b_ones], writes=[b_px])
                    S.op(dve, lambda e, px=px: e.tensor_scalar(rec[:], px[:, 0:ST], 1e-30, None, op0=ALU.max), reads=[b_px], writes=[b_rec])
                    S.op(dve, lambda e: e.reciprocal(rec[:], rec[:]), reads=[b_rec], writes=[b_rec])
                    S.op(dve, lambda e, PT=PT, hi=hi: e.tensor_tensor(Pn[hi][0][:], PT[:], rec[:], op=ALU.mult),
                         reads=[b_PT, b_rec], writes=[Pn[hi][1]])
                    po_, b_po_ = next_bank()
                    S.op(pe, lambda e, PT=PT, po_=po_: e.matmul(po_[0:64, 0:ST], NS["Vcmp"][:, g, :], PT[:], start=True, stop=True),
                         reads=[b_PT, b_K["Vcmp"]], writes=[b_po_])
                    pg, b_pg = gate(h, 0)
                    S.op(dve, lambda e, pg=pg: e.tensor_tensor(r2[:], pg[0:64, 0:ST], rec[0:64, :], op=ALU.mult), reads=[b_rec, b_pg], writes=[b_r2])
                    S.op(dve, lambda e, hi=hi, po_=po_: e.tensor_tensor(ocmp[:, hi, :], po_[0:64, 0:ST], r2[:], op=ALU.mult),
                         reads=[b_po_, b_r2], writes=[b_ocmp])
                for tq in range(JT):
                    pi_, b_pi = next_bank()
                    for hi in range(4):
                        S.op(pe, lambda e, hi=hi, tq=tq, pi_=pi_: e.matmul(pi_[:, 0:32], Pn[hi][0][:, tq * 128:(tq + 1) * 128], ovl[:],
                                                                        start=(hi == 0), stop=(hi == 3)),
                             reads=[Pn[hi][1]] + b_ctab, writes=[b_pi], inc=(hi == 3))
                    S.op(dve, lambda e, tq=tq, pi_=pi_: e.tensor_tensor(scq[:], pi_[:, 0:32], NS["fb"][:, st * JT + tq, :], op=ALU.add),
                         reads=[b_pi, b_K["ntab"]], writes=[b_scq])
                    S.op(dve, lambda e: e.max(out=m8[:], in_=scq[:]), reads=[b_scq], writes=[b_m8])
                    S.op(dve, lambda e: e.match_replace(out=scq2[:], in_to_replace=m8[:], in_values=scq[:], imm_value=-1e30),
                         reads=[b_scq, b_m8], writes=[b_scq2])
                    S.op(dve, lambda e: e.max(out=m8[:], in_=scq2[:]), reads=[b_scq2], writes=[b_m8])
                    S.op(dve, lambda e: e.tensor_scalar(scq2[:], scq[:], m8[:, 7:8], None, op0=ALU.is_ge), reads=[b_scq, b_m8], writes=[b_scq2])
                    S.op(dve, lambda e: e.tensor_scalar(scq2[:], scq2[:], -NEG, NEG, op0=ALU.mult, op1=ALU.add), reads=[b_scq2], writes=[b_scq2])
                    pT_, b_pT = next_bank()
                    S.op(pe, lambda e, pT_=pT_: e.transpose(pT_[0:32, 0:128], scq2[:], ident_f[:]), reads=[b_scq2, b_ident], writes=[b_pT])
                    S.op(act, lambda e, tq=tq, pT_=pT_: e.activation(negT[:, tq * 128:(tq + 1) * 128], pT_[0:32, 0:128], AF.Copy),
                         reads=[b_pT], writes=[b_negT])
                work = []
                for hi in range(4):
                    h = 4 * g + hi
                    for bi, br in enumerate((1, 2)):
                        if br == 1:
                            tiles = list(range(0, (q0 + ST) // 128))
                            Kt, Vt, bK, bV = NS["Kslc"], NS["Vslc"], b_K["Kslc"], b_K["Vslc"]
                        else:
                            tiles = list(range(max(0, (q0 - 512) // 128), (q0 + ST) // 128))
                            Kt, Vt, bK, bV = NS["Kwin"], NS["Vwin"], b_K["Kwin"], b_K["Vwin"]
                        for idx, kt in enumerate(tiles):
                            k0 = kt * 128
                            mms = [(Kt[0:73, g, k0:k0 + 128], Qaug[0:73, h, :], [bK, b_Q])]
                            if br == 1:
                                mms.append((NS["Emat"][:, k0:k0 + 128], negT[:], [b_K["ntab"], b_negT]))
                            if k0 >= q0:
                                j = (k0 - q0) // 128
                                mms.append((ident_b[:], Tc[:, 128 * (JT - 1 - j): 128 * (JT - 1 - j) + ST], b_ctab))
                            elif br == 2 and k0 < q0 - 512 + 128 * JT:
                                j = (k0 - (q0 - 512)) // 128
                                mms.append((ident_b[:], Tl[:, 128 * (JT - 1 - j): 128 * (JT - 1 - j) + ST], b_ctab))
                            work.append(dict(mms=mms, V=Vt[:, kt, g, :], bV=bV, first=(idx == 0), last=(idx == len(tiles) - 1),
                                             h=h, hi=hi, br=br, bi=bi))

                def emit_pv(w, PT, b_PT):
                    (pso, b_pso), (psd, b_psd) = RBANKS[2 * (w["bi"] % 2)], RBANKS[2 * (w["bi"] % 2) + 1]
                    S.op(pe, lambda e: e.matmul(pso[0:64, 0:ST], w["V"], PT[:], start=w["first"], stop=w["last"]),
                         reads=[b_PT, w["bV"]], writes=[b_pso], inc=w["last"])
                    S.op(pe, lambda e: e.matmul(psd[0:64, 0:ST], ones_b[:, 0:64], PT[:], start=w["first"], stop=w["last"]),
                         reads=[b_PT, b_ones], writes=[b_psd], inc=w["last"])
                    if not w["last"]:
                        return
                    h, hi, br = w["h"], w["hi"], w["br"]
                    S.op(dve, lambda e: e.reciprocal(dsb[:], psd[0:64, 0:ST]), reads=[b_psd], writes=[b_dsb])
                    pg, b_pg = gate(h, br)
                    S.op(dve, lambda e: e.tensor_tensor(r2[:], pg[0:64, 0:ST], dsb[:], op=ALU.mult), reads=[b_dsb, b_pg], writes=[b_r2])
                    S.op(dve, lambda e: e.tensor_tensor(otm[:], pso[0:64, 0:ST], r2[:], op=ALU.mult),
                         reads=[b_pso, b_r2], writes=[b_otm])
                    if br == 1:
                        S.op(dve, lambda e: e.tensor_tensor(oacc[:], otm[:], ocmp[:, hi, :], op=ALU.add),
                             reads=[b_otm, b_ocmp], writes=[b_oacc])
                    else:
                        S.op(dve, lambda e: e.tensor_tensor(oTn[:, h, :], otm[:], oacc[:], op=ALU.add),
                             reads=[b_otm, b_oacc], writes=[b_oTn])

                DEPTH = 3
                pend = []
                for w in work:
                    PT, b_PT = score_tile(w["mms"])
                    pend.append((w, PT, b_PT))
                    if len(pend) > DEPTH:
                        emit_pv(*pend.pop(0))
                while pend:
                    emit_pv(*pend.pop(0))
            for half in range(2):
                wv, b_w = load_w("w_nsa_out", [(half * 512, 512)], 16, rows_per_k=64)
                for oc4 in range(4):
                    oc = half * 4 + oc4
                    pt, b_p = next_bank()
                    for h in range(16):
                        S.op(pe, lambda e, h=h, oc4=oc4, pt=pt, wv=wv: e.matmul(pt[:, 0:ST], wv[:, h, oc4 * 128:(oc4 + 1) * 128], oTn[:, h, :],
                                                                             start=(h == 0), stop=(h == 15)),
                             reads=[b_w, b_oTn], writes=[b_p], inc=(h == 15))
                    resid_add(oc, pt, ST, False, modv(1, 2), b_p)
            S.barrier()

    def nsa_sample():
        ovl, ident_b = ctab["ovl"][0], ctab["ident_b"][0]
        NROT[0] = 4
        NT = 128
        with ExitStack() as ph:
            norm_mod(NT, True, gmul[:, 1, 0], modv(1, 0), hT, b_hT, ph)
            S.barrier()
        with ExitStack() as ph:
            def tab(name, shape, dt):
                t, b = sb(name, shape, dt, ph)
                S.dma(sp, t[:], cin[name], writes=[b])
                return t, b
            NS["Emat"], _ = sb("EmatS", [32, SEQ], BF16, ph)
            S.dma(sp, NS["Emat"][:], cin["Emat"], writes=[b_K["ntab"]])
            TnewS, b_Tnew = tab("TnewS", [128, 16, 32], BF16)
            TlS, b_TlS = tab("TlS", [128, 32], BF16)
            TcS, b_TcS = tab("TcS", [128, 1], BF16)
            fbS, b_fbS = tab("fbS", [8, 33], F32)
            ovl33, b_ovl33 = tab("ovl33", [128, 33], F32)
            pidx, b_pidx = tab("pidx", [128, 1], F32)
            ptab, b_ptab = sb("ptab", [128, NSEQ_S * NPAGE], I32, ph)
            ptf, b_ptf = sb("ptf", [128, NSEQ_S * NPAGE], F32, ph)
            pgidx, b_pgidx = sb("pgidx", [128, NSEQ_S * NPAGE], I32, ph)
            S.dma(sp, ptab[:], page_table.rearrange("b j -> (b j)").partition_broadcast(128), writes=[b_ptab])
            S.op(dve, lambda e: e.tensor_copy(ptf[:], ptab[:]), reads=[b_ptab], writes=[b_ptf])
            S.op(dve, lambda e: e.tensor_scalar(ptf[:], ptf[:], 128.0, pidx[:, 0:1], op0=ALU.mult, op1=ALU.add),
                 reads=[b_ptf, b_pidx], writes=[b_ptf])
            S.op(dve, lambda e: e.tensor_copy(pgidx[:], ptf[:]), reads=[b_ptf], writes=[b_pgidx])
            b_tabs = [b_Tnew, b_TlS, b_TcS, b_fbS, b_ovl33, b_K["ntab"]] + b_ctab
            for nm, ncol in (("Kslc", SEQ), ("Kwin", 512), ("Kcmp", 128)):
                NS[nm], _ = sb(nm + "S", [128, G_B, ncol], BF16, ph)
            NS["Vslc"], _ = sb("VslcS", [128, NPAGE, G_B, DH], BF16, ph)
            NS["Vwin"], _ = sb("VwinS", [128, 4, G_B, DH], BF16, ph)
            NS["Vcmp"], _ = sb("VcmpS", [128, G_B, DH], BF16, ph)
            NS["szv"], _ = sb("szvS", [128, G_B, 128], BF16, ph)
            NS["kraw"], _ = sb("krawS", [128, G_B, 512 + 16], BF16, ph)
            NS["vraw"], _ = sb("vrawS", [128, G_B, 512 + 16], BF16, ph)
            NS["w1s"], _ = sb("w1s", [128, 2, 16, 128], BF16, ph)
            for i_, w1_ in enumerate((w_ck1, w_cv1)):
                S.dma(pool, NS["w1s"][:, i_], w1_.rearrange("(i q) n -> q i n", q=128), writes=[b_K["cmpw"]])
            szS = sb("szS", [128, 2, G_B * 32], BF16, ph)
            for g in range(G_B):
                S.dma(sp, NS["Kslc"][64:73, g, :], cin["kconst"], writes=[b_K["Kslc"]])
                S.dma(sp, NS["Kwin"][64:73, g, :], cin["kconst"][:, SEQ - 512:SEQ], writes=[b_K["Kwin"]])
                S.dma(sp, NS["Kcmp"][64:73, g, :], cin["kconst_cmp"], writes=[b_K["Kcmp"]])
            S.op(dve, lambda e: e.memset(NS["Kcmp"][0:64], 0.0), writes=[b_K["Kcmp"]])
            pages = [sb(f"page{i}", [128, 1024], F32, ph) for i in range(3)]
            wins = [sb(f"wint{i}", [128, 512], F32, ph) for i in range(1)]
            QaugS, b_Q = sb("QaugS", [128, NSEQ_S, 128], BF16, ph)
            obr = [sb(f"obr{i}", [64, 16, NT], F32, ph) for i in range(3)]
            negTS, b_negT = sb("negTS", [32, 128], BF16, ph)
            PTs = [sb(f"PTS{i}", [128, 256], BF16, ph) for i in range(3)]
            pt_rr = [0]
            Pn, b_Pn = sb("PnS", [128, 32], F32, ph)
            rec, b_rec = sb("recS", [128, 32], F32, ph)
            scq, b_scq = sb("scqS", [8, 33], F32, ph)
            scq2, b_scq2 = sb("scq2S", [8, 33], F32, ph)
            m8, b_m8 = sb("m8S", [8, 8], F32, ph)
            oacc, b_oacc = sb("oaccS", [64, NT], F32, ph)
            otm, b_otm = sb("otmS", [64, NT], F32, ph)
            oTn, b_oTn = sb("oTnS", [64, 16, NT], BF16, ph)
            for b in range(NSEQ_S):
                S.dma(sp, QaugS[64:73, b, :].rearrange("p (h t) -> p h t", t=DEC_SEQ), cin["qconst"][:, :, SEQ:SEQ + DEC_SEQ], writes=[b_Q])
            for half in range(2):
                wv, b_w = load_w("w_nsa_in", [(half * 512, 512)], KC)
                for hh in range(8):
                    h = half * 8 + hh
                    pt, b_p = next_bank()
                    for k in range(KC):
                        S.op(pe, lambda e, k=k, hh=hh, pt=pt, wv=wv: e.matmul(pt[0:64, 0:NT], wv[:, k, hh * 64:(hh + 1) * 64], hT[:, k, 0:NT],
                                                                           start=(k == 0), stop=(k == KC - 1)),
                             reads=[b_w, b_hT], writes=[b_p], inc=(k == KC - 1))
                    S.op(act, lambda e, h=h, pt=pt: e.activation(QaugS[0:64, :, h * 8:(h + 1) * 8], v3(pt[0:64, 0:NT]), AF.Copy, scale=DH ** -0.5),
                         reads=[b_p], writes=[b_Q])
            def score_chunk(tiles):
                ps_, b_ps = next_bank()
                n = len(tiles)
                for i, mms in enumerate(tiles):
                    for m, (l_, r_, rb) in enumerate(mms):
                        S.op(pe, lambda e, l_=l_, r_=r_, i=i, m=m, ps_=ps_, mms=mms: e.matmul(
                            ps_[:, i * 32:(i + 1) * 32], l_, r_, start=(m == 0), stop=(m == len(mms) - 1)),
                            reads=rb, writes=[b_ps], inc=(i == n - 1 and m == len(mms) - 1))
                PT, b_PT = PTs[pt_rr[0] % 3]
                pt_rr[0] += 1
                S.op(act, lambda e, PT=PT, ps_=ps_: e.activation(PT[:, 0:32 * n], ps_[:, 0:32 * n], AF.Exp), reads=[b_ps], writes=[b_PT])
                return PT, b_PT

            SPE = mybir.EngineType.SP
            for b in range(NSEQ_S):
                S.op(dve, lambda e: e.memset(NS["kraw"][:, :, 0:16], 0.0), writes=[b_K["kraw"]])
                S.op(dve, lambda e: e.memset(NS["vraw"][:, :, 0:16], 0.0), writes=[b_K["vraw"]])
                for j in range(NPAGE):
                    pg, b_pg = pages[(b * NPAGE + j) % 3]
                    cidx = b * NPAGE + j
                    S.dma_gather(pool, pg[:], cache_kv, pgidx[:, cidx:cidx + 1], reads=[b_pgidx], writes=[b_pg])
                    S.op(dve, lambda e, j=j, pg=pg: e.tensor_copy(NS["Vslc"][:, j], pg[:, 768:1024].rearrange("p (g d) -> p g d", g=G_B)),
                         reads=[b_pg], writes=[b_K["Vslc"]])
                    for wi, (base, name, c0) in enumerate(((0, "kraw", 16 + (j % 4) * 128), (256, "vraw", 16 + (j % 4) * 128), (512, "Kslc", j * 128))):
                        pT_, b_pT = next_bank()
                        for g in range(G_B):
                            S.op(pe, lambda e, g=g, base=base, pg=pg, pT_=pT_: e.transpose(pT_[0:64, g * 128:(g + 1) * 128],
                                                                                        pg[:, base + g * 64: base + (g + 1) * 64], ident_f[:]),
                                 reads=[b_pg, b_ident], writes=[b_pT], inc=(g == G_B - 1))
                        dst = NS[name][0:64, :, c0:c0 + 128]
                        src = pT_[0:64, :].rearrange("p (g n) -> p g n", g=G_B)
                        if wi == 1:
                            S.op(dve, lambda e, dst=dst, src=src: e.tensor_copy(dst, src), reads=[b_pT], writes=[b_K[name]])
                        else:
                            S.op(act, lambda e, dst=dst, src=src: e.activation(dst, src, AF.Copy), reads=[b_pT], writes=[b_K[name]])
                        if wi < 2:
                            dstu = NS[name][64:128, :, c0 - 1:c0 - 1 + 128]
                            if wi == 1:
                                S.op(act, lambda e, dstu=dstu, src=src: e.activation(dstu, src, AF.Copy), reads=[b_pT], writes=[b_K[name]])
                            else:
                                S.op(dve, lambda e, dstu=dstu, src=src: e.tensor_copy(dstu, src), reads=[b_pT], writes=[b_K[name]])
                    if j % 4 == 3:
                        compress(0, ntok=512, s0_=32 * (j // 4), sz_pre=szS)
                for tl in range(4):
                    wt_, b_wt = wins[0]
                    S.dma(sp, wt_[:], state_win[b, tl * 128:(tl + 1) * 128, :], writes=[b_wt])
                    S.op(dve, lambda e, tl=tl, wt_=wt_: e.tensor_copy(NS["Vwin"][:, tl], wt_[:, 256:512].rearrange("p (g d) -> p g d", g=G_B)),
                         reads=[b_wt], writes=[b_K["Vwin"]])
                    pT_, b_pT = next_bank()
                    for g in range(G_B):
                        S.op(pe, lambda e, g=g, wt_=wt_, pT_=pT_: e.transpose(pT_[0:64, g * 128:(g + 1) * 128], wt_[:, g * 64:(g + 1) * 64], ident_f[:]),
                             reads=[b_wt, b_ident], writes=[b_pT], inc=(g == G_B - 1))
                    S.op(act, lambda e, tl=tl, pT_=pT_: e.activation(NS["Kwin"][0:64, :, tl * 128:(tl + 1) * 128],
                                                                   pT_[0:64, :].rearrange("p (g n) -> p g n", g=G_B), AF.Copy),
                         reads=[b_pT], writes=[b_K["Kwin"]])
                bs = slice(b * DEC_SEQ, (b + 1) * DEC_SEQ)
                for g in range(G_B):
                    Qg = QaugS[0:73, b, g * 32:(g + 1) * 32]
                    gs = slice(4 * g, 4 * g + 4)
                    PT, b_PT = score_chunk([[(NS["Kcmp"][0:73, g, :], Qg, [b_K["Kcmp"], b_Q]),
                                             (ident_b[:], TcS[:, 0:1].broadcast_to([128, 32]), b_tabs)]])
                    px, b_px = next_bank()
                    S.op(pe, lambda e, PT=PT, px=px: e.matmul(px[:, 0:32], ones_b[:], PT[:, 0:32], start=True, stop=True),
                         reads=[b_PT, b_ones], writes=[b_px])
                    S.op(dve, lambda e, px=px: e.tensor_scalar(rec[:], px[:, 0:32], 1e-30, None, op0=ALU.max), reads=[b_px], writes=[b_rec])
                    S.op(dve, lambda e: e.reciprocal(rec[:], rec[:]), reads=[b_rec], writes=[b_rec])
                    S.op(dve, lambda e, PT=PT: e.tensor_tensor(Pn[:], PT[:, 0:32], rec[:], op=ALU.mult), reads=[b_PT, b_rec], writes=[b_Pn])
                    po_, b_po_ = next_bank()
                    S.op(pe, lambda e, PT=PT, po_=po_: e.matmul(po_[0:64, 0:32], NS["Vcmp"][:, g, :], PT[:, 0:32], start=True, stop=True),
                         reads=[b_PT, b_K["Vcmp"]], writes=[b_po_])
                    S.op(dve, lambda e, po_=po_: e.tensor_tensor(obr[0][0][:, gs, bs], po_[0:64, 0:32].rearrange("p (h t) -> p h t", t=DEC_SEQ),
                                                                rec[0:64, :].rearrange("p (h t) -> p h t", t=DEC_SEQ), op=ALU.mult),
                         reads=[b_po_, b_rec], writes=[obr[0][1]])
                    pi_, b_pi = next_bank()
                    for hi in range(4):
                        S.op(pe, lambda e, hi=hi, pi_=pi_: e.matmul(pi_[0:8, 0:33], Pn[:, hi * 8:(hi + 1) * 8], ovl33[:], start=(hi == 0), stop=(hi == 3)),
                             reads=[b_Pn] + b_tabs, writes=[b_pi], inc=(hi == 3))
                    S.op(dve, lambda e, pi_=pi_: e.tensor_tensor(scq[:], pi_[0:8, 0:33], fbS[:], op=ALU.add), reads=[b_pi] + b_tabs, writes=[b_scq])
                    S.op(dve, lambda e: e.max(out=m8[:], in_=scq[:]), reads=[b_scq], writes=[b_m8])
                    S.op(dve, lambda e: e.match_replace(out=scq2[:], in_to_replace=m8[:], in_values=scq[:], imm_value=-1e30),
                         reads=[b_scq, b_m8], writes=[b_scq2])
                    S.op(dve, lambda e: e.max(out=m8[:], in_=scq2[:]), reads=[b_scq2], writes=[b_m8])
                    S.op(dve, lambda e: e.tensor_scalar(scq2[:], scq[:], m8[:, 7:8], None, op0=ALU.is_ge), reads=[b_scq, b_m8], writes=[b_scq2])
                    S.op(dve, lambda e: e.tensor_scalar(scq2[:], scq2[:], -NEG, NEG, op0=ALU.mult, op1=ALU.add), reads=[b_scq2], writes=[b_scq2])
                    pT_, b_pT = next_bank()
                    S.op(pe, lambda e, pT_=pT_: e.transpose(pT_[0:32, 0:8], scq2[:, 0:32], ident_f[0:8, 0:8]), reads=[b_scq2, b_ident], writes=[b_pT])
                    S.op(act, lambda e, pT_=pT_, g=g: e.activation(negTS[:, g * 32:(g + 1) * 32].rearrange("p (h t) -> p h t", t=DEC_SEQ),
                                                                 pT_[0:32, 0:8].unsqueeze(1).broadcast_to([32, 4, DEC_SEQ]), AF.Copy),
                         reads=[b_pT], writes=[b_negT])
                    for bi, br in enumerate((1, 2)):
                        (pso, b_pso), (psd, b_psd) = RBANKS[2 * bi], RBANKS[2 * bi + 1]
                        tl_list = []
                        if br == 1:
                            for kt in range(NPAGE):
                                tl_list.append(([(NS["Kslc"][0:73, g, kt * 128:(kt + 1) * 128], Qg, [b_K["Kslc"], b_Q]),
                                                 (NS["Emat"][:, kt * 128:(kt + 1) * 128], negTS[:, g * 32:(g + 1) * 32], [b_K["ntab"], b_negT])],
                                                NS["Vslc"][:, kt, g, :], b_K["Vslc"]))
                            tl_list.append(([(NS["KslcN"][0:73, g, :], Qg, [b_K["KslcN"], b_Q]), (ident_b[:], TnewS[:, b, :], b_tabs)],
                                            NS["VslcN"][:, g, :], b_K["VslcN"]))
                        else:
                            for kt in range(4):
                                mm_ = [(NS["Kwin"][0:73, g, kt * 128:(kt + 1) * 128], Qg, [b_K["Kwin"], b_Q])]
                                if kt == 0:
                                    mm_.append((ident_b[:], TlS[:], b_tabs))
                                tl_list.append((mm_, NS["Vwin"][:, kt, g, :], b_K["Vwin"]))
                            tl_list.append(([(NS["KwinN"][0:73, g, :], Qg, [b_K["KwinN"], b_Q]), (ident_b[:], TnewS[:, b, :], b_tabs)],
                                            NS["VwinN"][:, g, :], b_K["VwinN"]))
                        ntl = len(tl_list)
                        done = 0
                        for c0 in range(0, ntl, 8):
                            chunk = tl_list[c0:c0 + 8]
                            PT, b_PT = score_chunk([c[0] for c in chunk])
                            for i, (_, Vap, bV) in enumerate(chunk):
                                first, last = (done == 0), (done == ntl - 1)
                                S.op(pe, lambda e, PT=PT, i=i, Vap=Vap, first=first, last=last, pso=pso: e.matmul(
                                    pso[0:64, 0:32], Vap, PT[:, i * 32:(i + 1) * 32], start=first, stop=last),
                                    reads=[b_PT, bV], writes=[b_pso], inc=last)
                                S.op(pe, lambda e, PT=PT, i=i, first=first, last=last, psd=psd: e.matmul(
                                    psd[0:64, 0:32], ones_b[:, 0:64], PT[:, i * 32:(i + 1) * 32], start=first, stop=last),
                                    reads=[b_PT, b_ones], writes=[b_psd], inc=last)
                                done += 1
                        S.op(dve, lambda e, psd=psd: e.reciprocal(rec[0:64, :], psd[0:64, 0:32]), reads=[b_psd], writes=[b_rec])
                        S.op(dve, lambda e, pso=pso, br=br: e.tensor_tensor(obr[br][0][:, gs, bs], pso[0:64, 0:32].rearrange("p (h t) -> p h t", t=DEC_SEQ),
                                                                          rec[0:64, :].rearrange("p (h t) -> p h t", t=DEC_SEQ), op=ALU.mult),
                             reads=[b_pso, b_rec], writes=[obr[br][1]])
            wg, b_wg = load_w("w_nsa_in", [(1024, 48)], KC)
            GsS, b_GsS = sb("GsS", [48, NT], F32, ph)
            pgl, b_pgl = next_bank()
            for k in range(KC):
                S.op(pe, lambda e, k=k: e.matmul(pgl[0:48, 0:NT], wg[:, k, 0:48], hT[:, k, 0:NT], start=(k == 0), stop=(k == KC - 1)),
                     reads=[b_wg, b_hT], writes=[b_pgl], inc=(k == KC - 1))
            S.op(act, lambda e: e.activation(GsS[:], pgl[0:48, 0:NT], AF.Sigmoid), reads=[b_pgl], writes=[b_GsS])
            for h in range(16):
                for br in range(3):
                    col = 3 * h + br
                    pg_, b_pg_ = next_bank()
                    S.op(pe, lambda e, pg_=pg_, col=col: e.matmul(pg_[0:64, 0:NT], ident_f[0:48, col:col + 1].broadcast_to([48, 64]), GsS[:],
                                                               start=True, stop=True), reads=[b_GsS, b_ident], writes=[b_pg_])
                    if br == 0:
                        S.op(dve, lambda e, h=h, pg_=pg_: e.tensor_tensor(oacc[:], obr[0][0][:, h, :], pg_[0:64, 0:NT], op=ALU.mult),
                             reads=[obr[0][1], b_pg_], writes=[b_oacc])
                    else:
                        S.op(dve, lambda e, h=h, br=br, pg_=pg_: e.tensor_tensor(otm[:], obr[br][0][:, h, :], pg_[0:64, 0:NT], op=ALU.mult),
                             reads=[obr[br][1], b_pg_], writes=[b_otm])
                        if br == 1:
                            S.op(dve, lambda e: e.tensor_tensor(oacc[:], oacc[:], otm[:], op=ALU.add), reads=[b_oacc, b_otm], writes=[b_oacc])
                        else:
                            S.op(dve, lambda e, h=h: e.tensor_tensor(oTn[:, h, :], oacc[:], otm[:], op=ALU.add), reads=[b_oacc, b_otm], writes=[b_oTn])
            for half in range(2):
                wv, b_w = load_w("w_nsa_out", [(half * 512, 512)], 16, rows_per_k=64)
                for oc4 in range(4):
                    oc = half * 4 + oc4
                    pt, b_p = next_bank()
                    for h in range(16):
                        S.op(pe, lambda e, h=h, oc4=oc4, pt=pt, wv=wv: e.matmul(pt[:, 0:NT], wv[:, h, oc4 * 128:(oc4 + 1) * 128], oTn[:, h, :],
                                                                             start=(h == 0), stop=(h == 15)),
                             reads=[b_w, b_oTn], writes=[b_p], inc=(h == 15))
                    resid_add(oc, pt, NT, True, modv(1, 2), b_p)
            S.barrier()
            NS.pop("w1s", None)

    def final_norm_store(dst_rows, ntok, sample):
        with ExitStack() as ph:
            yT, b_yT = sb("yT", [128, KC, ntok], F32, ph)
            norm_mod(ntok, sample, gfin, zero17, yT, b_yT, ph)
            store_yT(dst_rows, ntok, yT, b_yT)

    if DEV["prompt"]:
        with ExitStack() as pst:
            P["Sst"], _ = sb("Sst", [128, H_A, 2, DV_A], F32, pst)
            P["Sbf"], _ = sb("Sbf", [128, H_A, 2, DV_A], BF16, pst)
            S.op(dve, lambda e: e.memset(P["Sst"][:], 0.0), writes=[b_Sst])
            nsa_on = DEV.get("nsa", True)
            if nsa_on:
                for nm in ("Kslc", "Kwin"):
                    NS[nm], _ = sb(nm, [128, G_B, SEQ], BF16, pst)
                NS["Kcmp"], _ = sb("Kcmp", [128, G_B, 128], BF16, pst)
                NS["Vslc"], _ = sb("Vslc", [128, SEQ // 128, G_B, DH], BF16, pst)
                NS["Vwin"], _ = sb("Vwin", [128, SEQ // 128, G_B, DH], BF16, pst)
                NS["Vcmp"], _ = sb("Vcmp", [128, G_B, DH], BF16, pst)
                NS["kraw"], _ = sb("kraw", [128, G_B, ST + 16], BF16, pst)
                NS["vraw"], _ = sb("vraw", [128, G_B, ST + 16], BF16, pst)
                NS["szv"], _ = sb("szv", [128, G_B, 128], BF16, pst)
                NS["w2"], _ = sb("w2", [128, 2, DH], BF16, pst)
                NS["bz"], _ = sb("bz", [128, 2], F32, pst)
                NS["Tcmp"], _ = sb("Tcmp", [128, SEQ], BF16, pst)
                NS["Emat"], _ = sb("Emat", [32, SEQ], BF16, pst)
                NS["fb"], _ = sb("fb", [128, SEQ // 128, 32], F32, pst)
                for nm in ("Tcmp", "Emat", "fb"):
                    S.dma(sp, NS[nm][:], cin[nm], writes=[b_K["ntab"]])
                for nm in ("Kslc", "Kwin"):
                    for g in range(G_B):
                        S.dma(sp, NS[nm][64:73, g, :], cin["kconst"], writes=[b_K[nm]])
                for g in range(G_B):
                    S.dma(sp, NS["Kcmp"][64:73, g, :], cin["kconst_cmp"], writes=[b_K["Kcmp"]])
                S.op(dve, lambda e: e.memset(NS["Kcmp"][0:64], 0.0), writes=[b_K["Kcmp"]])
                S.op(dve, lambda e: e.memset(NS["szv"][:], 0.0), writes=[b_K["szv"]])
                S.op(dve, lambda e: e.memset(NS["kraw"][:], 0.0), writes=[b_K["kraw"]])
                S.op(dve, lambda e: e.memset(NS["vraw"][:], 0.0), writes=[b_K["vraw"]])
                compress_setup()
            for st in range(NST):
                load_xT(x_p[st * ST:(st + 1) * ST, :], ST)
                retention(st, False)
                ffn(0, st, st == NST - 1, False)
                kv_proj(st, False)
                if nsa_on:
                    compress(st)
                    nsa_prompt(st)
                ffn(1, st, st == NST - 1, False)
                final_norm_store(y_p[st * ST:(st + 1) * ST, :], ST, False)
            S.dma(sp, ret_p.rearrange("h (c p) v -> p h c v", p=128), P["Sst"][:], reads=[b_Sst])
            S.barrier()
    if DEV["sample"]:
        with ExitStack() as sst_:
            P["cbufT"], _ = sb("cbufT", [128, 2, NFC, 2 * NSEQ_S], F32, sst_)
            nsa_on = DEV.get("nsa", True)
            if nsa_on:
                NS.clear()
                NS["w2"], _ = sb("w2S", [128, 2, DH], BF16, sst_)
                NS["bz"], _ = sb("bzS", [128, 2], F32, sst_)
                for nm in ("KslcN", "KwinN"):
                    NS[nm], _ = sb(nm, [128, G_B, 128], BF16, sst_)
                    for g in range(G_B):
                        S.dma(sp, NS[nm][64:73, g, :], cin["kconst_new"], writes=[b_K[nm]])
                NS["VslcN"], _ = sb("VslcN", [128, G_B, DH], BF16, sst_)
                NS["VwinN"], _ = sb("VwinN", [128, G_B, DH], BF16, sst_)
                compress_setup()
            load_conv_bufs()
            load_xT(x_s, 128)
            retention(0, True)
            ffn(0, 0, False, True)
            kv_proj(0, True)
            if nsa_on:
                nsa_sample()
            ffn(1, 0, False, True)
            final_norm_store(y_s, 128, True)
            S.barrier()
    S.finish()


_NC_CACHE = {}


def kernel(**inputs):
    f32 = lambda a: np.ascontiguousarray(np.asarray(a, dtype=np.float32))
    if "nc" not in _NC_CACHE:
        _NC_CACHE["nc"] = build_nc()
    nc = _NC_CACHE["nc"]
    shared = {k: f32(inputs[k]) for k in ["w_ada", "b_ada", "g_mix", "g_ffn", "w_ffn_in", "conv_w", "conv_b",
                                           "w_ffn_out", "w_ret_in", "w_ret_out", "g_final",
                                           "g_kv", "w_ada_kv", "b_ada_kv", "w_kv", "w_nsa_in", "w_nsa_out",
                                           "pe_ck", "pe_cv", "w_ck1", "w_ck2", "w_cv1", "w_cv2"]}
    for k in CONST_IN + CONST_DRAM:
        shared["k_" + k] = np.ascontiguousarray(CONST[k])
    xp, xs = f32(inputs["x_prompt"]), f32(inputs["x_sample"])
    cp, cs = f32(inputs["c_prompt"]), f32(inputs["c_sample"])
    sret, sconv, swin = f32(inputs["state_ret"]), f32(inputs["state_conv"]), f32(inputs["state_win"])
    cache = f32(inputs["cache_kv"]).reshape(NPHYS * 128, 1024)
    ptab = np.ascontiguousarray(np.asarray(inputs["page_table"], dtype=np.int32))
    cores = DEV["cores"] or list(range(N_CORES))
    in_maps = []
    for c in cores:
        m = dict(shared)
        sl = slice(c * NSEQ_S, (c + 1) * NSEQ_S)
        m["x_p"] = xp[c]
        m["x_s"] = xs[sl].reshape(NSEQ_S * DEC_SEQ, D)
        m["c_p"] = cp[c:c + 1]
        m["c_s"] = cs[sl]
        m["state_ret"] = sret[0, sl]
        m["state_conv"] = np.ascontiguousarray(sconv[:, sl]).reshape(2, NSEQ_S * 2, F2)
        m["state_win"] = swin[sl].reshape(NSEQ_S, 512, 512)
        m["cache_kv"] = cache
        m["page_table"] = ptab[sl]
        in_maps.append(m)
    if DEV.get("trace"):
        res = run_bass_kernel_spmd(nc, in_maps, core_ids=list(range(len(cores))), trace=True)
        print("DEV exec_time_ns:", res.exec_time_ns)
    else:
        res = run_bass_kernel_spmd(nc, in_maps, core_ids=list(range(len(cores))))
    R = list(res.results)
    if len(R) < N_CORES:
        full = [None] * N_CORES
        for c, r in zip(cores, R):
            full[c] = r
        z = {k: np.zeros_like(np.asarray(v)) for k, v in R[0].items()}
        R = [r if r is not None else z for r in full]
    cat = lambda k: np.stack([np.asarray(r[k], dtype=np.float32) for r in R])
    y_prompt = cat("y_p")
    y_sample = cat("y_s").reshape(128, DEC_SEQ, D)
    ret_prompt = cat("ret_p")[None]
    ret_sample = cat("ret_s").reshape(1, 128, H_A, DK_A, DV_A)
    conv_prompt = np.ascontiguousarray(cat("conv_p").transpose(1, 0, 2, 3))
    conv_sample = np.ascontiguousarray(cat("conv_s").transpose(1, 0, 2, 3, 4)).reshape(2, 128, 2, F2)
    kv_prompt = cat("kv_p").reshape(8, SEQ, 4, 4, 64)
    kv_sample = cat("kv_s").reshape(128, DEC_SEQ, 4, 4, 64)
    win_prompt = cat("win_p").reshape(8, 512, 2, 4, 64)
    win_sample = cat("win_s").reshape(128, 512, 2, 4, 64)
    return (y_prompt, y_sample, ret_prompt, ret_sample, conv_prompt, conv_sample,
            kv_prompt, kv_sample, win_prompt, win_sample)
```

```python
from contextlib import ExitStack
import numpy as np
import ml_dtypes
import concourse.bass as bass
import concourse.mybir as mybir
from concourse.bass_utils import run_bass_kernel_spmd

F32 = mybir.dt.float32
BF16 = mybir.dt.bfloat16
I32 = mybir.dt.int32
AF = mybir.ActivationFunctionType
ALU = mybir.AluOpType

D = 1024
KC = 8
SEQ = 2048
ST = 256
NST = SEQ // ST
H_A, DK_A, DV_A = 4, 256, 512
D_FF = 2816
F2 = 2 * D_FF
NFC = F2 // 128
NPAIR = D_FF // 128
EPS = 1e-6
NSEQ_S = 16
DEC_SEQ = 8
N_CORES = 8
NEG = -30000.0
NPHYS = 2560
NPAGE = 16
G_B, DH = 4, 64


class Eng:
    def __init__(self, name, e, sem, step=1, is_pe=False):
        self.name, self.e, self.sem, self.step, self.is_pe = name, e, sem, step, is_pe
        self.cnt = 0
        self.waited = {}


class Buf:
    __slots__ = ("name", "w", "r")

    def __init__(self, name):
        self.name, self.w, self.r = name, None, {}


class Sched:
    def __init__(self, nc, es, n_sp=12, n_pool=12, n_act=4):
        self.nc = nc
        sem = lambda n: es.enter_context(nc.semaphore(n))
        self.pe = Eng("pe", nc.tensor, sem("s_pe"), is_pe=True)
        self.act = Eng("act", nc.scalar, sem("s_act"))
        self.dve = Eng("dve", nc.vector, sem("s_dve"))
        self.pool = Eng("pool", nc.gpsimd, sem("s_pool"))
        self.sp = Eng("sp", nc.sync, sem("s_sp"))
        self.engs = [self.pe, self.act, self.dve, self.pool, self.sp]
        self.chans = {
            "sp": [Eng(f"c_sp{i}", None, sem(f"c_sp{i}"), step=16) for i in range(n_sp)],
            "pool": [Eng(f"c_pl{i}", None, sem(f"c_pl{i}"), step=16) for i in range(n_pool)],
            "act": [Eng(f"c_ac{i}", None, sem(f"c_ac{i}"), step=16) for i in range(n_act)],
        }
        self.rr = {"sp": 0, "pool": 0, "act": 0}
        self.bar_deps = {}

    def _deps(self, reads, writes):
        deps = {}
        for b in reads:
            if b.w is not None:
                deps[b.w[0]] = max(deps.get(b.w[0], 0), b.w[1])
        for b in writes:
            if b.w is not None:
                deps[b.w[0]] = max(deps.get(b.w[0], 0), b.w[1])
            for e, n in b.r.items():
                deps[e] = max(deps.get(e, 0), n)
        return deps

    def _wait(self, eng, deps):
        for f, n in deps.items():
            if f is eng and eng.is_pe:
                continue
            if eng.waited.get(f, 0) < n:
                eng.e.wait_ge(f.sem, n * f.step)
                eng.waited[f] = n

    def op(self, eng, fn, reads=(), writes=(), inc=True):
        self._wait(eng, self._deps(reads, writes))
        ins = fn(eng.e)
        n = eng.cnt + 1
        if inc:
            ins.then_inc(eng.sem, eng.step)
            eng.cnt = n
        for b in reads:
            b.r[eng] = max(b.r.get(eng, 0), n)
        for b in writes:
            b.w = (eng, n)
            b.r = {}
        return ins

    def dma(self, issuer, out, in_, reads=(), writes=(), persistent=False, **kw):
        lst = self.chans[issuer.name]
        ch = lst[self.rr[issuer.name] % len(lst)]
        self.rr[issuer.name] += 1
        deps = self._deps(reads, writes)
        if issuer is self.sp and not persistent:
            for f, n in self.bar_deps.items():
                deps[f] = max(deps.get(f, 0), n)
        if ch.cnt > 0:
            deps[ch] = max(deps.get(ch, 0), ch.cnt)
        self._wait(issuer, deps)
        issuer.e.dma_start(out=out, in_=in_, **kw).then_inc(ch.sem, 16)
        ch.cnt += 1
        for b in reads:
            b.r[ch] = max(b.r.get(ch, 0), ch.cnt)
        for b in writes:
            b.w = (ch, ch.cnt)
            b.r = {}

    def dma_gather(self, issuer, out, in_, idx_ap, reads=(), writes=()):
        lst = self.chans[issuer.name]
        ch = lst[self.rr[issuer.name] % len(lst)]
        self.rr[issuer.name] += 1
        deps = self._deps(reads, writes)
        if ch.cnt > 0:
            deps[ch] = max(deps.get(ch, 0), ch.cnt)
        self._wait(issuer, deps)
        issuer.e.indirect_dma_start(out=out, out_offset=None, in_=in_,
                                    in_offset=bass.IndirectOffsetOnAxis(ap=idx_ap, axis=0)).then_inc(ch.sem, 16)
        ch.cnt += 1
        for b in reads:
            b.r[ch] = max(b.r.get(ch, 0), ch.cnt)
        for b in writes:
            b.w = (ch, ch.cnt)
            b.r = {}

    def all_srcs(self):
        out = list(self.engs)
        for l in self.chans.values():
            out += l
        return out

    def barrier(self, engs=None):
        self.bar_deps = {f: f.cnt for f in self.all_srcs() if f.cnt > 0 and f is not self.sp}
        for e in (engs or self.engs):
            if e is self.sp:
                continue
            deps = {f: f.cnt for f in self.all_srcs() if f.cnt > 0 and not (f is e and e.is_pe)}
            self._wait(e, deps)

    def finish(self):
        deps = {f: f.cnt for f in self.all_srcs() if f.cnt > 0 and f is not self.sp}
        self._wait(self.sp, deps)


def _consts():
    c = {}
    c["ident_f"] = np.eye(128, dtype=np.float32)
    c["ones_b"] = np.ones((128, 128), dtype=ml_dtypes.bfloat16)
    lg = np.log1p(-np.exp2(-5.0 - np.arange(H_A, dtype=np.float64)))
    i = np.arange(128, dtype=np.float64)
    qdec = np.exp(lg[:, None] * i[None, :])
    kdec = np.exp(-lg[:, None] * i[None, :]) * (DK_A ** -0.5)
    c["qdec"] = np.broadcast_to(qdec[None], (128, H_A, 128)).astype(np.float32).copy()
    c["kdec"] = np.broadcast_to(kdec[None], (128, H_A, 128)).astype(np.float32).copy()
    c["kwdec"] = (np.exp(lg[None, :] * (127.0 - i[:, None])) * (DK_A ** -0.5)).astype(np.float32)
    c["cmask"] = (i[:, None] <= i[None, :]).astype(np.float32)
    c["gam"] = np.exp(lg).astype(np.float64)
    c["gam128"] = np.exp(128.0 * lg).astype(np.float64)
    c["gam8"] = np.exp(8.0 * lg).astype(np.float64)
    t8 = (np.arange(128) % 8).astype(np.float64)
    b8 = np.arange(128) // 8
    c["qdecS"] = np.broadcast_to(np.exp(lg[:, None] * t8[None, :])[None], (128, H_A, 128)).astype(np.float32).copy()
    c["kdecS"] = np.broadcast_to((np.exp(-lg[:, None] * t8[None, :]) * (DK_A ** -0.5))[None], (128, H_A, 128)).astype(np.float32).copy()
    c["kwdecS"] = (np.exp(lg[None, :] * (7.0 - t8[:, None])) * (DK_A ** -0.5)).astype(np.float32)
    c["cmaskS"] = ((b8[:, None] == b8[None, :]) & (t8[:, None] <= t8[None, :])).astype(np.float32)
    c["rowmask"] = (b8[:, None] == np.arange(16)[None, :]).astype(np.float32)
    bf = ml_dtypes.bfloat16
    c["ident_b"] = np.eye(128, dtype=bf)
    J = ST // 128
    k128 = np.arange(128)
    cc = np.arange(ST + 128 * (J - 1))
    c["Tc"] = np.where(k128[:, None] <= cc[None, :] - 128 * (J - 1), 0.0, NEG).astype(bf)
    c["Tl"] = np.where(k128[:, None] < cc[None, :] - 128 * (J - 1), NEG, 0.0).astype(bf)
    pos = np.arange(SEQ)
    c["Tcmp"] = np.where((k128[:, None] >= 1) & (16 * k128[:, None] + 15 <= pos[None, :]), 0.0, NEG).astype(bf)
    c["Emat"] = (pos[None, :] // 64 == np.arange(32)[:, None]).astype(bf)
    ib = k128 - 1
    jb = np.arange(32)
    c["ovl"] = ((ib[:, None] >= 0) & (16 * ib[:, None] <= 64 * jb[None, :] + 63)
                & (16 * ib[:, None] + 31 >= 64 * jb[None, :])).astype(np.float32)
    cur = pos // 64
    fbt = np.where(jb[None, :] > cur[:, None], -1e30,
                   np.where((jb[None, :] == 0) | (jb[None, :] == cur[:, None]) | (jb[None, :] == cur[:, None] - 1), 1e4, 0.0))
    c["fb"] = np.ascontiguousarray(fbt.reshape(SEQ // 128, 128, 32).transpose(1, 0, 2)).astype(np.float32)

    def split3(v):
        v = np.asarray(v, dtype=np.float64)
        a = v.astype(bf).astype(np.float64)
        b = (v - a).astype(bf).astype(np.float64)
        d = (v - a - b).astype(bf).astype(np.float64)
        return a, b, d

    def kconst(p):
        a, b = (p // 64).astype(np.float64), (p % 64).astype(np.float64)
        one = np.ones_like(a)
        return np.stack([a, a, a, b, b, b, one, one, one]).astype(bf)

    c["kconst"] = kconst(pos)
    cend = np.maximum(16 * k128 + 15, 0)
    c["kconst_cmp"] = kconst(cend)
    slopes = np.exp2(-8.0 * np.arange(1, 17, dtype=np.float32) / 16).astype(np.float32).astype(np.float64)
    s1, s2, s3 = split3(slopes)
    NP = SEQ + 64
    pp = np.arange(NP, dtype=np.float64)
    v1, v2, v3 = split3(-slopes[:, None] * pp[None, :])
    qc = np.zeros((9, 16, NP), dtype=np.float64)
    for r, sv in enumerate((s1, s2, s3)):
        qc[r] = 64.0 * sv[:, None]
        qc[3 + r] = sv[:, None]
    qc[6], qc[7], qc[8] = v1, v2, v3
    c["qconst"] = qc.astype(bf)
    c["kconst_new"] = kconst(SEQ + (k128 % 8))
    colt = np.arange(32) % 8
    kb, kt_ = k128 // 8, k128 % 8
    tn = np.where((kb[:, None, None] == np.arange(16)[None, :, None]) & (kt_[:, None, None] <= colt[None, None, :]), 0.0, NEG)
    c["TnewS"] = tn.astype(bf)
    c["TlS"] = np.where(k128[:, None] < colt[None, :], NEG, 0.0).astype(bf)
    c["TcS"] = np.where(k128[:, None] >= 1, 0.0, NEG).astype(bf)
    fbs = np.zeros((8, 33), dtype=np.float32)
    fbs[:, [0, 31, 32]] = 1e4
    c["fbS"] = fbs
    c["ovl33"] = np.concatenate([c["ovl"], np.zeros((128, 1), np.float32)], axis=1)
    c["pidx"] = np.arange(128, dtype=np.float32).reshape(128, 1)
    return c


CONST = _consts()
CONST_IN = ["ident_f", "ones_b", "qdec", "kdec", "kwdec", "cmask", "qdecS", "kdecS", "kwdecS", "cmaskS", "rowmask",
            "ident_b", "Tc", "Tl", "ovl"]
CONST_DRAM = ["Tcmp", "Emat", "fb", "kconst", "kconst_cmp", "qconst", "kconst_new", "TnewS", "TlS", "TcS", "fbS", "ovl33", "pidx"]
DEV = {"cores": None, "prompt": True, "sample": True}


def build_nc():
    nc = bass.Bass("TRN2", target_bir_lowering=False)
    es = ExitStack()
    with es:
        _build(nc, es)
    return nc


def _build(nc, es):
    def din(name, shape, dt=F32):
        return nc.dram_tensor(name, list(shape), dt, kind="ExternalInput").ap()

    def dout(name, shape, dt=F32):
        return nc.dram_tensor(name, list(shape), dt, kind="ExternalOutput").ap()

    x_p = din("x_p", [SEQ, D])
    x_s = din("x_s", [NSEQ_S * DEC_SEQ, D])
    c_p = din("c_p", [1, D])
    c_s = din("c_s", [NSEQ_S, D])
    state_ret = din("state_ret", [NSEQ_S, H_A, DK_A, DV_A])
    state_conv = din("state_conv", [2, NSEQ_S * 2, F2])
    state_win = din("state_win", [NSEQ_S, 512, 512])
    cache_kv = din("cache_kv", [NPHYS * 128, 1024])
    page_table = din("page_table", [NSEQ_S, NPAGE], I32)
    w_ada = din("w_ada", [2, D, 6 * D])
    b_ada = din("b_ada", [2, 6 * D])
    g_mix = din("g_mix", [2, D])
    g_ffn = din("g_ffn", [2, D])
    w_ffn_in = din("w_ffn_in", [2, D, F2])
    conv_w = din("conv_w", [2, 3, F2])
    conv_b = din("conv_b", [2, F2])
    w_ffn_out = din("w_ffn_out", [2, D_FF, D])
    w_ret_in = din("w_ret_in", [1, D, 6144])
    w_ret_out = din("w_ret_out", [1, 2048, D])
    g_final = din("g_final", [D])
    g_kv = din("g_kv", [D])
    w_ada_kv = din("w_ada_kv", [D, 2 * D])
    b_ada_kv = din("b_ada_kv", [2 * D])
    w_kv = din("w_kv", [D, 1536])
    w_nsa_in = din("w_nsa_in", [1, D, 1072])
    w_nsa_out = din("w_nsa_out", [1, D, D])
    pe_ck = din("pe_ck", [32, 64])
    pe_cv = din("pe_cv", [32, 64])
    w_ck1 = din("w_ck1", [2048, 128])
    w_ck2 = din("w_ck2", [128, 64])
    w_cv1 = din("w_cv1", [2048, 128])
    w_cv2 = din("w_cv2", [128, 64])
    cin = {}
    for k in CONST_IN + CONST_DRAM:
        a = CONST[k]
        cin[k] = din("k_" + k, a.shape, BF16 if a.dtype == ml_dtypes.bfloat16 else F32)

    y_p = dout("y_p", [SEQ, D])
    y_s = dout("y_s", [NSEQ_S * DEC_SEQ, D])
    ret_p = dout("ret_p", [H_A, DK_A, DV_A])
    ret_s = dout("ret_s", [NSEQ_S, H_A, DK_A, DV_A])
    conv_p = dout("conv_p", [2, 2, F2])
    conv_s = dout("conv_s", [2, NSEQ_S, 2, F2])
    kv_p = dout("kv_p", [SEQ, 1024])
    kv_s = dout("kv_s", [NSEQ_S * DEC_SEQ, 1024])
    win_p = dout("win_p", [512, 512])
    win_s = dout("win_s", [NSEQ_S, 512, 512])

    S = Sched(nc, es)
    pe, act, dve, pool, sp = S.pe, S.act, S.dve, S.pool, S.sp
    uid = [0]

    def sb(name, shape, dt=F32, stack=es):
        uid[0] += 1
        t = stack.enter_context(nc.sbuf_tensor(f"{name}_{uid[0]}", list(shape), dt))
        return t, Buf(name)

    banks = []
    for i in range(8):
        t = es.enter_context(nc.psum_tensor(f"bank{i}", [128, 512], F32))
        banks.append((t, Buf(f"bank{i}")))
    bank_rr = [0]

    NROT = [7]

    def next_bank():
        b = banks[bank_rr[0] % NROT[0]]
        bank_rr[0] += 1
        return b

    RBANK = banks[7]
    RBANKS = banks[4:8]

    ctab = {}
    for k in CONST_IN:
        a = CONST[k]
        t, b = sb(k, a.shape, BF16 if a.dtype == ml_dtypes.bfloat16 else F32)
        S.dma(sp, t[:], cin[k], writes=[b])
        ctab[k] = (t, b)
    ident_f, b_ident = ctab["ident_f"]
    ones_b, b_ones = ctab["ones_b"]
    b_ctab = [ctab[k][1] for k in CONST_IN]

    xT, b_xT = sb("xT", [128, KC, ST])
    hT, b_hT = sb("hT", [128, KC, ST], BF16)
    WB = 8192
    NWB = 3
    wbufs = [sb(f"wbuf{i}", [128, WB], BF16) for i in range(NWB)]
    wb_rr = [0]

    def next_wbuf():
        w = wbufs[wb_rr[0] % NWB]
        wb_rr[0] += 1
        return w

    NM = 1 + NSEQ_S
    modall, b_mod = sb("modall", [128, 2, 6 * KC, NM])
    gmul, b_gmul = sb("gmul", [128, 2, 2, KC, NM])
    gvec, b_gvec = sb("gvec", [128, 6, KC])
    modkv, b_modkv = sb("modkv", [128, 2 * KC, NM])
    gmkv, b_gmkv = sb("gmkv", [128, KC, NM])
    gfin, b_gfin = sb("gfin", [128, KC, NM])
    zero17, b_zero17 = sb("zero17", [128, KC, NM])
    cwt, b_cwt = sb("cwt", [128, 2, 3, NFC])
    cbt, b_cbt = sb("cbt", [128, 2, NFC])
    uhalo, b_uhalo = sb("uhalo", [128, 2, NFC, 2])
    hprev, b_hprev = sb("hprev", [128, 2, KC, 2], BF16)
    S.op(dve, lambda e: e.memset(hprev[:], 0.0), writes=[b_hprev])
    b_Sst, b_Sbf, b_cbufT = Buf("Sst"), Buf("Sbf"), Buf("cbufT")
    P = {}
    b_modall = [b_mod, b_gmul, b_gvec, b_modkv, b_gmkv, b_gfin, b_zero17]

    for l in range(2):
        S.dma(sp, cwt[:, l], conv_w[l].rearrange("t (c p) -> p t c", p=128), writes=[b_cwt],
              allow_slow_non_contiguous=True)
        S.dma(sp, cbt[:, l], conv_b[l].rearrange("(c p) -> p c", p=128), writes=[b_cbt],
              allow_slow_non_contiguous=True)
    for i, src in enumerate([g_mix[0], g_mix[1], g_ffn[0], g_ffn[1], g_final, g_kv]):
        S.dma(sp, gvec[:, i], src.rearrange("(c p) -> p c", p=128), writes=[b_gvec],
              allow_slow_non_contiguous=True)
    S.op(dve, lambda e: e.memset(uhalo[:], 0.0), writes=[b_uhalo])
    S.op(dve, lambda e: e.memset(zero17[:], 0.0), writes=[b_zero17])

    def bc(ap2, n):
        return ap2.unsqueeze(2).broadcast_to([128, ap2.shape[1], n])

    with ExitStack() as ph:
        cT, b_cT = sb("cT", [128, KC, NM], F32, ph)
        cTb, b_cTb = sb("cTb", [128, KC, NM], BF16, ph)
        badT, b_badT = sb("badT", [128, 2, 6 * KC], F32, ph)
        bkvT, b_bkvT = sb("bkvT", [128, 2 * KC], F32, ph)
        S.dma(sp, cT[:, :, 0], c_p[0].rearrange("(c p) -> p c", p=128), writes=[b_cT], allow_slow_non_contiguous=True)
        for sq_ in range(NSEQ_S):
            S.dma(sp, cT[:, :, 1 + sq_], c_s[sq_].rearrange("(c p) -> p c", p=128), writes=[b_cT], allow_slow_non_contiguous=True)
        for l in range(2):
            S.dma(sp, badT[:, l], b_ada[l].rearrange("(c p) -> p c", p=128), writes=[b_badT],
                  allow_slow_non_contiguous=True)
        S.dma(sp, bkvT[:], b_ada_kv.rearrange("(c p) -> p c", p=128), writes=[b_bkvT], allow_slow_non_contiguous=True)
        S.op(act, lambda e: e.activation(cTb[:], cT[:], AF.Silu), reads=[b_cT], writes=[b_cTb])

        def mod_block(wsrc, dst3, bias2):
            wt, b_w = next_wbuf()
            wv = wt[:, 0:KC * 1024].rearrange("p (k n) -> p k n", k=KC)
            S.dma(pool, wv, wsrc.rearrange("(k p) n -> p k n", p=128), writes=[b_w])
            pt, b_p = next_bank()
            for oc in range(8):
                for k in range(KC):
                    S.op(pe, lambda e, oc=oc, k=k: e.matmul(pt[:, oc * NM:(oc + 1) * NM], wv[:, k, oc * 128:(oc + 1) * 128],
                                                          cTb[:, k, :], start=(k == 0), stop=(k == KC - 1)),
                         reads=[b_w, b_cTb], writes=[b_p], inc=(oc == 7 and k == KC - 1))
            S.op(dve, lambda e: e.tensor_tensor(dst3, pt[:, 0:KC * NM].rearrange("p (k n) -> p k n", k=KC),
                                                bc(bias2, NM), op=ALU.add),
                 reads=[b_p, b_badT, b_bkvT], writes=b_modall)

        for l in range(2):
            for blk in range(6):
                mod_block(w_ada[l, :, blk * 1024:(blk + 1) * 1024], modall[:, l, blk * KC:(blk + 1) * KC, :],
                          badT[:, l, blk * KC:(blk + 1) * KC])
        for blk in range(2):
            mod_block(w_ada_kv[:, blk * 1024:(blk + 1) * 1024], modkv[:, blk * KC:(blk + 1) * KC, :],
                      bkvT[:, blk * KC:(blk + 1) * KC])
        for l in range(2):
            for sub in range(2):
                gi = l if sub == 0 else 2 + l
                sc = modall[:, l, (1 + 3 * sub) * KC:(2 + 3 * sub) * KC, :]
                S.op(dve, lambda e, l=l, sub=sub, gi=gi, sc=sc: e.scalar_tensor_tensor(
                    gmul[:, l, sub], sc, 1.0, bc(gvec[:, gi], NM), op0=ALU.add, op1=ALU.mult),
                    reads=b_modall, writes=b_modall)
        S.op(dve, lambda e: e.scalar_tensor_tensor(gmkv[:], modkv[:, KC:2 * KC, :], 1.0, bc(gvec[:, 5], NM),
                                                   op0=ALU.add, op1=ALU.mult), reads=b_modall, writes=b_modall)
        S.op(dve, lambda e: e.tensor_copy(gfin[:], bc(gvec[:, 4], NM)), reads=b_modall, writes=b_modall)
        S.barrier()

    def modv(l, which):
        return modall[:, l, which * KC:(which + 1) * KC, :]

    def v3(ap2):
        return ap2.rearrange("p (b t) -> p b t", t=DEC_SEQ)

    def load_xT(src_rows, ntok):
        with ExitStack() as ph:
            xin, b_xin = sb("xin", [128, ntok // 128, D], F32, ph)
            S.dma(sp, xin[:], src_rows.rearrange("(t p) d -> p t d", p=128), writes=[b_xin])
            for t in range(ntok // 128):
                for half in range(2):
                    pt, b_p = next_bank()
                    for j in range(4):
                        k = half * 4 + j
                        S.op(pe, lambda e, t=t, k=k, j=j: e.transpose(pt[:, j * 128:(j + 1) * 128],
                                                                    xin[:, t, k * 128:(k + 1) * 128], ident_f[:]),
                             reads=[b_xin, b_ident], writes=[b_p], inc=(j == 3))
                    S.op(act, lambda e, t=t, half=half: e.activation(
                        xT[:, half * 4:half * 4 + 4, t * 128:(t + 1) * 128],
                        pt[:].rearrange("p (k n) -> p k n", k=4), AF.Copy), reads=[b_p], writes=[b_xT])
            S.barrier()

    def store_yT(dst_rows, ntok, src, b_src):
        with ExitStack() as ph:
            yo, b_yo = sb("yo", [128, ntok // 128, D], F32, ph)
            for t in range(ntok // 128):
                for half in range(2):
                    pt, b_p = next_bank()
                    for j in range(4):
                        k = half * 4 + j
                        S.op(pe, lambda e, t=t, k=k, j=j: e.transpose(pt[:, j * 128:(j + 1) * 128],
                                                                    src[:, k, t * 128:(t + 1) * 128], ident_f[:]),
                             reads=[b_src, b_ident], writes=[b_p], inc=(j == 3))
                    S.op(act, lambda e, t=t, half=half: e.activation(yo[:, t, half * 512:(half + 1) * 512], pt[:], AF.Copy),
                         reads=[b_p], writes=[b_yo])
            S.dma(sp, dst_rows.rearrange("(t p) d -> p t d", p=128), yo[:], reads=[b_yo])
            S.barrier()

    def norm_mod(ntok, sample, gm3, sh3, dst, b_dst, ph):
        sq, b_sq = sb("nm_sq", [128, KC, ntok], BF16, ph)
        rstd, b_rstd = sb("nm_rstd", [128, ntok], F32, ph)
        tmp, b_tmp = sb("nm_tmp", [128, 2, ntok], F32, ph)
        S.op(act, lambda e: e.activation(sq[:], xT[:, :, 0:ntok], AF.Square), reads=[b_xT], writes=[b_sq])
        pt, b_p = next_bank()
        for k in range(KC):
            S.op(pe, lambda e, k=k: e.matmul(pt[:, 0:ntok], ones_b[:], sq[:, k, :], start=(k == 0), stop=(k == KC - 1)),
                 reads=[b_sq, b_ones], writes=[b_p], inc=(k == KC - 1))
        S.op(act, lambda e: e.activation(rstd[:], pt[:, 0:ntok], AF.Sqrt, bias=EPS, scale=1.0 / D),
             reads=[b_p], writes=[b_rstd])
        S.op(dve, lambda e: e.reciprocal(rstd[:], rstd[:]), reads=[b_rstd], writes=[b_rstd])
        tb = [Buf("nm_t0"), Buf("nm_t1")]
        for k in range(KC):
            S.op(dve, lambda e, k=k: e.tensor_tensor(tmp[:, k % 2], xT[:, k, 0:ntok], rstd[:], op=ALU.mult),
                 reads=[b_xT, b_rstd], writes=[tb[k % 2]])
            if not sample:
                S.op(act, lambda e, k=k: e.activation(dst[:, k, 0:ntok], tmp[:, k % 2], AF.Identity,
                                                    bias=sh3[:, k, 0:1], scale=gm3[:, k, 0:1]),
                     reads=[tb[k % 2]] + b_modall, writes=[b_dst])
            else:
                S.op(dve, lambda e, k=k: e.tensor_tensor(v3(tmp[:, k % 2]), v3(tmp[:, k % 2]), bc(gm3[:, k, 1:NM], DEC_SEQ), op=ALU.mult),
                     reads=[tb[k % 2]] + b_modall, writes=[tb[k % 2]])
                S.op(dve, lambda e, k=k: e.tensor_tensor(v3(dst[:, k, 0:ntok]), v3(tmp[:, k % 2]), bc(sh3[:, k, 1:NM], DEC_SEQ), op=ALU.add),
                     reads=[tb[k % 2]] + b_modall, writes=[b_dst])

    rtmp, b_rtmp = sb("rtmp", [128, 128])

    def resid_add(oc, pt, ntok, sample, ga3, b_p):
        if not sample:
            S.op(dve, lambda e: e.scalar_tensor_tensor(xT[:, oc, 0:ntok], pt[:, 0:ntok], ga3[:, oc, 0:1], xT[:, oc, 0:ntok],
                                                       op0=ALU.mult, op1=ALU.add),
                 reads=[b_p, b_xT] + b_modall, writes=[b_xT])
        else:
            S.op(dve, lambda e: e.tensor_tensor(v3(rtmp[:]), v3(pt[:, 0:ntok]), bc(ga3[:, oc, 1:NM], DEC_SEQ), op=ALU.mult),
                 reads=[b_p] + b_modall, writes=[b_rtmp])
            S.op(dve, lambda e: e.tensor_tensor(xT[:, oc, 0:ntok], xT[:, oc, 0:ntok], rtmp[:], op=ALU.add),
                 reads=[b_rtmp, b_xT], writes=[b_xT])

    WSRC = {"w_ret_in": w_ret_in[0], "w_ret_out": w_ret_out[0], "w_ffn_in0": w_ffn_in[0], "w_ffn_in1": w_ffn_in[1],
            "w_ffn_out0": w_ffn_out[0], "w_ffn_out1": w_ffn_out[1], "w_kv": w_kv, "w_nsa_in": w_nsa_in[0],
            "w_nsa_out": w_nsa_out[0]}
    WBF, b_WBF = {}, {}
    for nm, src in WSRC.items():
        WBF[nm] = nc.dram_tensor("bf_" + nm, list(src.shape), BF16, kind="Internal").ap()
        b_WBF[nm] = Buf("bf_" + nm)
    converted = set()

    def convert_w(nm):
        src = WSRC[nm]
        K_, N_ = src.shape
        for r0 in range(0, K_, 128):
            t_, b_t = next_wbuf()
            S.dma(pool, t_[:, 0:N_], src[r0:r0 + 128, :], writes=[b_t])
            S.dma(sp, WBF[nm][r0:r0 + 128, :], t_[:, 0:N_], reads=[b_t], writes=[b_WBF[nm]], persistent=True)
        converted.add(nm)

    def load_w(wname, col_ranges, nk, rows_per_k=128):
        if wname not in converted:
            convert_w(wname)
        wt, b_w = next_wbuf()
        tot = sum(n for _, n in col_ranges)
        assert nk * tot <= WB
        wv = wt[0:rows_per_k, 0:nk * tot].rearrange("p (k n) -> p k n", k=nk)
        o = 0
        for c0, n in col_ranges:
            S.dma(sp, wv[:, :, o:o + n], WBF[wname][:, c0:c0 + n].rearrange("(k p) n -> p k n", p=rows_per_k),
                  reads=[b_WBF[wname]], writes=[b_w], persistent=True)
            o += n
        return wv, b_w

    def proj_fm(wv, b_w, col, src, b_src, ntok, nk):
        pt, b_p = next_bank()
        for k in range(nk):
            S.op(pe, lambda e, k=k: e.matmul(pt[:, 0:ntok], wv[:, k, col:col + 128], src[:, k, 0:ntok],
                                            start=(k == 0), stop=(k == nk - 1)),
                 reads=[b_w, b_src], writes=[b_p], inc=(k == nk - 1))
        return pt, b_p

    def proj_tm(wv, b_w, col, ncol, src, b_src, t, nk):
        pt, b_p = next_bank()
        for k in range(nk):
            S.op(pe, lambda e, k=k: e.matmul(pt[:, 0:ncol], src[:, k, t * 128:(t + 1) * 128], wv[:, k, col:col + ncol],
                                            start=(k == 0), stop=(k == nk - 1)),
                 reads=[b_w, b_src], writes=[b_p], inc=(k == nk - 1))
        return pt, b_p

    def retention(st, sample):
        ntok = 128 if sample else ST
        nt = ntok // 128
        sfx = "S" if sample else ""
        qdec, kdec, kwdec, cmask = (ctab[k + sfx][0] for k in ("qdec", "kdec", "kwdec", "cmask"))
        rowmask = ctab["rowmask"][0]
        NROT[0] = 4 if sample else 7
        with ExitStack() as ph:
            norm_mod(ntok, sample, gmul[:, 0, 0], modv(0, 0), hT, b_hT, ph)
            yin, b_yin = sb("yin", [128, 16, ntok], BF16, ph)
            qT, b_qT = sb("qT", [128, 2, ntok], BF16, ph)
            kT, b_kT = sb("kT", [128, 2, ntok], BF16, ph)
            gT, b_gT = sb("gT", [128, 4, ntok], BF16, ph)
            vtk, b_vtk = sb("vtk", [128, nt, DV_A], BF16, ph)
            kwt, b_kwt = sb("kwt", [128, nt, DK_A], BF16, ph)
            scT, b_scT = sb("scT", [128, 128], BF16, ph)
            oT, b_oT = sb("oT", [128, 4, ntok], F32, ph)
            osq, b_osq = sb("osq", [128, 4, ntok], BF16, ph)
            orstd, b_orstd = sb("orstd", [128, ntok], F32, ph)
            otmp, b_otmp = sb("otmp", [128, ntok], F32, ph)
            if sample:
                s0 = [sb(f"s0_{i}", [128, 2, DV_A], F32, ph) for i in range(3)]
                s0b = [sb(f"s0b_{i}", [128, 2, DV_A], BF16, ph) for i in range(3)]
                sn = [sb(f"sn_{i}", [128, 2, DV_A], F32, ph) for i in range(3)]
                kwm = [sb(f"kwm_{i}", [128, DK_A], BF16, ph) for i in range(3)]
            W = "w_ret_in"
            for h in range(H_A):
                wv, b_w = load_w(W, [(h * 256, 256), (1024 + h * 256, 256)], KC)
                for which, dstT, b_d, dec in ((0, qT, b_qT, qdec), (1, kT, b_kT, kdec)):
                    for c in range(2):
                        pt, b_p = proj_fm(wv, b_w, which * 256 + c * 128, hT, b_hT, ntok, KC)
                        S.op(dve, lambda e, c=c, dstT=dstT, dec=dec, pt=pt: e.tensor_tensor(
                            dstT[:, c, :].rearrange("p (t n) -> p t n", n=128), pt[:, 0:ntok].rearrange("p (t n) -> p t n", n=128),
                            dec[:, h, :].unsqueeze(1).broadcast_to([128, nt, 128]), op=ALU.mult),
                             reads=[b_p] + b_ctab, writes=[b_d])
                for t in range(nt):
                    pt2, b_p2 = proj_tm(wv, b_w, 256, 256, hT, b_hT, t, KC)
                    S.op(dve, lambda e, t=t, pt2=pt2: e.tensor_scalar(kwt[:, t, :], pt2[:, 0:256], kwdec[:, h:h + 1], None, op0=ALU.mult),
                         reads=[b_p2] + b_ctab, writes=[b_kwt])
                wv, b_w = load_w(W, [(4096 + h * 512, 512)], KC)
                for c in range(4):
                    pt, b_p = proj_fm(wv, b_w, c * 128, hT, b_hT, ntok, KC)
                    S.op(act, lambda e, c=c, pt=pt: e.activation(gT[:, c, :], pt[:, 0:ntok], AF.Silu), reads=[b_p], writes=[b_gT])
                wv, b_w = load_w(W, [(2048 + h * 512, 512)], KC)
                for t in range(nt):
                    pt, b_p = proj_tm(wv, b_w, 0, 512, hT, b_hT, t, KC)
                    S.op(act, lambda e, t=t, pt=pt: e.activation(vtk[:, t, :], pt[:], AF.Copy), reads=[b_p], writes=[b_vtk])
                for t in range(nt):
                    first = (not sample) and (st == 0 and t == 0)
                    ts = slice(t * 128, (t + 1) * 128)
                    pt, b_p = next_bank()
                    for c in range(2):
                        S.op(pe, lambda e, c=c: e.matmul(pt[:, 0:128], kT[:, c, ts], qT[:, c, ts], start=(c == 0), stop=(c == 1)),
                             reads=[b_kT, b_qT], writes=[b_p], inc=(c == 1))
                    S.op(dve, lambda e: e.tensor_tensor(scT[:], pt[:, 0:128], cmask[:], op=ALU.mult),
                         reads=[b_p] + b_ctab, writes=[b_scT])
                    po, b_po = RBANK
                    if not sample:
                        for vc in range(4):
                            vs = slice(vc * 128, (vc + 1) * 128)
                            S.op(pe, lambda e, vc=vc, vs=vs: e.matmul(po[:, vs], vtk[:, t, vs], scT[:], start=True, stop=first),
                                 reads=[b_vtk, b_scT], writes=[b_po], inc=(first and vc == 3))
                            if not first:
                                for c in range(2):
                                    S.op(pe, lambda e, vc=vc, vs=vs, c=c: e.matmul(po[:, vs], P['Sbf'][:, h, c, vs], qT[:, c, ts],
                                                                               start=False, stop=(c == 1)),
                                         reads=[b_Sbf, b_qT], writes=[b_po], inc=(vc == 3 and c == 1))
                    else:
                        for vc in range(4):
                            vs = slice(vc * 128, (vc + 1) * 128)
                            S.op(pe, lambda e, vc=vc, vs=vs: e.matmul(RBANKS[vc][0][:, 0:128], vtk[:, t, vs], scT[:], start=True, stop=False),
                                 reads=[b_vtk, b_scT], writes=[RBANKS[vc][1]], inc=False)
                        def s0_load(bb):
                            t_, bt_ = s0[(h * NSEQ_S + bb) % 3]
                            S.dma(sp, t_[:], state_ret[bb, h].rearrange("(c p) v -> p c v", p=128), writes=[bt_])
                        s0_load(0)
                        s0_load(1)
                        for b in range(NSEQ_S):
                            i3 = (h * NSEQ_S + b) % 3
                            (s0t, b_s0), (s0bt, b_s0b), (snt, b_sn), (kwmt, b_kwm) = s0[i3], s0b[i3], sn[i3], kwm[i3]
                            if b + 2 < NSEQ_S:
                                s0_load(b + 2)
                            S.op(act, lambda e, s0t=s0t, s0bt=s0bt: e.activation(s0bt[:], s0t[:], AF.Copy, scale=float(CONST["gam"][h])),
                                 reads=[b_s0], writes=[b_s0b])
                            for vc in range(4):
                                for c in range(2):
                                    lastmm = (b == NSEQ_S - 1 and vc == 3 and c == 1)
                                    S.op(pe, lambda e, vc=vc, c=c, b=b, s0bt=s0bt: e.matmul(
                                        RBANKS[vc][0][:, b * 8: b * 8 + 8], s0bt[:, c, vc * 128:(vc + 1) * 128],
                                        qT[:, c, b * 8:(b + 1) * 8], start=False, stop=(b == NSEQ_S - 1 and c == 1)),
                                        reads=[b_s0b, b_qT], writes=[RBANKS[vc][1]], inc=(lastmm or (vc == 3 and c == 1)))
                            S.op(dve, lambda e, b=b, kwmt=kwmt: e.tensor_scalar(kwmt[:], kwt[:, 0, :], rowmask[:, b:b + 1], None, op0=ALU.mult),
                                 reads=[b_kwt] + b_ctab, writes=[b_kwm])
                            for c in range(2):
                                ps_, b_ps = next_bank()
                                S.op(pe, lambda e, c=c, kwmt=kwmt, ps_=ps_: e.matmul(ps_[:], kwmt[:, c * 128:(c + 1) * 128], vtk[:, 0, :], start=True, stop=True),
                                     reads=[b_kwm, b_vtk], writes=[b_ps])
                                S.op(dve, lambda e, c=c, s0t=s0t, snt=snt, ps_=ps_: e.scalar_tensor_tensor(
                                    snt[:, c, :], s0t[:, c, :], float(CONST["gam8"][h]), ps_[:], op0=ALU.mult, op1=ALU.add),
                                    reads=[b_ps, b_s0], writes=[b_sn])
                            S.dma(sp, ret_s[b, h].rearrange("(c p) v -> p c v", p=128), snt[:], reads=[b_sn])
                    if sample:
                        for vc in range(4):
                            S.op(act, lambda e, vc=vc: e.activation(oT[:, vc, ts], RBANKS[vc][0][:, 0:128], AF.Copy),
                                 reads=[RBANKS[vc][1]], writes=[b_oT])
                    else:
                        S.op(act, lambda e: e.activation(oT[:, :, ts], po[:].rearrange("p (v n) -> p v n", v=4), AF.Copy),
                             reads=[b_po], writes=[b_oT])
                    if not sample:
                        for c in range(2):
                            ps_, b_ps = next_bank()
                            S.op(pe, lambda e, c=c, ps_=ps_: e.matmul(ps_[:], kwt[:, t, c * 128:(c + 1) * 128], vtk[:, t, :], start=True, stop=True),
                                 reads=[b_kwt, b_vtk], writes=[b_ps])
                            S.op(dve, lambda e, c=c, ps_=ps_: e.scalar_tensor_tensor(P['Sst'][:, h, c, :], P['Sst'][:, h, c, :], float(CONST["gam128"][h]),
                                                                                  ps_[:], op0=ALU.mult, op1=ALU.add),
                                 reads=[b_ps, b_Sst], writes=[b_Sst])
                        S.op(act, lambda e: e.activation(P['Sbf'][:, h], P['Sst'][:, h], AF.Copy, scale=float(CONST["gam"][h])),
                             reads=[b_Sst], writes=[b_Sbf])
                S.op(act, lambda e: e.activation(osq[:], oT[:], AF.Square), reads=[b_oT], writes=[b_osq])
                pt, b_p = next_bank()
                for vc in range(4):
                    S.op(pe, lambda e, vc=vc: e.matmul(pt[:, 0:ntok], ones_b[:], osq[:, vc, :], start=(vc == 0), stop=(vc == 3)),
                         reads=[b_osq, b_ones], writes=[b_p], inc=(vc == 3))
                S.op(act, lambda e: e.activation(orstd[:], pt[:, 0:ntok], AF.Sqrt, bias=EPS, scale=1.0 / DV_A), reads=[b_p], writes=[b_orstd])
                S.op(dve, lambda e: e.reciprocal(orstd[:], orstd[:]), reads=[b_orstd], writes=[b_orstd])
                for vc in range(4):
                    S.op(dve, lambda e, vc=vc: e.tensor_tensor(otmp[:], oT[:, vc, :], orstd[:], op=ALU.mult),
                         reads=[b_oT, b_orstd], writes=[b_otmp])
                    S.op(dve, lambda e, vc=vc: e.tensor_tensor(yin[:, h * 4 + vc, :], otmp[:], gT[:, vc, :], op=ALU.mult),
                         reads=[b_otmp, b_gT], writes=[b_yin])
            for half in range(2):
                wv, b_w = load_w("w_ret_out", [(half * 512, 512)], 16)
                for oc4 in range(4):
                    oc = half * 4 + oc4
                    pt, b_p = proj_fm(wv, b_w, oc4 * 128, yin, b_yin, ntok, 16)
                    resid_add(oc, pt, ntok, sample, modv(0, 2), b_p)
            S.barrier()


    def load_conv_bufs():
        with ExitStack() as ph:
            rows, b_rows = sb("scrows", [2 * NSEQ_S, F2], F32, ph)
            for l in range(2):
                S.dma(sp, rows[:], state_conv[l], writes=[b_rows])
                for g0 in range(0, NFC, 16):
                    n = min(16, NFC - g0)
                    pt, b_p = next_bank()
                    for j in range(n):
                        ch = g0 + j
                        S.op(pe, lambda e, j=j, ch=ch: e.transpose(pt[:, j * 32:(j + 1) * 32], rows[:, ch * 128:(ch + 1) * 128],
                                                                 ident_f[0:32, 0:32]),
                             reads=[b_rows, b_ident], writes=[b_p], inc=(j == n - 1))
                    S.op(act, lambda e, l=l, g0=g0, n=n: e.activation(P['cbufT'][:, l, g0:g0 + n, :],
                                                                    pt[:, 0:n * 32].rearrange("p (c m) -> p c m", c=n), AF.Copy),
                         reads=[b_p], writes=[b_cbufT])
            S.barrier()

    def ffn(l, st, last, sample):
        ntok = 128 if sample else ST
        NROT[0] = 7
        with ExitStack() as ph:
            if sample:
                norm_mod(ntok, sample, gmul[:, l, 1], modv(l, 3), hT, b_hT, ph)
                hsrc, b_hsrc, nsrc = hT, b_hT, ntok
            else:
                hE, b_hE = sb("hE", [128, KC, ST + 2], BF16, ph)
                S.op(dve, lambda e: e.tensor_copy(hE[:, :, 0:2], hprev[:, l]), reads=[b_hprev], writes=[b_hE])
                norm_mod(ntok, sample, gmul[:, l, 1], modv(l, 3), hE[:, :, 2:ST + 2], b_hE, ph)
                S.op(dve, lambda e: e.tensor_copy(hprev[:, l], hE[:, :, ST:ST + 2]), reads=[b_hE], writes=[b_hprev])
                hsrc, b_hsrc, nsrc = hE, b_hE, ST + 2
            actT, b_actT = sb("actT", [128, NPAIR, ntok], BF16, ph)
            ncol = ntok + 2 if not sample else NSEQ_S * (DEC_SEQ + 2)
            ue = [sb(f"ue{i}", [128, ncol], F32, ph) for i in range(4)]
            zz = [sb(f"zz{i}", [128, ntok], F32, ph) for i in range(4)]
            sg, b_sg = sb("sg", [128, ntok], F32, ph)
            if sample:
                utok, b_utok = sb("utok", [128, F2], F32, ph)
            W = f"w_ffn_in{l}"
            for blk in range(NPAIR // 2):
                wv, b_w = load_w(W, [(blk * 256, 256), (D_FF + blk * 256, 256)], KC)
                if sample:
                    for ag in range(2):
                        pt, b_p = proj_tm(wv, b_w, ag * 256, 256, hT, b_hT, 0, KC)
                        c0 = ag * D_FF + blk * 256
                        S.op(act, lambda e, pt=pt, c0=c0: e.activation(utok[:, c0:c0 + 256], pt[:, 0:256], AF.Copy),
                             reads=[b_p], writes=[b_utok])
                for pi in range(2):
                    pair = blk * 2 + pi
                    zs = []
                    for ag in range(2):
                        chunk = pair + ag * NPAIR
                        pt, b_p = proj_fm(wv, b_w, ag * 256 + pi * 128, hsrc, b_hsrc, nsrc, KC)
                        (u, b_u) = ue[(pair * 2 + ag) % 4]
                        (z, b_z) = zz[(pair * 2 + ag) % 4]
                        if not sample:
                            S.op(act, lambda e, u=u, pt=pt: e.activation(u[:, 0:ntok + 2], pt[:, 0:ntok + 2], AF.Copy), reads=[b_p], writes=[b_u])
                            if last:
                                S.op(act, lambda e, u=u, chunk=chunk: e.activation(uhalo[:, l, chunk, :], u[:, ntok:ntok + 2], AF.Copy),
                                     reads=[b_u], writes=[b_uhalo])
                            u2, u1, u0, zv = u[:, 2:ntok + 2], u[:, 1:ntok + 1], u[:, 0:ntok], z[:]
                        else:
                            u3 = u[:].rearrange("p (b t) -> p b t", t=DEC_SEQ + 2)
                            S.op(pool, lambda e, u3=u3, chunk=chunk: e.tensor_copy(
                                u3[:, :, 0:2], P['cbufT'][:, l, chunk, :].rearrange("p (b j) -> p b j", j=2)),
                                reads=[b_cbufT], writes=[b_u])
                            S.op(act, lambda e, u3=u3, pt=pt: e.activation(u3[:, :, 2:DEC_SEQ + 2], v3(pt[:, 0:ntok]), AF.Copy),
                                 reads=[b_p], writes=[b_u])
                            u2, u1, u0, zv = u3[:, :, 2:DEC_SEQ + 2], u3[:, :, 1:DEC_SEQ + 1], u3[:, :, 0:DEC_SEQ], v3(z[:])
                        if not sample:
                            S.op(act, lambda e, zv=zv, pt=pt, chunk=chunk: e.activation(zv, pt[:, 2:ntok + 2], AF.Identity,
                                                                                     bias=cbt[:, l, chunk:chunk + 1], scale=cwt[:, l, 2, chunk:chunk + 1]),
                                 reads=[b_p, b_cwt, b_cbt], writes=[b_z])
                        else:
                            S.op(dve, lambda e, u2=u2, zv=zv, chunk=chunk: e.tensor_scalar(
                                zv, u2, cwt[:, l, 2, chunk:chunk + 1], cbt[:, l, chunk:chunk + 1],
                                op0=ALU.mult, op1=ALU.add), reads=[b_u, b_cwt, b_cbt], writes=[b_z])
                        S.op(dve, lambda e, u1=u1, zv=zv, chunk=chunk: e.scalar_tensor_tensor(
                            zv, u1, cwt[:, l, 1, chunk:chunk + 1], zv, op0=ALU.mult, op1=ALU.add),
                            reads=[b_u, b_cwt, b_z], writes=[b_z])
                        S.op(dve, lambda e, u0=u0, zv=zv, chunk=chunk: e.scalar_tensor_tensor(
                            zv, u0, cwt[:, l, 0, chunk:chunk + 1], zv, op0=ALU.mult, op1=ALU.add),
                            reads=[b_u, b_cwt, b_z], writes=[b_z])
                        zs.append((z, b_z))
                    S.op(act, lambda e, z=zs[1][0]: e.activation(sg[:], z[:], AF.Silu), reads=[zs[1][1]], writes=[b_sg])
                    S.op(dve, lambda e, z=zs[0][0], pair=pair: e.tensor_tensor(actT[:, pair, :], z[:], sg[:], op=ALU.mult),
                         reads=[zs[0][1], b_sg], writes=[b_actT])
            if sample:
                for tt in range(2):
                    S.dma(sp, conv_s[l, :, tt, :], utok[6 + tt:128:8, :], reads=[b_utok])
            elif last:
                for tt in range(2):
                    S.dma(sp, conv_p[l, tt].rearrange("(c p) -> p c", p=128), uhalo[:, l, :, tt], reads=[b_uhalo],
                          allow_slow_non_contiguous=True)
            for qt in range(4):
                wv, b_w = load_w(f"w_ffn_out{l}", [(qt * 256, 256)], NPAIR)
                for oc2 in range(2):
                    oc = qt * 2 + oc2
                    pt, b_p = proj_fm(wv, b_w, oc2 * 128, actT, b_actT, ntok, NPAIR)
                    resid_add(oc, pt, ntok, sample, modv(l, 5), b_p)
            S.barrier()

    NS = {}
    JT = ST // 128
    b_K = {k: Buf(k) for k in ("Kslc", "Kwin", "Kcmp", "Vslc", "Vwin", "Vcmp", "kraw", "vraw", "szv", "cmpw", "ntab",
                               "KslcN", "KwinN", "VslcN", "VwinN")}

    def kv_proj(st, sample):
        ntok = 128 if sample else ST
        NROT[0] = 7
        q0 = st * ST
        with ExitStack() as ph:
            norm_mod(ntok, sample, gmkv, modkv[:, 0:KC, :], hT, b_hT, ph)
            kvo, b_kvo = sb("kvo", [128, ntok // 128, 1536], F32, ph)
            for cb in range(3):
                wv, b_w = load_w("w_kv", [(cb * 512, 512)], KC)
                for t in range(ntok // 128):
                    pt, b_p = proj_tm(wv, b_w, 0, 512, hT, b_hT, t, KC)
                    S.op(act, lambda e, t=t, cb=cb, pt=pt: e.activation(kvo[:, t, cb * 512:(cb + 1) * 512], pt[:], AF.Copy),
                         reads=[b_p], writes=[b_kvo])
                if not DEV.get("nsa", True):
                    continue
                if sample:
                    fm = {0: [], 1: [(0, "KslcN", 0)], 2: [(0, "KwinN", 0)]}[cb]
                else:
                    fm = {0: [(0, "kraw", 16), (256, "vraw", 16)], 1: [(0, "Kslc", q0)], 2: [(0, "Kwin", q0)]}[cb]
                for lc, name, c0 in fm:
                    for g in range(G_B):
                        pt, b_p = next_bank()
                        for k in range(KC):
                            S.op(pe, lambda e, k=k, g=g, lc=lc, pt=pt: e.matmul(pt[0:64, 0:ntok], wv[:, k, lc + g * 64: lc + (g + 1) * 64],
                                                                              hT[:, k, 0:ntok], start=(k == 0), stop=(k == KC - 1)),
                                 reads=[b_w, b_hT], writes=[b_p], inc=(k == KC - 1))
                        S.op(dve, lambda e, g=g, name=name, c0=c0, pt=pt: e.tensor_copy(NS[name][0:64, g, c0:c0 + ntok], pt[0:64, 0:ntok]),
                             reads=[b_p], writes=[b_K[name]])
                        if name in ("kraw", "vraw"):
                            S.op(act, lambda e, g=g, name=name, c0=c0, pt=pt: e.activation(NS[name][64:128, g, c0 - 1:c0 - 1 + ntok], pt[0:64, 0:ntok], AF.Copy),
                                 reads=[b_p], writes=[b_K[name]])
            if not sample:
                S.dma(sp, kv_p[q0:q0 + ST, :].rearrange("(t p) n -> p t n", p=128), kvo[:, :, 0:1024], reads=[b_kvo])
                if q0 >= SEQ - 512:
                    w0 = q0 - (SEQ - 512)
                    S.dma(sp, win_p[w0:w0 + ST, :].rearrange("(t p) n -> p t n", p=128), kvo[:, :, 1024:1536], reads=[b_kvo])
                if DEV.get("nsa", True):
                    for t in range(ntok // 128):
                        kt = q0 // 128 + t
                        S.op(dve, lambda e, t=t, kt=kt: e.tensor_copy(NS["Vslc"][:, kt], kvo[:, t, 768:1024].rearrange("p (g d) -> p g d", g=G_B)),
                             reads=[b_kvo], writes=[b_K["Vslc"]])
                        S.op(dve, lambda e, t=t, kt=kt: e.tensor_copy(NS["Vwin"][:, kt], kvo[:, t, 1280:1536].rearrange("p (g d) -> p g d", g=G_B)),
                             reads=[b_kvo], writes=[b_K["Vwin"]])
            else:
                if DEV.get("nsa", True):
                    S.op(dve, lambda e: e.tensor_copy(NS["VslcN"][:], kvo[:, 0, 768:1024].rearrange("p (g d) -> p g d", g=G_B)),
                         reads=[b_kvo], writes=[b_K["VslcN"]])
                    S.op(dve, lambda e: e.tensor_copy(NS["VwinN"][:], kvo[:, 0, 1280:1536].rearrange("p (g d) -> p g d", g=G_B)),
                         reads=[b_kvo], writes=[b_K["VwinN"]])
                S.dma(sp, kv_s, kvo[:, 0, 0:1024], reads=[b_kvo])
                for b in range(NSEQ_S):
                    S.dma(sp, win_s[b, 504:512, :], kvo[b * 8:(b + 1) * 8, 0, 1024:1536], reads=[b_kvo])
                    S.dma(sp, win_s[b, 0:504, :], state_win[b, 8:512, :])
            S.barrier()

    def compress_setup():
        with ExitStack() as ph:
            peT, b_peT = sb("peT", [128, 2, 16], F32, ph)
            peTb, b_peTb = sb("peTb", [128, 2, 16], BF16, ph)
            for i, src in enumerate((pe_ck, pe_cv)):
                S.dma(sp, peT[:, i, :], src.rearrange("(i t) d -> (t d) i", t=2), writes=[b_peT], allow_slow_non_contiguous=True)
            S.op(dve, lambda e: e.tensor_copy(peTb[:], peT[:]), reads=[b_peT], writes=[b_peTb])
            for i, (w1, w2) in enumerate(((w_ck1, w_ck2), (w_cv1, w_cv2))):
                S.dma(pool, NS["w2"][:, i, :], w2, writes=[b_K["cmpw"]])
                wt, b_w = next_wbuf()
                wv = wt[:, 0:16 * 128].rearrange("p (s n) -> p s n", s=16)
                S.dma(pool, wv, w1.rearrange("(i q) n -> q i n", q=128), writes=[b_w])
                pt, b_p = next_bank()
                for sp_ in range(16):
                    S.op(pe, lambda e, sp_=sp_, i=i, wv=wv, pt=pt: e.matmul(pt[:, 0:1], wv[:, sp_, :], peTb[:, i, sp_:sp_ + 1],
                                                                         start=(sp_ == 0), stop=(sp_ == 15)),
                         reads=[b_w, b_peTb], writes=[b_p], inc=(sp_ == 15))
                S.op(dve, lambda e, i=i, pt=pt: e.tensor_copy(NS["bz"][:, i:i + 1], pt[:, 0:1]), reads=[b_p], writes=[b_K["cmpw"]])
            S.barrier()

    def compress(st, ntok=ST, s0_=None, sz_pre=None):
        if s0_ is None:
            s0_ = (st * ST) // 16
        nsl = ntok // 16
        with ExitStack() as ph:
            if sz_pre is None:
                sz, b_sz = sb("sz", [128, 2, G_B * nsl], BF16, ph)
            else:
                sz, b_sz = sz_pre
            for i, (w1, raw) in enumerate(((w_ck1, "kraw"), (w_cv1, "vraw"))):
                if "w1s" in NS:
                    wv, b_w = NS["w1s"][:, i], b_K["cmpw"]
                else:
                    wt, b_w = next_wbuf()
                    wv = wt[:, 0:16 * 128].rearrange("p (s n) -> p s n", s=16)
                    S.dma(pool, wv, w1.rearrange("(i q) n -> q i n", q=128), writes=[b_w])
                pz, b_pz = next_bank()
                for sp_ in range(16):
                    rhs = NS[raw][:, :, 2 * sp_:2 * sp_ + 16 * (nsl - 1) + 1:16]
                    S.op(pe, lambda e, sp_=sp_, wv=wv, rhs=rhs, pz=pz: e.matmul(pz[:, 0:G_B * nsl].rearrange("p (g m) -> p g m", g=G_B),
                                                                             wv[:, sp_, :], rhs, start=(sp_ == 0), stop=(sp_ == 15)),
                         reads=[b_w, b_K[raw]], writes=[b_pz], inc=(sp_ == 15))
                S.op(act, lambda e, i=i, pz=pz: e.activation(sz[:, i, :], pz[:, 0:G_B * nsl], AF.Silu, bias=NS["bz"][:, i:i + 1]),
                     reads=[b_pz, b_K["cmpw"]], writes=[b_sz])
                S.op(dve, lambda e, raw=raw: e.tensor_copy(NS[raw][0:64, :, 0:16], NS[raw][0:64, :, ntok:ntok + 16]),
                     reads=[b_K[raw]], writes=[b_K[raw]])
                S.op(dve, lambda e, raw=raw: e.tensor_copy(NS[raw][64:128, :, 0:15], NS[raw][64:128, :, ntok:ntok + 15]),
                     reads=[b_K[raw]], writes=[b_K[raw]])
            pk, b_pk = next_bank()
            S.op(pe, lambda e: e.matmul(pk[0:64, 0:G_B * nsl], NS["w2"][:, 0, :], sz[:, 0, :], start=True, stop=True),
                 reads=[b_sz, b_K["cmpw"]], writes=[b_pk])
            S.op(act, lambda e: e.activation(NS["Kcmp"][0:64, :, s0_:s0_ + nsl], pk[0:64, 0:G_B * nsl].rearrange("p (g m) -> p g m", g=G_B), AF.Copy),
                 reads=[b_pk], writes=[b_K["Kcmp"]])
            S.op(dve, lambda e: e.tensor_copy(NS["szv"][:, :, s0_:s0_ + nsl], sz[:, 1, :].rearrange("p (g m) -> p g m", g=G_B)),
                 reads=[b_sz], writes=[b_K["szv"]])
            for g in range(G_B):
                pv_, b_pv = next_bank()
                S.op(pe, lambda e, g=g, pv_=pv_: e.matmul(pv_[:, 0:64], NS["szv"][:, g, :], NS["w2"][:, 1, :], start=True, stop=True),
                     reads=[b_K["szv"], b_K["cmpw"]], writes=[b_pv])
                S.op(act, lambda e, g=g, pv_=pv_: e.activation(NS["Vcmp"][:, g, :], pv_[:, 0:64], AF.Copy), reads=[b_pv], writes=[b_K["Vcmp"]])
            if sz_pre is None:
                S.barrier()

    def nsa_prompt(st):
        q0 = st * ST
        NROT[0] = 4
        Tc, Tl, ovl, ident_b = (ctab[k][0] for k in ("Tc", "Tl", "ovl", "ident_b"))
        with ExitStack() as ph:
            norm_mod(ST, False, gmul[:, 1, 0], modv(1, 0), hT, b_hT, ph)
            S.barrier()
        with ExitStack() as ph:
            Qaug, b_Q = sb("Qaug", [128, 16, ST], BF16, ph)
            oTn, b_oTn = sb("oTn", [64, 16, ST], BF16, ph)
            negT, b_negT = sb("negT", [32, ST], BF16, ph)
            PTs = [sb(f"PT{i}", [128, ST], BF16, ph) for i in range(4)]
            pt_rr = [0]
            Pn = [sb(f"Pn{i}", [128, ST], F32, ph) for i in range(4)]
            ocmp, b_ocmp = sb("ocmp", [64, 4, ST], F32, ph)
            rec, b_rec = sb("rec", [128, ST], F32, ph)
            r2, b_r2 = sb("r2", [64, ST], F32, ph)
            oacc, b_oacc = sb("oacc", [64, ST], F32, ph)
            otm, b_otm = sb("otm", [64, ST], F32, ph)
            scq, b_scq = sb("scq", [128, 32], F32, ph)
            scq2, b_scq2 = sb("scq2", [128, 32], F32, ph)
            m8, b_m8 = sb("m8", [128, 8], F32, ph)
            S.dma(sp, Qaug[64:73, :, :], cin["qconst"][:, :, q0:q0 + ST], writes=[b_Q])
            for half in range(2):
                wv, b_w = load_w("w_nsa_in", [(half * 512, 512)], KC)
                for hh in range(8):
                    h = half * 8 + hh
                    pt, b_p = next_bank()
                    for k in range(KC):
                        S.op(pe, lambda e, k=k, hh=hh, pt=pt, wv=wv: e.matmul(pt[0:64, 0:ST], wv[:, k, hh * 64:(hh + 1) * 64], hT[:, k, 0:ST],
                                                                           start=(k == 0), stop=(k == KC - 1)),
                             reads=[b_w, b_hT], writes=[b_p], inc=(k == KC - 1))
                    S.op(act, lambda e, h=h, pt=pt: e.activation(Qaug[0:64, h, :], pt[0:64, 0:ST], AF.Copy, scale=DH ** -0.5),
                         reads=[b_p], writes=[b_Q])
            wg, b_wg = load_w("w_nsa_in", [(1024, 48)], KC)
            Gs, b_Gs = sb("Gs", [48, ST], F32, ph)
            dsb, b_dsb = sb("dsb", [64, ST], F32, ph)
            pgl, b_pgl = next_bank()
            for k in range(KC):
                S.op(pe, lambda e, k=k: e.matmul(pgl[0:48, 0:ST], wg[:, k, 0:48], hT[:, k, 0:ST], start=(k == 0), stop=(k == KC - 1)),
                     reads=[b_wg, b_hT], writes=[b_pgl], inc=(k == KC - 1))
            S.op(act, lambda e: e.activation(Gs[:], pgl[0:48, 0:ST], AF.Sigmoid), reads=[b_pgl], writes=[b_Gs])

            def gate(h, br):
                col = 3 * h + br
                pg, b_pg = next_bank()
                S.op(pe, lambda e, pg=pg: e.matmul(pg[0:64, 0:ST], ident_f[0:48, col:col + 1].broadcast_to([48, 64]), Gs[:],
                                                  start=True, stop=True), reads=[b_Gs, b_ident], writes=[b_pg])
                return pg, b_pg

            def score_tile(mms):
                ps_, b_ps = next_bank()
                for i, (l_, r_, rb) in enumerate(mms):
                    S.op(pe, lambda e, l_=l_, r_=r_, i=i, ps_=ps_: e.matmul(ps_[:, 0:ST], l_, r_, start=(i == 0), stop=(i == len(mms) - 1)),
                         reads=rb, writes=[b_ps], inc=(i == len(mms) - 1))
                PT, b_PT = PTs[pt_rr[0] % 4]
                pt_rr[0] += 1
                S.op(act, lambda e, PT=PT, ps_=ps_: e.activation(PT[:], ps_[:, 0:ST], AF.Exp), reads=[b_ps], writes=[b_PT])
                return PT, b_PT

            for g in range(G_B):
                for hi in range(4):
                    h = 4 * g + hi
                    PT, b_PT = score_tile([(NS["Kcmp"][0:73, g, :], Qaug[0:73, h, :], [b_K["Kcmp"], b_Q]),
                                           (ident_b[:], NS["Tcmp"][:, q0:q0 + ST], b_ctab + [b_K["ntab"]])])
                    px, b_px = next_bank()
                    S.op(pe, lambda e, PT=PT, px=px: e.matmul(px[:, 0:ST], ones_b[:], PT[:], start=True, stop=True),
                         reads=[b_PT, b_ones], writes=[b_px])
                    S.op(dve, lambda e, px=px: e.tensor_scalar(rec[:], px[:, 0:ST], 1e-30, None, op0=ALU.max), reads=[b_px], writes=[b_rec])
                    S.op(dve, lambda e: e.reciprocal(rec[:], rec[:]), reads=[b_rec], writes=[b_rec])
                    S.op(dve, lambda e, PT=PT, hi=hi: e.tensor_tensor(Pn[hi][0][:], PT[:], rec[:], op=ALU.mult),
                         reads=[b_PT, b_rec], writes=[Pn[hi][1]])
                    po_, b_po_ = next_bank()
                    S.op(pe, lambda e, PT=PT, po_=po_: e.matmul(po_[0:64, 0:ST], NS["Vcmp"][:, g, :], PT[:], start=True, stop=True),
                         reads=[b_PT, b_K["Vcmp"]], writes=[b_po_])
                    pg, b_pg = gate(h, 0)
                    S.op(dve, lambda e, pg=pg: e.tensor_tensor(r2[:], pg[0:64, 0:ST], rec[0:64, :], op=ALU.mult), reads=[b_rec, b_pg], writes=[b_r2])
                    S.op(dve, lambda e, hi=hi, po_=po_: e.tensor_tensor(ocmp[:, hi, :], po_[0:64, 0:ST], r2[:], op=ALU.mult),
                         reads=[b_po_, b_r2], writes=[b_ocmp])
                for tq in range(JT):
                    pi_, b_pi = next_bank()
                    for hi in range(4):
                        S.op(pe, lambda e, hi=hi, tq=tq, pi_=pi_: e.matmul(pi_[:, 0:32], Pn[hi][0][:, tq * 128:(tq + 1) * 128], ovl[:],
                                                                        start=(hi == 0), stop=(hi == 3)),
                             reads=[Pn[hi][1]] + b_ctab, writes=[b_pi], inc=(hi == 3))
                    S.op(dve, lambda e, tq=tq, pi_=pi_: e.tensor_tensor(scq[:], pi_[:, 0:32], NS["fb"][:, st * JT + tq, :], op=ALU.add),
                         reads=[b_pi, b_K["ntab"]], writes=[b_scq])
                    S.op(dve, lambda e: e.max(out=m8[:], in_=scq[:]), reads=[b_scq], writes=[b_m8])
                    S.op(dve, lambda e: e.match_replace(out=scq2[:], in_to_replace=m8[:], in_values=scq[:], imm_value=-1e30),
                         reads=[b_scq, b_m8], writes=[b_scq2])
                    S.op(dve, lambda e: e.max(out=m8[:], in_=scq2[:]), reads=[b_scq2], writes=[b_m8])
                    S.op(dve, lambda e: e.tensor_scalar(scq2[:], scq[:], m8[:, 7:8], None, op0=ALU.is_ge), reads=[b_scq, b_m8], writes=[b_scq2])
                    S.op(dve, lambda e: e.tensor_scalar(scq2[:], scq2[:], -NEG, NEG, op0=ALU.mult, op1=ALU.add), reads=[b_scq2], writes=[b_scq2])
                    pT_, b_pT = next_bank()
                    S.op(pe, lambda e, pT_=pT_: e.transpose(pT_[0:32, 0:128], scq2[:], ident_f[:]), reads=[b_scq2, b_ident], writes=[b_pT])
                    S.op(act, lambda e, tq=tq, pT_=pT_: e.activation(negT[:, tq * 128:(tq + 1) * 128], pT_[0:32, 0:128], AF.Copy),
                         reads=[b_pT], writes=[b_negT])
                work = []
                for hi in range(4):
                    h = 4 * g + hi
                    for bi, br in enumerate((1, 2)):
                        if br == 1:
                            tiles = list(range(0, (q0 + ST) // 128))
                            Kt, Vt, bK, bV = NS["Kslc"], NS["Vslc"], b_K["Kslc"], b_K["Vslc"]
                        else:
                            tiles = list(range(max(0, (q0 - 512) // 128), (q0 + ST) // 128))
                            Kt, Vt, bK, bV = NS["Kwin"], NS["Vwin"], b_K["Kwin"], b_K["Vwin"]
                        for idx, kt in enumerate(tiles):
                            k0 = kt * 128
                            mms = [(Kt[0:73, g, k0:k0 + 128], Qaug[0:73, h, :], [bK, b_Q])]
                            if br == 1:
                                mms.append((NS["Emat"][:, k0:k0 + 128], negT[:], [b_K["ntab"], b_negT]))
                            if k0 >= q0:
                                j = (k0 - q0) // 128
                                mms.append((ident_b[:], Tc[:, 128 * (JT - 1 - j): 128 * (JT - 1 - j) + ST], b_ctab))
                            elif br == 2 and k0 < q0 - 512 + 128 * JT:
                                j = (k0 - (q0 - 512)) // 128
                                mms.append((ident_b[:], Tl[:, 128 * (JT - 1 - j): 128 * (JT - 1 - j) + ST], b_ctab))
                            work.append(dict(mms=mms, V=Vt[:, kt, g, :], bV=bV, first=(idx == 0), last=(idx == len(tiles) - 1),
                                             h=h, hi=hi, br=br, bi=bi))

                def emit_pv(w, PT, b_PT):
                    (pso, b_pso), (psd, b_psd) = RBANKS[2 * (w["bi"] % 2)], RBANKS[2 * (w["bi"] % 2) + 1]
                    S.op(pe, lambda e: e.matmul(pso[0:64, 0:ST], w["V"], PT[:], start=w["first"], stop=w["last"]),
                         reads=[b_PT, w["bV"]], writes=[b_pso], inc=w["last"])
                    S.op(pe, lambda e: e.matmul(psd[0:64, 0:ST], ones_b[:, 0:64], PT[:], start=w["first"], stop=w["last"]),
                         reads=[b_PT, b_ones], writes=[b_psd], inc=w["last"])
                    if not w["last"]:
                        return
                    h, hi, br = w["h"], w["hi"], w["br"]
                    S.op(dve, lambda e: e.reciprocal(dsb[:], psd[0:64, 0:ST]), reads=[b_psd], writes=[b_dsb])
                    pg, b_pg = gate(h, br)
                    S.op(dve, lambda e: e.tensor_tensor(r2[:], pg[0:64, 0:ST], dsb[:], op=ALU.mult), reads=[b_dsb, b_pg], writes=[b_r2])
                    S.op(dve, lambda e: e.tensor_tensor(otm[:], pso[0:64, 0:ST], r2[:], op=ALU.mult),
                         reads=[b_pso, b_r2], writes=[b_otm])
                    if br == 1:
                        S.op(dve, lambda e: e.tensor_tensor(oacc[:], otm[:], ocmp[:, hi, :], op=ALU.add),
                             reads=[b_otm, b_ocmp], writes=[b_oacc])
                    else:
                        S.op(dve, lambda e: e.tensor_tensor(oTn[:, h, :], otm[:], oacc[:], op=ALU.add),
                             reads=[b_otm, b_oacc], writes=[b_oTn])

                DEPTH = 3
                pend = []
                for w in work:
                    PT, b_PT = score_tile(w["mms"])
                    pend.append((w, PT, b_PT))
                    if len(pend) > DEPTH:
                        emit_pv(*pend.pop(0))
                while pend:
                    emit_pv(*pend.pop(0))
            for half in range(2):
                wv, b_w = load_w("w_nsa_out", [(half * 512, 512)], 16, rows_per_k=64)
                for oc4 in range(4):
                    oc = half * 4 + oc4
                    pt, b_p = next_bank()
                    for h in range(16):
                        S.op(pe, lambda e, h=h, oc4=oc4, pt=pt, wv=wv: e.matmul(pt[:, 0:ST], wv[:, h, oc4 * 128:(oc4 + 1) * 128], oTn[:, h, :],
                                                                             start=(h == 0), stop=(h == 15)),
                             reads=[b_w, b_oTn], writes=[b_p], inc=(h == 15))
                    resid_add(oc, pt, ST, False, modv(1, 2), b_p)
            S.barrier()

    def nsa_sample():
        ovl, ident_b = ctab["ovl"][0], ctab["ident_b"][0]
        NROT[0] = 4
        NT = 128
        with ExitStack() as ph:
            norm_mod(NT, True, gmul[:, 1, 0], modv(1, 0), hT, b_hT, ph)
            S.barrier()
        with ExitStack() as ph:
            def tab(name, shape, dt):
                t, b = sb(name, shape, dt, ph)
                S.dma(sp, t[:], cin[name], writes=[b])
                return t, b
            NS["Emat"], _ = sb("EmatS", [32, SEQ], BF16, ph)
            S.dma(sp, NS["Emat"][:], cin["Emat"], writes=[b_K["ntab"]])
            TnewS, b_Tnew = tab("TnewS", [128, 16, 32], BF16)
            TlS, b_TlS = tab("TlS", [128, 32], BF16)
            TcS, b_TcS = tab("TcS", [128, 1], BF16)
            fbS, b_fbS = tab("fbS", [8, 33], F32)
            ovl33, b_ovl33 = tab("ovl33", [128, 33], F32)
            pidx, b_pidx = tab("pidx", [128, 1], F32)
            ptab, b_ptab = sb("ptab", [128, NSEQ_S * NPAGE], I32, ph)
            ptf, b_ptf = sb("ptf", [128, NSEQ_S * NPAGE], F32, ph)
            pgidx, b_pgidx = sb("pgidx", [128, NSEQ_S * NPAGE], I32, ph)
            S.dma(sp, ptab[:], page_table.rearrange("b j -> (b j)").partition_broadcast(128), writes=[b_ptab])
            S.op(dve, lambda e: e.tensor_copy(ptf[:], ptab[:]), reads=[b_ptab], writes=[b_ptf])
            S.op(dve, lambda e: e.tensor_scalar(ptf[:], ptf[:], 128.0, pidx[:, 0:1], op0=ALU.mult, op1=ALU.add),
                 reads=[b_ptf, b_pidx], writes=[b_ptf])
            S.op(dve, lambda e: e.tensor_copy(pgidx[:], ptf[:]), reads=[b_ptf], writes=[b_pgidx])
            b_tabs = [b_Tnew, b_TlS, b_TcS, b_fbS, b_ovl33, b_K["ntab"]] + b_ctab
            for nm, ncol in (("Kslc", SEQ), ("Kwin", 512), ("Kcmp", 128)):
                NS[nm], _ = sb(nm + "S", [128, G_B, ncol], BF16, ph)
            NS["Vslc"], _ = sb("VslcS", [128, NPAGE, G_B, DH], BF16, ph)
            NS["Vwin"], _ = sb("VwinS", [128, 4, G_B, DH], BF16, ph)
            NS["Vcmp"], _ = sb("VcmpS", [128, G_B, DH], BF16, ph)
            NS["szv"], _ = sb("szvS", [128, G_B, 128], BF16, ph)
            NS["kraw"], _ = sb("krawS", [128, G_B, 512 + 16], BF16, ph)
            NS["vraw"], _ = sb("vrawS", [128, G_B, 512 + 16], BF16, ph)
            NS["w1s"], _ = sb("w1s", [128, 2, 16, 128], BF16, ph)
            for i_, w1_ in enumerate((w_ck1, w_cv1)):
                S.dma(pool, NS["w1s"][:, i_], w1_.rearrange("(i q) n -> q i n", q=128), writes=[b_K["cmpw"]])
            szS = sb("szS", [128, 2, G_B * 32], BF16, ph)
            for g in range(G_B):
                S.dma(sp, NS["Kslc"][64:73, g, :], cin["kconst"], writes=[b_K["Kslc"]])
                S.dma(sp, NS["Kwin"][64:73, g, :], cin["kconst"][:, SEQ - 512:SEQ], writes=[b_K["Kwin"]])
                S.dma(sp, NS["Kcmp"][64:73, g, :], cin["kconst_cmp"], writes=[b_K["Kcmp"]])
            S.op(dve, lambda e: e.memset(NS["Kcmp"][0:64], 0.0), writes=[b_K["Kcmp"]])
            pages = [sb(f"page{i}", [128, 1024], F32, ph) for i in range(3)]
            wins = [sb(f"wint{i}", [128, 512], F32, ph) for i in range(1)]
            QaugS, b_Q = sb("QaugS", [128, NSEQ_S, 128], BF16, ph)
            obr = [sb(f"obr{i}", [64, 16, NT], F32, ph) for i in range(3)]
            negTS, b_negT = sb("negTS", [32, 128], BF16, ph)
            PTs = [sb(f"PTS{i}", [128, 256], BF16, ph) for i in range(3)]
            pt_rr = [0]
            Pn, b_Pn = sb("PnS", [128, 32], F32, ph)
            rec, b_rec = sb("recS", [128, 32], F32, ph)
            scq, b_scq = sb("scqS", [8, 33], F32, ph)
            scq2, b_scq2 = sb("scq2S", [8, 33], F32, ph)
            m8, b_m8 = sb("m8S", [8, 8], F32, ph)
            oacc, b_oacc = sb("oaccS", [64, NT], F32, ph)
            otm, b_otm = sb("otmS", [64, NT], F32, ph)
            oTn, b_oTn = sb("oTnS", [64, 16, NT], BF16, ph)
            for b in range(NSEQ_S):
                S.dma(sp, QaugS[64:73, b, :].rearrange("p (h t) -> p h t", t=DEC_SEQ), cin["qconst"][:, :, SEQ:SEQ + DEC_SEQ], writes=[b_Q])
            for half in range(2):
                wv, b_w = load_w("w_nsa_in", [(half * 512, 512)], KC)
                for hh in range(8):
                    h = half * 8 + hh
                    pt, b_p = next_bank()
                    for k in range(KC):
                        S.op(pe, lambda e, k=k, hh=hh, pt=pt, wv=wv: e.matmul(pt[0:64, 0:NT], wv[:, k, hh * 64:(hh + 1) * 64], hT[:, k, 0:NT],
                                                                           start=(k == 0), stop=(k == KC - 1)),
                             reads=[b_w, b_hT], writes=[b_p], inc=(k == KC - 1))
                    S.op(act, lambda e, h=h, pt=pt: e.activation(QaugS[0:64, :, h * 8:(h + 1) * 8], v3(pt[0:64, 0:NT]), AF.Copy, scale=DH ** -0.5),
                         reads=[b_p], writes=[b_Q])
            def score_chunk(tiles):
                ps_, b_ps = next_bank()
                n = len(tiles)
                for i, mms in enumerate(tiles):
                    for m, (l_, r_, rb) in enumerate(mms):
                        S.op(pe, lambda e, l_=l_, r_=r_, i=i, m=m, ps_=ps_, mms=mms: e.matmul(
                            ps_[:, i * 32:(i + 1) * 32], l_, r_, start=(m == 0), stop=(m == len(mms) - 1)),
                            reads=rb, writes=[b_ps], inc=(i == n - 1 and m == len(mms) - 1))
                PT, b_PT = PTs[pt_rr[0] % 3]
                pt_rr[0] += 1
                S.op(act, lambda e, PT=PT, ps_=ps_: e.activation(PT[:, 0:32 * n], ps_[:, 0:32 * n], AF.Exp), reads=[b_ps], writes=[b_PT])
                return PT, b_PT

            SPE = mybir.EngineType.SP
            for b in range(NSEQ_S):
                S.op(dve, lambda e: e.memset(NS["kraw"][:, :, 0:16], 0.0), writes=[b_K["kraw"]])
                S.op(dve, lambda e: e.memset(NS["vraw"][:, :, 0:16], 0.0), writes=[b_K["vraw"]])
                for j in range(NPAGE):
                    pg, b_pg = pages[(b * NPAGE + j) % 3]
                    cidx = b * NPAGE + j
                    S.dma_gather(pool, pg[:], cache_kv, pgidx[:, cidx:cidx + 1], reads=[b_pgidx], writes=[b_pg])
                    S.op(dve, lambda e, j=j, pg=pg: e.tensor_copy(NS["Vslc"][:, j], pg[:, 768:1024].rearrange("p (g d) -> p g d", g=G_B)),
                         reads=[b_pg], writes=[b_K["Vslc"]])
                    for wi, (base, name, c0) in enumerate(((0, "kraw", 16 + (j % 4) * 128), (256, "vraw", 16 + (j % 4) * 128), (512, "Kslc", j * 128))):
                        pT_, b_pT = next_bank()
                        for g in range(G_B):
                            S.op(pe, lambda e, g=g, base=base, pg=pg, pT_=pT_: e.transpose(pT_[0:64, g * 128:(g + 1) * 128],
                                                                                        pg[:, base + g * 64: base + (g + 1) * 64], ident_f[:]),
                                 reads=[b_pg, b_ident], writes=[b_pT], inc=(g == G_B - 1))
                        dst = NS[name][0:64, :, c0:c0 + 128]
                        src = pT_[0:64, :].rearrange("p (g n) -> p g n", g=G_B)
                        if wi == 1:
                            S.op(dve, lambda e, dst=dst, src=src: e.tensor_copy(dst, src), reads=[b_pT], writes=[b_K[name]])
                        else:
                            S.op(act, lambda e, dst=dst, src=src: e.activation(dst, src, AF.Copy), reads=[b_pT], writes=[b_K[name]])
                        if wi < 2:
                            dstu = NS[name][64:128, :, c0 - 1:c0 - 1 + 128]
                            if wi == 1:
                                S.op(act, lambda e, dstu=dstu, src=src: e.activation(dstu, src, AF.Copy), reads=[b_pT], writes=[b_K[name]])
                            else:
                                S.op(dve, lambda e, dstu=dstu, src=src: e.tensor_copy(dstu, src), reads=[b_pT], writes=[b_K[name]])
                    if j % 4 == 3:
                        compress(0, ntok=512, s0_=32 * (j // 4), sz_pre=szS)
                for tl in range(4):
                    wt_, b_wt = wins[0]
                    S.dma(sp, wt_[:], state_win[b, tl * 128:(tl + 1) * 128, :], writes=[b_wt])
                    S.op(dve, lambda e, tl=tl, wt_=wt_: e.tensor_copy(NS["Vwin"][:, tl], wt_[:, 256:512].rearrange("p (g d) -> p g d", g=G_B)),
                         reads=[b_wt], writes=[b_K["Vwin"]])
                    pT_, b_pT = next_bank()
                    for g in range(G_B):
                        S.op(pe, lambda e, g=g, wt_=wt_, pT_=pT_: e.transpose(pT_[0:64, g * 128:(g + 1) * 128], wt_[:, g * 64:(g + 1) * 64], ident_f[:]),
                             reads=[b_wt, b_ident], writes=[b_pT], inc=(g == G_B - 1))
                    S.op(act, lambda e, tl=tl, pT_=pT_: e.activation(NS["Kwin"][0:64, :, tl * 128:(tl + 1) * 128],
                                                                   pT_[0:64, :].rearrange("p (g n) -> p g n", g=G_B), AF.Copy),
                         reads=[b_pT], writes=[b_K["Kwin"]])
                bs = slice(b * DEC_SEQ, (b + 1) * DEC_SEQ)
                for g in range(G_B):
                    Qg = QaugS[0:73, b, g * 32:(g + 1) * 32]
                    gs = slice(4 * g, 4 * g + 4)
                    PT, b_PT = score_chunk([[(NS["Kcmp"][0:73, g, :], Qg, [b_K["Kcmp"], b_Q]),
                                             (ident_b[:], TcS[:, 0:1].broadcast_to([128, 32]), b_tabs)]])
                    px, b_px = next_bank()
                    S.op(pe, lambda e, PT=PT, px=px: e.matmul(px[:, 0:32], ones_b[:], PT[:, 0:32], start=True, stop=True),
                         reads=[b_PT, b_ones], writes=[b_px])
                    S.op(dve, lambda e, px=px: e.tensor_scalar(rec[:], px[:, 0:32], 1e-30, None, op0=ALU.max), reads=[b_px], writes=[b_rec])
                    S.op(dve, lambda e: e.reciprocal(rec[:], rec[:]), reads=[b_rec], writes=[b_rec])
                    S.op(dve, lambda e, PT=PT: e.tensor_tensor(Pn[:], PT[:, 0:32], rec[:], op=ALU.mult), reads=[b_PT, b_rec], writes=[b_Pn])
                    po_, b_po_ = next_bank()
                    S.op(pe, lambda e, PT=PT, po_=po_: e.matmul(po_[0:64, 0:32], NS["Vcmp"][:, g, :], PT[:, 0:32], start=True, stop=True),
                         reads=[b_PT, b_K["Vcmp"]], writes=[b_po_])
                    S.op(dve, lambda e, po_=po_: e.tensor_tensor(obr[0][0][:, gs, bs], po_[0:64, 0:32].rearrange("p (h t) -> p h t", t=DEC_SEQ),
                                                                rec[0:64, :].rearrange("p (h t) -> p h t", t=DEC_SEQ), op=ALU.mult),
                         reads=[b_po_, b_rec], writes=[obr[0][1]])
                    pi_, b_pi = next_bank()
                    for hi in range(4):
                        S.op(pe, lambda e, hi=hi, pi_=pi_: e.matmul(pi_[0:8, 0:33], Pn[:, hi * 8:(hi + 1) * 8], ovl33[:], start=(hi == 0), stop=(hi == 3)),
                             reads=[b_Pn] + b_tabs, writes=[b_pi], inc=(hi == 3))
                    S.op(dve, lambda e, pi_=pi_: e.tensor_tensor(scq[:], pi_[0:8, 0:33], fbS[:], op=ALU.add), reads=[b_pi] + b_tabs, writes=[b_scq])
                    S.op(dve, lambda e: e.max(out=m8[:], in_=scq[:]), reads=[b_scq], writes=[b_m8])
                    S.op(dve, lambda e: e.match_replace(out=scq2[:], in_to_replace=m8[:], in_values=scq[:], imm_value=-1e30),
                         reads=[b_scq, b_m8], writes=[b_scq2])
                    S.op(dve, lambda e: e.max(out=m8[:], in_=scq2[:]), reads=[b_scq2], writes=[b_m8])
                    S.op(dve, lambda e: e.tensor_scalar(scq2[:], scq[:], m8[:, 7:8], None, op0=ALU.is_ge), reads=[b_scq, b_m8], writes=[b_scq2])
                    S.op(dve, lambda e: e.tensor_scalar(scq2[:], scq2[:], -NEG, NEG, op0=ALU.mult, op1=ALU.add), reads=[b_scq2], writes=[b_scq2])
                    pT_, b_pT = next_bank()
                    S.op(pe, lambda e, pT_=pT_: e.transpose(pT_[0:32, 0:8], scq2[:, 0:32], ident_f[0:8, 0:8]), reads=[b_scq2, b_ident], writes=[b_pT])
                    S.op(act, lambda e, pT_=pT_, g=g: e.activation(negTS[:, g * 32:(g + 1) * 32].rearrange("p (h t) -> p h t", t=DEC_SEQ),
                                                                 pT_[0:32, 0:8].unsqueeze(1).broadcast_to([32, 4, DEC_SEQ]), AF.Copy),
                         reads=[b_pT], writes=[b_negT])
                    for bi, br in enumerate((1, 2)):
                        (pso, b_pso), (psd, b_psd) = RBANKS[2 * bi], RBANKS[2 * bi + 1]
                        tl_list = []
                        if br == 1:
                            for kt in range(NPAGE):
                                tl_list.append(([(NS["Kslc"][0:73, g, kt * 128:(kt + 1) * 128], Qg, [b_K["Kslc"], b_Q]),
                                                 (NS["Emat"][:, kt * 128:(kt + 1) * 128], negTS[:, g * 32:(g + 1) * 32], [b_K["ntab"], b_negT])],
                                                NS["Vslc"][:, kt, g, :], b_K["Vslc"]))
                            tl_list.append(([(NS["KslcN"][0:73, g, :], Qg, [b_K["KslcN"], b_Q]), (ident_b[:], TnewS[:, b, :], b_tabs)],
                                            NS["VslcN"][:, g, :], b_K["VslcN"]))
                        else:
                            for kt in range(4):
                                mm_ = [(NS["Kwin"][0:73, g, kt * 128:(kt + 1) * 128], Qg, [b_K["Kwin"], b_Q])]
                                if kt == 0:
                                    mm_.append((ident_b[:], TlS[:], b_tabs))
                                tl_list.append((mm_, NS["Vwin"][:, kt, g, :], b_K["Vwin"]))
                            tl_list.append(([(NS["KwinN"][0:73, g, :], Qg, [b_K["KwinN"], b_Q]), (ident_b[:], TnewS[:, b, :], b_tabs)],
                                            NS["VwinN"][:, g, :], b_K["VwinN"]))
                        ntl = len(tl_list)
                        done = 0
                        for c0 in range(0, ntl, 8):
                            chunk = tl_list[c0:c0 + 8]
                            PT, b_PT = score_chunk([c[0] for c in chunk])
                            for i, (_, Vap, bV) in enumerate(chunk):
                                first, last = (done == 0), (done == ntl - 1)
                                S.op(pe, lambda e, PT=PT, i=i, Vap=Vap, first=first, last=last, pso=pso: e.matmul(
                                    pso[0:64, 0:32], Vap, PT[:, i * 32:(i + 1) * 32], start=first, stop=last),
                                    reads=[b_PT, bV], writes=[b_pso], inc=last)
                                S.op(pe, lambda e, PT=PT, i=i, first=first, last=last, psd=psd: e.matmul(
                                    psd[0:64, 0:32], ones_b[:, 0:64], PT[:, i * 32:(i + 1) * 32], start=first, stop=last),
                                    reads=[b_PT, b_ones], writes=[b_psd], inc=last)
                                done += 1
                        S.op(dve, lambda e, psd=psd: e.reciprocal(rec[0:64, :], psd[0:64, 0:32]), reads=[b_psd], writes=[b_rec])
                        S.op(dve, lambda e, pso=pso, br=br: e.tensor_tensor(obr[br][0][:, gs, bs], pso[0:64, 0:32].rearrange("p (h t) -> p h t", t=DEC_SEQ),
                                                                          rec[0:64, :].rearrange("p (h t) -> p h t", t=DEC_SEQ), op=ALU.mult),
                             reads=[b_pso, b_rec], writes=[obr[br][1]])
            wg, b_wg = load_w("w_nsa_in", [(1024, 48)], KC)
            GsS, b_GsS = sb("GsS", [48, NT], F32, ph)
            pgl, b_pgl = next_bank()
            for k in range(KC):
                S.op(pe, lambda e, k=k: e.matmul(pgl[0:48, 0:NT], wg[:, k, 0:48], hT[:, k, 0:NT], start=(k == 0), stop=(k == KC - 1)),
                     reads=[b_wg, b_hT], writes=[b_pgl], inc=(k == KC - 1))
            S.op(act, lambda e: e.activation(GsS[:], pgl[0:48, 0:NT], AF.Sigmoid), reads=[b_pgl], writes=[b_GsS])
            for h in range(16):
                for br in range(3):
                    col = 3 * h + br
                    pg_, b_pg_ = next_bank()
                    S.op(pe, lambda e, pg_=pg_, col=col: e.matmul(pg_[0:64, 0:NT], ident_f[0:48, col:col + 1].broadcast_to([48, 64]), GsS[:],
                                                               start=True, stop=True), reads=[b_GsS, b_ident], writes=[b_pg_])
                    if br == 0:
                        S.op(dve, lambda e, h=h, pg_=pg_: e.tensor_tensor(oacc[:], obr[0][0][:, h, :], pg_[0:64, 0:NT], op=ALU.mult),
                             reads=[obr[0][1], b_pg_], writes=[b_oacc])
                    else:
                        S.op(dve, lambda e, h=h, br=br, pg_=pg_: e.tensor_tensor(otm[:], obr[br][0][:, h, :], pg_[0:64, 0:NT], op=ALU.mult),
                             reads=[obr[br][1], b_pg_], writes=[b_otm])
                        if br == 1:
                            S.op(dve, lambda e: e.tensor_tensor(oacc[:], oacc[:], otm[:], op=ALU.add), reads=[b_oacc, b_otm], writes=[b_oacc])
                        else:
                            S.op(dve, lambda e, h=h: e.tensor_tensor(oTn[:, h, :], oacc[:], otm[:], op=ALU.add), reads=[b_oacc, b_otm], writes=[b_oTn])
            for half in range(2):
                wv, b_w = load_w("w_nsa_out", [(half * 512, 512)], 16, rows_per_k=64)
                for oc4 in range(4):
                    oc = half * 4 + oc4
                    pt, b_p = next_bank()
                    for h in range(16):
                        S.op(pe, lambda e, h=h, oc4=oc4, pt=pt, wv=wv: e.matmul(pt[:, 0:NT], wv[:, h, oc4 * 128:(oc4 + 1) * 128], oTn[:, h, :],
                                                                             start=(h == 0), stop=(h == 15)),
                             reads=[b_w, b_oTn], writes=[b_p], inc=(h == 15))
                    resid_add(oc, pt, NT, True, modv(1, 2), b_p)
            S.barrier()
            NS.pop("w1s", None)

    def final_norm_store(dst_rows, ntok, sample):
        with ExitStack() as ph:
            yT, b_yT = sb("yT", [128, KC, ntok], F32, ph)
            norm_mod(ntok, sample, gfin, zero17, yT, b_yT, ph)
            store_yT(dst_rows, ntok, yT, b_yT)

    if DEV["prompt"]:
        with ExitStack() as pst:
            P["Sst"], _ = sb("Sst", [128, H_A, 2, DV_A], F32, pst)
            P["Sbf"], _ = sb("Sbf", [128, H_A, 2, DV_A], BF16, pst)
            S.op(dve, lambda e: e.memset(P["Sst"][:], 0.0), writes=[b_Sst])
            nsa_on = DEV.get("nsa", True)
            if nsa_on:
                for nm in ("Kslc", "Kwin"):
                    NS[nm], _ = sb(nm, [128, G_B, SEQ], BF16, pst)
                NS["Kcmp"], _ = sb("Kcmp", [128, G_B, 128], BF16, pst)
                NS["Vslc"], _ = sb("Vslc", [128, SEQ // 128, G_B, DH], BF16, pst)
                NS["Vwin"], _ = sb("Vwin", [128, SEQ // 128, G_B, DH], BF16, pst)
                NS["Vcmp"], _ = sb("Vcmp", [128, G_B, DH], BF16, pst)
                NS["kraw"], _ = sb("kraw", [128, G_B, ST + 16], BF16, pst)
                NS["vraw"], _ = sb("vraw", [128, G_B, ST + 16], BF16, pst)
                NS["szv"], _ = sb("szv", [128, G_B, 128], BF16, pst)
                NS["w2"], _ = sb("w2", [128, 2, DH], BF16, pst)
                NS["bz"], _ = sb("bz", [128, 2], F32, pst)
                NS["Tcmp"], _ = sb("Tcmp", [128, SEQ], BF16, pst)
                NS["Emat"], _ = sb("Emat", [32, SEQ], BF16, pst)
                NS["fb"], _ = sb("fb", [128, SEQ // 128, 32], F32, pst)
                for nm in ("Tcmp", "Emat", "fb"):
                    S.dma(sp, NS[nm][:], cin[nm], writes=[b_K["ntab"]])
                for nm in ("Kslc", "Kwin"):
                    for g in range(G_B):
                        S.dma(sp, NS[nm][64:73, g, :], cin["kconst"], writes=[b_K[nm]])
                for g in range(G_B):
                    S.dma(sp, NS["Kcmp"][64:73, g, :], cin["kconst_cmp"], writes=[b_K["Kcmp"]])
                S.op(dve, lambda e: e.memset(NS["Kcmp"][0:64], 0.0), writes=[b_K["Kcmp"]])
                S.op(dve, lambda e: e.memset(NS["szv"][:], 0.0), writes=[b_K["szv"]])
                S.op(dve, lambda e: e.memset(NS["kraw"][:], 0.0), writes=[b_K["kraw"]])
                S.op(dve, lambda e: e.memset(NS["vraw"][:], 0.0), writes=[b_K["vraw"]])
                compress_setup()
            for st in range(NST):
                load_xT(x_p[st * ST:(st + 1) * ST, :], ST)
                retention(st, False)
                ffn(0, st, st == NST - 1, False)
                kv_proj(st, False)
                if nsa_on:
                    compress(st)
                    nsa_prompt(st)
                ffn(1, st, st == NST - 1, False)
                final_norm_store(y_p[st * ST:(st + 1) * ST, :], ST, False)
            S.dma(sp, ret_p.rearrange("h (c p) v -> p h c v", p=128), P["Sst"][:], reads=[b_Sst])
            S.barrier()
    if DEV["sample"]:
        with ExitStack() as sst_:
            P["cbufT"], _ = sb("cbufT", [128, 2, NFC, 2 * NSEQ_S], F32, sst_)
            nsa_on = DEV.get("nsa", True)
            if nsa_on:
                NS.clear()
                NS["w2"], _ = sb("w2S", [128, 2, DH], BF16, sst_)
                NS["bz"], _ = sb("bzS", [128, 2], F32, sst_)
                for nm in ("KslcN", "KwinN"):
                    NS[nm], _ = sb(nm, [128, G_B, 128], BF16, sst_)
                    for g in range(G_B):
                        S.dma(sp, NS[nm][64:73, g, :], cin["kconst_new"], writes=[b_K[nm]])
                NS["VslcN"], _ = sb("VslcN", [128, G_B, DH], BF16, sst_)
                NS["VwinN"], _ = sb("VwinN", [128, G_B, DH], BF16, sst_)
                compress_setup()
            load_conv_bufs()
            load_xT(x_s, 128)
            retention(0, True)
            ffn(0, 0, False, True)
            kv_proj(0, True)
            if nsa_on:
                nsa_sample()
            ffn(1, 0, False, True)
            final_norm_store(y_s, 128, True)
            S.barrier()
    S.finish()


_NC_CACHE = {}


def kernel(**inputs):
    f32 = lambda a: np.ascontiguousarray(np.asarray(a, dtype=np.float32))
    if "nc" not in _NC_CACHE:
        _NC_CACHE["nc"] = build_nc()
    nc = _NC_CACHE["nc"]
    shared = {k: f32(inputs[k]) for k in ["w_ada", "b_ada", "g_mix", "g_ffn", "w_ffn_in", "conv_w", "conv_b",
                                           "w_ffn_out", "w_ret_in", "w_ret_out", "g_final",
                                           "g_kv", "w_ada_kv", "b_ada_kv", "w_kv", "w_nsa_in", "w_nsa_out",
                                           "pe_ck", "pe_cv", "w_ck1", "w_ck2", "w_cv1", "w_cv2"]}
    for k in CONST_IN + CONST_DRAM:
        shared["k_" + k] = np.ascontiguousarray(CONST[k])
    xp, xs = f32(inputs["x_prompt"]), f32(inputs["x_sample"])
    cp, cs = f32(inputs["c_prompt"]), f32(inputs["c_sample"])
    sret, sconv, swin = f32(inputs["state_ret"]), f32(inputs["state_conv"]), f32(inputs["state_win"])
    cache = f32(inputs["cache_kv"]).reshape(NPHYS * 128, 1024)
    ptab = np.ascontiguousarray(np.asarray(inputs["page_table"], dtype=np.int32))
    cores = DEV["cores"] or list(range(N_CORES))
    in_maps = []
    for c in cores:
        m = dict(shared)
        sl = slice(c * NSEQ_S, (c + 1) * NSEQ_S)
        m["x_p"] = xp[c]
        m["x_s"] = xs[sl].reshape(NSEQ_S * DEC_SEQ, D)
        m["c_p"] = cp[c:c + 1]
        m["c_s"] = cs[sl]
        m["state_ret"] = sret[0, sl]
        m["state_conv"] = np.ascontiguousarray(sconv[:, sl]).reshape(2, NSEQ_S * 2, F2)
        m["state_win"] = swin[sl].reshape(NSEQ_S, 512, 512)
        m["cache_kv"] = cache
        m["page_table"] = ptab[sl]
        in_maps.append(m)
    if DEV.get("trace"):
        res = run_bass_kernel_spmd(nc, in_maps, core_ids=list(range(len(cores))), trace=True)
        print("DEV exec_time_ns:", res.exec_time_ns)
    else:
        res = run_bass_kernel_spmd(nc, in_maps, core_ids=list(range(len(cores))))
    R = list(res.results)
    if len(R) < N_CORES:
        full = [None] * N_CORES
        for c, r in zip(cores, R):
            full[c] = r
        z = {k: np.zeros_like(np.asarray(v)) for k, v in R[0].items()}
        R = [r if r is not None else z for r in full]
    cat = lambda k: np.stack([np.asarray(r[k], dtype=np.float32) for r in R])
    y_prompt = cat("y_p")
    y_sample = cat("y_s").reshape(128, DEC_SEQ, D)
    ret_prompt = cat("ret_p")[None]
    ret_sample = cat("ret_s").reshape(1, 128, H_A, DK_A, DV_A)
    conv_prompt = np.ascontiguousarray(cat("conv_p").transpose(1, 0, 2, 3))
    conv_sample = np.ascontiguousarray(cat("conv_s").transpose(1, 0, 2, 3, 4)).reshape(2, 128, 2, F2)
    kv_prompt = cat("kv_p").reshape(8, SEQ, 4, 4, 64)
    kv_sample = cat("kv_s").reshape(128, DEC_SEQ, 4, 4, 64)
    win_prompt = cat("win_p").reshape(8, 512, 2, 4, 64)
    win_sample = cat("win_s").reshape(128, 512, 2, 4, 64)
    return (y_prompt, y_sample, ret_prompt, ret_sample, conv_prompt, conv_sample,
            kv_prompt, kv_sample, win_prompt, win_sample)
```

```python
from contextlib import ExitStack
import numpy as np
import ml_dtypes
import concourse.bass as bass
import concourse.mybir as mybir
from concourse.bass_utils import run_bass_kernel_spmd

F32 = mybir.dt.float32
BF16 = mybir.dt.bfloat16
I32 = mybir.dt.int32
AF = mybir.ActivationFunctionType
ALU = mybir.AluOpType

D = 1024
KC = 8
SEQ = 2048
ST = 256
NST = SEQ // ST
H_A, DK_A, DV_A = 4, 256, 512
D_FF = 2816
F2 = 2 * D_FF
NFC = F2 // 128
NPAIR = D_FF // 128
EPS = 1e-6
NSEQ_S = 16
DEC_SEQ = 8
N_CORES = 8
NEG = -30000.0
NPHYS = 2560
NPAGE = 16
G_B, DH = 4, 64


class Eng:
    def __init__(self, name, e, sem, step=1, is_pe=False):
        self.name, self.e, self.sem, self.step, self.is_pe = name, e, sem, step, is_pe
        self.cnt = 0
        self.waited = {}


class Buf:
    __slots__ = ("name", "w", "r")

    def __init__(self, name):
        self.name, self.w, self.r = name, None, {}


class Sched:
    def __init__(self, nc, es, n_sp=12, n_pool=12, n_act=4):
        self.nc = nc
        sem = lambda n: es.enter_context(nc.semaphore(n))
        self.pe = Eng("pe", nc.tensor, sem("s_pe"), is_pe=True)
        self.act = Eng("act", nc.scalar, sem("s_act"))
        self.dve = Eng("dve", nc.vector, sem("s_dve"))
        self.pool = Eng("pool", nc.gpsimd, sem("s_pool"))
        self.sp = Eng("sp", nc.sync, sem("s_sp"))
        self.engs = [self.pe, self.act, self.dve, self.pool, self.sp]
        self.chans = {
            "sp": [Eng(f"c_sp{i}", None, sem(f"c_sp{i}"), step=16) for i in range(n_sp)],
            "pool": [Eng(f"c_pl{i}", None, sem(f"c_pl{i}"), step=16) for i in range(n_pool)],
            "act": [Eng(f"c_ac{i}", None, sem(f"c_ac{i}"), step=16) for i in range(n_act)],
        }
        self.rr = {"sp": 0, "pool": 0, "act": 0}
        self.bar_deps = {}

    def _deps(self, reads, writes):
        deps = {}
        for b in reads:
            if b.w is not None:
                deps[b.w[0]] = max(deps.get(b.w[0], 0), b.w[1])
        for b in writes:
            if b.w is not None:
                deps[b.w[0]] = max(deps.get(b.w[0], 0), b.w[1])
            for e, n in b.r.items():
                deps[e] = max(deps.get(e, 0), n)
        return deps

    def _wait(self, eng, deps):
        for f, n in deps.items():
            if f is eng and eng.is_pe:
                continue
            if eng.waited.get(f, 0) < n:
                eng.e.wait_ge(f.sem, n * f.step)
                eng.waited[f] = n

    def op(self, eng, fn, reads=(), writes=(), inc=True):
        self._wait(eng, self._deps(reads, writes))
        ins = fn(eng.e)
        n = eng.cnt + 1
        if inc:
            ins.then_inc(eng.sem, eng.step)
            eng.cnt = n
        for b in reads:
            b.r[eng] = max(b.r.get(eng, 0), n)
        for b in writes:
            b.w = (eng, n)
            b.r = {}
        return ins

    def dma(self, issuer, out, in_, reads=(), writes=(), persistent=False, **kw):
        lst = self.chans[issuer.name]
        ch = lst[self.rr[issuer.name] % len(lst)]
        self.rr[issuer.name] += 1
        deps = self._deps(reads, writes)
        if issuer is self.sp and not persistent:
            for f, n in self.bar_deps.items():
                deps[f] = max(deps.get(f, 0), n)
        if ch.cnt > 0:
            deps[ch] = max(deps.get(ch, 0), ch.cnt)
        self._wait(issuer, deps)
        issuer.e.dma_start(out=out, in_=in_, **kw).then_inc(ch.sem, 16)
        ch.cnt += 1
        for b in reads:
            b.r[ch] = max(b.r.get(ch, 0), ch.cnt)
        for b in writes:
            b.w = (ch, ch.cnt)
            b.r = {}

    def dma_gather(self, issuer, out, in_, idx_ap, reads=(), writes=()):
        lst = self.chans[issuer.name]
        ch = lst[self.rr[issuer.name] % len(lst)]
        self.rr[issuer.name] += 1
        deps = self._deps(reads, writes)
        if ch.cnt > 0:
            deps[ch] = max(deps.get(ch, 0), ch.cnt)
        self._wait(issuer, deps)
        issuer.e.indirect_dma_start(out=out, out_offset=None, in_=in_,
                                    in_offset=bass.IndirectOffsetOnAxis(ap=idx_ap, axis=0)).then_inc(ch.sem, 16)
        ch.cnt += 1
        for b in reads:
            b.r[ch] = max(b.r.get(ch, 0), ch.cnt)
        for b in writes:
            b.w = (ch, ch.cnt)
            b.r = {}

    def all_srcs(self):
        out = list(self.engs)
        for l in self.chans.values():
            out += l
        return out

    def barrier(self, engs=None):
        self.bar_deps = {f: f.cnt for f in self.all_srcs() if f.cnt > 0 and f is not self.sp}
        for e in (engs or self.engs):
            if e is self.sp:
                continue
            deps = {f: f.cnt for f in self.all_srcs() if f.cnt > 0 and not (f is e and e.is_pe)}
            self._wait(e, deps)

    def finish(self):
        deps = {f: f.cnt for f in self.all_srcs() if f.cnt > 0 and f is not self.sp}
        self._wait(self.sp, deps)


def _consts():
    c = {}
    c["ident_f"] = np.eye(128, dtype=np.float32)
    c["ones_b"] = np.ones((128, 128), dtype=ml_dtypes.bfloat16)
    lg = np.log1p(-np.exp2(-5.0 - np.arange(H_A, dtype=np.float64)))
    i = np.arange(128, dtype=np.float64)
    qdec = np.exp(lg[:, None] * i[None, :])
    kdec = np.exp(-lg[:, None] * i[None, :]) * (DK_A ** -0.5)
    c["qdec"] = np.broadcast_to(qdec[None], (128, H_A, 128)).astype(np.float32).copy()
    c["kdec"] = np.broadcast_to(kdec[None], (128, H_A, 128)).astype(np.float32).copy()
    c["kwdec"] = (np.exp(lg[None, :] * (127.0 - i[:, None])) * (DK_A ** -0.5)).astype(np.float32)
    c["cmask"] = (i[:, None] <= i[None, :]).astype(np.float32)
    c["gam"] = np.exp(lg).astype(np.float64)
    c["gam128"] = np.exp(128.0 * lg).astype(np.float64)
    c["gam8"] = np.exp(8.0 * lg).astype(np.float64)
    t8 = (np.arange(128) % 8).astype(np.float64)
    b8 = np.arange(128) // 8
    c["qdecS"] = np.broadcast_to(np.exp(lg[:, None] * t8[None, :])[None], (128, H_A, 128)).astype(np.float32).copy()
    c["kdecS"] = np.broadcast_to((np.exp(-lg[:, None] * t8[None, :]) * (DK_A ** -0.5))[None], (128, H_A, 128)).astype(np.float32).copy()
    c["kwdecS"] = (np.exp(lg[None, :] * (7.0 - t8[:, None])) * (DK_A ** -0.5)).astype(np.float32)
    c["cmaskS"] = ((b8[:, None] == b8[None, :]) & (t8[:, None] <= t8[None, :])).astype(np.float32)
    c["rowmask"] = (b8[:, None] == np.arange(16)[None, :]).astype(np.float32)
    bf = ml_dtypes.bfloat16
    c["ident_b"] = np.eye(128, dtype=bf)
    J = ST // 128
    k128 = np.arange(128)
    cc = np.arange(ST + 128 * (J - 1))
    c["Tc"] = np.where(k128[:, None] <= cc[None, :] - 128 * (J - 1), 0.0, NEG).astype(bf)
    c["Tl"] = np.where(k128[:, None] < cc[None, :] - 128 * (J - 1), NEG, 0.0).astype(bf)
    pos = np.arange(SEQ)
    c["Tcmp"] = np.where((k128[:, None] >= 1) & (16 * k128[:, None] + 15 <= pos[None, :]), 0.0, NEG).astype(bf)
    c["Emat"] = (pos[None, :] // 64 == np.arange(32)[:, None]).astype(bf)
    ib = k128 - 1
    jb = np.arange(32)
    c["ovl"] = ((ib[:, None] >= 0) & (16 * ib[:, None] <= 64 * jb[None, :] + 63)
                & (16 * ib[:, None] + 31 >= 64 * jb[None, :])).astype(np.float32)
    cur = pos // 64
    fbt = np.where(jb[None, :] > cur[:, None], -1e30,
                   np.where((jb[None, :] == 0) | (jb[None, :] == cur[:, None]) | (jb[None, :] == cur[:, None] - 1), 1e4, 0.0))
    c["fb"] = np.ascontiguousarray(fbt.reshape(SEQ // 128, 128, 32).transpose(1, 0, 2)).astype(np.float32)

    def split3(v):
        v = np.asarray(v, dtype=np.float64)
        a = v.astype(bf).astype(np.float64)
        b = (v - a).astype(bf).astype(np.float64)
        d = (v - a - b).astype(bf).astype(np.float64)
        return a, b, d

    def kconst(p):
        a, b = (p // 64).astype(np.float64), (p % 64).astype(np.float64)
        one = np.ones_like(a)
        return np.stack([a, a, a, b, b, b, one, one, one]).astype(bf)

    c["kconst"] = kconst(pos)
    cend = np.maximum(16 * k128 + 15, 0)
    c["kconst_cmp"] = kconst(cend)
    slopes = np.exp2(-8.0 * np.arange(1, 17, dtype=np.float32) / 16).astype(np.float32).astype(np.float64)
    s1, s2, s3 = split3(slopes)
    NP = SEQ + 64
    pp = np.arange(NP, dtype=np.float64)
    v1, v2, v3 = split3(-slopes[:, None] * pp[None, :])
    qc = np.zeros((9, 16, NP), dtype=np.float64)
    for r, sv in enumerate((s1, s2, s3)):
        qc[r] = 64.0 * sv[:, None]
        qc[3 + r] = sv[:, None]
    qc[6], qc[7], qc[8] = v1, v2, v3
    c["qconst"] = qc.astype(bf)
    c["kconst_new"] = kconst(SEQ + (k128 % 8))
    colt = np.arange(32) % 8
    kb, kt_ = k128 // 8, k128 % 8
    tn = np.where((kb[:, None, None] == np.arange(16)[None, :, None]) & (kt_[:, None, None] <= colt[None, None, :]), 0.0, NEG)
    c["TnewS"] = tn.astype(bf)
    c["TlS"] = np.where(k128[:, None] < colt[None, :], NEG, 0.0).astype(bf)
    c["TcS"] = np.where(k128[:, None] >= 1, 0.0, NEG).astype(bf)
    fbs = np.zeros((8, 33), dtype=np.float32)
    fbs[:, [0, 31, 32]] = 1e4
    c["fbS"] = fbs
    c["ovl33"] = np.concatenate([c["ovl"], np.zeros((128, 1), np.float32)], axis=1)
    c["pidx"] = np.arange(128, dtype=np.float32).reshape(128, 1)
    return c


CONST = _consts()
CONST_IN = ["ident_f", "ones_b", "qdec", "kdec", "kwdec", "cmask", "qdecS", "kdecS", "kwdecS", "cmaskS", "rowmask",
            "ident_b", "Tc", "Tl", "ovl"]
CONST_DRAM = ["Tcmp", "Emat", "fb", "kconst", "kconst_cmp", "qconst", "kconst_new", "TnewS", "TlS", "TcS", "fbS", "ovl33", "pidx"]
DEV = {"cores": None, "prompt": True, "sample": True}


def build_nc():
    nc = bass.Bass("TRN2", target_bir_lowering=False)
    es = ExitStack()
    with es:
        _build(nc, es)
    return nc


def _build(nc, es):
    def din(name, shape, dt=F32):
        return nc.dram_tensor(name, list(shape), dt, kind="ExternalInput").ap()

    def dout(name, shape, dt=F32):
        return nc.dram_tensor(name, list(shape), dt, kind="ExternalOutput").ap()

    x_p = din("x_p", [SEQ, D])
    x_s = din("x_s", [NSEQ_S * DEC_SEQ, D])
    c_p = din("c_p", [1, D])
    c_s = din("c_s", [NSEQ_S, D])
    state_ret = din("state_ret", [NSEQ_S, H_A, DK_A, DV_A])
    state_conv = din("state_conv", [2, NSEQ_S * 2, F2])
    state_win = din("state_win", [NSEQ_S, 512, 512])
    cache_kv = din("cache_kv", [NPHYS * 128, 1024])
    page_table = din("page_table", [NSEQ_S, NPAGE], I32)
    w_ada = din("w_ada", [2, D, 6 * D])
    b_ada = din("b_ada", [2, 6 * D])
    g_mix = din("g_mix", [2, D])
    g_ffn = din("g_ffn", [2, D])
    w_ffn_in = din("w_ffn_in", [2, D, F2])
    conv_w = din("conv_w", [2, 3, F2])
    conv_b = din("conv_b", [2, F2])
    w_ffn_out = din("w_ffn_out", [2, D_FF, D])
    w_ret_in = din("w_ret_in", [1, D, 6144])
    w_ret_out = din("w_ret_out", [1, 2048, D])
    g_final = din("g_final", [D])
    g_kv = din("g_kv", [D])
    w_ada_kv = din("w_ada_kv", [D, 2 * D])
    b_ada_kv = din("b_ada_kv", [2 * D])
    w_kv = din("w_kv", [D, 1536])
    w_nsa_in = din("w_nsa_in", [1, D, 1072])
    w_nsa_out = din("w_nsa_out", [1, D, D])
    pe_ck = din("pe_ck", [32, 64])
    pe_cv = din("pe_cv", [32, 64])
    w_ck1 = din("w_ck1", [2048, 128])
    w_ck2 = din("w_ck2", [128, 64])
    w_cv1 = din("w_cv1", [2048, 128])
    w_cv2 = din("w_cv2", [128, 64])
    cin = {}
    for k in CONST_IN + CONST_DRAM:
        a = CONST[k]
        cin[k] = din("k_" + k, a.shape, BF16 if a.dtype == ml_dtypes.bfloat16 else F32)

    y_p = dout("y_p", [SEQ, D])
    y_s = dout("y_s", [NSEQ_S * DEC_SEQ, D])
    ret_p = dout("ret_p", [H_A, DK_A, DV_A])
    ret_s = dout("ret_s", [NSEQ_S, H_A, DK_A, DV_A])
    conv_p = dout("conv_p", [2, 2, F2])
    conv_s = dout("conv_s", [2, NSEQ_S, 2, F2])
    kv_p = dout("kv_p", [SEQ, 1024])
    kv_s = dout("kv_s", [NSEQ_S * DEC_SEQ, 1024])
    win_p = dout("win_p", [512, 512])
    win_s = dout("win_s", [NSEQ_S, 512, 512])

    S = Sched(nc, es)
    pe, act, dve, pool, sp = S.pe, S.act, S.dve, S.pool, S.sp
    uid = [0]

    def sb(name, shape, dt=F32, stack=es):
        uid[0] += 1
        t = stack.enter_context(nc.sbuf_tensor(f"{name}_{uid[0]}", list(shape), dt))
        return t, Buf(name)

    banks = []
    for i in range(8):
        t = es.enter_context(nc.psum_tensor(f"bank{i}", [128, 512], F32))
        banks.append((t, Buf(f"bank{i}")))
    bank_rr = [0]

    NROT = [7]

    def next_bank():
        b = banks[bank_rr[0] % NROT[0]]
        bank_rr[0] += 1
        return b

    RBANK = banks[7]
    RBANKS = banks[4:8]

    ctab = {}
    for k in CONST_IN:
        a = CONST[k]
        t, b = sb(k, a.shape, BF16 if a.dtype == ml_dtypes.bfloat16 else F32)
        S.dma(sp, t[:], cin[k], writes=[b])
        ctab[k] = (t, b)
    ident_f, b_ident = ctab["ident_f"]
    ones_b, b_ones = ctab["ones_b"]
    b_ctab = [ctab[k][1] for k in CONST_IN]

    xT, b_xT = sb("xT", [128, KC, ST])
    hT, b_hT = sb("hT", [128, KC, ST], BF16)
    WB = 8192
    NWB = 3
    wbufs = [sb(f"wbuf{i}", [128, WB], BF16) for i in range(NWB)]
    wb_rr = [0]

    def next_wbuf():
        w = wbufs[wb_rr[0] % NWB]
        wb_rr[0] += 1
        return w

    NM = 1 + NSEQ_S
    modall, b_mod = sb("modall", [128, 2, 6 * KC, NM])
    gmul, b_gmul = sb("gmul", [128, 2, 2, KC, NM])
    gvec, b_gvec = sb("gvec", [128, 6, KC])
    modkv, b_modkv = sb("modkv", [128, 2 * KC, NM])
    gmkv, b_gmkv = sb("gmkv", [128, KC, NM])
    gfin, b_gfin = sb("gfin", [128, KC, NM])
    zero17, b_zero17 = sb("zero17", [128, KC, NM])
    cwt, b_cwt = sb("cwt", [128, 2, 3, NFC])
    cbt, b_cbt = sb("cbt", [128, 2, NFC])
    uhalo, b_uhalo = sb("uhalo", [128, 2, NFC, 2])
    hprev, b_hprev = sb("hprev", [128, 2, KC, 2], BF16)
    S.op(dve, lambda e: e.memset(hprev[:], 0.0), writes=[b_hprev])
    b_Sst, b_Sbf, b_cbufT = Buf("Sst"), Buf("Sbf"), Buf("cbufT")
    P = {}
    b_modall = [b_mod, b_gmul, b_gvec, b_modkv, b_gmkv, b_gfin, b_zero17]

    for l in range(2):
        S.dma(sp, cwt[:, l], conv_w[l].rearrange("t (c p) -> p t c", p=128), writes=[b_cwt],
              allow_slow_non_contiguous=True)
        S.dma(sp, cbt[:, l], conv_b[l].rearrange("(c p) -> p c", p=128), writes=[b_cbt],
              allow_slow_non_contiguous=True)
    for i, src in enumerate([g_mix[0], g_mix[1], g_ffn[0], g_ffn[1], g_final, g_kv]):
        S.dma(sp, gvec[:, i], src.rearrange("(c p) -> p c", p=128), writes=[b_gvec],
              allow_slow_non_contiguous=True)
    S.op(dve, lambda e: e.memset(uhalo[:], 0.0), writes=[b_uhalo])
    S.op(dve, lambda e: e.memset(zero17[:], 0.0), writes=[b_zero17])

    def bc(ap2, n):
        return ap2.unsqueeze(2).broadcast_to([128, ap2.shape[1], n])

    with ExitStack() as ph:
        cT, b_cT = sb("cT", [128, KC, NM], F32, ph)
        cTb, b_cTb = sb("cTb", [128, KC, NM], BF16, ph)
        badT, b_badT = sb("badT", [128, 2, 6 * KC], F32, ph)
        bkvT, b_bkvT = sb("bkvT", [128, 2 * KC], F32, ph)
        S.dma(sp, cT[:, :, 0], c_p[0].rearrange("(c p) -> p c", p=128), writes=[b_cT], allow_slow_non_contiguous=True)
        for sq_ in range(NSEQ_S):
            S.dma(sp, cT[:, :, 1 + sq_], c_s[sq_].rearrange("(c p) -> p c", p=128), writes=[b_cT], allow_slow_non_contiguous=True)
        for l in range(2):
            S.dma(sp, badT[:, l], b_ada[l].rearrange("(c p) -> p c", p=128), writes=[b_badT],
                  allow_slow_non_contiguous=True)
        S.dma(sp, bkvT[:], b_ada_kv.rearrange("(c p) -> p c", p=128), writes=[b_bkvT], allow_slow_non_contiguous=True)
        S.op(act, lambda e: e.activation(cTb[:], cT[:], AF.Silu), reads=[b_cT], writes=[b_cTb])

        def mod_block(wsrc, dst3, bias2):
            wt, b_w = next_wbuf()
            wv = wt[:, 0:KC * 1024].rearrange("p (k n) -> p k n", k=KC)
            S.dma(pool, wv, wsrc.rearrange("(k p) n -> p k n", p=128), writes=[b_w])
            pt, b_p = next_bank()
            for oc in range(8):
                for k in range(KC):
                    S.op(pe, lambda e, oc=oc, k=k: e.matmul(pt[:, oc * NM:(oc + 1) * NM], wv[:, k, oc * 128:(oc + 1) * 128],
                                                          cTb[:, k, :], start=(k == 0), stop=(k == KC - 1)),
                         reads=[b_w, b_cTb], writes=[b_p], inc=(oc == 7 and k == KC - 1))
            S.op(dve, lambda e: e.tensor_tensor(dst3, pt[:, 0:KC * NM].rearrange("p (k n) -> p k n", k=KC),
                                                bc(bias2, NM), op=ALU.add),
                 reads=[b_p, b_badT, b_bkvT], writes=b_modall)

        for l in range(2):
            for blk in range(6):
                mod_block(w_ada[l, :, blk * 1024:(blk + 1) * 1024], modall[:, l, blk * KC:(blk + 1) * KC, :],
                          badT[:, l, blk * KC:(blk + 1) * KC])
        for blk in range(2):
            mod_block(w_ada_kv[:, blk * 1024:(blk + 1) * 1024], modkv[:, blk * KC:(blk + 1) * KC, :],
                      bkvT[:, blk * KC:(blk + 1) * KC])
        for l in range(2):
            for sub in range(2):
                gi = l if sub == 0 else 2 + l
                sc = modall[:, l, (1 + 3 * sub) * KC:(2 + 3 * sub) * KC, :]
                S.op(dve, lambda e, l=l, sub=sub, gi=gi, sc=sc: e.scalar_tensor_tensor(
                    gmul[:, l, sub], sc, 1.0, bc(gvec[:, gi], NM), op0=ALU.add, op1=ALU.mult),
                    reads=b_modall, writes=b_modall)
        S.op(dve, lambda e: e.scalar_tensor_tensor(gmkv[:], modkv[:, KC:2 * KC, :], 1.0, bc(gvec[:, 5], NM),
                                                   op0=ALU.add, op1=ALU.mult), reads=b_modall, writes=b_modall)
        S.op(dve, lambda e: e.tensor_copy(gfin[:], bc(gvec[:, 4], NM)), reads=b_modall, writes=b_modall)
        S.barrier()

    def modv(l, which):
        return modall[:, l, which * KC:(which + 1) * KC, :]

    def v3(ap2):
        return ap2.rearrange("p (b t) -> p b t", t=DEC_SEQ)

    def load_xT(src_rows, ntok):
        with ExitStack() as ph:
            xin, b_xin = sb("xin", [128, ntok // 128, D], F32, ph)
            S.dma(sp, xin[:], src_rows.rearrange("(t p) d -> p t d", p=128), writes=[b_xin])
            for t in range(ntok // 128):
                for half in range(2):
                    pt, b_p = next_bank()
                    for j in range(4):
                        k = half * 4 + j
                        S.op(pe, lambda e, t=t, k=k, j=j: e.transpose(pt[:, j * 128:(j + 1) * 128],
                                                                    xin[:, t, k * 128:(k + 1) * 128], ident_f[:]),
                             reads=[b_xin, b_ident], writes=[b_p], inc=(j == 3))
                    S.op(act, lambda e, t=t, half=half: e.activation(
                        xT[:, half * 4:half * 4 + 4, t * 128:(t + 1) * 128],
                        pt[:].rearrange("p (k n) -> p k n", k=4), AF.Copy), reads=[b_p], writes=[b_xT])
            S.barrier()

    def store_yT(dst_rows, ntok, src, b_src):
        with ExitStack() as ph:
            yo, b_yo = sb("yo", [128, ntok // 128, D], F32, ph)
            for t in range(ntok // 128):
                for half in range(2):
                    pt, b_p = next_bank()
                    for j in range(4):
                        k = half * 4 + j
                        S.op(pe, lambda e, t=t, k=k, j=j: e.transpose(pt[:, j * 128:(j + 1) * 128],
                                                                    src[:, k, t * 128:(t + 1) * 128], ident_f[:]),
                             reads=[b_src, b_ident], writes=[b_p], inc=(j == 3))
                    S.op(act, lambda e, t=t, half=half: e.activation(yo[:, t, half * 512:(half + 1) * 512], pt[:], AF.Copy),
                         reads=[b_p], writes=[b_yo])
            S.dma(sp, dst_rows.rearrange("(t p) d -> p t d", p=128), yo[:], reads=[b_yo])
            S.barrier()

    def norm_mod(ntok, sample, gm3, sh3, dst, b_dst, ph):
        sq, b_sq = sb("nm_sq", [128, KC, ntok], BF16, ph)
        rstd, b_rstd = sb("nm_rstd", [128, ntok], F32, ph)
        tmp, b_tmp = sb("nm_tmp", [128, 2, ntok], F32, ph)
        S.op(act, lambda e: e.activation(sq[:], xT[:, :, 0:ntok], AF.Square), reads=[b_xT], writes=[b_sq])
        pt, b_p = next_bank()
        for k in range(KC):
            S.op(pe, lambda e, k=k: e.matmul(pt[:, 0:ntok], ones_b[:], sq[:, k, :], start=(k == 0), stop=(k == KC - 1)),
                 reads=[b_sq, b_ones], writes=[b_p], inc=(k == KC - 1))
        S.op(act, lambda e: e.activation(rstd[:], pt[:, 0:ntok], AF.Sqrt, bias=EPS, scale=1.0 / D),
             reads=[b_p], writes=[b_rstd])
        S.op(dve, lambda e: e.reciprocal(rstd[:], rstd[:]), reads=[b_rstd], writes=[b_rstd])
        tb = [Buf("nm_t0"), Buf("nm_t1")]
        for k in range(KC):
            S.op(dve, lambda e, k=k: e.tensor_tensor(tmp[:, k % 2], xT[:, k, 0:ntok], rstd[:], op=ALU.mult),
                 reads=[b_xT, b_rstd], writes=[tb[k % 2]])
            if not sample:
                S.op(act, lambda e, k=k: e.activation(dst[:, k, 0:ntok], tmp[:, k % 2], AF.Identity,
                                                    bias=sh3[:, k, 0:1], scale=gm3[:, k, 0:1]),
                     reads=[tb[k % 2]] + b_modall, writes=[b_dst])
            else:
                S.op(dve, lambda e, k=k: e.tensor_tensor(v3(tmp[:, k % 2]), v3(tmp[:, k % 2]), bc(gm3[:, k, 1:NM], DEC_SEQ), op=ALU.mult),
                     reads=[tb[k % 2]] + b_modall, writes=[tb[k % 2]])
                S.op(dve, lambda e, k=k: e.tensor_tensor(v3(dst[:, k, 0:ntok]), v3(tmp[:, k % 2]), bc(sh3[:, k, 1:NM], DEC_SEQ), op=ALU.add),
                     reads=[tb[k % 2]] + b_modall, writes=[b_dst])

    rtmp, b_rtmp = sb("rtmp", [128, 128])

    def resid_add(oc, pt, ntok, sample, ga3, b_p):
        if not sample:
            S.op(dve, lambda e: e.scalar_tensor_tensor(xT[:, oc, 0:ntok], pt[:, 0:ntok], ga3[:, oc, 0:1], xT[:, oc, 0:ntok],
                                                       op0=ALU.mult, op1=ALU.add),
                 reads=[b_p, b_xT] + b_modall, writes=[b_xT])
        else:
            S.op(dve, lambda e: e.tensor_tensor(v3(rtmp[:]), v3(pt[:, 0:ntok]), bc(ga3[:, oc, 1:NM], DEC_SEQ), op=ALU.mult),
                 reads=[b_p] + b_modall, writes=[b_rtmp])
            S.op(dve, lambda e: e.tensor_tensor(xT[:, oc, 0:ntok], xT[:, oc, 0:ntok], rtmp[:], op=ALU.add),
                 reads=[b_rtmp, b_xT], writes=[b_xT])

    WSRC = {"w_ret_in": w_ret_in[0], "w_ret_out": w_ret_out[0], "w_ffn_in0": w_ffn_in[0], "w_ffn_in1": w_ffn_in[1],
            "w_ffn_out0": w_ffn_out[0], "w_ffn_out1": w_ffn_out[1], "w_kv": w_kv, "w_nsa_in": w_nsa_in[0],
            "w_nsa_out": w_nsa_out[0]}
    WBF, b_WBF = {}, {}
    for nm, src in WSRC.items():
        WBF[nm] = nc.dram_tensor("bf_" + nm, list(src.shape), BF16, kind="Internal").ap()
        b_WBF[nm] = Buf("bf_" + nm)
    converted = set()

    def convert_w(nm):
        src = WSRC[nm]
        K_, N_ = src.shape
        for r0 in range(0, K_, 128):
            t_, b_t = next_wbuf()
            S.dma(pool, t_[:, 0:N_], src[r0:r0 + 128, :], writes=[b_t])
            S.dma(sp, WBF[nm][r0:r0 + 128, :], t_[:, 0:N_], reads=[b_t], writes=[b_WBF[nm]], persistent=True)
        converted.add(nm)

    def load_w(wname, col_ranges, nk, rows_per_k=128):
        if wname not in converted:
            convert_w(wname)
        wt, b_w = next_wbuf()
        tot = sum(n for _, n in col_ranges)
        assert nk * tot <= WB
        wv = wt[0:rows_per_k, 0:nk * tot].rearrange("p (k n) -> p k n", k=nk)
        o = 0
        for c0, n in col_ranges:
            S.dma(sp, wv[:, :, o:o + n], WBF[wname][:, c0:c0 + n].rearrange("(k p) n -> p k n", p=rows_per_k),
                  reads=[b_WBF[wname]], writes=[b_w], persistent=True)
            o += n
        return wv, b_w

    def proj_fm(wv, b_w, col, src, b_src, ntok, nk):
        pt, b_p = next_bank()
        for k in range(nk):
            S.op(pe, lambda e, k=k: e.matmul(pt[:, 0:ntok], wv[:, k, col:col + 128], src[:, k, 0:ntok],
                                            start=(k == 0), stop=(k == nk - 1)),
                 reads=[b_w, b_src], writes=[b_p], inc=(k == nk - 1))
        return pt, b_p

    def proj_tm(wv, b_w, col, ncol, src, b_src, t, nk):
        pt, b_p = next_bank()
        for k in range(nk):
            S.op(pe, lambda e, k=k: e.matmul(pt[:, 0:ncol], src[:, k, t * 128:(t + 1) * 128], wv[:, k, col:col + ncol],
                                            start=(k == 0), stop=(k == nk - 1)),
                 reads=[b_w, b_src], writes=[b_p], inc=(k == nk - 1))
        return pt, b_p

    def retention(st, sample):
        ntok = 128 if sample else ST
        nt = ntok // 128
        sfx = "S" if sample else ""
        qdec, kdec, kwdec, cmask = (ctab[k + sfx][0] for k in ("qdec", "kdec", "kwdec", "cmask"))
        rowmask = ctab["rowmask"][0]
        NROT[0] = 4 if sample else 7
        with ExitStack() as ph:
            norm_mod(ntok, sample, gmul[:, 0, 0], modv(0, 0), hT, b_hT, ph)
            yin, b_yin = sb("yin", [128, 16, ntok], BF16, ph)
            qT, b_qT = sb("qT", [128, 2, ntok], BF16, ph)
            kT, b_kT = sb("kT", [128, 2, ntok], BF16, ph)
            gT, b_gT = sb("gT", [128, 4, ntok], BF16, ph)
            vtk, b_vtk = sb("vtk", [128, nt, DV_A], BF16, ph)
            kwt, b_kwt = sb("kwt", [128, nt, DK_A], BF16, ph)
            scT, b_scT = sb("scT", [128, 128], BF16, ph)
            oT, b_oT = sb("oT", [128, 4, ntok], F32, ph)
            osq, b_osq = sb("osq", [128, 4, ntok], BF16, ph)
            orstd, b_orstd = sb("orstd", [128, ntok], F32, ph)
            otmp, b_otmp = sb("otmp", [128, ntok], F32, ph)
            if sample:
                s0 = [sb(f"s0_{i}", [128, 2, DV_A], F32, ph) for i in range(3)]
                s0b = [sb(f"s0b_{i}", [128, 2, DV_A], BF16, ph) for i in range(3)]
                sn = [sb(f"sn_{i}", [128, 2, DV_A], F32, ph) for i in range(3)]
                kwm = [sb(f"kwm_{i}", [128, DK_A], BF16, ph) for i in range(3)]
            W = "w_ret_in"
            for h in range(H_A):
                wv, b_w = load_w(W, [(h * 256, 256), (1024 + h * 256, 256)], KC)
                for which, dstT, b_d, dec in ((0, qT, b_qT, qdec), (1, kT, b_kT, kdec)):
                    for c in range(2):
                        pt, b_p = proj_fm(wv, b_w, which * 256 + c * 128, hT, b_hT, ntok, KC)
                        S.op(dve, lambda e, c=c, dstT=dstT, dec=dec, pt=pt: e.tensor_tensor(
                            dstT[:, c, :].rearrange("p (t n) -> p t n", n=128), pt[:, 0:ntok].rearrange("p (t n) -> p t n", n=128),
                            dec[:, h, :].unsqueeze(1).broadcast_to([128, nt, 128]), op=ALU.mult),
                             reads=[b_p] + b_ctab, writes=[b_d])
                for t in range(nt):
                    pt2, b_p2 = proj_tm(wv, b_w, 256, 256, hT, b_hT, t, KC)
                    S.op(dve, lambda e, t=t, pt2=pt2: e.tensor_scalar(kwt[:, t, :], pt2[:, 0:256], kwdec[:, h:h + 1], None, op0=ALU.mult),
                         reads=[b_p2] + b_ctab, writes=[b_kwt])
                wv, b_w = load_w(W, [(4096 + h * 512, 512)], KC)
                for c in range(4):
                    pt, b_p = proj_fm(wv, b_w, c * 128, hT, b_hT, ntok, KC)
                    S.op(act, lambda e, c=c, pt=pt: e.activation(gT[:, c, :], pt[:, 0:ntok], AF.Silu), reads=[b_p], writes=[b_gT])
                wv, b_w = load_w(W, [(2048 + h * 512, 512)], KC)
                for t in range(nt):
                    pt, b_p = proj_tm(wv, b_w, 0, 512, hT, b_hT, t, KC)
                    S.op(act, lambda e, t=t, pt=pt: e.activation(vtk[:, t, :], pt[:], AF.Copy), reads=[b_p], writes=[b_vtk])
                for t in range(nt):
                    first = (not sample) and (st == 0 and t == 0)
                    ts = slice(t * 128, (t + 1) * 128)
                    pt, b_p = next_bank()
                    for c in range(2):
                        S.op(pe, lambda e, c=c: e.matmul(pt[:, 0:128], kT[:, c, ts], qT[:, c, ts], start=(c == 0), stop=(c == 1)),
                             reads=[b_kT, b_qT], writes=[b_p], inc=(c == 1))
                    S.op(dve, lambda e: e.tensor_tensor(scT[:], pt[:, 0:128], cmask[:], op=ALU.mult),
                         reads=[b_p] + b_ctab, writes=[b_scT])
                    po, b_po = RBANK
                    if not sample:
                        for vc in range(4):
                            vs = slice(vc * 128, (vc + 1) * 128)
                            S.op(pe, lambda e, vc=vc, vs=vs: e.matmul(po[:, vs], vtk[:, t, vs], scT[:], start=True, stop=first),
                                 reads=[b_vtk, b_scT], writes=[b_po], inc=(first and vc == 3))
                            if not first:
                                for c in range(2):
                                    S.op(pe, lambda e, vc=vc, vs=vs, c=c: e.matmul(po[:, vs], P['Sbf'][:, h, c, vs], qT[:, c, ts],
                                                                               start=False, stop=(c == 1)),
                                         reads=[b_Sbf, b_qT], writes=[b_po], inc=(vc == 3 and c == 1))
                    else:
                        for vc in range(4):
                            vs = slice(vc * 128, (vc + 1) * 128)
                            S.op(pe, lambda e, vc=vc, vs=vs: e.matmul(RBANKS[vc][0][:, 0:128], vtk[:, t, vs], scT[:], start=True, stop=False),
                                 reads=[b_vtk, b_scT], writes=[RBANKS[vc][1]], inc=False)
                        def s0_load(bb):
                            t_, bt_ = s0[(h * NSEQ_S + bb) % 3]
                            S.dma(sp, t_[:], state_ret[bb, h].rearrange("(c p) v -> p c v", p=128), writes=[bt_])
                        s0_load(0)
                        s0_load(1)
                        for b in range(NSEQ_S):
                            i3 = (h * NSEQ_S + b) % 3
                            (s0t, b_s0), (s0bt, b_s0b), (snt, b_sn), (kwmt, b_kwm) = s0[i3], s0b[i3], sn[i3], kwm[i3]
                            if b + 2 < NSEQ_S:
                                s0_load(b + 2)
                            S.op(act, lambda e, s0t=s0t, s0bt=s0bt: e.activation(s0bt[:], s0t[:], AF.Copy, scale=float(CONST["gam"][h])),
                                 reads=[b_s0], writes=[b_s0b])
                            for vc in range(4):
                                for c in range(2):
                                    lastmm = (b == NSEQ_S - 1 and vc == 3 and c == 1)
                                    S.op(pe, lambda e, vc=vc, c=c, b=b, s0bt=s0bt: e.matmul(
                                        RBANKS[vc][0][:, b * 8: b * 8 + 8], s0bt[:, c, vc * 128:(vc + 1) * 128],
                                        qT[:, c, b * 8:(b + 1) * 8], start=False, stop=(b == NSEQ_S - 1 and c == 1)),
                                        reads=[b_s0b, b_qT], writes=[RBANKS[vc][1]], inc=(lastmm or (vc == 3 and c == 1)))
                            S.op(dve, lambda e, b=b, kwmt=kwmt: e.tensor_scalar(kwmt[:], kwt[:, 0, :], rowmask[:, b:b + 1], None, op0=ALU.mult),
                                 reads=[b_kwt] + b_ctab, writes=[b_kwm])
                            for c in range(2):
                                ps_, b_ps = next_bank()
                                S.op(pe, lambda e, c=c, kwmt=kwmt, ps_=ps_: e.matmul(ps_[:], kwmt[:, c * 128:(c + 1) * 128], vtk[:, 0, :], start=True, stop=True),
                                     reads=[b_kwm, b_vtk], writes=[b_ps])
                                S.op(dve, lambda e, c=c, s0t=s0t, snt=snt, ps_=ps_: e.scalar_tensor_tensor(
                                    snt[:, c, :], s0t[:, c, :], float(CONST["gam8"][h]), ps_[:], op0=ALU.mult, op1=ALU.add),
                                    reads=[b_ps, b_s0], writes=[b_sn])
                            S.dma(sp, ret_s[b, h].rearrange("(c p) v -> p c v", p=128), snt[:], reads=[b_sn])
                    if sample:
                        for vc in range(4):
                            S.op(act, lambda e, vc=vc: e.activation(oT[:, vc, ts], RBANKS[vc][0][:, 0:128], AF.Copy),
                                 reads=[RBANKS[vc][1]], writes=[b_oT])
                    else:
                        S.op(act, lambda e: e.activation(oT[:, :, ts], po[:].rearrange("p (v n) -> p v n", v=4), AF.Copy),
                             reads=[b_po], writes=[b_oT])
                    if not sample:
                        for c in range(2):
                            ps_, b_ps = next_bank()
                            S.op(pe, lambda e, c=c, ps_=ps_: e.matmul(ps_[:], kwt[:, t, c * 128:(c + 1) * 128], vtk[:, t, :], start=True, stop=True),
                                 reads=[b_kwt, b_vtk], writes=[b_ps])
                            S.op(dve, lambda e, c=c, ps_=ps_: e.scalar_tensor_tensor(P['Sst'][:, h, c, :], P['Sst'][:, h, c, :], float(CONST["gam128"][h]),
                                                                                  ps_[:], op0=ALU.mult, op1=ALU.add),
                                 reads=[b_ps, b_Sst], writes=[b_Sst])
                        S.op(act, lambda e: e.activation(P['Sbf'][:, h], P['Sst'][:, h], AF.Copy, scale=float(CONST["gam"][h])),
                             reads=[b_Sst], writes=[b_Sbf])
                S.op(act, lambda e: e.activation(osq[:], oT[:], AF.Square), reads=[b_oT], writes=[b_osq])
                pt, b_p = next_bank()
                for vc in range(4):
                    S.op(pe, lambda e, vc=vc: e.matmul(pt[:, 0:ntok], ones_b[:], osq[:, vc, :], start=(vc == 0), stop=(vc == 3)),
                         reads=[b_osq, b_ones], writes=[b_p], inc=(vc == 3))
                S.op(act, lambda e: e.activation(orstd[:], pt[:, 0:ntok], AF.Sqrt, bias=EPS, scale=1.0 / DV_A), reads=[b_p], writes=[b_orstd])
                S.op(dve, lambda e: e.reciprocal(orstd[:], orstd[:]), reads=[b_orstd], writes=[b_orstd])
                for vc in range(4):
                    S.op(dve, lambda e, vc=vc: e.tensor_tensor(otmp[:], oT[:, vc, :], orstd[:], op=ALU.mult),
                         reads=[b_oT, b_orstd], writes=[b_otmp])
                    S.op(dve, lambda e, vc=vc: e.tensor_tensor(yin[:, h * 4 + vc, :], otmp[:], gT[:, vc, :], op=ALU.mult),
                         reads=[b_otmp, b_gT], writes=[b_yin])
            for half in range(2):
                wv, b_w = load_w("w_ret_out", [(half * 512, 512)], 16)
                for oc4 in range(4):
                    oc = half * 4 + oc4
                    pt, b_p = proj_fm(wv, b_w, oc4 * 128, yin, b_yin, ntok, 16)
                    resid_add(oc, pt, ntok, sample, modv(0, 2), b_p)
            S.barrier()


    def load_conv_bufs():
        with ExitStack() as ph:
            rows, b_rows = sb("scrows", [2 * NSEQ_S, F2], F32, ph)
            for l in range(2):
                S.dma(sp, rows[:], state_conv[l], writes=[b_rows])
                for g0 in range(0, NFC, 16):
                    n = min(16, NFC - g0)
                    pt, b_p = next_bank()
                    for j in range(n):
                        ch = g0 + j
                        S.op(pe, lambda e, j=j, ch=ch: e.transpose(pt[:, j * 32:(j + 1) * 32], rows[:, ch * 128:(ch + 1) * 128],
                                                                 ident_f[0:32, 0:32]),
                             reads=[b_rows, b_ident], writes=[b_p], inc=(j == n - 1))
                    S.op(act, lambda e, l=l, g0=g0, n=n: e.activation(P['cbufT'][:, l, g0:g0 + n, :],
                                                                    pt[:, 0:n * 32].rearrange("p (c m) -> p c m", c=n), AF.Copy),
                         reads=[b_p], writes=[b_cbufT])
            S.barrier()

    def ffn(l, st, last, sample):
        ntok = 128 if sample else ST
        NROT[0] = 7
        with ExitStack() as ph:
            if sample:
                norm_mod(ntok, sample, gmul[:, l, 1], modv(l, 3), hT, b_hT, ph)
                hsrc, b_hsrc, nsrc = hT, b_hT, ntok
            else:
                hE, b_hE = sb("hE", [128, KC, ST + 2], BF16, ph)
                S.op(dve, lambda e: e.tensor_copy(hE[:, :, 0:2], hprev[:, l]), reads=[b_hprev], writes=[b_hE])
                norm_mod(ntok, sample, gmul[:, l, 1], modv(l, 3), hE[:, :, 2:ST + 2], b_hE, ph)
                S.op(dve, lambda e: e.tensor_copy(hprev[:, l], hE[:, :, ST:ST + 2]), reads=[b_hE], writes=[b_hprev])
                hsrc, b_hsrc, nsrc = hE, b_hE, ST + 2
            actT, b_actT = sb("actT", [128, NPAIR, ntok], BF16, ph)
            ncol = ntok + 2 if not sample else NSEQ_S * (DEC_SEQ + 2)
            ue = [sb(f"ue{i}", [128, ncol], F32, ph) for i in range(4)]
            zz = [sb(f"zz{i}", [128, ntok], F32, ph) for i in range(4)]
            sg, b_sg = sb("sg", [128, ntok], F32, ph)
            if sample:
                utok, b_utok = sb("utok", [128, F2], F32, ph)
            W = f"w_ffn_in{l}"
            for blk in range(NPAIR // 2):
                wv, b_w = load_w(W, [(blk * 256, 256), (D_FF + blk * 256, 256)], KC)
                if sample:
                    for ag in range(2):
                        pt, b_p = proj_tm(wv, b_w, ag * 256, 256, hT, b_hT, 0, KC)
                        c0 = ag * D_FF + blk * 256
                        S.op(act, lambda e, pt=pt, c0=c0: e.activation(utok[:, c0:c0 + 256], pt[:, 0:256], AF.Copy),
                             reads=[b_p], writes=[b_utok])
                for pi in range(2):
                    pair = blk * 2 + pi
                    zs = []
                    for ag in range(2):
                        chunk = pair + ag * NPAIR
                        pt, b_p = proj_fm(wv, b_w, ag * 256 + pi * 128, hsrc, b_hsrc, nsrc, KC)
                        (u, b_u) = ue[(pair * 2 + ag) % 4]
                        (z, b_z) = zz[(pair * 2 + ag) % 4]
                        if not sample:
                            S.op(act, lambda e, u=u, pt=pt: e.activation(u[:, 0:ntok + 2], pt[:, 0:ntok + 2], AF.Copy), reads=[b_p], writes=[b_u])
                            if last:
                                S.op(act, lambda e, u=u, chunk=chunk: e.activation(uhalo[:, l, chunk, :], u[:, ntok:ntok + 2], AF.Copy),
                                     reads=[b_u], writes=[b_uhalo])
                            u2, u1, u0, zv = u[:, 2:ntok + 2], u[:, 1:ntok + 1], u[:, 0:ntok], z[:]
                        else:
                            u3 = u[:].rearrange("p (b t) -> p b t", t=DEC_SEQ + 2)
                            S.op(pool, lambda e, u3=u3, chunk=chunk: e.tensor_copy(
                                u3[:, :, 0:2], P['cbufT'][:, l, chunk, :].rearrange("p (b j) -> p b j", j=2)),
                                reads=[b_cbufT], writes=[b_u])
                            S.op(act, lambda e, u3=u3, pt=pt: e.activation(u3[:, :, 2:DEC_SEQ + 2], v3(pt[:, 0:ntok]), AF.Copy),
                                 reads=[b_p], writes=[b_u])
                            u2, u1, u0, zv = u3[:, :, 2:DEC_SEQ + 2], u3[:, :, 1:DEC_SEQ + 1], u3[:, :, 0:DEC_SEQ], v3(z[:])
                        if not sample:
                            S.op(act, lambda e, zv=zv, pt=pt, chunk=chunk: e.activation(zv, pt[:, 2:ntok + 2], AF.Identity,
                                                                                     bias=cbt[:, l, chunk:chunk + 1], scale=cwt[:, l, 2, chunk:chunk + 1]),
                                 reads=[b_p, b_cwt, b_cbt], writes=[b_z])
                        else:
                            S.op(dve, lambda e, u2=u2, zv=zv, chunk=chunk: e.tensor_scalar(
                                zv, u2, cwt[:, l, 2, chunk:chunk + 1], cbt[:, l, chunk:chunk + 1],
                                op0=ALU.mult, op1=ALU.add), reads=[b_u, b_cwt, b_cbt], writes=[b_z])
                        S.op(dve, lambda e, u1=u1, zv=zv, chunk=chunk: e.scalar_tensor_tensor(
                            zv, u1, cwt[:, l, 1, chunk:chunk + 1], zv, op0=ALU.mult, op1=ALU.add),
                            reads=[b_u, b_cwt, b_z], writes=[b_z])
                        S.op(dve, lambda e, u0=u0, zv=zv, chunk=chunk: e.scalar_tensor_tensor(
                            zv, u0, cwt[:, l, 0, chunk:chunk + 1], zv, op0=ALU.mult, op1=ALU.add),
                            reads=[b_u, b_cwt, b_z], writes=[b_z])
                        zs.append((z, b_z))
                    S.op(act, lambda e, z=zs[1][0]: e.activation(sg[:], z[:], AF.Silu), reads=[zs[1][1]], writes=[b_sg])
                    S.op(dve, lambda e, z=zs[0][0], pair=pair: e.tensor_tensor(actT[:, pair, :], z[:], sg[:], op=ALU.mult),
                         reads=[zs[0][1], b_sg], writes=[b_actT])
            if sample:
                for tt in range(2):
                    S.dma(sp, conv_s[l, :, tt, :], utok[6 + tt:128:8, :], reads=[b_utok])
            elif last:
                for tt in range(2):
                    S.dma(sp, conv_p[l, tt].rearrange("(c p) -> p c", p=128), uhalo[:, l, :, tt], reads=[b_uhalo],
                          allow_slow_non_contiguous=True)
            for qt in range(4):
                wv, b_w = load_w(f"w_ffn_out{l}", [(qt * 256, 256)], NPAIR)
                for oc2 in range(2):
                    oc = qt * 2 + oc2
                    pt, b_p = proj_fm(wv, b_w, oc2 * 128, actT, b_actT, ntok, NPAIR)
                    resid_add(oc, pt, ntok, sample, modv(l, 5), b_p)
            S.barrier()

    NS = {}
    JT = ST // 128
    b_K = {k: Buf(k) for k in ("Kslc", "Kwin", "Kcmp", "Vslc", "Vwin", "Vcmp", "kraw", "vraw", "szv", "cmpw", "ntab",
                               "KslcN", "KwinN", "VslcN", "VwinN")}

    def kv_proj(st, sample):
        ntok = 128 if sample else ST
        NROT[0] = 7
        q0 = st * ST
        with ExitStack() as ph:
            norm_mod(ntok, sample, gmkv, modkv[:, 0:KC, :], hT, b_hT, ph)
            kvo, b_kvo = sb("kvo", [128, ntok // 128, 1536], F32, ph)
            for cb in range(3):
                wv, b_w = load_w("w_kv", [(cb * 512, 512)], KC)
                for t in range(ntok // 128):
                    pt, b_p = proj_tm(wv, b_w, 0, 512, hT, b_hT, t, KC)
                    S.op(act, lambda e, t=t, cb=cb, pt=pt: e.activation(kvo[:, t, cb * 512:(cb + 1) * 512], pt[:], AF.Copy),
                         reads=[b_p], writes=[b_kvo])
                if not DEV.get("nsa", True):
                    continue
                if sample:
                    fm = {0: [], 1: [(0, "KslcN", 0)], 2: [(0, "KwinN", 0)]}[cb]
                else:
                    fm = {0: [(0, "kraw", 16), (256, "vraw", 16)], 1: [(0, "Kslc", q0)], 2: [(0, "Kwin", q0)]}[cb]
                for lc, name, c0 in fm:
                    for g in range(G_B):
                        pt, b_p = next_bank()
                        for k in range(KC):
                            S.op(pe, lambda e, k=k, g=g, lc=lc, pt=pt: e.matmul(pt[0:64, 0:ntok], wv[:, k, lc + g * 64: lc + (g + 1) * 64],
                                                                              hT[:, k, 0:ntok], start=(k == 0), stop=(k == KC - 1)),
                                 reads=[b_w, b_hT], writes=[b_p], inc=(k == KC - 1))
                        S.op(dve, lambda e, g=g, name=name, c0=c0, pt=pt: e.tensor_copy(NS[name][0:64, g, c0:c0 + ntok], pt[0:64, 0:ntok]),
                             reads=[b_p], writes=[b_K[name]])
                        if name in ("kraw", "vraw"):
                            S.op(act, lambda e, g=g, name=name, c0=c0, pt=pt: e.activation(NS[name][64:128, g, c0 - 1:c0 - 1 + ntok], pt[0:64, 0:ntok], AF.Copy),
                                 reads=[b_p], writes=[b_K[name]])
            if not sample:
                S.dma(sp, kv_p[q0:q0 + ST, :].rearrange("(t p) n -> p t n", p=128), kvo[:, :, 0:1024], reads=[b_kvo])
                if q0 >= SEQ - 512:
                    w0 = q0 - (SEQ - 512)
                    S.dma(sp, win_p[w0:w0 + ST, :].rearrange("(t p) n -> p t n", p=128), kvo[:, :, 1024:1536], reads=[b_kvo])
                if DEV.get("nsa", True):
                    for t in range(ntok // 128):
                        kt = q0 // 128 + t
                        S.op(dve, lambda e, t=t, kt=kt: e.tensor_copy(NS["Vslc"][:, kt], kvo[:, t, 768:1024].rearrange("p (g d) -> p g d", g=G_B)),
                             reads=[b_kvo], writes=[b_K["Vslc"]])
                        S.op(dve, lambda e, t=t, kt=kt: e.tensor_copy(NS["Vwin"][:, kt], kvo[:, t, 1280:1536].rearrange("p (g d) -> p g d", g=G_B)),
                             reads=[b_kvo], writes=[b_K["Vwin"]])
            else:
                if DEV.get("nsa", True):
                    S.op(dve, lambda e: e.tensor_copy(NS["VslcN"][:], kvo[:, 0, 768:1024].rearrange("p (g d) -> p g d", g=G_B)),
                         reads=[b_kvo], writes=[b_K["VslcN"]])
                    S.op(dve, lambda e: e.tensor_copy(NS["VwinN"][:], kvo[:, 0, 1280:1536].rearrange("p (g d) -> p g d", g=G_B)),
                         reads=[b_kvo], writes=[b_K["VwinN"]])
                S.dma(sp, kv_s, kvo[:, 0, 0:1024], reads=[b_kvo])
                for b in range(NSEQ_S):
                    S.dma(sp, win_s[b, 504:512, :], kvo[b * 8:(b + 1) * 8, 0, 1024:1536], reads=[b_kvo])
                    S.dma(sp, win_s[b, 0:504, :], state_win[b, 8:512, :])
            S.barrier()

    def compress_setup():
        with ExitStack() as ph:
            peT, b_peT = sb("peT", [128, 2, 16], F32, ph)
            peTb, b_peTb = sb("peTb", [128, 2, 16], BF16, ph)
            for i, src in enumerate((pe_ck, pe_cv)):
                S.dma(sp, peT[:, i, :], src.rearrange("(i t) d -> (t d) i", t=2), writes=[b_peT], allow_slow_non_contiguous=True)
            S.op(dve, lambda e: e.tensor_copy(peTb[:], peT[:]), reads=[b_peT], writes=[b_peTb])
            for i, (w1, w2) in enumerate(((w_ck1, w_ck2), (w_cv1, w_cv2))):
                S.dma(pool, NS["w2"][:, i, :], w2, writes=[b_K["cmpw"]])
                wt, b_w = next_wbuf()
                wv = wt[:, 0:16 * 128].rearrange("p (s n) -> p s n", s=16)
                S.dma(pool, wv, w1.rearrange("(i q) n -> q i n", q=128), writes=[b_w])
                pt, b_p = next_bank()
                for sp_ in range(16):
                    S.op(pe, lambda e, sp_=sp_, i=i, wv=wv, pt=pt: e.matmul(pt[:, 0:1], wv[:, sp_, :], peTb[:, i, sp_:sp_ + 1],
                                                                         start=(sp_ == 0), stop=(sp_ == 15)),
                         reads=[b_w, b_peTb], writes=[b_p], inc=(sp_ == 15))
                S.op(dve, lambda e, i=i, pt=pt: e.tensor_copy(NS["bz"][:, i:i + 1], pt[:, 0:1]), reads=[b_p], writes=[b_K["cmpw"]])
            S.barrier()

    def compress(st, ntok=ST, s0_=None, sz_pre=None):
        if s0_ is None:
            s0_ = (st * ST) // 16
        nsl = ntok // 16
        with ExitStack() as ph:
            if sz_pre is None:
                sz, b_sz = sb("sz", [128, 2, G_B * nsl], BF16, ph)
            else:
                sz, b_sz = sz_pre
            for i, (w1, raw) in enumerate(((w_ck1, "kraw"), (w_cv1, "vraw"))):
                if "w1s" in NS:
                    wv, b_w = NS["w1s"][:, i], b_K["cmpw"]
                else:
                    wt, b_w = next_wbuf()
                    wv = wt[:, 0:16 * 128].rearrange("p (s n) -> p s n", s=16)
                    S.dma(pool, wv, w1.rearrange("(i q) n -> q i n", q=128), writes=[b_w])
                pz, b_pz = next_bank()
                for sp_ in range(16):
                    rhs = NS[raw][:, :, 2 * sp_:2 * sp_ + 16 * (nsl - 1) + 1:16]
                    S.op(pe, lambda e, sp_=sp_, wv=wv, rhs=rhs, pz=pz: e.matmul(pz[:, 0:G_B * nsl].rearrange("p (g m) -> p g m", g=G_B),
                                                                             wv[:, sp_, :], rhs, start=(sp_ == 0), stop=(sp_ == 15)),
                         reads=[b_w, b_K[raw]], writes=[b_pz], inc=(sp_ == 15))
                S.op(act, lambda e, i=i, pz=pz: e.activation(sz[:, i, :], pz[:, 0:G_B * nsl], AF.Silu, bias=NS["bz"][:, i:i + 1]),
                     reads=[b_pz, b_K["cmpw"]], writes=[b_sz])
                S.op(dve, lambda e, raw=raw: e.tensor_copy(NS[raw][0:64, :, 0:16], NS[raw][0:64, :, ntok:ntok + 16]),
                     reads=[b_K[raw]], writes=[b_K[raw]])
                S.op(dve, lambda e, raw=raw: e.tensor_copy(NS[raw][64:128, :, 0:15], NS[raw][64:128, :, ntok:ntok + 15]),
                     reads=[b_K[raw]], writes=[b_K[raw]])
            pk, b_pk = next_bank()
            S.op(pe, lambda e: e.matmul(pk[0:64, 0:G_B * nsl], NS["w2"][:, 0, :], sz[:, 0, :], start=True, stop=True),
                 reads=[b_sz, b_K["cmpw"]], writes=[b_pk])
            S.op(act, lambda e: e.activation(NS["Kcmp"][0:64, :, s0_:s0_ + nsl], pk[0:64, 0:G_B * nsl].rearrange("p (g m) -> p g m", g=G_B), AF.Copy),
                 reads=[b_pk], writes=[b_K["Kcmp"]])
            S.op(dve, lambda e: e.tensor_copy(NS["szv"][:, :, s0_:s0_ + nsl], sz[:, 1, :].rearrange("p (g m) -> p g m", g=G_B)),
                 reads=[b_sz], writes=[b_K["szv"]])
            for g in range(G_B):
                pv_, b_pv = next_bank()
                S.op(pe, lambda e, g=g, pv_=pv_: e.matmul(pv_[:, 0:64], NS["szv"][:, g, :], NS["w2"][:, 1, :], start=True, stop=True),
                     reads=[b_K["szv"], b_K["cmpw"]], writes=[b_pv])
                S.op(act, lambda e, g=g, pv_=pv_: e.activation(NS["Vcmp"][:, g, :], pv_[:, 0:64], AF.Copy), reads=[b_pv], writes=[b_K["Vcmp"]])
            if sz_pre is None:
                S.barrier()

    def nsa_prompt(st):
        q0 = st * ST
        NROT[0] = 4
        Tc, Tl, ovl, ident_b = (ctab[k][0] for k in ("Tc", "Tl", "ovl", "ident_b"))
        with ExitStack() as ph:
            norm_mod(ST, False, gmul[:, 1, 0], modv(1, 0), hT, b_hT, ph)
            S.barrier()
        with ExitStack() as ph:
            Qaug, b_Q = sb("Qaug", [128, 16, ST], BF16, ph)
            oTn, b_oTn = sb("oTn", [64, 16, ST], BF16, ph)
            negT, b_negT = sb("negT", [32, ST], BF16, ph)
            PTs = [sb(f"PT{i}", [128, ST], BF16, ph) for i in range(4)]
            pt_rr = [0]
            Pn = [sb(f"Pn{i}", [128, ST], F32, ph) for i in range(4)]
            ocmp, b_ocmp = sb("ocmp", [64, 4, ST], F32, ph)
            rec, b_rec = sb("rec", [128, ST], F32, ph)
            r2, b_r2 = sb("r2", [64, ST], F32, ph)
            oacc, b_oacc = sb("oacc", [64, ST], F32, ph)
            otm, b_otm = sb("otm", [64, ST], F32, ph)
            scq, b_scq = sb("scq", [128, 32], F32, ph)
            scq2, b_scq2 = sb("scq2", [128, 32], F32, ph)
            m8, b_m8 = sb("m8", [128, 8], F32, ph)
            S.dma(sp, Qaug[64:73, :, :], cin["qconst"][:, :, q0:q0 + ST], writes=[b_Q])
            for half in range(2):
                wv, b_w = load_w("w_nsa_in", [(half * 512, 512)], KC)
                for hh in range(8):
                    h = half * 8 + hh
                    pt, b_p = next_bank()
                    for k in range(KC):
                        S.op(pe, lambda e, k=k, hh=hh, pt=pt, wv=wv: e.matmul(pt[0:64, 0:ST], wv[:, k, hh * 64:(hh + 1) * 64], hT[:, k, 0:ST],
                                                                           start=(k == 0), stop=(k == KC - 1)),
                             reads=[b_w, b_hT], writes=[b_p], inc=(k == KC - 1))
                    S.op(act, lambda e, h=h, pt=pt: e.activation(Qaug[0:64, h, :], pt[0:64, 0:ST], AF.Copy, scale=DH ** -0.5),
                         reads=[b_p], writes=[b_Q])
            wg, b_wg = load_w("w_nsa_in", [(1024, 48)], KC)
            Gs, b_Gs = sb("Gs", [48, ST], F32, ph)
            dsb, b_dsb = sb("dsb", [64, ST], F32, ph)
            pgl, b_pgl = next_bank()
            for k in range(KC):
                S.op(pe, lambda e, k=k: e.matmul(pgl[0:48, 0:ST], wg[:, k, 0:48], hT[:, k, 0:ST], start=(k == 0), stop=(k == KC - 1)),
                     reads=[b_wg, b_hT], writes=[b_pgl], inc=(k == KC - 1))
            S.op(act, lambda e: e.activation(Gs[:], pgl[0:48, 0:ST], AF.Sigmoid), reads=[b_pgl], writes=[b_Gs])

            def gate(h, br):
                col = 3 * h + br
                pg, b_pg = next_bank()
                S.op(pe, lambda e, pg=pg: e.matmul(pg[0:64, 0:ST], ident_f[0:48, col:col + 1].broadcast_to([48, 64]), Gs[:],
                                                  start=True, stop=True), reads=[b_Gs, b_ident], writes=[b_pg])
                return pg, b_pg

            def score_tile(mms):
                ps_, b_ps = next_bank()
                for i, (l_, r_, rb) in enumerate(mms):
                    S.op(pe, lambda e, l_=l_, r_=r_, i=i, ps_=ps_: e.matmul(ps_[:, 0:ST], l_, r_, start=(i == 0), stop=(i == len(mms) - 1)),
                         reads=rb, writes=[b_ps], inc=(i == len(mms) - 1))
                PT, b_PT = PTs[pt_rr[0] % 4]
                pt_rr[0] += 1
                S.op(act, lambda e, PT=PT, ps_=ps_: e.activation(PT[:], ps_[:, 0:ST], AF.Exp), reads=[b_ps], writes=[b_PT])
                return PT, b_PT

            for g in range(G_B):
                for hi in range(4):
                    h = 4 * g + hi
                    PT, b_PT = score_tile([(NS["Kcmp"][0:73, g, :], Qaug[0:73, h, :], [b_K["Kcmp"], b_Q]),
                                           (ident_b[:], NS["Tcmp"][:, q0:q0 + ST], b_ctab + [b_K["ntab"]])])
                    px, b_px = next_bank()
                    S.op(pe, lambda e, PT=PT, px=px: e.matmul(px[:, 0:ST], ones_b[:], PT[:], start=True, stop=True),
                         reads=[b_PT, b_ones], writes=[b_px])
                    S.op(dve, lambda e, px=px: e.tensor_scalar(rec[:], px[:, 0:ST], 1e-30, None, op0=ALU.max), reads=[b_px], writes=[b_rec])
                    S.op(dve, lambda e: e.reciprocal(rec[:], rec[:]), reads=[b_rec], writes=[b_rec])
                    S.op(dve, lambda e, PT=PT, hi=hi: e.tensor_tensor(Pn[hi][0][:], PT[:], rec[:], op=ALU.mult),
                         reads=[b_PT, b_rec], writes=[Pn[hi][1]])
                    po_, b_po_ = next_bank()
                    S.op(pe, lambda e, PT=PT, po_=po_: e.matmul(po_[0:64, 0:ST], NS["Vcmp"][:, g, :], PT[:], start=True, stop=True),
                         reads=[b_PT, b_K["Vcmp"]], writes=[b_po_])
                    pg, b_pg = gate(h, 0)
                    S.op(dve, lambda e, pg=pg: e.tensor_tensor(r2[:], pg[0:64, 0:ST], rec[0:64, :], op=ALU.mult), reads=[b_rec, b_pg], writes=[b_r2])
                    S.op(dve, lambda e, hi=hi, po_=po_: e.tensor_tensor(ocmp[:, hi, :], po_[0:64, 0:ST], r2[:], op=ALU.mult),
                         reads=[b_po_, b_r2], writes=[b_ocmp])
                for tq in range(JT):
                    pi_, b_pi = next_bank()
                    for hi in range(4):
                        S.op(pe, lambda e, hi=hi, tq=tq, pi_=pi_: e.matmul(pi_[:, 0:32], Pn[hi][0][:, tq * 128:(tq + 1) * 128], ovl[:],
                                                                        start=(hi == 0), stop=(hi == 3)),
                             reads=[Pn[hi][1]] + b_ctab, writes=[b_pi], inc=(hi == 3))
                    S.op(dve, lambda e, tq=tq, pi_=pi_: e.tensor_tensor(scq[:], pi_[:, 0:32], NS["fb"][:, st * JT + tq, :], op=ALU.add),
                         reads=[b_pi, b_K["ntab"]], writes=[b_scq])
                    S.op(dve, lambda e: e.max(out=m8[:], in_=scq[:]), reads=[b_scq], writes=[b_m8])
                    S.op(dve, lambda e: e.match_replace(out=scq2[:], in_to_replace=m8[:], in_values=scq[:], imm_value=-1e30),
                         reads=[b_scq, b_m8], writes=[b_scq2])
                    S.op(dve, lambda e: e.max(out=m8[:], in_=scq2[:]), reads=[b_scq2], writes=[b_m8])
                    S.op(dve, lambda e: e.tensor_scalar(scq2[:], scq[:], m8[:, 7:8], None, op0=ALU.is_ge), reads=[b_scq, b_m8], writes=[b_scq2])
                    S.op(dve, lambda e: e.tensor_scalar(scq2[:], scq2[:], -NEG, NEG, op0=ALU.mult, op1=ALU.add), reads=[b_scq2], writes=[b_scq2])
                    pT_, b_pT = next_bank()
                    S.op(pe, lambda e, pT_=pT_: e.transpose(pT_[0:32, 0:128], scq2[:], ident_f[:]), reads=[b_scq2, b_ident], writes=[b_pT])
                    S.op(act, lambda e, tq=tq, pT_=pT_: e.activation(negT[:, tq * 128:(tq + 1) * 128], pT_[0:32, 0:128], AF.Copy),
                         reads=[b_pT], writes=[b_negT])
                work = []
                for hi in range(4):
                    h = 4 * g + hi
                    for bi, br in enumerate((1, 2)):
                        if br == 1:
                            tiles = list(range(0, (q0 + ST) // 128))
                            Kt, Vt, bK, bV = NS["Kslc"], NS["Vslc"], b_K["Kslc"], b_K["Vslc"]
                        else:
                            tiles = list(range(max(0, (q0 - 512) // 128), (q0 + ST) // 128))
                            Kt, Vt, bK, bV = NS["Kwin"], NS["Vwin"], b_K["Kwin"], b_K["Vwin"]
                        for idx, kt in enumerate(tiles):
                            k0 = kt * 128
                            mms = [(Kt[0:73, g, k0:k0 + 128], Qaug[0:73, h, :], [bK, b_Q])]
                            if br == 1:
                                mms.append((NS["Emat"][:, k0:k0 + 128], negT[:], [b_K["ntab"], b_negT]))
                            if k0 >= q0:
                                j = (k0 - q0) // 128
                                mms.append((ident_b[:], Tc[:, 128 * (JT - 1 - j): 128 * (JT - 1 - j) + ST], b_ctab))
                            elif br == 2 and k0 < q0 - 512 + 128 * JT:
                                j = (k0 - (q0 - 512)) // 128
                                mms.append((ident_b[:], Tl[:, 128 * (JT - 1 - j): 128 * (JT - 1 - j) + ST], b_ctab))
                            work.append(dict(mms=mms, V=Vt[:, kt, g, :], bV=bV, first=(idx == 0), last=(idx == len(tiles) - 1),
                                             h=h, hi=hi, br=br, bi=bi))

                def emit_pv(w, PT, b_PT):
                    (pso, b_pso), (psd, b_psd) = RBANKS[2 * (w["bi"] % 2)], RBANKS[2 * (w["bi"] % 2) + 1]
                    S.op(pe, lambda e: e.matmul(pso[0:64, 0:ST], w["V"], PT[:], start=w["first"], stop=w["last"]),
                         reads=[b_PT, w["bV"]], writes=[b_pso], inc=w["last"])
                    S.op(pe, lambda e: e.matmul(psd[0:64, 0:ST], ones_b[:, 0:64], PT[:], start=w["first"], stop=w["last"]),
                         reads=[b_PT, b_ones], writes=[b_psd], inc=w["last"])
                    if not w["last"]:
                        return
                    h, hi, br = w["h"], w["hi"], w["br"]
                    S.op(dve, lambda e: e.reciprocal(dsb[:], psd[0:64, 0:ST]), reads=[b_psd], writes=[b_dsb])
                    pg, b_pg = gate(h, br)
                    S.op(dve, lambda e: e.tensor_tensor(r2[:], pg[0:64, 0:ST], dsb[:], op=ALU.mult), reads=[b_dsb, b_pg], writes=[b_r2])
                    S.op(dve, lambda e: e.tensor_tensor(otm[:], pso[0:64, 0:ST], r2[:], op=ALU.mult),
                         reads=[b_pso, b_r2], writes=[b_otm])
                    if br == 1:
                        S.op(dve, lambda e: e.tensor_tensor(oacc[:], otm[:], ocmp[:, hi, :], op=ALU.add),
                             reads=[b_otm, b_ocmp], writes=[b_oacc])
                    else:
                        S.op(dve, lambda e: e.tensor_tensor(oTn[:, h, :], otm[:], oacc[:], op=ALU.add),
                             reads=[b_otm, b_oacc], writes=[b_oTn])

                DEPTH = 3
                pend = []
                for w in work:
                    PT, b_PT = score_tile(w["mms"])
                    pend.append((w, PT, b_PT))
                    if len(pend) > DEPTH:
                        emit_pv(*pend.pop(0))
                while pend:
                    emit_pv(*pend.pop(0))
            for half in range(2):
                wv, b_w = load_w("w_nsa_out", [(half * 512, 512)], 16, rows_per_k=64)
                for oc4 in range(4):
                    oc = half * 4 + oc4
                    pt, b_p = next_bank()
                    for h in range(16):
                        S.op(pe, lambda e, h=h, oc4=oc4, pt=pt, wv=wv: e.matmul(pt[:, 0:ST], wv[:, h, oc4 * 128:(oc4 + 1) * 128], oTn[:, h, :],
                                                                             start=(h == 0), stop=(h == 15)),
                             reads=[b_w, b_oTn], writes=[b_p], inc=(h == 15))
                    resid_add(oc, pt, ST, False, modv(1, 2), b_p)
            S.barrier()

    def nsa_sample():
        ovl, ident_b = ctab["ovl"][0], ctab["ident_b"][0]
        NROT[0] = 4
        NT = 128
        with ExitStack() as ph:
            norm_mod(NT, True, gmul[:, 1, 0], modv(1, 0), hT, b_hT, ph)
            S.barrier()
        with ExitStack() as ph:
            def tab(name, shape, dt):
                t, b = sb(name, shape, dt, ph)
                S.dma(sp, t[:], cin[name], writes=[b])
                return t, b
            NS["Emat"], _ = sb("EmatS", [32, SEQ], BF16, ph)
            S.dma(sp, NS["Emat"][:], cin["Emat"], writes=[b_K["ntab"]])
            TnewS, b_Tnew = tab("TnewS", [128, 16, 32], BF16)
            TlS, b_TlS = tab("TlS", [128, 32], BF16)
            TcS, b_TcS = tab("TcS", [128, 1], BF16)
            fbS, b_fbS = tab("fbS", [8, 33], F32)
            ovl33, b_ovl33 = tab("ovl33", [128, 33], F32)
            pidx, b_pidx = tab("pidx", [128, 1], F32)
            ptab, b_ptab = sb("ptab", [128, NSEQ_S * NPAGE], I32, ph)
            ptf, b_ptf = sb("ptf", [128, NSEQ_S * NPAGE], F32, ph)
            pgidx, b_pgidx = sb("pgidx", [128, NSEQ_S * NPAGE], I32, ph)
            S.dma(sp, ptab[:], page_table.rearrange("b j -> (b j)").partition_broadcast(128), writes=[b_ptab])
            S.op(dve, lambda e: e.tensor_copy(ptf[:], ptab[:]), reads=[b_ptab], writes=[b_ptf])
            S.op(dve, lambda e: e.tensor_scalar(ptf[:], ptf[:], 128.0, pidx[:, 0:1], op0=ALU.mult, op1=ALU.add),
                 reads=[b_ptf, b_pidx], writes=[b_ptf])
            S.op(dve, lambda e: e.tensor_copy(pgidx[:], ptf[:]), reads=[b_ptf], writes=[b_pgidx])
            b_tabs = [b_Tnew, b_TlS, b_TcS, b_fbS, b_ovl33, b_K["ntab"]] + b_ctab
            for nm, ncol in (("Kslc", SEQ), ("Kwin", 512), ("Kcmp", 128)):
                NS[nm], _ = sb(nm + "S", [128, G_B, ncol], BF16, ph)
            NS["Vslc"], _ = sb("VslcS", [128, NPAGE, G_B, DH], BF16, ph)
            NS["Vwin"], _ = sb("VwinS", [128, 4, G_B, DH], BF16, ph)
            NS["Vcmp"], _ = sb("VcmpS", [128, G_B, DH], BF16, ph)
            NS["szv"], _ = sb("szvS", [128, G_B, 128], BF16, ph)
            NS["kraw"], _ = sb("krawS", [128, G_B, 512 + 16], BF16, ph)
            NS["vraw"], _ = sb("vrawS", [128, G_B, 512 + 16], BF16, ph)
            NS["w1s"], _ = sb("w1s", [128, 2, 16, 128], BF16, ph)
            for i_, w1_ in enumerate((w_ck1, w_cv1)):
                S.dma(pool, NS["w1s"][:, i_], w1_.rearrange("(i q) n -> q i n", q=128), writes=[b_K["cmpw"]])
            szS = sb("szS", [128, 2, G_B * 32], BF16, ph)
            for g in range(G_B):
                S.dma(sp, NS["Kslc"][64:73, g, :], cin["kconst"], writes=[b_K["Kslc"]])
                S.dma(sp, NS["Kwin"][64:73, g, :], cin["kconst"][:, SEQ - 512:SEQ], writes=[b_K["Kwin"]])
                S.dma(sp, NS["Kcmp"][64:73, g, :], cin["kconst_cmp"], writes=[b_K["Kcmp"]])
            S.op(dve, lambda e: e.memset(NS["Kcmp"][0:64], 0.0), writes=[b_K["Kcmp"]])
            pages = [sb(f"page{i}", [128, 1024], F32, ph) for i in range(3)]
            wins = [sb(f"wint{i}", [128, 512], F32, ph) for i in range(1)]
            QaugS, b_Q = sb("QaugS", [128, NSEQ_S, 128], BF16, ph)
            obr = [sb(f"obr{i}", [64, 16, NT], F32, ph) for i in range(3)]
            negTS, b_negT = sb("negTS", [32, 128], BF16, ph)
            PTs = [sb(f"PTS{i}", [128, 256], BF16, ph) for i in range(4)]
            pt_rr = [0]
            Pn, b_Pn = sb("PnS", [128, 32], F32, ph)
            rec, b_rec = sb("recS", [128, 32], F32, ph)
            scq, b_scq = sb("scqS", [8, 33], F32, ph)
            scq2, b_scq2 = sb("scq2S", [8, 33], F32, ph)
            m8, b_m8 = sb("m8S", [8, 8], F32, ph)
            otm, b_otm = sb("otmS", [64, NT], F32, ph)
            oTn, b_oTn = sb("oTnS", [64, 16, NT], BF16, ph)
            for b in range(NSEQ_S):
                S.dma(sp, QaugS[64:73, b, :].rearrange("p (h t) -> p h t", t=DEC_SEQ), cin["qconst"][:, :, SEQ:SEQ + DEC_SEQ], writes=[b_Q])
            for half in range(2):
                wv, b_w = load_w("w_nsa_in", [(half * 512, 512)], KC)
                for hh in range(8):
                    h = half * 8 + hh
                    pt, b_p = next_bank()
                    for k in range(KC):
                        S.op(pe, lambda e, k=k, hh=hh, pt=pt, wv=wv: e.matmul(pt[0:64, 0:NT], wv[:, k, hh * 64:(hh + 1) * 64], hT[:, k, 0:NT],
                                                                           start=(k == 0), stop=(k == KC - 1)),
                             reads=[b_w, b_hT], writes=[b_p], inc=(k == KC - 1))
                    S.op(act, lambda e, h=h, pt=pt: e.activation(QaugS[0:64, :, h * 8:(h + 1) * 8], v3(pt[0:64, 0:NT]), AF.Copy, scale=DH ** -0.5),
                         reads=[b_p], writes=[b_Q])
            def score_chunk(tiles):
                ps_, b_ps = next_bank()
                n = len(tiles)
                for i, mms in enumerate(tiles):
                    for m, (l_, r_, rb) in enumerate(mms):
                        S.op(pe, lambda e, l_=l_, r_=r_, i=i, m=m, ps_=ps_, mms=mms: e.matmul(
                            ps_[:, i * 32:(i + 1) * 32], l_, r_, start=(m == 0), stop=(m == len(mms) - 1)),
                            reads=rb, writes=[b_ps], inc=(i == n - 1 and m == len(mms) - 1))
                PT, b_PT = PTs[pt_rr[0] % 4]
                pt_rr[0] += 1
                S.op(act, lambda e, PT=PT, ps_=ps_: e.activation(PT[:, 0:32 * n], ps_[:, 0:32 * n], AF.Exp), reads=[b_ps], writes=[b_PT])
                return PT, b_PT

            SPE = mybir.EngineType.SP
            for b in range(NSEQ_S):
                S.op(dve, lambda e: e.memset(NS["kraw"][:, :, 0:16], 0.0), writes=[b_K["kraw"]])
                S.op(dve, lambda e: e.memset(NS["vraw"][:, :, 0:16], 0.0), writes=[b_K["vraw"]])
                for j in range(NPAGE):
                    pg, b_pg = pages[(b * NPAGE + j) % 3]
                    cidx = b * NPAGE + j
                    S.dma_gather(pool, pg[:], cache_kv, pgidx[:, cidx:cidx + 1], reads=[b_pgidx], writes=[b_pg])
                    S.op(dve, lambda e, j=j, pg=pg: e.tensor_copy(NS["Vslc"][:, j], pg[:, 768:1024].rearrange("p (g d) -> p g d", g=G_B)),
                         reads=[b_pg], writes=[b_K["Vslc"]])
                    for wi, (base, name, c0) in enumerate(((0, "kraw", 16 + (j % 4) * 128), (256, "vraw", 16 + (j % 4) * 128), (512, "Kslc", j * 128))):
                        pT_, b_pT = next_bank()
                        for g in range(G_B):
                            S.op(pe, lambda e, g=g, base=base, pg=pg, pT_=pT_: e.transpose(pT_[0:64, g * 128:(g + 1) * 128],
                                                                                        pg[:, base + g * 64: base + (g + 1) * 64], ident_f[:]),
                                 reads=[b_pg, b_ident], writes=[b_pT], inc=(g == G_B - 1))
                        dst = NS[name][0:64, :, c0:c0 + 128]
                        src = pT_[0:64, :].rearrange("p (g n) -> p g n", g=G_B)
                        if wi == 1:
                            S.op(dve, lambda e, dst=dst, src=src: e.tensor_copy(dst, src), reads=[b_pT], writes=[b_K[name]])
                        else:
                            S.op(act, lambda e, dst=dst, src=src: e.activation(dst, src, AF.Copy), reads=[b_pT], writes=[b_K[name]])
                        if wi < 2:
                            dstu = NS[name][64:128, :, c0 - 1:c0 - 1 + 128]
                            if wi == 1:
                                S.op(act, lambda e, dstu=dstu, src=src: e.activation(dstu, src, AF.Copy), reads=[b_pT], writes=[b_K[name]])
                            else:
                                S.op(dve, lambda e, dstu=dstu, src=src: e.tensor_copy(dstu, src), reads=[b_pT], writes=[b_K[name]])
                    if j % 4 == 3:
                        compress(0, ntok=512, s0_=32 * (j // 4), sz_pre=szS)
                for tl in range(4):
                    wt_, b_wt = wins[0]
                    S.dma(sp, wt_[:], state_win[b, tl * 128:(tl + 1) * 128, :], writes=[b_wt])
                    S.op(dve, lambda e, tl=tl, wt_=wt_: e.tensor_copy(NS["Vwin"][:, tl], wt_[:, 256:512].rearrange("p (g d) -> p g d", g=G_B)),
                         reads=[b_wt], writes=[b_K["Vwin"]])
                    pT_, b_pT = next_bank()
                    for g in range(G_B):
                        S.op(pe, lambda e, g=g, wt_=wt_, pT_=pT_: e.transpose(pT_[0:64, g * 128:(g + 1) * 128], wt_[:, g * 64:(g + 1) * 64], ident_f[:]),
                             reads=[b_wt, b_ident], writes=[b_pT], inc=(g == G_B - 1))
                    S.op(act, lambda e, tl=tl, pT_=pT_: e.activation(NS["Kwin"][0:64, :, tl * 128:(tl + 1) * 128],
                                                                   pT_[0:64, :].rearrange("p (g n) -> p g n", g=G_B), AF.Copy),
                         reads=[b_pT], writes=[b_K["Kwin"]])
                bs = slice(b * DEC_SEQ, (b + 1) * DEC_SEQ)
                for g in range(G_B):
                    Qg = QaugS[0:73, b, g * 32:(g + 1) * 32]
                    gs = slice(4 * g, 4 * g + 4)
                    PT, b_PT = score_chunk([[(NS["Kcmp"][0:73, g, :], Qg, [b_K["Kcmp"], b_Q]),
                                             (ident_b[:], TcS[:, 0:1].broadcast_to([128, 32]), b_tabs)]])
                    px, b_px = next_bank()
                    S.op(pe, lambda e, PT=PT, px=px: e.matmul(px[:, 0:32], ones_b[:], PT[:, 0:32], start=True, stop=True),
                         reads=[b_PT, b_ones], writes=[b_px])
                    S.op(dve, lambda e, px=px: e.tensor_scalar(rec[:], px[:, 0:32], 1e-30, None, op0=ALU.max), reads=[b_px], writes=[b_rec])
                    S.op(dve, lambda e: e.reciprocal(rec[:], rec[:]), reads=[b_rec], writes=[b_rec])
                    S.op(dve, lambda e, PT=PT: e.tensor_tensor(Pn[:], PT[:, 0:32], rec[:], op=ALU.mult), reads=[b_PT, b_rec], writes=[b_Pn])
                    po_, b_po_ = next_bank()
                    S.op(pe, lambda e, PT=PT, po_=po_: e.matmul(po_[0:64, 0:32], NS["Vcmp"][:, g, :], PT[:, 0:32], start=True, stop=True),
                         reads=[b_PT, b_K["Vcmp"]], writes=[b_po_])
                    S.op(dve, lambda e, po_=po_: e.tensor_tensor(obr[0][0][:, gs, bs], po_[0:64, 0:32].rearrange("p (h t) -> p h t", t=DEC_SEQ),
                                                                rec[0:64, :].rearrange("p (h t) -> p h t", t=DEC_SEQ), op=ALU.mult),
                         reads=[b_po_, b_rec], writes=[obr[0][1]])
                    pi_, b_pi = next_bank()
                    for hi in range(4):
                        S.op(pe, lambda e, hi=hi, pi_=pi_: e.matmul(pi_[0:8, 0:33], Pn[:, hi * 8:(hi + 1) * 8], ovl33[:], start=(hi == 0), stop=(hi == 3)),
                             reads=[b_Pn] + b_tabs, writes=[b_pi], inc=(hi == 3))
                    S.op(dve, lambda e, pi_=pi_: e.tensor_tensor(scq[:], pi_[0:8, 0:33], fbS[:], op=ALU.add), reads=[b_pi] + b_tabs, writes=[b_scq])
                    S.op(dve, lambda e: e.max(out=m8[:], in_=scq[:]), reads=[b_scq], writes=[b_m8])
                    S.op(dve, lambda e: e.match_replace(out=scq2[:], in_to_replace=m8[:], in_values=scq[:], imm_value=-1e30),
                         reads=[b_scq, b_m8], writes=[b_scq2])
                    S.op(dve, lambda e: e.max(out=m8[:], in_=scq2[:]), reads=[b_scq2], writes=[b_m8])
                    S.op(dve, lambda e: e.tensor_scalar(scq2[:], scq[:], m8[:, 7:8], None, op0=ALU.is_ge), reads=[b_scq, b_m8], writes=[b_scq2])
                    S.op(dve, lambda e: e.tensor_scalar(scq2[:], scq2[:], -NEG, NEG, op0=ALU.mult, op1=ALU.add), reads=[b_scq2], writes=[b_scq2])
                    pT_, b_pT = next_bank()
                    S.op(pe, lambda e, pT_=pT_: e.transpose(pT_[0:32, 0:8], scq2[:, 0:32], ident_f[0:8, 0:8]), reads=[b_scq2, b_ident], writes=[b_pT])
                    S.op(act, lambda e, pT_=pT_, g=g: e.activation(negTS[:, g * 32:(g + 1) * 32].rearrange("p (h t) -> p h t", t=DEC_SEQ),
                                                                 pT_[0:32, 0:8].unsqueeze(1).broadcast_to([32, 4, DEC_SEQ]), AF.Copy),
                         reads=[b_pT], writes=[b_negT])
                    pend_br = []
                    for bi, br in enumerate((1, 2)):
                        (pso, b_pso), (psd, b_psd) = RBANKS[2 * bi], RBANKS[2 * bi + 1]
                        tl_list = []
                        if br == 1:
                            for kt in range(NPAGE):
                                tl_list.append(([(NS["Kslc"][0:73, g, kt * 128:(kt + 1) * 128], Qg, [b_K["Kslc"], b_Q]),
                                                 (NS["Emat"][:, kt * 128:(kt + 1) * 128], negTS[:, g * 32:(g + 1) * 32], [b_K["ntab"], b_negT])],
                                                NS["Vslc"][:, kt, g, :], b_K["Vslc"]))
                            tl_list.append(([(NS["KslcN"][0:73, g, :], Qg, [b_K["KslcN"], b_Q]), (ident_b[:], TnewS[:, b, :], b_tabs)],
                                            NS["VslcN"][:, g, :], b_K["VslcN"]))
                        else:
                            for kt in range(4):
                                mm_ = [(NS["Kwin"][0:73, g, kt * 128:(kt + 1) * 128], Qg, [b_K["Kwin"], b_Q])]
                                if kt == 0:
                                    mm_.append((ident_b[:], TlS[:], b_tabs))
                                tl_list.append((mm_, NS["Vwin"][:, kt, g, :], b_K["Vwin"]))
                            tl_list.append(([(NS["KwinN"][0:73, g, :], Qg, [b_K["KwinN"], b_Q]), (ident_b[:], TnewS[:, b, :], b_tabs)],
                                            NS["VwinN"][:, g, :], b_K["VwinN"]))
                        ntl = len(tl_list)
                        chunks = []
                        for c0 in range(0, ntl, 8):
                            chunk = tl_list[c0:c0 + 8]
                            PT, b_PT = score_chunk([c[0] for c in chunk])
                            chunks.append((chunk, PT, b_PT))
                        pend_br.append((br, pso, b_pso, psd, b_psd, ntl, chunks))
                    for (br, pso, b_pso, psd, b_psd, ntl, chunks) in pend_br:
                        done = 0
                        for chunk, PT, b_PT in chunks:
                            for i, (_, Vap, bV) in enumerate(chunk):
                                first, last = (done == 0), (done == ntl - 1)
                                S.op(pe, lambda e, PT=PT, i=i, Vap=Vap, first=first, last=last, pso=pso: e.matmul(
                                    pso[0:64, 0:32], Vap, PT[:, i * 32:(i + 1) * 32], start=first, stop=last),
                                    reads=[b_PT, bV], writes=[b_pso], inc=last)
                                S.op(pe, lambda e, PT=PT, i=i, first=first, last=last, psd=psd: e.matmul(
                                    psd[0:64, 0:32], ones_b[:, 0:64], PT[:, i * 32:(i + 1) * 32], start=first, stop=last),
                                    reads=[b_PT, b_ones], writes=[b_psd], inc=last)
                                done += 1
                        S.op(dve, lambda e, psd=psd: e.reciprocal(rec[0:64, :], psd[0:64, 0:32]), reads=[b_psd], writes=[b_rec])
                        S.op(dve, lambda e, pso=pso, br=br: e.tensor_tensor(obr[br][0][:, gs, bs], pso[0:64, 0:32].rearrange("p (h t) -> p h t", t=DEC_SEQ),
                                                                          rec[0:64, :].rearrange("p (h t) -> p h t", t=DEC_SEQ), op=ALU.mult),
                             reads=[b_pso, b_rec], writes=[obr[br][1]])
            wg, b_wg = load_w("w_nsa_in", [(1024, 48)], KC)
            GsS, b_GsS = sb("GsS", [48, NT], F32, ph)
            pgl, b_pgl = next_bank()
            for k in range(KC):
                S.op(pe, lambda e, k=k: e.matmul(pgl[0:48, 0:NT], wg[:, k, 0:48], hT[:, k, 0:NT], start=(k == 0), stop=(k == KC - 1)),
                     reads=[b_wg, b_hT], writes=[b_pgl], inc=(k == KC - 1))
            S.op(act, lambda e: e.activation(GsS[:], pgl[0:48, 0:NT], AF.Sigmoid), reads=[b_pgl], writes=[b_GsS])
            for h in range(16):
                for br in range(3):
                    col = 3 * h + br
                    pg_, b_pg_ = next_bank()
                    S.op(pe, lambda e, pg_=pg_, col=col: e.matmul(pg_[0:64, 0:NT], ident_f[0:48, col:col + 1].broadcast_to([48, 64]), GsS[:],
                                                               start=True, stop=True), reads=[b_GsS, b_ident], writes=[b_pg_])
                    if br == 0:
                        S.op(dve, lambda e, h=h, pg_=pg_: e.tensor_tensor(obr[0][0][:, h, :], obr[0][0][:, h, :], pg_[0:64, 0:NT], op=ALU.mult),
                             reads=[obr[0][1], b_pg_], writes=[obr[0][1]])
                    else:
                        S.op(dve, lambda e, h=h, br=br, pg_=pg_: e.tensor_tensor(otm[:], obr[br][0][:, h, :], pg_[0:64, 0:NT], op=ALU.mult),
                             reads=[obr[br][1], b_pg_], writes=[b_otm])
                        if br == 1:
                            S.op(dve, lambda e, h=h: e.tensor_tensor(obr[0][0][:, h, :], obr[0][0][:, h, :], otm[:], op=ALU.add),
                                 reads=[obr[0][1], b_otm], writes=[obr[0][1]])
                        else:
                            S.op(dve, lambda e, h=h: e.tensor_tensor(oTn[:, h, :], obr[0][0][:, h, :], otm[:], op=ALU.add),
                                 reads=[obr[0][1], b_otm], writes=[b_oTn])
            for half in range(2):
                wv, b_w = load_w("w_nsa_out", [(half * 512, 512)], 16, rows_per_k=64)
                for oc4 in range(4):
                    oc = half * 4 + oc4
                    pt, b_p = next_bank()
                    for h in range(16):
                        S.op(pe, lambda e, h=h, oc4=oc4, pt=pt, wv=wv: e.matmul(pt[:, 0:NT], wv[:, h, oc4 * 128:(oc4 + 1) * 128], oTn[:, h, :],
                                                                             start=(h == 0), stop=(h == 15)),
                             reads=[b_w, b_oTn], writes=[b_p], inc=(h == 15))
                    resid_add(oc, pt, NT, True, modv(1, 2), b_p)
            S.barrier()
            NS.pop("w1s", None)

    def final_norm_store(dst_rows, ntok, sample):
        with ExitStack() as ph:
            yT, b_yT = sb("yT", [128, KC, ntok], F32, ph)
            norm_mod(ntok, sample, gfin, zero17, yT, b_yT, ph)
            store_yT(dst_rows, ntok, yT, b_yT)

    if DEV["prompt"]:
        with ExitStack() as pst:
            P["Sst"], _ = sb("Sst", [128, H_A, 2, DV_A], F32, pst)
            P["Sbf"], _ = sb("Sbf", [128, H_A, 2, DV_A], BF16, pst)
            S.op(dve, lambda e: e.memset(P["Sst"][:], 0.0), writes=[b_Sst])
            nsa_on = DEV.get("nsa", True)
            if nsa_on:
                for nm in ("Kslc", "Kwin"):
                    NS[nm], _ = sb(nm, [128, G_B, SEQ], BF16, pst)
                NS["Kcmp"], _ = sb("Kcmp", [128, G_B, 128], BF16, pst)
                NS["Vslc"], _ = sb("Vslc", [128, SEQ // 128, G_B, DH], BF16, pst)
                NS["Vwin"], _ = sb("Vwin", [128, SEQ // 128, G_B, DH], BF16, pst)
                NS["Vcmp"], _ = sb("Vcmp", [128, G_B, DH], BF16, pst)
                NS["kraw"], _ = sb("kraw", [128, G_B, ST + 16], BF16, pst)
                NS["vraw"], _ = sb("vraw", [128, G_B, ST + 16], BF16, pst)
                NS["szv"], _ = sb("szv", [128, G_B, 128], BF16, pst)
                NS["w2"], _ = sb("w2", [128, 2, DH], BF16, pst)
                NS["bz"], _ = sb("bz", [128, 2], F32, pst)
                NS["Tcmp"], _ = sb("Tcmp", [128, SEQ], BF16, pst)
                NS["Emat"], _ = sb("Emat", [32, SEQ], BF16, pst)
                NS["fb"], _ = sb("fb", [128, SEQ // 128, 32], F32, pst)
                for nm in ("Tcmp", "Emat", "fb"):
                    S.dma(sp, NS[nm][:], cin[nm], writes=[b_K["ntab"]])
                for nm in ("Kslc", "Kwin"):
                    for g in range(G_B):
                        S.dma(sp, NS[nm][64:73, g, :], cin["kconst"], writes=[b_K[nm]])
                for g in range(G_B):
                    S.dma(sp, NS["Kcmp"][64:73, g, :], cin["kconst_cmp"], writes=[b_K["Kcmp"]])
                S.op(dve, lambda e: e.memset(NS["Kcmp"][0:64], 0.0), writes=[b_K["Kcmp"]])
                S.op(dve, lambda e: e.memset(NS["szv"][:], 0.0), writes=[b_K["szv"]])
                S.op(dve, lambda e: e.memset(NS["kraw"][:], 0.0), writes=[b_K["kraw"]])
                S.op(dve, lambda e: e.memset(NS["vraw"][:], 0.0), writes=[b_K["vraw"]])
                compress_setup()
            for st in range(NST):
                load_xT(x_p[st * ST:(st + 1) * ST, :], ST)
                retention(st, False)
                ffn(0, st, st == NST - 1, False)
                kv_proj(st, False)
                if nsa_on:
                    compress(st)
                    nsa_prompt(st)
                ffn(1, st, st == NST - 1, False)
                final_norm_store(y_p[st * ST:(st + 1) * ST, :], ST, False)
            S.dma(sp, ret_p.rearrange("h (c p) v -> p h c v", p=128), P["Sst"][:], reads=[b_Sst])
            S.barrier()
    if DEV["sample"]:
        with ExitStack() as sst_:
            P["cbufT"], _ = sb("cbufT", [128, 2, NFC, 2 * NSEQ_S], F32, sst_)
            nsa_on = DEV.get("nsa", True)
            if nsa_on:
                NS.clear()
                NS["w2"], _ = sb("w2S", [128, 2, DH], BF16, sst_)
                NS["bz"], _ = sb("bzS", [128, 2], F32, sst_)
                for nm in ("KslcN", "KwinN"):
                    NS[nm], _ = sb(nm, [128, G_B, 128], BF16, sst_)
                    for g in range(G_B):
                        S.dma(sp, NS[nm][64:73, g, :], cin["kconst_new"], writes=[b_K[nm]])
                NS["VslcN"], _ = sb("VslcN", [128, G_B, DH], BF16, sst_)
                NS["VwinN"], _ = sb("VwinN", [128, G_B, DH], BF16, sst_)
                compress_setup()
            load_conv_bufs()
            load_xT(x_s, 128)
            retention(0, True)
            ffn(0, 0, False, True)
            kv_proj(0, True)
            if nsa_on:
                nsa_sample()
            ffn(1, 0, False, True)
            final_norm_store(y_s, 128, True)
            S.barrier()
    S.finish()


_NC_CACHE = {}


def kernel(**inputs):
    f32 = lambda a: np.ascontiguousarray(np.asarray(a, dtype=np.float32))
    if "nc" not in _NC_CACHE:
        _NC_CACHE["nc"] = build_nc()
    nc = _NC_CACHE["nc"]
    shared = {k: f32(inputs[k]) for k in ["w_ada", "b_ada", "g_mix", "g_ffn", "w_ffn_in", "conv_w", "conv_b",
                                           "w_ffn_out", "w_ret_in", "w_ret_out", "g_final",
                                           "g_kv", "w_ada_kv", "b_ada_kv", "w_kv", "w_nsa_in", "w_nsa_out",
                                           "pe_ck", "pe_cv", "w_ck1", "w_ck2", "w_cv1", "w_cv2"]}
    for k in CONST_IN + CONST_DRAM:
        shared["k_" + k] = np.ascontiguousarray(CONST[k])
    xp, xs = f32(inputs["x_prompt"]), f32(inputs["x_sample"])
    cp, cs = f32(inputs["c_prompt"]), f32(inputs["c_sample"])
    sret, sconv, swin = f32(inputs["state_ret"]), f32(inputs["state_conv"]), f32(inputs["state_win"])
    cache = f32(inputs["cache_kv"]).reshape(NPHYS * 128, 1024)
    ptab = np.ascontiguousarray(np.asarray(inputs["page_table"], dtype=np.int32))
    cores = DEV["cores"] or list(range(N_CORES))
    in_maps = []
    for c in cores:
        m = dict(shared)
        sl = slice(c * NSEQ_S, (c + 1) * NSEQ_S)
        m["x_p"] = xp[c]
        m["x_s"] = xs[sl].reshape(NSEQ_S * DEC_SEQ, D)
        m["c_p"] = cp[c:c + 1]
        m["c_s"] = cs[sl]
        m["state_ret"] = sret[0, sl]
        m["state_conv"] = np.ascontiguousarray(sconv[:, sl]).reshape(2, NSEQ_S * 2, F2)
        m["state_win"] = swin[sl].reshape(NSEQ_S, 512, 512)
        m["cache_kv"] = cache
        m["page_table"] = ptab[sl]
        in_maps.append(m)
    if DEV.get("trace"):
        res = run_bass_kernel_spmd(nc, in_maps, core_ids=list(range(len(cores))), trace=True)
        print("DEV exec_time_ns:", res.exec_time_ns)
    else:
        res = run_bass_kernel_spmd(nc, in_maps, core_ids=list(range(len(cores))))
    R = list(res.results)
    if len(R) < N_CORES:
        full = [None] * N_CORES
        for c, r in zip(cores, R):
            full[c] = r
        z = {k: np.zeros_like(np.asarray(v)) for k, v in R[0].items()}
        R = [r if r is not None else z for r in full]
    cat = lambda k: np.stack([np.asarray(r[k], dtype=np.float32) for r in R])
    y_prompt = cat("y_p")
    y_sample = cat("y_s").reshape(128, DEC_SEQ, D)
    ret_prompt = cat("ret_p")[None]
    ret_sample = cat("ret_s").reshape(1, 128, H_A, DK_A, DV_A)
    conv_prompt = np.ascontiguousarray(cat("conv_p").transpose(1, 0, 2, 3))
    conv_sample = np.ascontiguousarray(cat("conv_s").transpose(1, 0, 2, 3, 4)).reshape(2, 128, 2, F2)
    kv_prompt = cat("kv_p").reshape(8, SEQ, 4, 4, 64)
    kv_sample = cat("kv_s").reshape(128, DEC_SEQ, 4, 4, 64)
    win_prompt = cat("win_p").reshape(8, 512, 2, 4, 64)
    win_sample = cat("win_s").reshape(128, 512, 2, 4, 64)
    return (y_prompt, y_sample, ret_prompt, ret_sample, conv_prompt, conv_sample,
            kv_prompt, kv_sample, win_prompt, win_sample)
```
